# Optimizing a Trainium2 kernel written in Bass

```python
import jax, jax.numpy as jnp
from jax import lax
import numpy as np

D_MODEL = 2048
BATCH = 16
SEQ = 2048
DEPTH = 2

GRID_W = 64
CTX_LEN = 256
EPS = 1e-6

NA_HEADS = 8
NA_HEAD_DIM = 128
WIN_R = 8
WIN_C = 16

MLA_HEADS = 8
MLA_Q_RANK = 512
MLA_KV_RANK = 512
MLA_NOPE_DIM = 128
MLA_ROPE_DIM = 64
MLA_V_DIM = 128
MLA_SCALE = (MLA_NOPE_DIM + MLA_ROPE_DIM) ** -0.5
Q_BLOCK = 128
ROPE_BASE = 10000.0

DN_QK_HEADS = 16
DN_V_HEADS = 32
DN_K_DIM = 128
DN_V_DIM = 128
CONV_K = 5
CHUNK = 64

NA_WIDTH = NA_HEADS * NA_HEAD_DIM
MLA_WIDTH = MLA_HEADS * MLA_V_DIM
AB_MIX_WIDTH = NA_WIDTH + MLA_WIDTH
AB_SPLITS = (NA_WIDTH, NA_WIDTH, NA_WIDTH, MLA_Q_RANK, MLA_KV_RANK, MLA_ROPE_DIM, AB_MIX_WIDTH)
AB_IN_WIDTH = sum(AB_SPLITS)
DN_QK_WIDTH = DN_QK_HEADS * DN_K_DIM
DN_MIX_WIDTH = DN_V_HEADS * DN_V_DIM
DN_CONV_WIDTH = 2 * DN_QK_WIDTH + DN_MIX_WIDTH
DN_SPLITS = (DN_CONV_WIDTH, DN_MIX_WIDTH, 2 * DN_V_HEADS, 2 * DN_V_HEADS)
DN_IN_WIDTH = sum(DN_SPLITS)

F32 = jnp.float32

kernel_name = 'hybrid_na_mla_gdn_prefix_trunk'


def split_cols(t, sizes):
    return jnp.split(t, np.cumsum(sizes)[:-1].tolist(), axis=-1)


def rmsnorm(x, gain):
    xf = x.astype(F32)
    y = xf * lax.rsqrt(jnp.mean(xf * xf, axis=-1, keepdims=True) + EPS)
    return (y * gain.astype(F32)).astype(x.dtype)


def l2norm(x):
    xf = x.astype(F32)
    return (xf * lax.rsqrt(jnp.sum(xf * xf, axis=-1, keepdims=True) + EPS)).astype(x.dtype)


def adaln_params(cond, w_mod, b_mod):
    return jnp.split(jax.nn.silu(cond) @ w_mod + b_mod, 3, axis=-1)


def modulate(x, gain, shift, scale):
    return rmsnorm(x, gain) * (1 + scale) + shift


def to_heads(t, n_heads):
    b, t_len, _ = t.shape
    return t.reshape(b, t_len, n_heads, -1).transpose(0, 2, 1, 3)


def merge_heads(o):
    b, h, t_len, d = o.shape
    return o.transpose(0, 2, 1, 3).reshape(b, t_len, h * d)


def rope_2d(x):
    t_len, r_dim = x.shape[-2], x.shape[-1]
    half = r_dim // 2
    quarter = half // 2
    pos = jnp.arange(t_len)
    inv = ROPE_BASE ** (-jnp.arange(quarter, dtype=F32) / quarter)

    def rot(xa, p):
        ang = p.astype(F32)[:, None] * inv[None, :]
        cos, sin = jnp.cos(ang), jnp.sin(ang)
        x1, x2 = xa[..., :quarter], xa[..., quarter:]
        return jnp.concatenate([x1 * cos - x2 * sin, x1 * sin + x2 * cos], axis=-1)

    xf = x.astype(F32)
    out = jnp.concatenate([rot(xf[..., :half], pos // GRID_W), rot(xf[..., half:], pos % GRID_W)], axis=-1)
    return out.astype(x.dtype)


def joint_softmax(s_a, s_b, dtype):
    m = jnp.maximum(s_a.max(-1, keepdims=True), s_b.max(-1, keepdims=True))
    e_a = jnp.exp(s_a - m)
    e_b = jnp.exp(s_b - m)
    inv = 1.0 / (e_a.sum(-1, keepdims=True) + e_b.sum(-1, keepdims=True))
    return (e_a * inv).astype(dtype), (e_b * inv).astype(dtype)


def softmax_attend(q, k, v, scale):
    s = jnp.einsum('bhqd,bhkd->bhqk', q, k).astype(F32) * scale
    p = jax.nn.softmax(s, axis=-1).astype(v.dtype)
    return jnp.einsum('bhqk,bhkd->bhqd', p, v)


def neighbourhood_index(rows):
    kr = min(WIN_R, rows)
    r = jnp.arange(rows)
    col = jnp.arange(GRID_W)
    r0 = jnp.clip(r - kr // 2, 0, rows - kr)
    c0 = jnp.clip(col - WIN_C // 2, 0, GRID_W - WIN_C)
    key_rows = r0[:, None] + jnp.arange(kr)
    key_cols = c0[:, None] + jnp.arange(WIN_C)
    idx = key_rows[:, None, :, None] * GRID_W + key_cols[None, :, None, :]
    idx = idx.reshape(rows, GRID_W, kr * WIN_C)
    d_row = key_rows - r[:, None] + (WIN_R - 1)
    d_col = key_cols - col[:, None] + (WIN_C - 1)
    return idx, d_row, d_col


def neighbourhood_attention(q, k, v, k_ctx, v_ctx, rpb):
    b, h, s_len, d = q.shape
    rows = s_len // GRID_W
    idx, d_row, d_col = neighbourhood_index(rows)
    scale = d ** -0.5
    q_rows = q.reshape(b, h, rows, GRID_W, d).transpose(2, 0, 1, 3, 4)

    def row_block(args):
        q_i, idx_i, dr_i = args
        k_g = jnp.take(k, idx_i, axis=2)
        v_g = jnp.take(v, idx_i, axis=2)
        bias = rpb[:, dr_i[None, :, None], d_col[:, None, :]].reshape(h, GRID_W, -1)
        s_nb = jnp.einsum('bhqd,bhqkd->bhqk', q_i, k_g).astype(F32) * scale + bias.astype(F32)
        s_cx = jnp.einsum('bhqd,bhkd->bhqk', q_i, k_ctx).astype(F32) * scale
        p_nb, p_cx = joint_softmax(s_nb, s_cx, v.dtype)
        return jnp.einsum('bhqk,bhqkd->bhqd', p_nb, v_g) + jnp.einsum('bhqk,bhkd->bhqd', p_cx, v_ctx)

    o = lax.map(row_block, (q_rows, idx, d_row))
    return o.transpose(1, 2, 0, 3, 4).reshape(b, h, s_len, d)


def mla_latent_attention(qn, qr, kn, kr, v, kn_c, kr_c, v_c):
    b, h, s_len, _ = qn.shape
    nb = s_len // Q_BLOCK
    blocks = lambda t: t.reshape(b, h, nb, Q_BLOCK, t.shape[-1]).transpose(2, 0, 1, 3, 4)

    def attend(args):
        qn_i, qr_i = args
        s_lat = (jnp.einsum('bhqd,bhkd->bhqk', qn_i, kn) + jnp.einsum('bhqd,bkd->bhqk', qr_i, kr)).astype(F32) * MLA_SCALE
        s_ctx = (jnp.einsum('bhqd,bhkd->bhqk', qn_i, kn_c) + jnp.einsum('bhqd,bkd->bhqk', qr_i, kr_c)).astype(F32) * MLA_SCALE
        p_lat, p_ctx = joint_softmax(s_lat, s_ctx, v.dtype)
        return jnp.einsum('bhqk,bhkd->bhqd', p_lat, v) + jnp.einsum('bhqk,bhkd->bhqd', p_ctx, v_c)

    o = lax.map(attend, (blocks(qn), blocks(qr)))
    return o.transpose(1, 2, 0, 3, 4).reshape(b, h, s_len, -1)


def mla_context_attention(qn, qr, kn, kr, v):
    s = (jnp.einsum('bhqd,bhkd->bhqk', qn, kn) + jnp.einsum('bhqd,bkd->bhqk', qr, kr)).astype(F32) * MLA_SCALE
    p = jax.nn.softmax(s, axis=-1).astype(v.dtype)
    return jnp.einsum('bhqk,bhkd->bhqd', p, v)


def na_mla_layer(x, ctx, c, c_ctx, w_mod, b_mod, norm, w_in, rpb, q_norm, w_qb, kv_norm, w_kvb, w_out, update_ctx):
    sh, sc, gt = adaln_params(c, w_mod, b_mod)
    sh_c, sc_c, gt_c = adaln_params(c_ctx, w_mod, b_mod)
    hx = modulate(x, norm, sh[:, None], sc[:, None]) @ w_in
    hc = modulate(ctx, norm, sh_c, sc_c) @ w_in
    qa, ka, va, cq, ckv, kr, z = split_cols(hx, AB_SPLITS)
    qa_c, ka_c, va_c, cq_c, ckv_c, kr_c, z_c = split_cols(hc, AB_SPLITS)

    def mla_q(cq_t):
        q = to_heads(rmsnorm(cq_t, q_norm) @ w_qb, MLA_HEADS)
        return q[..., :MLA_NOPE_DIM], q[..., MLA_NOPE_DIM:]

    def mla_kv(ckv_t):
        kv = to_heads(rmsnorm(ckv_t, kv_norm) @ w_kvb, MLA_HEADS)
        return kv[..., :MLA_NOPE_DIM], kv[..., MLA_NOPE_DIM:]

    ka_ch, va_ch = to_heads(ka_c, NA_HEADS), to_heads(va_c, NA_HEADS)
    o_na = neighbourhood_attention(to_heads(qa, NA_HEADS), to_heads(ka, NA_HEADS), to_heads(va, NA_HEADS), ka_ch, va_ch, rpb)

    qn, qr = mla_q(cq)
    kn, v = mla_kv(ckv)
    kn_c, v_c = mla_kv(ckv_c)
    o_mla = mla_latent_attention(qn, rope_2d(qr), kn, v=v, kr=rope_2d(kr), kn_c=kn_c, kr_c=kr_c, v_c=v_c)

    y = (jnp.concatenate([merge_heads(o_na), merge_heads(o_mla)], axis=-1) * jax.nn.silu(z)) @ w_out
    x = x + gt[:, None] * y
    if update_ctx:
        o_na_c = softmax_attend(to_heads(qa_c, NA_HEADS), ka_ch, va_ch, NA_HEAD_DIM ** -0.5)
        qn_c, qr_c = mla_q(cq_c)
        o_mla_c = mla_context_attention(qn_c, qr_c, kn_c, kr_c, v_c)
        y_c = (jnp.concatenate([merge_heads(o_na_c), merge_heads(o_mla_c)], axis=-1) * jax.nn.silu(z_c)) @ w_out
        ctx = ctx + gt_c * y_c
    return x, ctx


def depthwise_conv(x, w):
    ch = x.shape[-1]
    return lax.conv_general_dilated(x, w[:, None, :], window_strides=(1,), padding=[(CONV_K // 2, CONV_K // 2)],
                                    dimension_numbers=('NWC', 'WIO', 'NWC'), feature_group_count=ch)


def dn_qk_heads(t, scale):
    b, t_len, _ = t.shape
    t = l2norm(t.reshape(b, t_len, DN_QK_HEADS, DN_K_DIM)) * scale
    return jnp.repeat(t, DN_V_HEADS // DN_QK_HEADS, axis=2).transpose(0, 2, 1, 3)


def deltanet_inputs(h, conv_w, a_log, dt_bias):
    b, t_len, _ = h.shape
    qkv, z, b_raw, a_raw = split_cols(h, DN_SPLITS)
    qkv = jax.nn.silu(depthwise_conv(qkv, conv_w))
    q, k, v = split_cols(qkv, (DN_QK_WIDTH, DN_QK_WIDTH, DN_MIX_WIDTH))
    q = dn_qk_heads(q, DN_K_DIM ** -0.5)
    k = dn_qk_heads(k, 1.0)
    v = to_heads(v, DN_V_HEADS)
    dirs = lambda t: t.astype(F32).reshape(b, t_len, 2, DN_V_HEADS).transpose(2, 0, 3, 1)
    beta = jax.nn.sigmoid(dirs(b_raw))
    g = -jnp.exp(a_log.astype(F32))[:, None, :, None] * jax.nn.softplus(dirs(a_raw) + dt_bias.astype(F32)[:, None, :, None])
    return q, k, v, z, g, beta


def gated_delta_chunked(q, k, v, g, beta, state0):
    b, h, t_len, _ = q.shape
    dv = v.shape[-1]
    n = t_len // CHUNK
    chunks = lambda t: t.astype(F32).reshape(b, h, n, CHUNK, *t.shape[3:])
    qf, kf, vf, gf, bf = map(chunks, (q, k, v, g, beta))
    gf = jnp.cumsum(gf, axis=-1)
    lower = jnp.tril(jnp.ones((CHUNK, CHUNK), bool))
    strict = jnp.tril(jnp.ones((CHUNK, CHUNK), bool), -1)
    diff = gf[..., :, None] - gf[..., None, :]
    decay = jnp.where(lower, jnp.exp(jnp.where(lower, diff, 0.0)), 0.0)
    kb = kf * bf[..., None]
    lmat = jnp.where(strict, jnp.einsum('bhncd,bhnsd->bhncs', kb, kf) * decay, 0.0)
    rhs = jnp.concatenate([vf * bf[..., None], kb * jnp.exp(gf)[..., None]], axis=-1)
    sol = lax.linalg.triangular_solve(lmat + jnp.eye(CHUNK, dtype=F32), rhs, left_side=True, lower=True)
    u, w = sol[..., :dv], sol[..., dv:]
    intra = jnp.where(lower, jnp.einsum('bhncd,bhnsd->bhncs', qf, kf) * decay, 0.0)

    def step(state, xs):
        q_i, k_i, u_i, w_i, g_i, a_i = xs
        v_new = u_i - jnp.einsum('bhcd,bhde->bhce', w_i, state)
        o_i = jnp.einsum('bhcd,bhde->bhce', q_i * jnp.exp(g_i)[..., None], state) + jnp.einsum('bhcs,bhse->bhce', a_i, v_new)
        g_last = g_i[..., -1:]
        state = state * jnp.exp(g_last)[..., None] + jnp.einsum('bhcd,bhce->bhde', k_i * jnp.exp(g_last - g_i)[..., None], v_new)
        return state, o_i

    xs = tuple(jnp.moveaxis(t, 2, 0) for t in (qf, kf, u, w, gf, intra))
    state, o = lax.scan(step, state0, xs)
    o = jnp.moveaxis(o, 0, 2).reshape(b, h, t_len, dv)
    return o.astype(v.dtype), state


def flip_t(t):
    return jnp.flip(t, axis=2)


def deltanet_out(o, z, o_norm, w_out):
    b, t_len = z.shape[:2]
    o = rmsnorm(o.transpose(0, 2, 1, 3), o_norm)
    return (o.reshape(b, t_len, DN_MIX_WIDTH) * jax.nn.silu(z)) @ w_out


def deltanet_layer(x, ctx, c, c_ctx, w_mod, b_mod, norm, w_in, conv_w, a_log, dt_bias, o_norm, w_out, update_ctx):
    sh, sc, gt = adaln_params(c, w_mod, b_mod)
    sh_c, sc_c, gt_c = adaln_params(c_ctx, w_mod, b_mod)
    qx, kx, vx, zx, gx, bx = deltanet_inputs(modulate(x, norm, sh[:, None], sc[:, None]) @ w_in, conv_w, a_log, dt_bias)
    qc, kc, vc, zc, gc, bc = deltanet_inputs(modulate(ctx, norm, sh_c, sc_c) @ w_in, conv_w, a_log, dt_bias)
    zero = jnp.zeros((ctx.shape[0], DN_V_HEADS, DN_K_DIM, DN_V_DIM), F32)
    o_cf, s_cf = gated_delta_chunked(qc, kc, vc, gc[0], bc[0], zero)
    o_xf, _ = gated_delta_chunked(qx, kx, vx, gx[0], bx[0], s_cf)
    o_cb, s_cb = gated_delta_chunked(flip_t(qc), flip_t(kc), flip_t(vc), flip_t(gc[1]), flip_t(bc[1]), zero)
    o_xb, _ = gated_delta_chunked(flip_t(qx), flip_t(kx), flip_t(vx), flip_t(gx[1]), flip_t(bx[1]), s_cb)
    x = x + gt[:, None] * deltanet_out(o_xf + flip_t(o_xb), zx, o_norm, w_out)
    if update_ctx:
        ctx = ctx + gt_c * deltanet_out(o_cf + flip_t(o_cb), zc, o_norm, w_out)
    return x, ctx


def setup_inputs(seed: int = 0) -> dict:
    key = jax.random.key(seed)
    ks = iter(jax.random.split(key, 32))
    nrm = lambda shape, scale: jax.random.normal(next(ks), shape, F32) * scale
    ne, no = (DEPTH + 1) // 2, DEPTH // 2
    d = D_MODEL
    dt = jnp.exp(jax.random.uniform(next(ks), (no, 2, DN_V_HEADS), F32, minval=float(np.log(1e-3)), maxval=float(np.log(1e-1))))
    return {
        'x': nrm((BATCH, SEQ, d), 1.0),
        'c': nrm((BATCH, d), 1.0),
        'ctx': nrm((BATCH, CTX_LEN, d), 1.0),
        'c_ctx': nrm((d,), 1.0),
        'ab_w_mod': nrm((ne, d, 3 * d), d ** -0.5),
        'ab_b_mod': nrm((ne, 3 * d), 0.02),
        'ab_norm': 1.0 + nrm((ne, d), 0.1),
        'ab_w_in': nrm((ne, d, AB_IN_WIDTH), d ** -0.5),
        'ab_rpb': nrm((ne, NA_HEADS, 2 * WIN_R - 1, 2 * WIN_C - 1), 0.5),
        'ab_q_norm': 1.0 + nrm((ne, MLA_Q_RANK), 0.1),
        'ab_w_qb': nrm((ne, MLA_Q_RANK, MLA_HEADS * (MLA_NOPE_DIM + MLA_ROPE_DIM)), MLA_Q_RANK ** -0.5),
        'ab_kv_norm': 1.0 + nrm((ne, MLA_KV_RANK), 0.1),
        'ab_w_kvb': nrm((ne, MLA_KV_RANK, MLA_HEADS * (MLA_NOPE_DIM + MLA_V_DIM)), MLA_KV_RANK ** -0.5),
        'ab_w_out': nrm((ne, AB_MIX_WIDTH, d), AB_MIX_WIDTH ** -0.5),
        'dn_w_mod': nrm((no, d, 3 * d), d ** -0.5),
        'dn_b_mod': nrm((no, 3 * d), 0.02),
        'dn_norm': 1.0 + nrm((no, d), 0.1),
        'dn_w_in': nrm((no, d, DN_IN_WIDTH), d ** -0.5),
        'dn_conv': nrm((no, CONV_K, DN_CONV_WIDTH), CONV_K ** -0.5),
        'dn_a_log': jnp.log(jax.random.uniform(next(ks), (no, 2, DN_V_HEADS), F32, minval=1.0, maxval=16.0)),
        'dn_dt_bias': dt + jnp.log(-jnp.expm1(-dt)),
        'dn_o_norm': 1.0 + nrm((no, DN_V_DIM), 0.1),
        'dn_w_out': nrm((no, DN_MIX_WIDTH, d), DN_MIX_WIDTH ** -0.5),
        'final_norm': 1.0 + nrm((d,), 0.1),
    }


def reference(x, c, ctx, c_ctx, ab_w_mod, ab_b_mod, ab_norm, ab_w_in, ab_rpb, ab_q_norm, ab_w_qb, ab_kv_norm, ab_w_kvb,
              ab_w_out, dn_w_mod, dn_b_mod, dn_norm, dn_w_in, dn_conv, dn_a_log, dn_dt_bias, dn_o_norm, dn_w_out, final_norm):
    for i in range(DEPTH):
        j = i // 2
        update_ctx = i < DEPTH - 1
        if i % 2 == 0:
            x, ctx = na_mla_layer(x, ctx, c, c_ctx, ab_w_mod[j], ab_b_mod[j], ab_norm[j], ab_w_in[j], ab_rpb[j],
                                  ab_q_norm[j], ab_w_qb[j], ab_kv_norm[j], ab_w_kvb[j], ab_w_out[j], update_ctx)
        else:
            x, ctx = deltanet_layer(x, ctx, c, c_ctx, dn_w_mod[j], dn_b_mod[j], dn_norm[j], dn_w_in[j], dn_conv[j],
                                    dn_a_log[j], dn_dt_bias[j], dn_o_norm[j], dn_w_out[j], update_ctx)
    return rmsnorm(x, final_norm)
```

```python
import numpy as np
import ml_dtypes
from contextlib import ExitStack
import concourse.bass as bass
import concourse.mybir as mybir
from concourse.bass_utils import run_bass_kernel_spmd

F32 = mybir.dt.float32
BF16 = mybir.dt.bfloat16
AF = mybir.ActivationFunctionType
ALU = mybir.AluOpType
NPBF = ml_dtypes.bfloat16

D = 2048
SEQ = 2048
CTX = 256
T = SEQ + CTX
NTT = T // 128
NB = 2
EPS = 1e-6
TG = [(0, 512), (512, 512), (1024, 512), (1536, 512), (2048, 256)]


class Op:
    __slots__ = ("eng", "fn", "deps", "marked", "val", "sem", "is_dma")


class Buf:
    __slots__ = ("w", "r")

    def __init__(self):
        self.w = None
        self.r = []


class Sched:
    ENGS = ("pe", "act", "dve", "pool", "sp")
    ENGOBJ = {"pe": "tensor", "act": "scalar", "dve": "vector", "pool": "gpsimd", "sp": "sync"}

    def __init__(self, nc, es):
        self.nc = nc
        self.es = es
        self.ops = {e: [] for e in self.ENGS}
        self.bufs = {}
        self.sems = {e: es.enter_context(nc.semaphore("s_" + e)) for e in self.ENGS}
        self.dsems = {}
        self.dpool = []
        self.last_dma = {}
        self.nops = 0

    DRAMKEYS = {"mod", "F0", "VA", "M0", "VM", "OG0", "X1", "X2", "F1", "BA", "OG1", "OUT", "ST"}

    def dsem(self, key):
        if key not in self.dsems:
            i = len(self.dsems)
            if i >= len(self.dpool):
                self.dpool.append([self.es.enter_context(self.nc.semaphore("d_%d" % i)), 0])
            self.dsems[key] = self.dpool[i]
        return self.dsems[key]

    def op(self, eng, fn, reads=(), writes=(), dma=None):
        o = Op()
        o.eng = eng
        o.fn = fn
        o.deps = []
        o.marked = False
        o.val = None
        o.sem = None
        o.is_dma = dma is not None
        self.nops += 1
        if dma is not None:
            dk = None
            for k in list(writes) + list(reads):
                if not (k in self.DRAMKEYS or k.startswith("DR_")):
                    dk = k
                    break
            assert dk is not None, (reads, writes)
            d = self.dsem(dk)
            d[1] += 16
            o.sem = d[0]
            o.val = d[1]
            o.marked = True
            self.last_dma[id(d)] = o
        deps = {}
        for k in reads:
            b = self.bufs.get(k)
            if b is None:
                b = self.bufs[k] = Buf()
            if b.w is not None:
                deps[id(b.w)] = b.w
        for k in writes:
            b = self.bufs.get(k)
            if b is None:
                b = self.bufs[k] = Buf()
            if b.w is not None:
                deps[id(b.w)] = b.w
            for r in b.r:
                deps[id(r)] = r
        for k in reads:
            self.bufs[k].r.append(o)
        for k in writes:
            b = self.bufs[k]
            b.w = o
            b.r = []
        for d in deps.values():
            if d is o:
                continue
            if d.eng == "pe" and eng == "pe" and not d.is_dma:
                continue
            d.marked = True
            o.deps.append(d)
        self.ops[eng].append(o)
        return o

    def barrier(self):
        lasts = []
        for e in self.ENGS:
            for o in reversed(self.ops[e]):
                if not o.is_dma and o.fn is not None:
                    o.marked = True
                    lasts.append(o)
                    break
        lasts += list(self.last_dma.values())
        for e in self.ENGS:
            o = Op()
            o.eng = e
            o.fn = None
            o.deps = list(lasts)
            o.marked = False
            o.val = None
            o.sem = None
            o.is_dma = False
            self.ops[e].append(o)
        self.bufs = {}
        self.dsems = {}

    def finalize(self):
        for e in self.ENGS:
            c = 0
            for o in self.ops[e]:
                if o.is_dma or o.fn is None:
                    continue
                if o.marked:
                    c += 1
                    o.val = c
                    o.sem = self.sems[e]
        nc = self.nc
        with nc.Block() as block:
            for e in self.ENGS:
                ops = self.ops[e]

                def body(engine, ops=ops, e=e):
                    seen = {}
                    for o in ops:
                        for d in o.deps:
                            k = id(d.sem)
                            if seen.get(k, 0) >= d.val:
                                continue
                            seen[k] = d.val
                            engine.wait_ge(d.sem, d.val)
                        if o.fn is None:
                            continue
                        ins = o.fn(engine)
                        if o.is_dma:
                            ins.then_inc(o.sem, 16)
                        elif o.marked:
                            ins.then_inc(o.sem, 1)
                    if e == "sp":
                        for (s, v) in self.dpool:
                            if v > 0:
                                engine.wait_ge(s, v)

                getattr(block, self.ENGOBJ[e])(body)


class Arena:
    def __init__(self, nc, es, nwords=51200):
        self.t = es.enter_context(nc.sbuf_tensor("arena", [128, nwords], F32))
        self.n = nwords
        self.off = 0
        self.uid = 0

    def reset(self):
        self.off = 0

    def alloc(self, nelem, dtype=F32):
        nbytes = nelem * (4 if dtype == F32 else 2)
        words = (nbytes + 31) // 32 * 8
        assert self.off + words <= self.n, "SBUF arena overflow %d+%d" % (self.off, words)
        ap = self.t[:, self.off:self.off + words]
        self.off += words
        if dtype != F32:
            ap = ap.bitcast(dtype)
        return ap[:, 0:nelem]

    def key(self, name):
        self.uid += 1
        return "%s#%d" % (name, self.uid)


class Ctx:
    pass


def _v3(ap, a):
    return ap.rearrange("p (a b) -> p a b", a=a)


def phase0(C):
    nc, S, A = C.nc, C.S, C.A
    A.reset()
    csT = A.alloc(48)
    cs3 = _v3(csT, 16)
    bm = A.alloc(6144)
    osb = [A.alloc(2048), A.alloc(2048)]
    wr = [A.alloc(2048) for _ in range(4)]
    ps = C.ps
    for v in range(3):
        S.op("sp", lambda e, v=v: e.dma_start(out=cs3[:, :, v], in_=C.dram["cvec"][v, :].rearrange("(k p) -> p k", p=128),
                                              allow_slow_non_contiguous=True), writes=["csT"], dma="p0c")
    S.op("act", lambda e: e.activation(out=csT, in_=csT, func=AF.Silu), reads=["csT"], writes=["csT"])
    it = 0
    oi = 0
    for l in range(2):
        wm = C.dram["w_mod%d" % l]
        bmod = C.dram["b_mod%d" % l]
        S.op("sp", lambda e, bmod=bmod: e.dma_start(out=bm[0:3, :], in_=bmod.unsqueeze(0).to_broadcast([3, 6144])),
             writes=["bm"], dma="p0b")
        for g in range(3):
            for k in range(16):
                slot = it % 4
                it += 1
                wt = wr[slot]
                S.op("sp", lambda e, wt=wt, k=k, g=g, wm=wm: e.dma_start(out=wt, in_=wm[k * 128:(k + 1) * 128, g * 2048:(g + 1) * 2048]),
                     writes=["p0w%d" % slot], dma="p0w%d" % slot)
                for n in range(4):
                    S.op("pe", lambda e, wt=wt, k=k, n=n: e.matmul(ps[0:3, n * 512:(n + 1) * 512], lhsT=cs3[:, k, :], rhs=wt[:, n * 512:(n + 1) * 512],
                                                                    start=(k == 0), stop=(k == 15)),
                         reads=["csT", "p0w%d" % slot], writes=["p0ps%d" % n])
            ob = osb[oi % 2]
            okey = "p0o%d" % (oi % 2)
            oi += 1
            for n in range(4):
                S.op("dve", lambda e, ob=ob, n=n, g=g: e.tensor_tensor(out=ob[0:3, n * 512:(n + 1) * 512], in0=ps[0:3, n * 512:(n + 1) * 512],
                                                                       in1=bm[0:3, g * 2048 + n * 512:g * 2048 + (n + 1) * 512], op=ALU.add),
                     reads=["p0ps%d" % n, "bm"], writes=[okey])
            S.op("sp", lambda e, ob=ob, l=l, g=g: e.dma_start(out=C.dram["mod"][l, :, g * 2048:(g + 1) * 2048], in_=ob[0:3, :]),
                 reads=[okey], writes=["mod"], dma="p0o")
    S.barrier()


def load_modvecs(C, l, b, gain, tag):
    S, A = C.S, C.A
    g = A.alloc(16)
    S.op("sp", lambda e: e.dma_start(out=g, in_=gain.rearrange("(k p) -> p k", p=128), allow_slow_non_contiguous=True),
         writes=[tag + "g"], dma=tag + "v")
    res = []
    for vi, v in enumerate((b, 2)):
        sc = A.alloc(16)
        sh = A.alloc(16)
        S.op("sp", lambda e, sc=sc, v=v: e.dma_start(out=sc, in_=C.dram["mod"][l, v, 2048:4096].rearrange("(k p) -> p k", p=128),
                                                     allow_slow_non_contiguous=True), reads=["mod"], writes=[tag + "sc%d" % vi], dma=tag + "v")
        S.op("sp", lambda e, sh=sh, v=v: e.dma_start(out=sh, in_=C.dram["mod"][l, v, 0:2048].rearrange("(k p) -> p k", p=128),
                                                     allow_slow_non_contiguous=True), reads=["mod"], writes=[tag + "sh%d" % vi], dma=tag + "v")
        S.op("dve", lambda e, sc=sc: e.scalar_tensor_tensor(out=sc, in0=sc, scalar=1.0, in1=g, op0=ALU.add, op1=ALU.mult),
             reads=[tag + "sc%d" % vi, tag + "g"], writes=[tag + "sc%d" % vi])
        res.append((sc, sh, tag + "sc%d" % vi, tag + "sh%d" % vi))
    return res


def build_xmT(C, xmT, xkey, src_tiles, mv, tag):
    S, A = C.S, C.A
    xr = [A.alloc(D) for _ in range(2)]
    xh = [A.alloc(D, BF16) for _ in range(2)]
    junk = A.alloc(D, BF16)
    st = [A.alloc(4) for _ in range(2)]
    xm3 = _v3(xmT, 16)
    for tt in range(NTT):
        sl = tt % 2
        xt, xb, s4 = xr[sl], xh[sl], st[sl]
        kx, kb_, ks = tag + "x%d" % sl, tag + "xh%d" % sl, tag + "st%d" % sl
        ms, sh, kms, ksh = mv[0] if tt < 16 else mv[1]
        S.op("sp", lambda e, xt=xt, tt=tt: e.dma_start(out=xt, in_=src_tiles[tt]), writes=[kx], dma=kx)
        S.op("act", lambda e, xt=xt, s4=s4: e.activation(out=junk, in_=xt, func=AF.Square, accum_out=s4[:, 0:1]),
             reads=[kx], writes=[tag + "junk", ks])
        S.op("act", lambda e, s4=s4: e.activation(out=s4[:, 1:2], in_=s4[:, 0:1], func=AF.Sqrt, scale=1.0 / D, bias=C.eps_t[:, 0:1]),
             reads=[ks], writes=[ks])
        S.op("dve", lambda e, s4=s4: e.reciprocal(out=s4[:, 2:3], in_=s4[:, 1:2]), reads=[ks], writes=[ks])
        S.op("dve", lambda e, xt=xt, xb=xb, s4=s4: e.tensor_scalar(out=xb, in0=xt, scalar1=s4[:, 2:3], scalar2=None, op0=ALU.mult),
             reads=[kx, ks], writes=[kb_])
        pb = C.ps[:, (tt % 2) * 1024:(tt % 2) * 1024 + 1024].bitcast(BF16)
        kp = tag + "tp%d" % (tt % 2)
        for k in range(16):
            S.op("pe", lambda e, pb=pb, xb=xb, k=k: e.transpose(out=pb[:, k * 128:(k + 1) * 128], in_=xb[:, k * 128:(k + 1) * 128], identity=C.ident),
                 reads=[kb_, "ident"], writes=[kp])
        for k in range(16):
            o_ = xm3[:, k, tt * 128:(tt + 1) * 128]
            i_ = pb[:, k * 128:(k + 1) * 128]
            if k % 2 == 0:
                S.op("act", lambda e, o_=o_, i_=i_, k=k, ms=ms, sh=sh: e.activation(out=o_, in_=i_, func=AF.Identity, bias=sh[:, k:k + 1], scale=ms[:, k:k + 1]),
                     reads=[kp, kms, ksh], writes=[xkey + str(tt)])
            else:
                S.op("dve", lambda e, o_=o_, i_=i_, k=k, ms=ms, sh=sh: e.tensor_scalar(out=o_, in0=i_, scalar1=ms[:, k:k + 1], scalar2=sh[:, k:k + 1], op0=ALU.mult, op1=ALU.add),
                     reads=[kp, kms, ksh], writes=[xkey + str(tt)])


def proj_fm(C, xm3, xkey, w3, wkey, ncols, evac, tag, m_off=0):
    S = C.S
    for sb in range((ncols + 127) // 128):
        m = min(128, ncols - sb * 128)
        for gi, (t0, tn) in enumerate(TG):
            bank = 4 + (C.pcnt % 4)
            C.pcnt += 1
            pa = C.ps[0:m, bank * 512:bank * 512 + tn]
            pk = "psb%d" % bank
            for k in range(16):
                S.op("pe", lambda e, pa=pa, k=k, sb=sb, m=m, t0=t0, tn=tn: e.matmul(pa, lhsT=w3[:, k, sb * 128:sb * 128 + m], rhs=xm3[:, k, t0:t0 + tn],
                                                                                  start=(k == 0), stop=(k == 15)),
                     reads=[wkey] + [xkey + str(t) for t in range(t0 // 128, (t0 + tn) // 128)], writes=[pk])
            evac(sb, gi, t0, tn, pa, pk, m)


def phase1(C):
    nc, S, A = C.nc, C.S, C.A
    w_in = C.dram["ab_w_in"]
    segs = [("qa", 0, 512), ("qa", 512, 512), ("ka", 1024, 512), ("ka", 1536, 512), ("va", 2048, 512), ("va", 2560, 512),
            ("cq", 3072, 512), ("ckv", 3584, 512), ("kr", 4096, 64), ("krsw", 4096, 64),
            ("z", 4160, 512), ("z", 4672, 512), ("z", 5184, 512), ("z", 5696, 512)]
    frow = {"qa": 0, "ka": 1024 - 1024, "cq": 2048 - 3072, "ckv": 2560 - 3584, "z": 3072 - 4160}
    for b in range(NB):
        _phase1_b(C, b, segs, frow)


def _phase1_b(C, b, segs, frow):
    nc, S, A = C.nc, C.S, C.A
    w_in = C.dram["ab_w_in"]
    if True:
        A.reset()
        mv = load_modvecs(C, 0, b, C.dram["ab_norm"], "p1m%d" % b)
        xmT = A.alloc(16 * T, BF16)
        xm3 = _v3(xmT, 16)
        xkey = "xmT"
        mark = A.off
        tiles = [C.dram["x"][b, tt * 128:(tt + 1) * 128, :] for tt in range(16)] + [C.dram["ctx"][b, tt * 128:(tt + 1) * 128, :] for tt in range(2)]
        build_xmT(C, xmT, xkey, tiles, mv, "p1b")
        pass
        wr = [A.alloc(16 * 512, BF16) for _ in range(2)]
        stg = [A.alloc(T, BF16) for _ in range(3)]
        vst = [A.alloc(512, BF16) for _ in range(2)]
        krp = A.alloc(T)
        kro = A.alloc(T, BF16)
        cos_t = A.alloc(SEQ)
        sin_t = A.alloc(SEQ)
        tmpf = A.alloc(512)
        tmpg = A.alloc(512)
        S.op("sp", lambda e: e.dma_start(out=cos_t[0:64, :], in_=C.dram["rope_cos"]), writes=["cos"], dma="p1c")
        S.op("sp", lambda e: e.dma_start(out=sin_t[0:64, :], in_=C.dram["rope_sin"]), writes=["sin"], dma="p1c")
        F0 = C.dram["F0"]
        VA = C.dram["VA"]
        sc = [0]
        for wi, (nm, c0, ncw) in enumerate(segs):
            slot = wi % 2
            w3 = _v3(wr[slot], 16)[:, :, 0:ncw]
            wkey = "p1w%d" % slot
            if nm == "krsw":
                for (d0, s0) in ((0, 16), (16, 0), (32, 48), (48, 32)):
                    S.op("pool", lambda e, w3=w3, d0=d0, s0=s0: e.dma_start(out=w3[:, :, d0:d0 + 16],
                                                                           in_=w_in[:, 4096 + s0:4096 + s0 + 16].rearrange("(k p) c -> p k c", p=128)),
                         writes=[wkey], dma=wkey)
            else:
                S.op("pool", lambda e, w3=w3, c0=c0, ncw=ncw: e.dma_start(out=w3, in_=w_in[:, c0:c0 + ncw].rearrange("(k p) c -> p k c", p=128)),
                     writes=[wkey], dma=wkey)
            if nm == "va":
                for tt in range(NTT):
                    bank = 4 + (C.pcnt % 4)
                    C.pcnt += 1
                    pa = C.ps[:, bank * 512:bank * 512 + 512]
                    pk = "psb%d" % bank
                    for k in range(16):
                        S.op("pe", lambda e, pa=pa, k=k, tt=tt, w3=w3: e.matmul(pa, lhsT=xm3[:, k, tt * 128:(tt + 1) * 128], rhs=w3[:, k, :], start=(k == 0), stop=(k == 15)),
                             reads=[wkey, xkey + str(tt)], writes=[pk])
                    vs = vst[tt % 2]
                    vk = "p1vs%d" % (tt % 2)
                    eng = "act" if tt % 2 == 0 else "dve"
                    if eng == "act":
                        S.op("act", lambda e, vs=vs, pa=pa: e.activation(out=vs, in_=pa, func=AF.Copy), reads=[pk], writes=[vk])
                    else:
                        S.op("dve", lambda e, vs=vs, pa=pa: e.tensor_copy(out=vs, in_=pa), reads=[pk], writes=[vk])
                    S.op("sp", lambda e, vs=vs, tt=tt, c0=c0: e.dma_start(out=VA[b, tt * 128:(tt + 1) * 128, c0 - 2048:c0 - 2048 + 512], in_=vs),
                         reads=[vk], writes=["VA"], dma=vk)
                continue

            def evac(sb, gi, t0, tn, pa, pk, m, nm=nm, c0=c0):
                if nm in ("kr", "krsw"):
                    if nm == "kr":
                        S.op("dve", lambda e: e.tensor_copy(out=krp[0:64, t0:t0 + tn], in_=pa), reads=[pk], writes=["krp"])
                    else:
                        if gi < 4:
                            S.op("dve", lambda e: e.tensor_tensor(out=tmpf[0:64, 0:tn], in0=pa, in1=sin_t[0:64, t0:t0 + tn], op=ALU.mult),
                                 reads=[pk, "sin"], writes=["tmpf"])
                            S.op("pool", lambda e: e.tensor_tensor(out=tmpg[0:64, 0:tn], in0=krp[0:64, t0:t0 + tn], in1=cos_t[0:64, t0:t0 + tn], op=ALU.mult),
                                 reads=["krp", "cos"], writes=["tmpg"])
                            S.op("dve", lambda e: e.tensor_tensor(out=kro[0:64, t0:t0 + tn], in0=tmpf[0:64, 0:tn], in1=tmpg[0:64, 0:tn], op=ALU.add),
                                 reads=["tmpf", "tmpg"], writes=["kro"])
                        if gi == 4:
                            S.op("dve", lambda e: e.tensor_copy(out=kro[0:64, t0:t0 + tn], in_=krp[0:64, t0:t0 + tn]), reads=["krp"], writes=["kro"])
                            S.op("sp", lambda e: e.dma_start(out=F0[b, 5120:5184, :], in_=kro[0:64, :]), reads=["kro"], writes=["F0"], dma="p1kr")
                    return
                if gi == 0:
                    sc[0] += 1
                si = sc[0] % 3
                sg = stg[si]
                sk = "p1st%d" % si
                func = AF.Silu if nm == "z" else AF.Copy
                if nm == "z" or (gi % 2 == 0):
                    S.op("act", lambda e: e.activation(out=sg[0:m, t0:t0 + tn], in_=pa, func=func), reads=[pk], writes=[sk])
                else:
                    S.op("dve", lambda e: e.tensor_copy(out=sg[0:m, t0:t0 + tn], in_=pa), reads=[pk], writes=[sk])
                if gi == 4:
                    r0 = c0 + sb * 128 + frow[nm]
                    S.op("sp", lambda e: e.dma_start(out=F0[b, r0:r0 + m, :], in_=sg[0:m, :]), reads=[sk], writes=["F0"], dma=sk)

            proj_fm(C, xm3, xkey, w3, wkey, ncw, evac, "p1")
        S.barrier()


def rope_tables():
    quarter = 16
    inv = (10000.0 ** (-np.arange(quarter, dtype=np.float32) / quarter)).astype(np.float32)
    pos = np.arange(SEQ)
    cos = np.zeros((64, SEQ), np.float32)
    sin = np.zeros((64, SEQ), np.float32)
    for half, p in ((0, pos // 64), (1, pos % 64)):
        ang = p.astype(np.float32)[None, :] * inv[:, None]
        c, s = np.cos(ang), np.sin(ang)
        cos[half * 32:half * 32 + 16] = c
        cos[half * 32 + 16:half * 32 + 32] = c
        sin[half * 32:half * 32 + 16] = -s
        sin[half * 32 + 16:half * 32 + 32] = s
    return cos, sin


def make_ctx(nc, es, dram):
    C = Ctx()
    C.nc = nc
    C.S = Sched(nc, es)
    C.es = es
    C.A = Arena(nc, es)
    C.ps = es.enter_context(nc.psum_tensor("ps", [128, 4096], F32))
    C.dram = dram
    C.pcnt = 0
    C.na_geo = na_geometry()
    C.cst = es.enter_context(nc.sbuf_tensor("cst", [128, 512], F32))
    C.ident = C.cst[:, 0:64].bitcast(BF16)
    C.eps_t = C.cst[:, 64:65]
    C.ones = C.cst[:, 72:136].bitcast(BF16)
    C.S.op("sp", lambda e: e.dma_start(out=C.ident, in_=dram["ident"]), writes=["ident"], dma="cst")
    C.S.op("pool", lambda e: e.memset(C.eps_t, EPS), writes=["eps"])
    C.S.op("pool", lambda e: e.memset(C.ones, 1.0), writes=["ones"])
    C.lnq = C.cst[:, 65:66]
    C.zero_t = C.cst[:, 66:67]
    C.cw = C.cst[:, 136:456]
    C.S.op("pool", lambda e: e.memset(C.lnq, float(np.log(128.0 ** -0.5))), writes=["lnq"])
    C.S.op("pool", lambda e: e.memset(C.zero_t, 0.0), writes=["zero"])
    if "dn_conv" in dram:
        cw3 = C.cw.rearrange("p (blk j) -> p blk j", j=5)
        for j in range(5):
            for q4 in range(8):
                C.S.op("sp", lambda e, j=j, q4=q4: e.dma_start(out=cw3[:, q4 * 8:(q4 + 1) * 8, j], in_=dram["dn_conv"][j, q4 * 1024:(q4 + 1) * 1024].rearrange("(blk p) -> p blk", p=128),
                                                          allow_slow_non_contiguous=True), writes=["cw"], dma="cw")
    C.S.barrier()
    return C


MLA_SCALE = 192.0 ** -0.5
NA_SCALE = 128.0 ** -0.5


def phase2(C):
    for b in range(NB):
        _phase2_b(C, b)


def _phase2_b(C, b):
    nc, S, A = C.nc, C.S, C.A
    A.reset()
    F0, M0, VM = C.dram["F0"], C.dram["M0"], C.dram["VM"]
    wqb = A.alloc(4 * 1536, BF16)
    wqb3 = _v3(wqb, 4)
    wqsw = A.alloc(4 * 512, BF16)
    wqsw4 = wqsw.rearrange("p (k h c) -> p k h c", k=4, h=8)
    wkvb = A.alloc(4 * 2048, BF16)
    wkvb3 = _v3(wkvb, 4)
    wkvb4 = wkvb.rearrange("p (k h c) -> p k h c", k=4, h=8)
    cq = A.alloc(4 * T, BF16)
    ckv = A.alloc(4 * T, BF16)
    cqn = A.alloc(4 * T, BF16)
    ckvn = A.alloc(4 * T, BF16)
    sq = [A.alloc(4 * 512, BF16) for _ in range(2)]
    lnv = [A.alloc(512) for _ in range(2)]
    rst = [A.alloc(512) for _ in range(2)]
    qnm = A.alloc(4)
    kvnm = A.alloc(4)
    cos_t = A.alloc(SEQ)
    sin_t = A.alloc(SEQ)
    stg = [A.alloc(T, BF16) for _ in range(3)]
    vst = [A.alloc(1024, BF16) for _ in range(2)]
    t1 = [A.alloc(512) for _ in range(2)]
    t2 = [A.alloc(512) for _ in range(2)]
    wq_d, wkv_d = C.dram["ab_w_qb"], C.dram["ab_w_kvb"]
    S.op("pool", lambda e: e.dma_start(out=wqb3, in_=wq_d.rearrange("(k p) c -> p k c", p=128)), writes=["wqb"], dma="p2w")
    S.op("pool", lambda e: e.dma_start(out=wkvb3, in_=wkv_d.rearrange("(k p) c -> p k c", p=128)), writes=["wkvb"], dma="p2w")
    wq4 = wq_d.rearrange("(k p) (h c) -> p k h c", p=128, h=8)
    for k in range(4):
        for (d0, s0) in ((0, 16), (16, 0), (32, 48), (48, 32)):
            S.op("pool", lambda e, k=k, d0=d0, s0=s0: e.dma_start(out=wqsw4[:, k, :, d0:d0 + 16], in_=wq4[:, k, :, 128 + s0:128 + s0 + 16]),
                 writes=["wqsw"], dma="p2w")
    S.op("sp", lambda e: e.dma_start(out=_v3(cq, 4), in_=F0[b, 2048:2560, :].rearrange("(k p) t -> p k t", p=128)), reads=["F0"], writes=["cq"], dma="p2a")
    S.op("sp", lambda e: e.dma_start(out=_v3(ckv, 4), in_=F0[b, 2560:3072, :].rearrange("(k p) t -> p k t", p=128)), reads=["F0"], writes=["ckv"], dma="p2a")
    S.op("sp", lambda e: e.dma_start(out=qnm, in_=C.dram["ab_q_norm"].rearrange("(k p) -> p k", p=128), allow_slow_non_contiguous=True), writes=["qnm"], dma="p2a")
    S.op("sp", lambda e: e.dma_start(out=kvnm, in_=C.dram["ab_kv_norm"].rearrange("(k p) -> p k", p=128), allow_slow_non_contiguous=True), writes=["kvnm"], dma="p2a")
    S.op("sp", lambda e: e.dma_start(out=cos_t[0:64, :], in_=C.dram["rope_cos"]), writes=["cos"], dma="p2a")
    S.op("sp", lambda e: e.dma_start(out=sin_t[0:64, :], in_=C.dram["rope_sin"]), writes=["sin"], dma="p2a")
    it = 0
    for (src, skey, nrm, nkey, dst, dkey) in ((cq, "cq", qnm, "qnm", cqn, "cqn"), (ckv, "ckv", kvnm, "kvnm", ckvn, "ckvn")):
        s3, d3 = _v3(src, 4), _v3(dst, 4)
        for gi, (t0, tn) in enumerate(TG):
            sl = it % 2
            it += 1
            q3 = _v3(sq[sl], 4)
            S.op("pool", lambda e, q3=q3, s3=s3, t0=t0, tn=tn: e.tensor_tensor(out=q3[:, :, 0:tn], in0=s3[:, :, t0:t0 + tn], in1=s3[:, :, t0:t0 + tn], op=ALU.mult),
                 reads=[skey], writes=["sq%d" % sl])
            bank = 4 + sl
            pa = C.ps[:, bank * 512:bank * 512 + tn]
            for k in range(4):
                S.op("pe", lambda e, pa=pa, q3=q3, k=k, tn=tn: e.matmul(pa, lhsT=C.ones, rhs=q3[:, k, 0:tn], start=(k == 0), stop=(k == 3)),
                     reads=["ones", "sq%d" % sl], writes=["psb%d" % bank])
            lv, rs = lnv[sl], rst[sl]
            S.op("act", lambda e, lv=lv, pa=pa, tn=tn: e.activation(out=lv[:, 0:tn], in_=pa, func=AF.Ln, scale=1.0 / 512, bias=C.eps_t[:, 0:1]),
                 reads=["psb%d" % bank], writes=["lnv%d" % sl])
            S.op("act", lambda e, lv=lv, rs=rs, tn=tn: e.activation(out=rs[:, 0:tn], in_=lv[:, 0:tn], func=AF.Exp, scale=-0.5),
                 reads=["lnv%d" % sl], writes=["rst%d" % sl])
            for k in range(4):
                S.op("dve", lambda e, d3=d3, s3=s3, k=k, t0=t0, tn=tn, nrm=nrm, rs=rs: e.scalar_tensor_tensor(
                    out=d3[:, k, t0:t0 + tn], in0=s3[:, k, t0:t0 + tn], scalar=nrm[:, k:k + 1], in1=rs[:, 0:tn], op0=ALU.mult, op1=ALU.mult),
                    reads=[skey, nkey, "rst%d" % sl], writes=[dkey])
    cqn3, ckvn3 = _v3(cqn, 4), _v3(ckvn, 4)
    sc = [0]

    def small_proj(lhs_fn, rhs3, rkey, wkey, m, gi, t0, tn):
        bank = 4 + (C.pcnt % 4)
        C.pcnt += 1
        pa = C.ps[0:m, bank * 512:bank * 512 + tn]
        for k in range(4):
            S.op("pe", lambda e, pa=pa, k=k: e.matmul(pa, lhsT=lhs_fn(k), rhs=rhs3[:, k, t0:t0 + tn], start=(k == 0), stop=(k == 3)),
                 reads=[wkey, rkey], writes=["psb%d" % bank])
        return pa, "psb%d" % bank

    for h in range(8):
        for (nm, lhs_fn, rhs3, rkey, wkey, row0) in (
                ("qn", lambda k, h=h: wqb3[:, k, h * 192:h * 192 + 128], cqn3, "cqn", "wqb", h * 128),
                ("kn", lambda k, h=h: wkvb3[:, k, h * 256:h * 256 + 128], ckvn3, "ckvn", "wkvb", 1536 + h * 128)):
            sc[0] += 1
            si = sc[0] % 3
            sg, sk = stg[si], "p2st%d" % si
            for gi, (t0, tn) in enumerate(TG):
                pa, pk = small_proj(lhs_fn, rhs3, rkey, wkey, 128, gi, t0, tn)
                if gi % 2 == 0:
                    S.op("act", lambda e, sg=sg, pa=pa, t0=t0, tn=tn: e.activation(out=sg[:, t0:t0 + tn], in_=pa, func=AF.Copy), reads=[pk], writes=[sk])
                else:
                    S.op("dve", lambda e, sg=sg, pa=pa, t0=t0, tn=tn: e.tensor_copy(out=sg[:, t0:t0 + tn], in_=pa), reads=[pk], writes=[sk])
            S.op("sp", lambda e, sg=sg, row0=row0: e.dma_start(out=M0[b, row0:row0 + 128, :], in_=sg), reads=[sk], writes=["M0"], dma=sk)
        sc[0] += 1
        si = sc[0] % 3
        sg, sk = stg[si], "p2st%d" % si
        for gi, (t0, tn) in enumerate(TG):
            pa, pk = small_proj(lambda k, h=h: wqb3[:, k, h * 192 + 128:h * 192 + 192], cqn3, "cqn", "wqb", 64, gi, t0, tn)
            if gi == 4:
                S.op("dve", lambda e, sg=sg, pa=pa, t0=t0, tn=tn: e.tensor_copy(out=sg[0:64, t0:t0 + tn], in_=pa), reads=[pk], writes=[sk])
                continue
            pb, pkb = small_proj(lambda k, h=h: wqsw4[:, k, h, :], cqn3, "cqn", "wqsw", 64, gi, t0, tn)
            a1, a2 = t1[gi % 2], t2[gi % 2]
            S.op("dve", lambda e, a1=a1, pa=pa, t0=t0, tn=tn: e.tensor_tensor(out=a1[0:64, 0:tn], in0=pa, in1=cos_t[0:64, t0:t0 + tn], op=ALU.mult),
                 reads=[pk, "cos"], writes=["t1%d" % (gi % 2)])
            S.op("dve", lambda e, a2=a2, pb=pb, t0=t0, tn=tn: e.tensor_tensor(out=a2[0:64, 0:tn], in0=pb, in1=sin_t[0:64, t0:t0 + tn], op=ALU.mult),
                 reads=[pkb, "sin"], writes=["t2%d" % (gi % 2)])
            S.op("pool", lambda e, a1=a1, a2=a2, sg=sg, t0=t0, tn=tn: e.tensor_tensor(out=sg[0:64, t0:t0 + tn], in0=a1[0:64, 0:tn], in1=a2[0:64, 0:tn], op=ALU.add),
                 reads=["t1%d" % (gi % 2), "t2%d" % (gi % 2)], writes=[sk])
        S.op("sp", lambda e, sg=sg, h=h: e.dma_start(out=M0[b, 1024 + h * 64:1024 + h * 64 + 64, :], in_=sg[0:64, :]), reads=[sk], writes=["M0"], dma=sk)
    for tt in range(NTT):
        vs, vk = vst[tt % 2], "p2vs%d" % (tt % 2)
        for half in range(2):
            bank = 4 + (C.pcnt % 4)
            C.pcnt += 1
            pa = C.ps[:, bank * 512:bank * 512 + 512]
            for k in range(4):
                S.op("pe", lambda e, pa=pa, k=k, tt=tt, half=half: e.matmul(pa.rearrange("p (h c) -> p h c", h=4), lhsT=ckvn3[:, k, tt * 128:(tt + 1) * 128],
                                                                         rhs=wkvb4[:, k, half * 4:half * 4 + 4, 128:256], start=(k == 0), stop=(k == 3)),
                     reads=["wkvb", "ckvn"], writes=["psb%d" % bank])
            if half == 0:
                S.op("act", lambda e, vs=vs, pa=pa: e.activation(out=vs[:, 0:512], in_=pa, func=AF.Copy), reads=["psb%d" % bank], writes=[vk])
            else:
                S.op("dve", lambda e, vs=vs, pa=pa: e.tensor_copy(out=vs[:, 512:1024], in_=pa), reads=["psb%d" % bank], writes=[vk])
        S.op("sp", lambda e, vs=vs, tt=tt: e.dma_start(out=VM[b, tt * 128:(tt + 1) * 128, :], in_=vs), reads=[vk], writes=["VM"], dma=vk)
    S.barrier()


def na_geometry():
    rows = 32
    r = np.arange(rows)
    r0 = np.clip(r - 4, 0, rows - 8)
    col = np.arange(64)
    c0 = np.clip(col - 8, 0, 64 - 16)
    tiles = {}
    per_m = []
    for m in range(16):
        lo = min(r0[2 * m], r0[2 * m + 1])
        hi = max(r0[2 * m], r0[2 * m + 1]) + 7
        lst = []
        for kb in range(lo // 2, hi // 2 + 1):
            memb = tuple(tuple(bool(r0[2 * m + bq] <= 2 * kb + a <= r0[2 * m + bq] + 7) for bq in range(2)) for a in range(2))
            key = (kb - m, memb)
            if key not in tiles:
                tiles[key] = len(tiles)
            lst.append((kb, tiles[key]))
        per_m.append(lst)
    nt = len(tiles)
    mask = np.zeros((nt, 128, 128), np.float32)
    drow = np.zeros((nt, 128, 128), np.int64)
    dcol = np.zeros((nt, 128, 128), np.int64)
    for (delta, memb), ti in tiles.items():
        for a in range(2):
            for bq in range(2):
                kc = np.arange(64)[:, None]
                qc = np.arange(64)[None, :]
                ok = memb[a][bq] & (kc >= c0[qc]) & (kc <= c0[qc] + 15)
                dr = 2 * delta + a - bq + 7
                dc = kc - qc + 15
                blk = (slice(a * 64, a * 64 + 64), slice(bq * 64, bq * 64 + 64))
                mask[ti][blk] = np.where(ok, 0.0, -20000.0)
                drow[ti][blk] = np.clip(np.where(ok, dr, 0), 0, 14)
                dcol[ti][blk] = np.clip(np.where(ok, dc, 0), 0, 30)
    return per_m, nt, mask, drow, dcol


def phase3(C):
    for b in range(NB):
        _phase3_b(C, b)


def _phase3_b(C, b):
    nc, S, A = C.nc, C.S, C.A
    A.reset()
    F0, M0, VM, VA, OG = C.dram["F0"], C.dram["M0"], C.dram["VM"], C.dram["VA"], C.dram["OG0"]
    per_m, nt, _, _, _ = C.na_geo
    krT = A.alloc(T, BF16)
    S.op("sp", lambda e: e.dma_start(out=krT[0:64, :], in_=F0[b, 5120:5184, :]), reads=["F0"], writes=["krT"], dma="p3k")
    nam = A.alloc(nt * 128)
    S.op("sp", lambda e: e.dma_start(out=_v3(nam, nt), in_=C.dram["na_mask"].rearrange("t k q -> k t q")), writes=["nam"], dma="p3k")
    hb = []
    for i in range(2):
        hb.append(dict(k=A.alloc(T, BF16), q=A.alloc(T, BF16), qr=A.alloc(T, BF16), v=A.alloc(NTT * 128, BF16), sz=A.alloc(T, BF16),
                       bias=A.alloc(nt * 128)))
    pT = [A.alloc(512, BF16) for _ in range(4)]
    sbf = [A.alloc(128) for _ in range(3)]
    rinv = [A.alloc(512) for _ in range(2)]
    tmp = [A.alloc(512) for _ in range(2)]
    ogs = [A.alloc(T, BF16) for _ in range(2)]
    cnt = dict(s=0, p=0, g=0, sb=0)

    def attend_block(lhsT_list, rhs_list, tn, reads, o_cols, first, last, vlhs, vkey, scale, bias=None, bkey=None, acc=None):
        bank = cnt["s"] % 4
        cnt["s"] += 1
        pS = C.ps[:, bank * 512:bank * 512 + tn]
        pk = "psb%d" % bank
        n = len(lhsT_list)
        for i in range(n):
            S.op("pe", lambda e, i=i: e.matmul(pS[0:128, :], lhsT=lhsT_list[i], rhs=rhs_list[i], start=(i == 0), stop=(i == n - 1)),
                 reads=reads, writes=[pk])
        slot = cnt["p"] % 4
        cnt["p"] += 1
        p_ = pT[slot][:, 0:tn]
        pkey = "pT%d" % slot
        if bias is None:
            S.op("act", lambda e: e.activation(out=p_, in_=pS, func=AF.Exp, scale=scale), reads=[pk], writes=[pkey])
        else:
            sslot = cnt["sb"] % 3
            cnt["sb"] += 1
            sb_ = sbf[sslot][:, 0:tn]
            S.op("dve", lambda e: e.scalar_tensor_tensor(out=sb_, in0=pS, scalar=scale, in1=bias, op0=ALU.mult, op1=ALU.add),
                 reads=[pk, bkey], writes=["sbf%d" % sslot])
            S.op("act", lambda e: e.activation(out=p_, in_=sb_, func=AF.Exp), reads=["sbf%d" % sslot], writes=[pkey])
        po, ps_, ok_, sk_ = acc
        S.op("pe", lambda e: e.matmul(po[:, o_cols[0]:o_cols[0] + tn], lhsT=vlhs, rhs=p_, start=first, stop=last), reads=[pkey, vkey], writes=[ok_])
        S.op("pe", lambda e: e.matmul(ps_[:, o_cols[0]:o_cols[0] + tn], lhsT=C.ones, rhs=p_, start=first, stop=last), reads=[pkey, "ones"], writes=[sk_])

    def finish_group(acc, tn, sz, szkey, og, ogkey, t0):
        po, ps_, ok_, sk_ = acc
        g = cnt["g"] % 2
        cnt["g"] += 1
        S.op("dve", lambda e: e.reciprocal(out=rinv[g][:, 0:tn], in_=ps_[:, 0:tn]), reads=[sk_], writes=["rinv%d" % g])
        S.op("dve", lambda e: e.tensor_tensor(out=tmp[g][:, 0:tn], in0=po[:, 0:tn], in1=rinv[g][:, 0:tn], op=ALU.mult),
             reads=[ok_, "rinv%d" % g], writes=["tmp%d" % g])
        S.op("pool", lambda e: e.tensor_tensor(out=og[:, t0:t0 + tn], in0=tmp[g][:, 0:tn], in1=sz[:, t0:t0 + tn], op=ALU.mult),
             reads=["tmp%d" % g, szkey], writes=[ogkey])

    def acc_banks():
        g = cnt["g"] % 2
        return (C.ps[:, (4 + g) * 512:(5 + g) * 512], C.ps[:, (6 + g) * 512:(7 + g) * 512], "psb%d" % (4 + g), "psb%d" % (6 + g))

    for hh in range(16):
        is_mla = hh >= 8
        h = hh % 8
        B_ = hb[hh % 2]
        hk = "hb%d" % (hh % 2)
        og, ogkey = ogs[hh % 2], "ogs%d" % (hh % 2)
        if is_mla:
            S.op("sp", lambda e, B_=B_, h=h: e.dma_start(out=B_["k"], in_=M0[b, 1536 + h * 128:1536 + (h + 1) * 128, :]), reads=["M0"], writes=[hk + "k"], dma=hk)
            S.op("sp", lambda e, B_=B_, h=h: e.dma_start(out=B_["q"], in_=M0[b, h * 128:(h + 1) * 128, :]), reads=["M0"], writes=[hk + "q"], dma=hk)
            S.op("sp", lambda e, B_=B_, h=h: e.dma_start(out=B_["qr"][0:64, :], in_=M0[b, 1024 + h * 64:1024 + (h + 1) * 64, :]), reads=["M0"], writes=[hk + "qr"], dma=hk)
            S.op("sp", lambda e, B_=B_, h=h: e.dma_start(out=_v3(B_["v"], NTT), in_=VM[b, :, h * 128:(h + 1) * 128].rearrange("(kb p) d -> p kb d", p=128)),
                 reads=["VM"], writes=[hk + "v"], dma=hk)
            S.op("sp", lambda e, B_=B_, h=h: e.dma_start(out=B_["sz"], in_=F0[b, 3072 + 1024 + h * 128:3072 + 1024 + (h + 1) * 128, :]), reads=["F0"], writes=[hk + "sz"], dma=hk)
        else:
            S.op("sp", lambda e, B_=B_, h=h: e.dma_start(out=B_["k"], in_=F0[b, 1024 + h * 128:1024 + (h + 1) * 128, :]), reads=["F0"], writes=[hk + "k"], dma=hk)
            S.op("sp", lambda e, B_=B_, h=h: e.dma_start(out=B_["q"], in_=F0[b, h * 128:(h + 1) * 128, :]), reads=["F0"], writes=[hk + "q"], dma=hk)
            S.op("sp", lambda e, B_=B_, h=h: e.dma_start(out=_v3(B_["v"], NTT), in_=VA[b, :, h * 128:(h + 1) * 128].rearrange("(kb p) d -> p kb d", p=128)),
                 reads=["VA"], writes=[hk + "v"], dma=hk)
            S.op("sp", lambda e, B_=B_, h=h: e.dma_start(out=B_["sz"], in_=F0[b, 3072 + h * 128:3072 + (h + 1) * 128, :]), reads=["F0"], writes=[hk + "sz"], dma=hk)
            S.op("sp", lambda e, B_=B_, h=h: e.dma_start(out=_v3(B_["bias"], nt), in_=C.dram["na_bias"][h].rearrange("t k q -> k t q")), writes=[hk + "b"], dma=hk)
            S.op("pool", lambda e, B_=B_: e.tensor_tensor(out=B_["bias"], in0=B_["bias"], in1=nam, op=ALU.add), reads=[hk + "b", "nam"], writes=[hk + "b"])
        k_, q_, qr_, v3_, sz_ = B_["k"], B_["q"], B_["qr"], _v3(B_["v"], NTT), B_["sz"]
        b3_ = _v3(B_["bias"], nt)
        for gi, (t0, tn) in enumerate(TG):
            acc = acc_banks()
            if is_mla:
                kbs = list(range(18)) if gi < 4 else [16, 17]
                for i, kb in enumerate(kbs):
                    attend_block([k_[:, kb * 128:(kb + 1) * 128], krT[0:64, kb * 128:(kb + 1) * 128]], [q_[:, t0:t0 + tn], qr_[0:64, t0:t0 + tn]], tn,
                                 [hk + "k", hk + "q", hk + "qr", "krT"], (0,), i == 0, i == len(kbs) - 1, v3_[:, kb, :], hk + "v", MLA_SCALE, acc=acc)
            else:
                for i, kb in enumerate((16, 17)):
                    attend_block([k_[:, kb * 128:(kb + 1) * 128]], [q_[:, t0:t0 + tn]], tn, [hk + "k", hk + "q"], (0,), i == 0, (gi == 4 and i == 1),
                                 v3_[:, kb, :], hk + "v", NA_SCALE, acc=acc)
                if gi < 4:
                    for pr in range(4):
                        m = gi * 4 + pr
                        lst = per_m[m]
                        for j, (kb, ti) in enumerate(lst):
                            attend_block([k_[:, kb * 128:(kb + 1) * 128]], [q_[:, m * 128:(m + 1) * 128]], 128, [hk + "k", hk + "q"], (pr * 128,), False,
                                         (pr == 3 and j == len(lst) - 1), v3_[:, kb, :], hk + "v", NA_SCALE, bias=b3_[:, ti, :], bkey=hk + "b", acc=acc)
            finish_group(acc, tn, sz_, hk + "sz", og, ogkey, t0)
        row0 = (1024 if is_mla else 0) + h * 128
        S.op("sp", lambda e, og=og, row0=row0: e.dma_start(out=OG[b, row0:row0 + 128, :], in_=og), reads=[ogkey], writes=["OG0"], dma=ogkey)
    S.barrier()


def out_proj(C, b, l, OGd, nk, w_out, res_tiles, dst_tiles, ntt_list, tag, cbw=512, ntok=T):
    nc, S, A = C.nc, C.S, C.A
    A.reset()
    og = A.alloc(nk * ntok, BF16)
    og3 = _v3(og, nk)
    hk = nk // 2
    S.op("sp", lambda e: e.dma_start(out=og3[:, 0:hk, :], in_=OGd[0:hk * 128, :].rearrange("(k p) t -> p k t", p=128)), reads=["DR_OG"], writes=["og"], dma=tag + "og")
    S.op("sp", lambda e: e.dma_start(out=og3[:, hk:nk, :], in_=OGd[hk * 128:nk * 128, :].rearrange("(k p) t -> p k t", p=128)), reads=["DR_OG"], writes=["og"], dma=tag + "og")
    vs = (b, 2) if len(ntt_list) > 16 else (b,)
    wr = A.alloc(nk * cbw, BF16)
    wx = [A.alloc(nk * cbw, BF16) for _ in vs]
    gt = [A.alloc(cbw) for _ in range(2)]
    xr = [A.alloc(cbw) for _ in range(3)]
    xo = [A.alloc(cbw) for _ in range(3)]
    it = 0
    for cb in range(D // cbw):
        w3 = _v3(wr, nk)
        S.op("pool", lambda e, cb=cb, w3=w3: e.dma_start(out=w3, in_=w_out[:, cb * cbw:(cb + 1) * cbw].rearrange("(k p) c -> p k c", p=128)), writes=["wr"], dma=tag + "w")
        for vi, v in enumerate(vs):
            S.op("sp", lambda e, vi=vi, v=v, cb=cb: e.dma_start(out=gt[vi], in_=C.dram["mod"][l, v, 4096 + cb * cbw:4096 + (cb + 1) * cbw].unsqueeze(0).to_broadcast([128, cbw])),
                 reads=["mod"], writes=["gt%d" % vi], dma=tag + "gt")
            S.op("dve" if vi == 0 else "pool", lambda e, vi=vi, w3=w3: e.tensor_tensor(out=_v3(wx[vi], nk), in0=w3, in1=gt[vi].unsqueeze(1).to_broadcast([128, nk, cbw]), op=ALU.mult),
                 reads=["wr", "gt%d" % vi], writes=["wx%d" % vi])
        for tt in ntt_list:
            vi = 0 if tt < 16 else 1
            wv = _v3(wx[vi], nk)
            bank = 4 + (C.pcnt % 4)
            C.pcnt += 1
            pa = C.ps[:, bank * 512:bank * 512 + cbw]
            for k in range(nk):
                S.op("pe", lambda e, pa=pa, k=k, tt=tt, wv=wv: e.matmul(pa, lhsT=og3[:, k, tt * 128:(tt + 1) * 128], rhs=wv[:, k, :], start=(k == 0), stop=(k == nk - 1)),
                     reads=["og", "wx%d" % vi], writes=["psb%d" % bank])
            sl = it % 3
            it += 1
            S.op("sp", lambda e, sl=sl, tt=tt, cb=cb: e.dma_start(out=xr[sl], in_=res_tiles(tt, cb)), reads=["DR_res"], writes=["xr%d" % sl], dma=tag + "xr%d" % sl)
            S.op("dve", lambda e, sl=sl, pa=pa: e.tensor_tensor(out=xo[sl], in0=pa, in1=xr[sl], op=ALU.add), reads=["psb%d" % bank, "xr%d" % sl], writes=["xo%d" % sl])
            S.op("sp", lambda e, sl=sl, tt=tt, cb=cb: e.dma_start(out=dst_tiles(tt, cb), in_=xo[sl]), reads=["xo%d" % sl], writes=["DR_dst"], dma=tag + "xo%d" % sl)
    S.barrier()


def phase4(C):
    for b in range(NB):
        def res(tt, cb, b=b):
            if tt < 16:
                return C.dram["x"][b, tt * 128:(tt + 1) * 128, cb * 512:(cb + 1) * 512]
            return C.dram["ctx"][b, (tt - 16) * 128:(tt - 15) * 128, cb * 512:(cb + 1) * 512]

        def dst(tt, cb, b=b):
            return C.dram["X1"][b, tt * 128:(tt + 1) * 128, cb * 512:(cb + 1) * 512]
        out_proj(C, b, 0, C.dram["OG0"][b], 16, C.dram["ab_w_out"], res, dst, list(range(NTT)), "p4")


HP = 2052 + 260


def phase5(C):
    for b in range(NB):
        _phase5_b(C, b)


def _phase5_b(C, b):
    nc, S, A = C.nc, C.S, C.A
    A.reset()
    w_in = C.dram["dn_w_in"]
    X1, F1, BA = C.dram["X1"], C.dram["F1"], C.dram["BA"]
    mv = load_modvecs(C, 1, b, C.dram["dn_norm"], "p5m%d" % b)
    xmT = A.alloc(16 * T, BF16)
    xm3 = _v3(xmT, 16)
    xkey = "xmT"
    mark = A.off
    tiles = [X1[b, tt * 128:(tt + 1) * 128, :] for tt in range(NTT)]
    build_xmT(C, xmT, xkey, tiles, mv, "p5b")
    S.barrier()
    A.off = mark
    wr = [A.alloc(16 * 512, BF16) for _ in range(2)]
    stg = [A.alloc(T, BF16) for _ in range(3)]
    hp = [A.alloc(HP, BF16) for _ in range(2)]
    dg = [A.alloc(5 * 128, BF16) for _ in range(2)]
    sT = [A.alloc(512, BF16) for _ in range(3)]
    sq = [A.alloc(512, BF16) for _ in range(2)]
    lnv = [A.alloc(512) for _ in range(2)]
    rst = [A.alloc(512) for _ in range(2)]
    bast = A.alloc(NTT * 128)
    cw3 = C.cw.rearrange("p (blk j) -> p blk j", j=5)
    for i in range(2):
        S.op("pool", lambda e, i=i: e.memset(hp[i], 0.0), writes=["hp%d" % i])
    segs = [("q", i * 512, 512) for i in range(4)] + [("k", 2048 + i * 512, 512) for i in range(4)] + [("v", 4096 + i * 512, 512) for i in range(8)] + \
           [("z", 8192 + i * 512, 512) for i in range(8)] + [("ba", 12288, 128)]
    cnt = dict(st=0, hp=0, c=0, n=0, s=0)
    for wi, (nm, c0, ncw) in enumerate(segs):
        slot = wi % 2
        w3 = _v3(wr[slot], 16)[:, :, 0:ncw]
        wkey = "p5w%d" % slot
        S.op("pool", lambda e, w3=w3, c0=c0, ncw=ncw: e.dma_start(out=w3, in_=w_in[:, c0:c0 + ncw].rearrange("(k p) c -> p k c", p=128)),
             writes=[wkey], dma=wkey)
        if nm == "ba":
            b3 = _v3(bast, NTT)
            for tt in range(NTT):
                bank = 4 + (C.pcnt % 4)
                C.pcnt += 1
                pa = C.ps[:, bank * 512:bank * 512 + 128]
                for k in range(16):
                    S.op("pe", lambda e, pa=pa, k=k, tt=tt, w3=w3: e.matmul(pa, lhsT=xm3[:, k, tt * 128:(tt + 1) * 128], rhs=w3[:, k, :], start=(k == 0), stop=(k == 15)),
                         reads=[wkey, xkey + str(tt)], writes=["psb%d" % bank])
                S.op("dve", lambda e, pa=pa, tt=tt: e.tensor_copy(out=b3[:, tt, :], in_=pa), reads=["psb%d" % bank], writes=["bast"])
            S.op("sp", lambda e: e.dma_start(out=BA[b].rearrange("(tt p) c -> p tt c", p=128), in_=b3), reads=["bast"], writes=["BA"], dma="bast")
            continue

        def evac(sb, gi, t0, tn, pa, pk, m, nm=nm, c0=c0):
            if nm == "z":
                if gi == 0:
                    cnt["st"] += 1
                si = cnt["st"] % 3
                sg, sk = stg[si], "p5st%d" % si
                S.op("act", lambda e: e.activation(out=sg[:, t0:t0 + tn], in_=pa, func=AF.Silu), reads=[pk], writes=[sk])
                if gi == 4:
                    r0 = c0 + sb * 128
                    S.op("sp", lambda e: e.dma_start(out=F1[b, r0:r0 + 128, :], in_=sg), reads=[sk], writes=["F1"], dma=sk)
                return
            if gi == 0:
                cnt["hp"] += 1
            hi = cnt["hp"] % 2
            hb_, hk_ = hp[hi], "hp%d" % hi
            off = 2 + t0 if gi < 4 else 2052 + 2
            if gi % 2 == 0:
                S.op("act", lambda e: e.activation(out=hb_[:, off:off + tn], in_=pa, func=AF.Copy), reads=[pk], writes=[hk_])
            else:
                S.op("dve", lambda e: e.tensor_copy(out=hb_[:, off:off + tn], in_=pa), reads=[pk], writes=[hk_])
            if gi < 4:
                return
            blk = (c0 + sb * 128) // 128
            di = cnt["hp"] % 2
            d3 = _v3(dg[di], 5)
            for j in range(5):
                S.op("pool", lambda e, j=j: e.tensor_scalar(out=d3[:, j, :], in0=C.ident, scalar1=cw3[:, blk, j:j + 1], scalar2=None, op0=ALU.mult),
                     reads=["ident", "cw"], writes=["dg%d" % di])
            cnt["st"] += 1
            si = cnt["st"] % 3
            sg, sk = stg[si], "p5st%d" % si
            outs = []
            for g2, (u0, un) in enumerate(TG):
                bank = cnt["c"] % 4
                cnt["c"] += 1
                pc = C.ps[:, bank * 512:bank * 512 + un]
                base = u0 if g2 < 4 else 2052
                for j in range(5):
                    S.op("pe", lambda e, pc=pc, j=j, base=base, un=un: e.matmul(pc, lhsT=d3[:, j, :], rhs=hb_[:, base + j:base + j + un], start=(j == 0), stop=(j == 4)),
                         reads=["dg%d" % di, hk_], writes=["psb%d" % bank])
                if nm == "v":
                    S.op("act", lambda e, pc=pc, u0=u0, un=un: e.activation(out=sg[:, u0:u0 + un], in_=pc, func=AF.Silu), reads=["psb%d" % bank], writes=[sk])
                else:
                    ssl = cnt["s"] % 3
                    cnt["s"] += 1
                    s_ = sT[ssl]
                    S.op("act", lambda e, pc=pc, s_=s_, un=un: e.activation(out=s_[:, 0:un], in_=pc, func=AF.Silu), reads=["psb%d" % bank], writes=["sT%d" % ssl])
                    outs.append((s_, "sT%d" % ssl, u0, un))
                    if len(outs) == 3 or g2 == 4:
                        pend = []
                        for (s2, s2k, v0, vn) in outs:
                            nsl = cnt["n"] % 2
                            cnt["n"] += 1
                            S.op("pool", lambda e, s2=s2, vn=vn, nsl=nsl: e.tensor_tensor(out=sq[nsl][:, 0:vn], in0=s2[:, 0:vn], in1=s2[:, 0:vn], op=ALU.mult),
                                 reads=[s2k], writes=["sq%d" % nsl])
                            bank2 = cnt["c"] % 4
                            cnt["c"] += 1
                            p3 = C.ps[:, bank2 * 512:bank2 * 512 + vn]
                            S.op("pe", lambda e, p3=p3, nsl=nsl, vn=vn: e.matmul(p3, lhsT=C.ones, rhs=sq[nsl][:, 0:vn], start=True, stop=True),
                                 reads=["ones", "sq%d" % nsl], writes=["psb%d" % bank2])
                            S.op("act", lambda e, p3=p3, nsl=nsl, vn=vn: e.activation(out=lnv[nsl][:, 0:vn], in_=p3, func=AF.Ln, bias=C.eps_t[:, 0:1]),
                                 reads=["psb%d" % bank2], writes=["lnv%d" % nsl])
                            pend.append((s2, s2k, v0, vn, nsl))
                            if len(pend) == 2 or (s2 is outs[-1][0]):
                                for (s3_, s3k, w0, wn, ns2) in pend:
                                    bias_ap = C.lnq[:, 0:1] if nm == "q" else C.zero_t[:, 0:1]
                                    S.op("act", lambda e, ns2=ns2, wn=wn, bias_ap=bias_ap: e.activation(out=rst[ns2][:, 0:wn], in_=lnv[ns2][:, 0:wn], func=AF.Exp, scale=-0.5, bias=bias_ap),
                                         reads=["lnv%d" % ns2], writes=["rst%d" % ns2])
                                    S.op("dve", lambda e, s3_=s3_, w0=w0, wn=wn, ns2=ns2: e.tensor_tensor(out=sg[:, w0:w0 + wn], in0=s3_[:, 0:wn], in1=rst[ns2][:, 0:wn], op=ALU.mult),
                                         reads=[s3k, "rst%d" % ns2], writes=[sk])
                                pend = []
                        outs = []
            r0 = c0 + sb * 128
            S.op("sp", lambda e: e.dma_start(out=F1[b, r0:r0 + 128, :], in_=sg), reads=[sk], writes=["F1"], dma=sk)

        proj_fm(C, xm3, xkey, w3, wkey, ncw, evac, "p5")
    S.barrier()


FSEQ = [16, 17] + list(range(16))
BSEQ = [17, 16] + list(range(15, -1, -1))


def dn_level_masks():
    s_ = np.arange(128)[:, None]
    c_ = np.arange(128)[None, :]
    out = np.zeros((128, 7, 4, 2, 128), np.float32)
    for k in range(1, 8):
        h = 1 << (k - 1)
        same = (s_ // (2 * h)) == (c_ // (2 * h))
        ur = same & ((s_ % (2 * h)) < h) & ((c_ % (2 * h)) >= h)
        ll = ur.T
        for j in range(4):
            fwd = j < 2
            out[:, k - 1, j, 0, :] = -(ur if fwd else ll).astype(np.float32)
            out[:, k - 1, j, 1, :] = -(ll if fwd else ur).astype(np.float32)
    id8 = np.zeros((128, 4, 2, 128), np.float32)
    id8[:, :, :, :] = np.eye(128, dtype=np.float32)[:, None, None, :]
    return out.reshape(128, 7, 1024), id8.reshape(128, 1024)


def dn_level_masks2():
    lm, _ = dn_level_masks()
    lm = lm.reshape(128, 7, 4, 2, 128)
    return np.ascontiguousarray(lm[:, :, 0::2, :, :]).reshape(128, 7, 512)


def dn_masks():
    s = np.arange(128)[:, None]
    c = np.arange(128)[None, :]
    incl = np.stack([(s <= c), (s <= c), (s >= c), (s >= c)], 0).astype(np.float32)
    strict = np.stack([(s < c), (s < c), (s > c), (s > c)], 0).astype(np.float32)
    ident4 = np.stack([np.eye(128, dtype=np.float32)] * 4, 0)
    return incl.transpose(1, 0, 2).copy(), strict.transpose(1, 0, 2).copy(), ident4.transpose(1, 0, 2).copy()


def phase6(C):
    for b in getattr(C, "p6_batches", range(NB)):
        _phase6_b(C, b)


def dump(C, name, ap, readkeys):
    if not getattr(C, "debug", False):
        return
    t = C.nc.dram_tensor("dbg_" + name, list(ap.shape), ap.dtype, kind="ExternalOutput").ap()
    if ap.shape[1] * (4 if ap.dtype == F32 else 2) > 2048:
        C.S.op("sp", lambda e: e.dma_start(out=t, in_=ap), reads=readkeys, writes=["DR_dbg"], dma=1)
        return
    if not hasattr(C, "dbg_stage"):
        C.dbg_stage = C.es.enter_context(C.nc.sbuf_tensor("dbgst", [128, 512], F32))
    st = C.dbg_stage[:, 0:ap.shape[1]] if ap.dtype == F32 else C.dbg_stage[:, 0:(ap.shape[1] + 1) // 2].bitcast(BF16)[:, 0:ap.shape[1]]
    C.S.op("dve", lambda e: e.tensor_copy(out=st, in_=ap), reads=readkeys, writes=["dbgst"])
    C.S.op("sp", lambda e: e.dma_start(out=t, in_=st), reads=["dbgst"], writes=["DR_dbg"], dma=1)


def _phase6_b(C, b):
    nc, S, A = C.nc, C.S, C.A
    A.reset()
    F1, BA, OG1 = C.dram["F1"], C.dram["BA"], C.dram["OG1"]
    mincl = A.alloc(512)
    mstr = A.alloc(512)
    ones_f = A.alloc(128)
    onorm = A.alloc(1)
    S.op("sp", lambda e: e.dma_start(out=_v3(mincl, 4), in_=C.dram["dn_mincl"]), writes=["mincl"], dma=1)
    S.op("sp", lambda e: e.dma_start(out=_v3(mstr, 4), in_=C.dram["dn_mstrict"]), writes=["mstr"], dma=1)
    S.op("sp", lambda e: e.dma_start(out=onorm, in_=C.dram["dn_o_norm"].rearrange("(p o) -> p o", o=1), allow_slow_non_contiguous=True), writes=["onorm"], dma=1)
    S.op("pool", lambda e: e.memset(ones_f, 1.0), writes=["ones_f"])
    mincl3, mstr3 = _v3(mincl, 4), _v3(mstr, 4)
    lmask = A.alloc(7 * 512, BF16)
    lm4 = lmask.rearrange("p (k d x) -> p k d x", k=7, d=2)
    S.op("sp", lambda e: e.dma_start(out=_v3(lmask, 7), in_=C.dram["dn_lmask2"]), writes=["lmask"], dma=1)
    beta = A.alloc(NTT * 64)
    gg = A.alloc(NTT * 64)
    mark6 = A.off
    ba = A.alloc(NTT * 128)
    ba3 = _v3(ba, NTT)
    S.op("sp", lambda e: e.dma_start(out=ba3, in_=BA[b].rearrange("(tt p) c -> p tt c", p=128)), reads=["BA"], writes=["ba"], dma=1)
    tA = A.alloc(NTT * 64)
    tB = A.alloc(NTT * 64)
    alog = A.alloc(64)
    dtb = A.alloc(64)
    one_t = A.alloc(1)
    S.op("pool", lambda e: e.memset(one_t, 1.0), writes=["one_t"])
    S.op("sp", lambda e: e.dma_start(out=alog, in_=C.dram["dn_a_log"].rearrange("d h -> (d h)").unsqueeze(0).to_broadcast([128, 64])), writes=["alog"], dma=1)
    S.op("sp", lambda e: e.dma_start(out=dtb, in_=C.dram["dn_dt_bias"].rearrange("d h -> (d h)").unsqueeze(0).to_broadcast([128, 64])), writes=["dtb"], dma=1)
    beta3, gg3, tA3, tB3 = _v3(beta, NTT), _v3(gg, NTT), _v3(tA, NTT), _v3(tB, NTT)
    S.op("act", lambda e: e.activation(out=tA3, in_=ba3[:, :, 0:64], func=AF.Exp, scale=-1.0), reads=["ba"], writes=["tA"])
    S.op("dve", lambda e: e.tensor_scalar(out=tA, in0=tA, scalar1=1.0, scalar2=None, op0=ALU.add), reads=["tA"], writes=["tA"])
    S.op("dve", lambda e: e.reciprocal(out=beta, in_=tA), reads=["tA"], writes=["beta"])
    S.op("dve", lambda e: e.tensor_tensor(out=tB3, in0=ba3[:, :, 64:128], in1=dtb.unsqueeze(1).to_broadcast([128, NTT, 64]), op=ALU.add), reads=["ba", "dtb"], writes=["tB"])
    S.op("dve", lambda e: e.scalar_tensor_tensor(out=tA, in0=tB, scalar=-1.0, in1=tB, op0=ALU.mult, op1=ALU.max), reads=["tB", "beta"], writes=["tA"])
    S.op("act", lambda e: e.activation(out=tA, in_=tA, func=AF.Exp, scale=-1.0), reads=["tA"], writes=["tA"])
    S.op("act", lambda e: e.activation(out=tA, in_=tA, func=AF.Ln, bias=one_t[:, 0:1]), reads=["tA", "one_t"], writes=["tA"])
    S.op("dve", lambda e: e.scalar_tensor_tensor(out=tB, in0=tB, scalar=0.0, in1=tA, op0=ALU.max, op1=ALU.add), reads=["tA", "tB"], writes=["tB"])
    S.op("act", lambda e: e.activation(out=alog, in_=alog, func=AF.Exp), reads=["alog"], writes=["alog"])
    S.op("dve", lambda e: e.scalar_tensor_tensor(out=gg3, in0=tB3, scalar=-1.0, in1=alog.unsqueeze(1).to_broadcast([128, NTT, 64]), op0=ALU.mult, op1=ALU.mult),
         reads=["tB", "alog"], writes=["gg"])
    S.barrier()
    A.off = mark6
    SHARED = {"gg", "beta", "mincl", "mstr", "lmask", "ident", "ones", "onorm", "ones_f", "eps", "zero", "lnq"}
    S0 = S

    class _SlotSched:
        def __init__(self, si):
            self.si = si

        def op(self, eng, fn, reads=(), writes=(), dma=None):
            f = lambda k: k if (k in SHARED or k in Sched.DRAMKEYS or k.startswith("DR_")) else "s%d_%s" % (self.si, k)
            return S0.op(eng, fn, reads=[f(k) for k in reads], writes=[f(k) for k in writes], dma=dma)

    def run_slot(si, head_list):
        S = _SlotSched(si)
        hbufs = [dict(q=A.alloc(T, BF16), k=A.alloc(T, BF16), v=A.alloc(2 * T, BF16))]
        ktok = A.alloc(NTT * 128, BF16)
        vtok = A.alloc(NTT * 256, BF16)
        oacc = A.alloc(2 * SEQ)
        o3 = _v3(oacc, 2)
        szb1 = A.alloc(SEQ, BF16)
        def mk():
            dec_ = A.alloc(512)
            tmp_ = A.alloc(512)
            tmp2_ = A.alloc(512)
            return dict(grep=A.alloc(512), d1=dec_, dec=dec_, gam=A.alloc(512), bm=A.alloc(512), tmp=tmp_, tmp2=tmp2_, t3=tmp_, t4=tmp2_,
                        x12=A.alloc(12), e12=A.alloc(12), negb=A.alloc(4), xn=A.alloc(1024, BF16), rt=A.alloc(1024, BF16), yy=A.alloc(1024, BF16), xnm=[A.alloc(1024, BF16) for _ in range(2)],
                        xb=dec_, intra=A.alloc(512, BF16), gq=A.alloc(512, BF16), kd=A.alloc(512, BF16), vd=A.alloc(512, BF16), vn=A.alloc(512, BF16))
        stp = [mk()]
        S4 = A.alloc(512)
        S4b = A.alloc(512, BF16)
        sqb = [A.alloc(512, BF16)] * 2
        lnv = [A.alloc(512)] * 2
        rst = lnv
        osum = [A.alloc(512) for _ in range(2)]
        ps = C.ps
        pbase = si * 2048
        kG = kS1 = "psb%d" % (pbase // 512)
        kAB = kS2 = "psb%d" % (pbase // 512 + 1)
        kM = kT = kC = kN = "psM%d" % (pbase // 512)
        psG = ps[:, pbase:pbase + 512]
        psG3 = _v3(psG, 4)
        psS1 = psG
        psA = ps[:, pbase + 512:pbase + 768]
        psB = ps[:, pbase + 768:pbase + 1024]
        psS2 = ps[:, pbase + 512:pbase + 1024]
        psM = ps[:, pbase + 1024:pbase + 2048]
        psM3 = _v3(psM, 4)
        psT = ps[:, pbase + 1024:pbase + 1536]
        psTb = psT.bitcast(BF16)
        psC = ps[:, pbase + 1536:pbase + 1540]
        psN = psT

        for g in head_list:
            H = hbufs[0]
            hk = "h6"
            S.op("sp", lambda e, H=H, g=g: e.dma_start(out=H["q"], in_=F1[b, g * 128:(g + 1) * 128, :]), reads=["F1"], writes=[hk + "q"], dma=1)
            S.op("sp", lambda e, H=H, g=g: e.dma_start(out=H["k"], in_=F1[b, 2048 + g * 128:2048 + (g + 1) * 128, :]), reads=["F1"], writes=[hk + "k"], dma=1)
            S.op("sp", lambda e, H=H, g=g: e.dma_start(out=_v3(H["v"], 2), in_=F1[b, 4096 + 2 * g * 128:4096 + (2 * g + 2) * 128, :].rearrange("(v p) t -> p v t", p=128)),
                 reads=["F1"], writes=[hk + "v"], dma=1)
            QT, KT, VT3 = H["q"], H["k"], _v3(H["v"], 2)
            kt3 = _v3(ktok, NTT)
            vt4 = vtok.rearrange("p (t v d) -> p t v d", t=NTT, v=2)
            jobs = [("k", tt, 0) for tt in range(NTT)] + [("v", tt, vh) for tt in range(NTT) for vh in range(2)]
            groups = [jobs[0:8], jobs[8:16], jobs[16:18]] + [jobs[18 + i:18 + i + 8] for i in range(0, 36, 8)]
            for j0, grp in enumerate(groups):
                yield
                j0 = j0 * 8
                for i, (kind, tt, vh) in enumerate(grp):
                    src = KT[:, tt * 128:(tt + 1) * 128] if kind == "k" else VT3[:, vh, tt * 128:(tt + 1) * 128]
                    S.op("pe", lambda e, i=i, src=src: e.transpose(out=psTb[:, i * 128:(i + 1) * 128], in_=src, identity=C.ident),
                         reads=[hk + "k", hk + "v", "ident"], writes=[kT])
                kind0, tt0, vh0 = grp[0]
                n = len(grp)
                if kind0 == "k":
                    dst = ktok[:, tt0 * 128:(tt0 + n) * 128]
                    dk_ = "ktok"
                else:
                    dst = vtok[:, (tt0 * 2 + vh0) * 128:(tt0 * 2 + vh0 + n) * 128]
                    dk_ = "vtok"
                if (j0 // 8) % 2 == 0:
                    S.op("act", lambda e, dst=dst, n=n: e.activation(out=dst, in_=psTb[:, 0:n * 128], func=AF.Copy), reads=[kT], writes=[dk_])
                else:
                    S.op("dve", lambda e, dst=dst, n=n: e.tensor_copy(out=dst, in_=psTb[:, 0:n * 128]), reads=[kT], writes=[dk_])
            S.op("pool", lambda e: e.memset(S4, 0.0), writes=["S4"])
            S.op("pool", lambda e: e.memset(S4b, 0.0), writes=["S4b"])
            S43, S4b3 = _v3(S4, 4), _v3(S4b, 4)
            c0 = 2 * g
            def step(s, part, g=g, H=H, hk=hk, QT=QT, KT=KT, VT3=VT3, kt3=kt3, vt4=vt4, c0=c0, S43=S43, S4b3=S4b3):
                P = stp[0]
                pk = "st0"
                blks = (FSEQ[s], BSEQ[s])
                cols = [(d * 32 + c0) for d in range(2)]
                grep3, d13, dec3, gam3, bm3, tmp3, tmp23 = [_v3(P[n_], 4) for n_ in ("grep", "d1", "dec", "gam", "bm", "tmp", "tmp2")]
                xb3, intra3, gq3, kd3, vd3, vn3, t33, t43 = [_v3(P[n_], 4) for n_ in ("xb", "intra", "gq", "kd", "vd", "vn", "t3", "t4")]
                x12, e12, negb = P["x12"], P["e12"], P["negb"]
                RT = P["rt"].rearrange("p (j o c) -> p j o c", j=4, o=2)
                krt = pk + "rt"
                xn4 = P["xn"].rearrange("p (j o c) -> p j o c", j=4, o=2)
                kxn = pk + "xn"
                if part == "A":
                    yield
                    for d in range(2):
                        gs = gg3[:, blks[d], cols[d]:cols[d] + 2]
                        S.op("pool", lambda e, d=d, gs=gs: e.tensor_copy(out=grep3[:, 2 * d:2 * d + 2, :], in_=gs.unsqueeze(2).to_broadcast([128, 2, 128])),
                             reads=["gg"], writes=[pk + "grep"])
                        S.op("dve", lambda e, d=d: e.tensor_scalar(out=negb[:, 2 * d:2 * d + 2], in0=beta3[:, blks[d], cols[d]:cols[d] + 2], scalar1=-1.0, scalar2=None, op0=ALU.mult),
                             reads=["beta"], writes=[pk + "negb"])
                        S.op("pool", lambda e, d=d: e.tensor_tensor(out=bm3[:, 2 * d:2 * d + 2, :], in0=mstr3[:, 2 * d:2 * d + 2, :],
                                                                    in1=beta3[:, blks[d], cols[d]:cols[d] + 2].unsqueeze(2).to_broadcast([128, 2, 128]), op=ALU.mult),
                             reads=["beta", "mstr"], writes=[pk + "bm"])
                    yield
                    for j in range(4):
                        d = j // 2
                        S.op("pe", lambda e, j=j, d=d: e.matmul(psG3[:, j, :], lhsT=grep3[:, j, :], rhs=mincl3[:, 2 * d, :], start=True, stop=True),
                             reads=[pk + "grep", "mincl"], writes=[kG])
                    yield
                    for d in range(2):
                        S.op("pe", lambda e, d=d: e.matmul(psC[:, 2 * d:2 * d + 2], lhsT=mincl3[:, 2 * d, :], rhs=gg3[:, blks[d], cols[d]:cols[d] + 2], start=True, stop=True),
                             reads=["gg", "mincl"], writes=[kC])
                    yield
                    for d in range(2):
                        kb_ = KT[:, blks[d] * 128:(blks[d] + 1) * 128]
                        qb_ = QT[:, blks[d] * 128:(blks[d] + 1) * 128]
                        S.op("pe", lambda e, d=d, kb_=kb_: e.matmul(psA[:, d * 128:(d + 1) * 128], lhsT=kb_, rhs=kb_, start=True, stop=True), reads=[hk + "k"], writes=[kAB])
                        S.op("pe", lambda e, d=d, kb_=kb_, qb_=qb_: e.matmul(psB[:, d * 128:(d + 1) * 128], lhsT=kb_, rhs=qb_, start=True, stop=True), reads=[hk + "k", hk + "q"], writes=[kAB])
                    yield
                    S.op("dve", lambda e: e.tensor_copy(out=x12[:, 0:4], in_=psC), reads=[kC], writes=[pk + "x12"])
                    yield
                    for d in range(2):
                        last = 127 if d == 0 else 0
                        S.op("dve", lambda e, d=d, last=last: e.tensor_copy(out=x12[:, 8 + 2 * d:10 + 2 * d], in_=psG3[:, 2 * d:2 * d + 2, last]), reads=[kG], writes=[pk + "x12"])
                    yield
                    S.op("dve", lambda e: e.tensor_tensor(out=x12[:, 4:8], in0=x12[:, 8:12], in1=x12[:, 0:4], op=ALU.subtract), reads=[pk + "x12"], writes=[pk + "x12"])
                    yield
                    S.op("act", lambda e: e.activation(out=e12, in_=x12, func=AF.Exp), reads=[pk + "x12"], writes=[pk + "e12"])
                    yield
                    S.op("dve", lambda e: e.tensor_tensor(out=d13, in0=psG3, in1=x12[:, 0:4].unsqueeze(2).to_broadcast([128, 4, 128]), op=ALU.subtract),
                         reads=[kG, pk + "x12"], writes=[pk + "dec"])
                    yield
                    S.op("pool", lambda e: e.tensor_scalar(out=P["d1"], in0=P["d1"], scalar1=0.0, scalar2=-80.0, op0=ALU.min, op1=ALU.max), reads=[pk + "dec"], writes=[pk + "dec"])
                    yield
                    S.op("act", lambda e: e.activation(out=P["dec"], in_=P["d1"], func=AF.Exp), reads=[pk + "dec"], writes=[pk + "dec"])
                    yield
                    S.op("act", lambda e: e.activation(out=P["gam"], in_=psG, func=AF.Exp), reads=[kG], writes=[pk + "gam"])
                    dec4 = P["dec"].rearrange("p (d v c) -> p d v c", d=2, v=2)
                    psA4 = psA.rearrange("p (d c) -> p d c", d=2).unsqueeze(2).to_broadcast([128, 2, 2, 128])
                    psB4 = psB.rearrange("p (d c) -> p d c", d=2).unsqueeze(2).to_broadcast([128, 2, 2, 128])
                    yield
                    S.op("dve", lambda e, psA4=psA4, dec4=dec4: e.tensor_tensor(out=P["tmp"].rearrange("p (d v c) -> p d v c", d=2, v=2), in0=psA4, in1=dec4, op=ALU.mult),
                         reads=[kAB, pk + "dec"], writes=[pk + "tmp"])
                    yield
                    S.op("pool", lambda e: e.tensor_tensor(out=xn4[:, :, 0, :], in0=tmp3, in1=bm3, op=ALU.mult), reads=[pk + "tmp", pk + "bm"], writes=[kxn])
                    yield
                    S.op("dve", lambda e, psB4=psB4, dec4=dec4: e.tensor_tensor(out=P["tmp2"].rearrange("p (d v c) -> p d v c", d=2, v=2), in0=psB4, in1=dec4, op=ALU.mult),
                         reads=[kAB, pk + "dec"], writes=[pk + "tmp2"])
                    yield
                    S.op("pool", lambda e: e.tensor_tensor(out=intra3, in0=tmp23, in1=mincl3, op=ALU.mult), reads=[pk + "tmp2", "mincl"], writes=[pk + "intra"])
                    yield
                    for d in range(2):
                        qb_ = QT[:, blks[d] * 128:(blks[d] + 1) * 128]
                        S.op("pool", lambda e, d=d, qb_=qb_: e.tensor_tensor(out=gq3[:, 2 * d:2 * d + 2, :], in0=qb_.unsqueeze(1).to_broadcast([128, 2, 128]), in1=gam3[:, 2 * d:2 * d + 2, :], op=ALU.mult),
                             reads=[hk + "q", pk + "gam"], writes=[pk + "gq"])
                        S.op("pool", lambda e, d=d: e.tensor_tensor(out=kd3[:, 2 * d:2 * d + 2, :], in0=kt3[:, blks[d], :].unsqueeze(1).to_broadcast([128, 2, 128]),
                                                                    in1=e12[:, 4 + 2 * d:6 + 2 * d].unsqueeze(2).to_broadcast([128, 2, 128]), op=ALU.mult),
                             reads=["ktok", pk + "e12"], writes=[pk + "kd"])
                    xn4 = P["xn"].rearrange("p (j o c) -> p j o c", j=4, o=2)
                    rt4 = P["rt"].rearrange("p (j o c) -> p j o c", j=4, o=2)
                    yy4 = P["yy"].rearrange("p (j o c) -> p j o c", j=4, o=2)
                    psM4 = psM.rearrange("p (j o c) -> p j o c", j=4, o=2)
                    kxn, krt_, kyy = pk + "xn", pk + "rt", pk + "yy"
                    yield
                    pass
                    yield
                    for j in range(4):
                        S.op("pe", lambda e, j=j: e.transpose(out=psTb[:, j * 128:(j + 1) * 128], in_=xn4[:, j, 0, :], identity=C.ident), reads=[kxn, "ident"], writes=[kT])
                    yield
                    S.op("act", lambda e: e.activation(out=xn4[:, :, 1, :], in_=_v3(psTb[:, 0:512], 4), func=AF.Copy), reads=[kT], writes=[kxn])
                    def mask_level(lv):
                        dst = P["xnm"][lv % 2]
                        for d in range(2):
                            S.op("pool", lambda e, d=d: e.tensor_tensor(out=_v3(dst[:, d * 512:(d + 1) * 512], 2), in0=_v3(P["xn"][:, d * 512:(d + 1) * 512], 2),
                                                                       in1=lm4[:, lv, d, :].unsqueeze(1).to_broadcast([128, 2, 256]), op=ALU.mult),
                                 reads=[kxn, "lmask"], writes=[pk + "xnm%d" % (lv % 2)])
                    yield
                    mask_level(0)
                    yield
                    id8b = C.ident.unsqueeze(1).to_broadcast([128, 8, 128])
                    S.op("dve", lambda e: e.tensor_tensor(out=_v3(P["rt"], 8), in0=_v3(P["xnm"][0], 8), in1=id8b, op=ALU.add), reads=[pk + "xnm0", "ident"], writes=[krt_])
                    mask_level(1)
                    for lv in range(1, 7):
                        xm4 = P["xnm"][lv % 2].rearrange("p (j o c) -> p j o c", j=4, o=2)
                        kx = pk + "xnm%d" % (lv % 2)
                        yield
                        for j in range(4):
                            S.op("pe", lambda e, j=j, xm4=xm4: e.matmul(psM4[:, j, 0, :], lhsT=xm4[:, j, 1, :], rhs=rt4[:, j, 0, :], start=True, stop=True), reads=[kx, krt_], writes=[kM])
                            S.op("pe", lambda e, j=j, xm4=xm4: e.matmul(psM4[:, j, 1, :], lhsT=xm4[:, j, 0, :], rhs=rt4[:, j, 1, :], start=True, stop=True), reads=[kx, krt_], writes=[kM])
                        yield
                        S.op("dve", lambda e: e.tensor_tensor(out=_v3(P["yy"], 8), in0=_v3(psM, 8), in1=id8b, op=ALU.add), reads=[kM, "ident"], writes=[kyy])
                        if lv < 6:
                            mask_level(lv + 1)
                        yield
                        for j in range(4):
                            S.op("pe", lambda e, j=j: e.matmul(psM4[:, j, 0, :], lhsT=rt4[:, j, 1, :], rhs=yy4[:, j, 0, :], start=True, stop=True), reads=[kyy, krt_], writes=[kM])
                            if lv < 6:
                                S.op("pe", lambda e, j=j: e.matmul(psM4[:, j, 1, :], lhsT=rt4[:, j, 0, :], rhs=yy4[:, j, 1, :], start=True, stop=True), reads=[kyy, krt_], writes=[kM])
                        yield
                        if lv < 6:
                            S.op("act", lambda e: e.activation(out=P["rt"], in_=psM, func=AF.Copy), reads=[kM], writes=[krt_])
                        else:
                            S.op("act", lambda e: e.activation(out=rt4[:, :, 0, :], in_=psM4[:, :, 0, :], func=AF.Copy), reads=[kM], writes=[krt_])
                    RT = rt4
                    krt = krt_
                    return
                yield
                for j in range(4):
                    d = j // 2
                    kb_ = KT[:, blks[d] * 128:(blks[d] + 1) * 128]
                    S.op("pe", lambda e, j=j, kb_=kb_: e.matmul(psS1[:, j * 128:(j + 1) * 128], lhsT=kb_, rhs=S4b3[:, j, :], start=True, stop=True), reads=[hk + "k", "S4b"], writes=[kS1])
                yield
                S.op("dve", lambda e: e.tensor_tensor(out=t33, in0=_v3(psS1, 4), in1=e12[:, 0:4].unsqueeze(2).to_broadcast([128, 4, 128]), op=ALU.mult),
                     reads=[kS1, pk + "e12"], writes=[pk + "tmp"])
                yield
                for d in range(2):
                    S.op("pool" if d == 0 else "dve", lambda e, d=d: e.tensor_tensor(out=vd3[:, 2 * d:2 * d + 2, :], in0=t33[:, 2 * d:2 * d + 2, :], in1=vt4[:, blks[d], :, :], op=ALU.subtract),
                         reads=[pk + "tmp", "vtok"], writes=[pk + "vd"])
                yield
                for j in range(4):
                    S.op("pe", lambda e, j=j: e.matmul(psS2[:, j * 128:(j + 1) * 128], lhsT=RT[:, j, 0, :], rhs=vd3[:, j, :], start=True, stop=True), reads=[krt, pk + "vd"], writes=[kS2])
                yield
                S.op("dve", lambda e: e.tensor_tensor(out=vn3, in0=_v3(psS2, 4), in1=negb.unsqueeze(2).to_broadcast([128, 4, 128]), op=ALU.mult),
                     reads=[kS2, pk + "negb"], writes=[pk + "vn"])
                if s >= 2:
                    for j in range(4):
                        S.op("pe", lambda e, j=j: e.matmul(psS1[:, j * 128:(j + 1) * 128], lhsT=S4b3[:, j, :], rhs=gq3[:, j, :], start=True, stop=False), reads=["S4b", pk + "gq"], writes=[kS1])
                        S.op("pe", lambda e, j=j: e.matmul(psS1[:, j * 128:(j + 1) * 128], lhsT=vn3[:, j, :], rhs=intra3[:, j, :], start=False, stop=True), reads=[pk + "vn", pk + "intra"], writes=[kS1])
                    for d in range(2):
                        dstv = o3[:, :, blks[d] * 128:(blks[d] + 1) * 128]
                        srcv = _v3(psS1[:, d * 256:(d + 1) * 256], 2)
                        if s <= 9:
                            S.op("act", lambda e, dstv=dstv, srcv=srcv: e.activation(out=dstv, in_=srcv, func=AF.Copy), reads=[kS1], writes=["oacc"])
                        else:
                            S.op("dve", lambda e, dstv=dstv, srcv=srcv: e.tensor_tensor(out=dstv, in0=srcv, in1=dstv, op=ALU.add), reads=[kS1, "oacc"], writes=["oacc"])
                if s < NTT - 1:
                    for j in range(4):
                        S.op("pe", lambda e, j=j: e.matmul(psS2[:, j * 128:(j + 1) * 128], lhsT=kd3[:, j, :], rhs=vn3[:, j, :], start=True, stop=True), reads=[pk + "kd", pk + "vn"], writes=[kS2])
                    S.op("pool", lambda e: e.tensor_tensor(out=t43, in0=S43, in1=e12[:, 8:12].unsqueeze(2).to_broadcast([128, 4, 128]), op=ALU.mult),
                         reads=["S4", pk + "e12"], writes=[pk + "tmp2"])
                    S.op("dve", lambda e: e.tensor_tensor(out=S4, in0=psS2, in1=P["t4"], op=ALU.add), reads=[kS2, pk + "tmp2"], writes=["S4"])
                    S.op("act", lambda e: e.activation(out=S4b, in_=S4, func=AF.Copy), reads=["S4"], writes=["S4b"])
            nst_ = getattr(C, "p6_nsteps", NTT)
            for s_ in range(nst_):
                yield from step(s_, "A")
                yield from step(s_, "S")
            for vh in range(2):
                S.op("sp", lambda e, vh=vh, g=g: e.dma_start(out=szb1, in_=F1[b, 8192 + (2 * g + vh) * 128:8192 + (2 * g + vh + 1) * 128, 0:SEQ]),
                     reads=["F1"], writes=["szb1"], dma=1)
                for gi in range(4):
                    yield
                    t0 = gi * 512
                    sl = gi % 2
                    a_ = o3[:, vh, t0:t0 + 512]
                    S.op("pool", lambda e, a_=a_, sl=sl: e.tensor_tensor(out=sqb[sl], in0=a_, in1=a_, op=ALU.mult), reads=["oacc"], writes=["sqb6"])
                    S.op("pe", lambda e, sl=sl: e.matmul(psS1, lhsT=C.ones, rhs=sqb[sl], start=True, stop=True), reads=["ones", "sqb6"], writes=[kS1])
                    S.op("act", lambda e, sl=sl: e.activation(out=lnv[sl], in_=psS1, func=AF.Ln, scale=1.0 / 128, bias=C.eps_t[:, 0:1]), reads=[kS1], writes=["lnv6"])
                    S.op("act", lambda e, sl=sl: e.activation(out=rst[sl], in_=lnv[sl], func=AF.Exp, scale=-0.5), reads=["lnv6"], writes=["lnv6"])
                    S.op("dve", lambda e, sl=sl, a_=a_: e.scalar_tensor_tensor(out=osum[sl], in0=a_, scalar=onorm[:, 0:1], in1=rst[sl], op0=ALU.mult, op1=ALU.mult),
                         reads=["oacc", "lnv6", "onorm"], writes=["osum%d" % sl])
                    S.op("pool", lambda e, sl=sl, t0=t0: e.tensor_tensor(out=szb1[:, t0:t0 + 512], in0=osum[sl], in1=szb1[:, t0:t0 + 512], op=ALU.mult),
                         reads=["osum%d" % sl, "szb1"], writes=["szb1"])
                r0 = (2 * g + vh) * 128
                S.op("sp", lambda e, r0=r0: e.dma_start(out=OG1[b, r0:r0 + 128, :], in_=szb1), reads=["szb1"], writes=["OG1"], dma=1)

    heads_all = list(getattr(C, "p6_heads", range(16)))
    gens = [run_slot(0, heads_all[0::2]), run_slot(1, heads_all[1::2])]
    while gens:
        for g_ in list(gens):
            try:
                next(g_)
            except StopIteration:
                gens.remove(g_)
    S.barrier()


def phase7(C):
    nc, S, A = C.nc, C.S, C.A
    for b in range(NB):
        def res(tt, cb, b=b):
            return C.dram["X1"][b, tt * 128:(tt + 1) * 128, cb * 256:(cb + 1) * 256]

        def dst(tt, cb, b=b):
            return C.dram["X2"][b, tt * 128:(tt + 1) * 128, cb * 256:(cb + 1) * 256]
        out_proj(C, b, 1, C.dram["OG1"][b], 32, C.dram["dn_w_out"], res, dst, list(range(16)), "p7", cbw=256, ntok=SEQ)
    A.reset()
    fn = A.alloc(D)
    S.op("sp", lambda e: e.dma_start(out=fn, in_=C.dram["final_norm"].unsqueeze(0).to_broadcast([128, D])), writes=["fn"], dma=1)
    xr = [A.alloc(D) for _ in range(3)]
    xo = [A.alloc(D) for _ in range(3)]
    junk = A.alloc(D, BF16)
    st = [A.alloc(4) for _ in range(3)]
    it = 0
    for b in range(NB):
        for tt in range(16):
            sl = it % 3
            it += 1
            xt, xo_, s4 = xr[sl], xo[sl], st[sl]
            S.op("sp", lambda e, xt=xt, b=b, tt=tt: e.dma_start(out=xt, in_=C.dram["X2"][b, tt * 128:(tt + 1) * 128, :]), reads=["X2"], writes=["fxr%d" % sl], dma=1)
            S.op("act", lambda e, xt=xt, s4=s4: e.activation(out=junk, in_=xt, func=AF.Square, accum_out=s4[:, 0:1]), reads=["fxr%d" % sl], writes=["fjunk", "fst%d" % sl])
            S.op("act", lambda e, s4=s4: e.activation(out=s4[:, 1:2], in_=s4[:, 0:1], func=AF.Sqrt, scale=1.0 / D, bias=C.eps_t[:, 0:1]), reads=["fst%d" % sl], writes=["fst%d" % sl])
            S.op("dve", lambda e, s4=s4: e.reciprocal(out=s4[:, 2:3], in_=s4[:, 1:2]), reads=["fst%d" % sl], writes=["fst%d" % sl])
            S.op("dve", lambda e, xt=xt, xo_=xo_, s4=s4: e.scalar_tensor_tensor(out=xo_, in0=xt, scalar=s4[:, 2:3], in1=fn, op0=ALU.mult, op1=ALU.mult),
                 reads=["fxr%d" % sl, "fst%d" % sl, "fn"], writes=["fxo%d" % sl])
            S.op("sp", lambda e, xo_=xo_, b=b, tt=tt: e.dma_start(out=C.dram["OUT"][b, tt * 128:(tt + 1) * 128, :], in_=xo_), reads=["fxo%d" % sl], writes=["OUT"], dma=1)
    S.barrier()


NCORES = 8
_PHASES = (phase0, phase1, phase2, phase3, phase4, phase5, phase6, phase7)


def _host_inputs(inp):
    cos, sin = rope_tables()
    per_m, nt, mask, drow, dcol = na_geometry()
    rpb = np.asarray(inp["ab_rpb"][0], np.float32)
    nab = np.stack([rpb[h][drow, dcol] for h in range(8)], 0).astype(np.float32)
    mi, ms, id4 = dn_masks()
    lm, id8 = dn_level_masks()
    f = lambda a: np.ascontiguousarray(np.asarray(a, np.float32))
    shared = {
        "w_mod0": f(inp["ab_w_mod"][0]), "w_mod1": f(inp["dn_w_mod"][0]), "b_mod0": f(inp["ab_b_mod"][0]), "b_mod1": f(inp["dn_b_mod"][0]),
        "ab_norm": f(inp["ab_norm"][0]), "ab_w_in": f(inp["ab_w_in"][0]), "ab_w_qb": f(inp["ab_w_qb"][0]), "ab_w_kvb": f(inp["ab_w_kvb"][0]),
        "ab_q_norm": f(inp["ab_q_norm"][0]), "ab_kv_norm": f(inp["ab_kv_norm"][0]), "ab_w_out": f(inp["ab_w_out"][0]),
        "na_mask": mask, "na_bias": nab, "rope_cos": cos, "rope_sin": sin, "ident": np.eye(128, dtype=np.float32).astype(NPBF),
        "dn_norm": f(inp["dn_norm"][0]), "dn_w_in": f(inp["dn_w_in"][0]), "dn_conv": f(inp["dn_conv"][0]), "dn_a_log": f(inp["dn_a_log"][0]),
        "dn_dt_bias": f(inp["dn_dt_bias"][0]), "dn_o_norm": f(inp["dn_o_norm"][0]), "dn_w_out": f(inp["dn_w_out"][0]), "final_norm": f(inp["final_norm"]),
        "dn_mincl": mi, "dn_mstrict": ms, "dn_ident4": id4.astype(NPBF), "dn_lmask2": dn_level_masks2().astype(NPBF),
    }
    maps = []
    for i in range(NCORES):
        m = dict(shared)
        m["x"] = f(inp["x"][NB * i:NB * (i + 1)])
        m["ctx"] = f(inp["ctx"][NB * i:NB * (i + 1)])
        m["cvec"] = np.concatenate([f(inp["c"][NB * i:NB * (i + 1)]), f(inp["c_ctx"])[None]], 0)
        maps.append(m)
    return maps


_INTERNAL = {
    "mod": ([2, 3, 6144], F32), "F0": ([NB, 5184, T], BF16), "VA": ([NB, T, 1024], BF16), "M0": ([NB, 2560, T], BF16), "VM": ([NB, T, 1024], BF16),
    "OG0": ([NB, 2048, T], BF16), "X1": ([NB, T, D], F32), "F1": ([NB, 12288, T], BF16), "BA": ([NB, T, 128], F32), "OG1": ([NB, 4096, SEQ], BF16),
    "X2": ([NB, SEQ, D], F32),
}


def build_program(maps0, phases=_PHASES):
    nc = bass.Bass("TRN2", target_bir_lowering=False)
    with ExitStack() as es:
        dram = {}
        for nm, a in maps0.items():
            dram[nm] = nc.dram_tensor(nm, list(a.shape), BF16 if a.dtype == NPBF else F32, kind="ExternalInput").ap()
        for nm, (shape, dt_) in _INTERNAL.items():
            dram[nm] = nc.dram_tensor(nm, shape, dt_, kind="Internal").ap()
        dram["OUT"] = nc.dram_tensor("OUT", [NB, SEQ, D], F32, kind="ExternalOutput").ap()
        C = make_ctx(nc, es, dram)
        for p in phases:
            p(C)
        C.S.finalize()
    return nc


def kernel(**inputs):
    maps = _host_inputs(inputs)
    nc = build_program(maps[0])
    res = run_bass_kernel_spmd(nc, maps, core_ids=list(range(NCORES)))
    out = np.concatenate([np.asarray(r["OUT"], np.float32) for r in res.results], axis=0)
    return out
```

```python
import numpy as np
import ml_dtypes
from contextlib import ExitStack
import concourse.bass as bass
import concourse.mybir as mybir
from concourse.bass_utils import run_bass_kernel_spmd

F32 = mybir.dt.float32
BF16 = mybir.dt.bfloat16
AF = mybir.ActivationFunctionType
ALU = mybir.AluOpType
NPBF = ml_dtypes.bfloat16

D = 2048
SEQ = 2048
CTX = 256
T = SEQ + CTX
NTT = T // 128
NB = 2
EPS = 1e-6
TG = [(0, 512), (512, 512), (1024, 512), (1536, 512), (2048, 256)]


class Op:
    __slots__ = ("eng", "fn", "deps", "marked", "val", "sem", "is_dma")


class Buf:
    __slots__ = ("w", "r")

    def __init__(self):
        self.w = None
        self.r = []


class Sched:
    ENGS = ("pe", "act", "dve", "pool", "sp")
    ENGOBJ = {"pe": "tensor", "act": "scalar", "dve": "vector", "pool": "gpsimd", "sp": "sync"}

    def __init__(self, nc, es):
        self.nc = nc
        self.es = es
        self.ops = {e: [] for e in self.ENGS}
        self.bufs = {}
        self.sems = {e: es.enter_context(nc.semaphore("s_" + e)) for e in self.ENGS}
        self.dsems = {}
        self.dpool = []
        self.last_dma = {}
        self.nops = 0

    DRAMKEYS = {"mod", "F0", "VA", "M0", "VM", "OG0", "X1", "X2", "F1", "BA", "OG1", "OUT", "ST"}

    def dsem(self, key):
        if key not in self.dsems:
            i = len(self.dsems)
            if i >= len(self.dpool):
                self.dpool.append([self.es.enter_context(self.nc.semaphore("d_%d" % i)), 0])
            self.dsems[key] = self.dpool[i]
        return self.dsems[key]

    def op(self, eng, fn, reads=(), writes=(), dma=None):
        o = Op()
        o.eng = eng
        o.fn = fn
        o.deps = []
        o.marked = False
        o.val = None
        o.sem = None
        o.is_dma = dma is not None
        self.nops += 1
        if dma is not None:
            dk = None
            for k in list(writes) + list(reads):
                if not (k in self.DRAMKEYS or k.startswith("DR_")):
                    dk = k
                    break
            assert dk is not None, (reads, writes)
            d = self.dsem(dk)
            d[1] += 16
            o.sem = d[0]
            o.val = d[1]
            o.marked = True
            self.last_dma[id(d)] = o
        deps = {}
        for k in reads:
            b = self.bufs.get(k)
            if b is None:
                b = self.bufs[k] = Buf()
            if b.w is not None:
                deps[id(b.w)] = b.w
        for k in writes:
            b = self.bufs.get(k)
            if b is None:
                b = self.bufs[k] = Buf()
            if b.w is not None:
                deps[id(b.w)] = b.w
            for r in b.r:
                deps[id(r)] = r
        for k in reads:
            self.bufs[k].r.append(o)
        for k in writes:
            b = self.bufs[k]
            b.w = o
            b.r = []
        for d in deps.values():
            if d is o:
                continue
            if d.eng == "pe" and eng == "pe" and not d.is_dma:
                continue
            d.marked = True
            o.deps.append(d)
        self.ops[eng].append(o)
        return o

    def barrier(self):
        lasts = []
        for e in self.ENGS:
            for o in reversed(self.ops[e]):
                if not o.is_dma and o.fn is not None:
                    o.marked = True
                    lasts.append(o)
                    break
        lasts += list(self.last_dma.values())
        for e in self.ENGS:
            o = Op()
            o.eng = e
            o.fn = None
            o.deps = list(lasts)
            o.marked = False
            o.val = None
            o.sem = None
            o.is_dma = False
            self.ops[e].append(o)
        self.bufs = {}
        self.dsems = {}

    def finalize(self):
        for e in self.ENGS:
            c = 0
            for o in self.ops[e]:
                if o.is_dma or o.fn is None:
                    continue
                if o.marked:
                    c += 1
                    o.val = c
                    o.sem = self.sems[e]
        nc = self.nc
        with nc.Block() as block:
            for e in self.ENGS:
                ops = self.ops[e]

                def body(engine, ops=ops, e=e):
                    seen = {}
                    for o in ops:
                        for d in o.deps:
                            k = id(d.sem)
                            if seen.get(k, 0) >= d.val:
                                continue
                            seen[k] = d.val
                            engine.wait_ge(d.sem, d.val)
                        if o.fn is None:
                            continue
                        ins = o.fn(engine)
                        if o.is_dma:
                            ins.then_inc(o.sem, 16)
                        elif o.marked:
                            ins.then_inc(o.sem, 1)
                    if e == "sp":
                        for (s, v) in self.dpool:
                            if v > 0:
                                engine.wait_ge(s, v)

                getattr(block, self.ENGOBJ[e])(body)


class Arena:
    def __init__(self, nc, es, nwords=51200):
        self.t = es.enter_context(nc.sbuf_tensor("arena", [128, nwords], F32))
        self.n = nwords
        self.off = 0
        self.uid = 0

    def reset(self):
        self.off = 0

    def alloc(self, nelem, dtype=F32):
        nbytes = nelem * (4 if dtype == F32 else 2)
        words = (nbytes + 31) // 32 * 8
        assert self.off + words <= self.n, "SBUF arena overflow %d+%d" % (self.off, words)
        ap = self.t[:, self.off:self.off + words]
        self.off += words
        if dtype != F32:
            ap = ap.bitcast(dtype)
        return ap[:, 0:nelem]

    def key(self, name):
        self.uid += 1
        return "%s#%d" % (name, self.uid)


class Ctx:
    pass


def _v3(ap, a):
    return ap.rearrange("p (a b) -> p a b", a=a)


def phase0(C):
    nc, S, A = C.nc, C.S, C.A
    A.reset()
    csT = A.alloc(48)
    cs3 = _v3(csT, 16)
    bm = A.alloc(6144)
    osb = [A.alloc(2048), A.alloc(2048)]
    wr = [A.alloc(2048) for _ in range(4)]
    ps = C.ps
    for v in range(3):
        S.op("sp", lambda e, v=v: e.dma_start(out=cs3[:, :, v], in_=C.dram["cvec"][v, :].rearrange("(k p) -> p k", p=128),
                                              allow_slow_non_contiguous=True), writes=["csT"], dma="p0c")
    S.op("act", lambda e: e.activation(out=csT, in_=csT, func=AF.Silu), reads=["csT"], writes=["csT"])
    it = 0
    oi = 0
    for l in range(2):
        wm = C.dram["w_mod%d" % l]
        bmod = C.dram["b_mod%d" % l]
        S.op("sp", lambda e, bmod=bmod: e.dma_start(out=bm[0:3, :], in_=bmod.unsqueeze(0).to_broadcast([3, 6144])),
             writes=["bm"], dma="p0b")
        for g in range(3):
            for k in range(16):
                slot = it % 4
                it += 1
                wt = wr[slot]
                S.op("sp", lambda e, wt=wt, k=k, g=g, wm=wm: e.dma_start(out=wt, in_=wm[k * 128:(k + 1) * 128, g * 2048:(g + 1) * 2048]),
                     writes=["p0w%d" % slot], dma="p0w%d" % slot)
                for n in range(4):
                    S.op("pe", lambda e, wt=wt, k=k, n=n: e.matmul(ps[0:3, n * 512:(n + 1) * 512], lhsT=cs3[:, k, :], rhs=wt[:, n * 512:(n + 1) * 512],
                                                                    start=(k == 0), stop=(k == 15)),
                         reads=["csT", "p0w%d" % slot], writes=["p0ps%d" % n])
            ob = osb[oi % 2]
            okey = "p0o%d" % (oi % 2)
            oi += 1
            for n in range(4):
                S.op("dve", lambda e, ob=ob, n=n, g=g: e.tensor_tensor(out=ob[0:3, n * 512:(n + 1) * 512], in0=ps[0:3, n * 512:(n + 1) * 512],
                                                                       in1=bm[0:3, g * 2048 + n * 512:g * 2048 + (n + 1) * 512], op=ALU.add),
                     reads=["p0ps%d" % n, "bm"], writes=[okey])
            S.op("sp", lambda e, ob=ob, l=l, g=g: e.dma_start(out=C.dram["mod"][l, :, g * 2048:(g + 1) * 2048], in_=ob[0:3, :]),
                 reads=[okey], writes=["mod"], dma="p0o")
    S.barrier()


def load_modvecs(C, l, b, gain, tag):
    S, A = C.S, C.A
    g = A.alloc(16)
    S.op("sp", lambda e: e.dma_start(out=g, in_=gain.rearrange("(k p) -> p k", p=128), allow_slow_non_contiguous=True),
         writes=[tag + "g"], dma=tag + "v")
    res = []
    for vi, v in enumerate((b, 2)):
        sc = A.alloc(16)
        sh = A.alloc(16)
        S.op("sp", lambda e, sc=sc, v=v: e.dma_start(out=sc, in_=C.dram["mod"][l, v, 2048:4096].rearrange("(k p) -> p k", p=128),
                                                     allow_slow_non_contiguous=True), reads=["mod"], writes=[tag + "sc%d" % vi], dma=tag + "v")
        S.op("sp", lambda e, sh=sh, v=v: e.dma_start(out=sh, in_=C.dram["mod"][l, v, 0:2048].rearrange("(k p) -> p k", p=128),
                                                     allow_slow_non_contiguous=True), reads=["mod"], writes=[tag + "sh%d" % vi], dma=tag + "v")
        S.op("dve", lambda e, sc=sc: e.scalar_tensor_tensor(out=sc, in0=sc, scalar=1.0, in1=g, op0=ALU.add, op1=ALU.mult),
             reads=[tag + "sc%d" % vi, tag + "g"], writes=[tag + "sc%d" % vi])
        res.append((sc, sh, tag + "sc%d" % vi, tag + "sh%d" % vi))
    return res


def build_xmT(C, xmT, xkey, src_tiles, mv, tag):
    S, A = C.S, C.A
    xr = [A.alloc(D) for _ in range(2)]
    xh = [A.alloc(D, BF16) for _ in range(2)]
    junk = A.alloc(D, BF16)
    st = [A.alloc(4) for _ in range(2)]
    xm3 = _v3(xmT, 16)
    for tt in range(NTT):
        sl = tt % 2
        xt, xb, s4 = xr[sl], xh[sl], st[sl]
        kx, kb_, ks = tag + "x%d" % sl, tag + "xh%d" % sl, tag + "st%d" % sl
        ms, sh, kms, ksh = mv[0] if tt < 16 else mv[1]
        S.op("sp", lambda e, xt=xt, tt=tt: e.dma_start(out=xt, in_=src_tiles[tt]), writes=[kx], dma=kx)
        S.op("act", lambda e, xt=xt, s4=s4: e.activation(out=junk, in_=xt, func=AF.Square, accum_out=s4[:, 0:1]),
             reads=[kx], writes=[tag + "junk", ks])
        S.op("act", lambda e, s4=s4: e.activation(out=s4[:, 1:2], in_=s4[:, 0:1], func=AF.Sqrt, scale=1.0 / D, bias=C.eps_t[:, 0:1]),
             reads=[ks], writes=[ks])
        S.op("dve", lambda e, s4=s4: e.reciprocal(out=s4[:, 2:3], in_=s4[:, 1:2]), reads=[ks], writes=[ks])
        S.op("dve", lambda e, xt=xt, xb=xb, s4=s4: e.tensor_scalar(out=xb, in0=xt, scalar1=s4[:, 2:3], scalar2=None, op0=ALU.mult),
             reads=[kx, ks], writes=[kb_])
        pb = C.ps[:, (tt % 2) * 1024:(tt % 2) * 1024 + 1024].bitcast(BF16)
        kp = tag + "tp%d" % (tt % 2)
        for k in range(16):
            S.op("pe", lambda e, pb=pb, xb=xb, k=k: e.transpose(out=pb[:, k * 128:(k + 1) * 128], in_=xb[:, k * 128:(k + 1) * 128], identity=C.ident),
                 reads=[kb_, "ident"], writes=[kp])
        for k in range(16):
            o_ = xm3[:, k, tt * 128:(tt + 1) * 128]
            i_ = pb[:, k * 128:(k + 1) * 128]
            if k % 2 == 0:
                S.op("act", lambda e, o_=o_, i_=i_, k=k, ms=ms, sh=sh: e.activation(out=o_, in_=i_, func=AF.Identity, bias=sh[:, k:k + 1], scale=ms[:, k:k + 1]),
                     reads=[kp, kms, ksh], writes=[xkey + str(tt)])
            else:
                S.op("dve", lambda e, o_=o_, i_=i_, k=k, ms=ms, sh=sh: e.tensor_scalar(out=o_, in0=i_, scalar1=ms[:, k:k + 1], scalar2=sh[:, k:k + 1], op0=ALU.mult, op1=ALU.add),
                     reads=[kp, kms, ksh], writes=[xkey + str(tt)])


def proj_fm(C, xm3, xkey, w3, wkey, ncols, evac, tag, m_off=0):
    S = C.S
    for sb in range((ncols + 127) // 128):
        m = min(128, ncols - sb * 128)
        for gi, (t0, tn) in enumerate(TG):
            bank = 4 + (C.pcnt % 4)
            C.pcnt += 1
            pa = C.ps[0:m, bank * 512:bank * 512 + tn]
            pk = "psb%d" % bank
            for k in range(16):
                S.op("pe", lambda e, pa=pa, k=k, sb=sb, m=m, t0=t0, tn=tn: e.matmul(pa, lhsT=w3[:, k, sb * 128:sb * 128 + m], rhs=xm3[:, k, t0:t0 + tn],
                                                                                  start=(k == 0), stop=(k == 15)),
                     reads=[wkey] + [xkey + str(t) for t in range(t0 // 128, (t0 + tn) // 128)], writes=[pk])
            evac(sb, gi, t0, tn, pa, pk, m)


def phase1(C):
    nc, S, A = C.nc, C.S, C.A
    w_in = C.dram["ab_w_in"]
    segs = [("qa", 0, 512), ("qa", 512, 512), ("ka", 1024, 512), ("ka", 1536, 512), ("va", 2048, 512), ("va", 2560, 512),
            ("cq", 3072, 512), ("ckv", 3584, 512), ("kr", 4096, 64), ("krsw", 4096, 64),
            ("z", 4160, 512), ("z", 4672, 512), ("z", 5184, 512), ("z", 5696, 512)]
    frow = {"qa": 0, "ka": 1024 - 1024, "cq": 2048 - 3072, "ckv": 2560 - 3584, "z": 3072 - 4160}
    for b in range(NB):
        _phase1_b(C, b, segs, frow)


def _phase1_b(C, b, segs, frow):
    nc, S, A = C.nc, C.S, C.A
    w_in = C.dram["ab_w_in"]
    if True:
        A.reset()
        mv = load_modvecs(C, 0, b, C.dram["ab_norm"], "p1m%d" % b)
        xmT = A.alloc(16 * T, BF16)
        xm3 = _v3(xmT, 16)
        xkey = "xmT"
        mark = A.off
        tiles = [C.dram["x"][b, tt * 128:(tt + 1) * 128, :] for tt in range(16)] + [C.dram["ctx"][b, tt * 128:(tt + 1) * 128, :] for tt in range(2)]
        build_xmT(C, xmT, xkey, tiles, mv, "p1b")
        pass
        wr = [A.alloc(16 * 512, BF16) for _ in range(2)]
        stg = [A.alloc(T, BF16) for _ in range(3)]
        vst = [A.alloc(512, BF16) for _ in range(2)]
        krp = A.alloc(T)
        kro = A.alloc(T, BF16)
        cos_t = A.alloc(SEQ)
        sin_t = A.alloc(SEQ)
        tmpf = A.alloc(512)
        tmpg = A.alloc(512)
        S.op("sp", lambda e: e.dma_start(out=cos_t[0:64, :], in_=C.dram["rope_cos"]), writes=["cos"], dma="p1c")
        S.op("sp", lambda e: e.dma_start(out=sin_t[0:64, :], in_=C.dram["rope_sin"]), writes=["sin"], dma="p1c")
        F0 = C.dram["F0"]
        VA = C.dram["VA"]
        sc = [0]
        for wi, (nm, c0, ncw) in enumerate(segs):
            slot = wi % 2
            w3 = _v3(wr[slot], 16)[:, :, 0:ncw]
            wkey = "p1w%d" % slot
            if nm == "krsw":
                for (d0, s0) in ((0, 16), (16, 0), (32, 48), (48, 32)):
                    S.op("pool", lambda e, w3=w3, d0=d0, s0=s0: e.dma_start(out=w3[:, :, d0:d0 + 16],
                                                                           in_=w_in[:, 4096 + s0:4096 + s0 + 16].rearrange("(k p) c -> p k c", p=128)),
                         writes=[wkey], dma=wkey)
            else:
                S.op("pool", lambda e, w3=w3, c0=c0, ncw=ncw: e.dma_start(out=w3, in_=w_in[:, c0:c0 + ncw].rearrange("(k p) c -> p k c", p=128)),
                     writes=[wkey], dma=wkey)
            if nm == "va":
                for tt in range(NTT):
                    bank = 4 + (C.pcnt % 4)
                    C.pcnt += 1
                    pa = C.ps[:, bank * 512:bank * 512 + 512]
                    pk = "psb%d" % bank
                    for k in range(16):
                        S.op("pe", lambda e, pa=pa, k=k, tt=tt, w3=w3: e.matmul(pa, lhsT=xm3[:, k, tt * 128:(tt + 1) * 128], rhs=w3[:, k, :], start=(k == 0), stop=(k == 15)),
                             reads=[wkey, xkey + str(tt)], writes=[pk])
                    vs = vst[tt % 2]
                    vk = "p1vs%d" % (tt % 2)
                    eng = "act" if tt % 2 == 0 else "dve"
                    if eng == "act":
                        S.op("act", lambda e, vs=vs, pa=pa: e.activation(out=vs, in_=pa, func=AF.Copy), reads=[pk], writes=[vk])
                    else:
                        S.op("dve", lambda e, vs=vs, pa=pa: e.tensor_copy(out=vs, in_=pa), reads=[pk], writes=[vk])
                    S.op("sp", lambda e, vs=vs, tt=tt, c0=c0: e.dma_start(out=VA[b, tt * 128:(tt + 1) * 128, c0 - 2048:c0 - 2048 + 512], in_=vs),
                         reads=[vk], writes=["VA"], dma=vk)
                continue

            def evac(sb, gi, t0, tn, pa, pk, m, nm=nm, c0=c0):
                if nm in ("kr", "krsw"):
                    if nm == "kr":
                        S.op("dve", lambda e: e.tensor_copy(out=krp[0:64, t0:t0 + tn], in_=pa), reads=[pk], writes=["krp"])
                    else:
                        if gi < 4:
                            S.op("dve", lambda e: e.tensor_tensor(out=tmpf[0:64, 0:tn], in0=pa, in1=sin_t[0:64, t0:t0 + tn], op=ALU.mult),
                                 reads=[pk, "sin"], writes=["tmpf"])
                            S.op("pool", lambda e: e.tensor_tensor(out=tmpg[0:64, 0:tn], in0=krp[0:64, t0:t0 + tn], in1=cos_t[0:64, t0:t0 + tn], op=ALU.mult),
                                 reads=["krp", "cos"], writes=["tmpg"])
                            S.op("dve", lambda e: e.tensor_tensor(out=kro[0:64, t0:t0 + tn], in0=tmpf[0:64, 0:tn], in1=tmpg[0:64, 0:tn], op=ALU.add),
                                 reads=["tmpf", "tmpg"], writes=["kro"])
                        if gi == 4:
                            S.op("dve", lambda e: e.tensor_copy(out=kro[0:64, t0:t0 + tn], in_=krp[0:64, t0:t0 + tn]), reads=["krp"], writes=["kro"])
                            S.op("sp", lambda e: e.dma_start(out=F0[b, 5120:5184, :], in_=kro[0:64, :]), reads=["kro"], writes=["F0"], dma="p1kr")
                    return
                if gi == 0:
                    sc[0] += 1
                si = sc[0] % 3
                sg = stg[si]
                sk = "p1st%d" % si
                func = AF.Silu if nm == "z" else AF.Copy
                if nm == "z" or (gi % 2 == 0):
                    S.op("act", lambda e: e.activation(out=sg[0:m, t0:t0 + tn], in_=pa, func=func), reads=[pk], writes=[sk])
                else:
                    S.op("dve", lambda e: e.tensor_copy(out=sg[0:m, t0:t0 + tn], in_=pa), reads=[pk], writes=[sk])
                if gi == 4:
                    r0 = c0 + sb * 128 + frow[nm]
                    S.op("sp", lambda e: e.dma_start(out=F0[b, r0:r0 + m, :], in_=sg[0:m, :]), reads=[sk], writes=["F0"], dma=sk)

            proj_fm(C, xm3, xkey, w3, wkey, ncw, evac, "p1")
        S.barrier()


def rope_tables():
    quarter = 16
    inv = (10000.0 ** (-np.arange(quarter, dtype=np.float32) / quarter)).astype(np.float32)
    pos = np.arange(SEQ)
    cos = np.zeros((64, SEQ), np.float32)
    sin = np.zeros((64, SEQ), np.float32)
    for half, p in ((0, pos // 64), (1, pos % 64)):
        ang = p.astype(np.float32)[None, :] * inv[:, None]
        c, s = np.cos(ang), np.sin(ang)
        cos[half * 32:half * 32 + 16] = c
        cos[half * 32 + 16:half * 32 + 32] = c
        sin[half * 32:half * 32 + 16] = -s
        sin[half * 32 + 16:half * 32 + 32] = s
    return cos, sin


def make_ctx(nc, es, dram):
    C = Ctx()
    C.nc = nc
    C.S = Sched(nc, es)
    C.es = es
    C.A = Arena(nc, es)
    C.ps = es.enter_context(nc.psum_tensor("ps", [128, 4096], F32))
    C.dram = dram
    C.pcnt = 0
    C.na_geo = na_geometry()
    C.cst = es.enter_context(nc.sbuf_tensor("cst", [128, 512], F32))
    C.ident = C.cst[:, 0:64].bitcast(BF16)
    C.eps_t = C.cst[:, 64:65]
    C.ones = C.cst[:, 72:136].bitcast(BF16)
    C.S.op("sp", lambda e: e.dma_start(out=C.ident, in_=dram["ident"]), writes=["ident"], dma="cst")
    C.S.op("pool", lambda e: e.memset(C.eps_t, EPS), writes=["eps"])
    C.S.op("pool", lambda e: e.memset(C.ones, 1.0), writes=["ones"])
    C.lnq = C.cst[:, 65:66]
    C.zero_t = C.cst[:, 66:67]
    C.cw = C.cst[:, 136:456]
    C.S.op("pool", lambda e: e.memset(C.lnq, float(np.log(128.0 ** -0.5))), writes=["lnq"])
    C.S.op("pool", lambda e: e.memset(C.zero_t, 0.0), writes=["zero"])
    if "dn_conv" in dram:
        cw3 = C.cw.rearrange("p (blk j) -> p blk j", j=5)
        for j in range(5):
            for q4 in range(8):
                C.S.op("sp", lambda e, j=j, q4=q4: e.dma_start(out=cw3[:, q4 * 8:(q4 + 1) * 8, j], in_=dram["dn_conv"][j, q4 * 1024:(q4 + 1) * 1024].rearrange("(blk p) -> p blk", p=128),
                                                          allow_slow_non_contiguous=True), writes=["cw"], dma="cw")
    C.S.barrier()
    return C


MLA_SCALE = 192.0 ** -0.5
NA_SCALE = 128.0 ** -0.5


def phase2(C):
    for b in range(NB):
        _phase2_b(C, b)


def _phase2_b(C, b):
    nc, S, A = C.nc, C.S, C.A
    A.reset()
    F0, M0, VM = C.dram["F0"], C.dram["M0"], C.dram["VM"]
    wqb = A.alloc(4 * 1536, BF16)
    wqb3 = _v3(wqb, 4)
    wqsw = A.alloc(4 * 512, BF16)
    wqsw4 = wqsw.rearrange("p (k h c) -> p k h c", k=4, h=8)
    wkvb = A.alloc(4 * 2048, BF16)
    wkvb3 = _v3(wkvb, 4)
    wkvb4 = wkvb.rearrange("p (k h c) -> p k h c", k=4, h=8)
    cq = A.alloc(4 * T, BF16)
    ckv = A.alloc(4 * T, BF16)
    cqn = A.alloc(4 * T, BF16)
    ckvn = A.alloc(4 * T, BF16)
    sq = [A.alloc(4 * 512, BF16) for _ in range(2)]
    lnv = [A.alloc(512) for _ in range(2)]
    rst = [A.alloc(512) for _ in range(2)]
    qnm = A.alloc(4)
    kvnm = A.alloc(4)
    cos_t = A.alloc(SEQ)
    sin_t = A.alloc(SEQ)
    stg = [A.alloc(T, BF16) for _ in range(3)]
    vst = [A.alloc(1024, BF16) for _ in range(2)]
    t1 = [A.alloc(512) for _ in range(2)]
    t2 = [A.alloc(512) for _ in range(2)]
    wq_d, wkv_d = C.dram["ab_w_qb"], C.dram["ab_w_kvb"]
    S.op("pool", lambda e: e.dma_start(out=wqb3, in_=wq_d.rearrange("(k p) c -> p k c", p=128)), writes=["wqb"], dma="p2w")
    S.op("pool", lambda e: e.dma_start(out=wkvb3, in_=wkv_d.rearrange("(k p) c -> p k c", p=128)), writes=["wkvb"], dma="p2w")
    wq4 = wq_d.rearrange("(k p) (h c) -> p k h c", p=128, h=8)
    for k in range(4):
        for (d0, s0) in ((0, 16), (16, 0), (32, 48), (48, 32)):
            S.op("pool", lambda e, k=k, d0=d0, s0=s0: e.dma_start(out=wqsw4[:, k, :, d0:d0 + 16], in_=wq4[:, k, :, 128 + s0:128 + s0 + 16]),
                 writes=["wqsw"], dma="p2w")
    S.op("sp", lambda e: e.dma_start(out=_v3(cq, 4), in_=F0[b, 2048:2560, :].rearrange("(k p) t -> p k t", p=128)), reads=["F0"], writes=["cq"], dma="p2a")
    S.op("sp", lambda e: e.dma_start(out=_v3(ckv, 4), in_=F0[b, 2560:3072, :].rearrange("(k p) t -> p k t", p=128)), reads=["F0"], writes=["ckv"], dma="p2a")
    S.op("sp", lambda e: e.dma_start(out=qnm, in_=C.dram["ab_q_norm"].rearrange("(k p) -> p k", p=128), allow_slow_non_contiguous=True), writes=["qnm"], dma="p2a")
    S.op("sp", lambda e: e.dma_start(out=kvnm, in_=C.dram["ab_kv_norm"].rearrange("(k p) -> p k", p=128), allow_slow_non_contiguous=True), writes=["kvnm"], dma="p2a")
    S.op("sp", lambda e: e.dma_start(out=cos_t[0:64, :], in_=C.dram["rope_cos"]), writes=["cos"], dma="p2a")
    S.op("sp", lambda e: e.dma_start(out=sin_t[0:64, :], in_=C.dram["rope_sin"]), writes=["sin"], dma="p2a")
    it = 0
    for (src, skey, nrm, nkey, dst, dkey) in ((cq, "cq", qnm, "qnm", cqn, "cqn"), (ckv, "ckv", kvnm, "kvnm", ckvn, "ckvn")):
        s3, d3 = _v3(src, 4), _v3(dst, 4)
        for gi, (t0, tn) in enumerate(TG):
            sl = it % 2
            it += 1
            q3 = _v3(sq[sl], 4)
            S.op("pool", lambda e, q3=q3, s3=s3, t0=t0, tn=tn: e.tensor_tensor(out=q3[:, :, 0:tn], in0=s3[:, :, t0:t0 + tn], in1=s3[:, :, t0:t0 + tn], op=ALU.mult),
                 reads=[skey], writes=["sq%d" % sl])
            bank = 4 + sl
            pa = C.ps[:, bank * 512:bank * 512 + tn]
            for k in range(4):
                S.op("pe", lambda e, pa=pa, q3=q3, k=k, tn=tn: e.matmul(pa, lhsT=C.ones, rhs=q3[:, k, 0:tn], start=(k == 0), stop=(k == 3)),
                     reads=["ones", "sq%d" % sl], writes=["psb%d" % bank])
            lv, rs = lnv[sl], rst[sl]
            S.op("act", lambda e, lv=lv, pa=pa, tn=tn: e.activation(out=lv[:, 0:tn], in_=pa, func=AF.Ln, scale=1.0 / 512, bias=C.eps_t[:, 0:1]),
                 reads=["psb%d" % bank], writes=["lnv%d" % sl])
            S.op("act", lambda e, lv=lv, rs=rs, tn=tn: e.activation(out=rs[:, 0:tn], in_=lv[:, 0:tn], func=AF.Exp, scale=-0.5),
                 reads=["lnv%d" % sl], writes=["rst%d" % sl])
            for k in range(4):
                S.op("dve", lambda e, d3=d3, s3=s3, k=k, t0=t0, tn=tn, nrm=nrm, rs=rs: e.scalar_tensor_tensor(
                    out=d3[:, k, t0:t0 + tn], in0=s3[:, k, t0:t0 + tn], scalar=nrm[:, k:k + 1], in1=rs[:, 0:tn], op0=ALU.mult, op1=ALU.mult),
                    reads=[skey, nkey, "rst%d" % sl], writes=[dkey])
    cqn3, ckvn3 = _v3(cqn, 4), _v3(ckvn, 4)
    sc = [0]

    def small_proj(lhs_fn, rhs3, rkey, wkey, m, gi, t0, tn):
        bank = 4 + (C.pcnt % 4)
        C.pcnt += 1
        pa = C.ps[0:m, bank * 512:bank * 512 + tn]
        for k in range(4):
            S.op("pe", lambda e, pa=pa, k=k: e.matmul(pa, lhsT=lhs_fn(k), rhs=rhs3[:, k, t0:t0 + tn], start=(k == 0), stop=(k == 3)),
                 reads=[wkey, rkey], writes=["psb%d" % bank])
        return pa, "psb%d" % bank

    for h in range(8):
        for (nm, lhs_fn, rhs3, rkey, wkey, row0) in (
                ("qn", lambda k, h=h: wqb3[:, k, h * 192:h * 192 + 128], cqn3, "cqn", "wqb", h * 128),
                ("kn", lambda k, h=h: wkvb3[:, k, h * 256:h * 256 + 128], ckvn3, "ckvn", "wkvb", 1536 + h * 128)):
            sc[0] += 1
            si = sc[0] % 3
            sg, sk = stg[si], "p2st%d" % si
            for gi, (t0, tn) in enumerate(TG):
                pa, pk = small_proj(lhs_fn, rhs3, rkey, wkey, 128, gi, t0, tn)
                if gi % 2 == 0:
                    S.op("act", lambda e, sg=sg, pa=pa, t0=t0, tn=tn: e.activation(out=sg[:, t0:t0 + tn], in_=pa, func=AF.Copy), reads=[pk], writes=[sk])
                else:
                    S.op("dve", lambda e, sg=sg, pa=pa, t0=t0, tn=tn: e.tensor_copy(out=sg[:, t0:t0 + tn], in_=pa), reads=[pk], writes=[sk])
            S.op("sp", lambda e, sg=sg, row0=row0: e.dma_start(out=M0[b, row0:row0 + 128, :], in_=sg), reads=[sk], writes=["M0"], dma=sk)
        sc[0] += 1
        si = sc[0] % 3
        sg, sk = stg[si], "p2st%d" % si
        for gi, (t0, tn) in enumerate(TG):
            pa, pk = small_proj(lambda k, h=h: wqb3[:, k, h * 192 + 128:h * 192 + 192], cqn3, "cqn", "wqb", 64, gi, t0, tn)
            if gi == 4:
                S.op("dve", lambda e, sg=sg, pa=pa, t0=t0, tn=tn: e.tensor_copy(out=sg[0:64, t0:t0 + tn], in_=pa), reads=[pk], writes=[sk])
                continue
            pb, pkb = small_proj(lambda k, h=h: wqsw4[:, k, h, :], cqn3, "cqn", "wqsw", 64, gi, t0, tn)
            a1, a2 = t1[gi % 2], t2[gi % 2]
            S.op("dve", lambda e, a1=a1, pa=pa, t0=t0, tn=tn: e.tensor_tensor(out=a1[0:64, 0:tn], in0=pa, in1=cos_t[0:64, t0:t0 + tn], op=ALU.mult),
                 reads=[pk, "cos"], writes=["t1%d" % (gi % 2)])
            S.op("dve", lambda e, a2=a2, pb=pb, t0=t0, tn=tn: e.tensor_tensor(out=a2[0:64, 0:tn], in0=pb, in1=sin_t[0:64, t0:t0 + tn], op=ALU.mult),
                 reads=[pkb, "sin"], writes=["t2%d" % (gi % 2)])
            S.op("pool", lambda e, a1=a1, a2=a2, sg=sg, t0=t0, tn=tn: e.tensor_tensor(out=sg[0:64, t0:t0 + tn], in0=a1[0:64, 0:tn], in1=a2[0:64, 0:tn], op=ALU.add),
                 reads=["t1%d" % (gi % 2), "t2%d" % (gi % 2)], writes=[sk])
        S.op("sp", lambda e, sg=sg, h=h: e.dma_start(out=M0[b, 1024 + h * 64:1024 + h * 64 + 64, :], in_=sg[0:64, :]), reads=[sk], writes=["M0"], dma=sk)
    for tt in range(NTT):
        vs, vk = vst[tt % 2], "p2vs%d" % (tt % 2)
        for half in range(2):
            bank = 4 + (C.pcnt % 4)
            C.pcnt += 1
            pa = C.ps[:, bank * 512:bank * 512 + 512]
            for k in range(4):
                S.op("pe", lambda e, pa=pa, k=k, tt=tt, half=half: e.matmul(pa.rearrange("p (h c) -> p h c", h=4), lhsT=ckvn3[:, k, tt * 128:(tt + 1) * 128],
                                                                         rhs=wkvb4[:, k, half * 4:half * 4 + 4, 128:256], start=(k == 0), stop=(k == 3)),
                     reads=["wkvb", "ckvn"], writes=["psb%d" % bank])
            if half == 0:
                S.op("act", lambda e, vs=vs, pa=pa: e.activation(out=vs[:, 0:512], in_=pa, func=AF.Copy), reads=["psb%d" % bank], writes=[vk])
            else:
                S.op("dve", lambda e, vs=vs, pa=pa: e.tensor_copy(out=vs[:, 512:1024], in_=pa), reads=["psb%d" % bank], writes=[vk])
        S.op("sp", lambda e, vs=vs, tt=tt: e.dma_start(out=VM[b, tt * 128:(tt + 1) * 128, :], in_=vs), reads=[vk], writes=["VM"], dma=vk)
    S.barrier()


def na_geometry():
    rows = 32
    r = np.arange(rows)
    r0 = np.clip(r - 4, 0, rows - 8)
    col = np.arange(64)
    c0 = np.clip(col - 8, 0, 64 - 16)
    tiles = {}
    per_m = []
    for m in range(16):
        lo = min(r0[2 * m], r0[2 * m + 1])
        hi = max(r0[2 * m], r0[2 * m + 1]) + 7
        lst = []
        for kb in range(lo // 2, hi // 2 + 1):
            memb = tuple(tuple(bool(r0[2 * m + bq] <= 2 * kb + a <= r0[2 * m + bq] + 7) for bq in range(2)) for a in range(2))
            key = (kb - m, memb)
            if key not in tiles:
                tiles[key] = len(tiles)
            lst.append((kb, tiles[key]))
        per_m.append(lst)
    nt = len(tiles)
    mask = np.zeros((nt, 128, 128), np.float32)
    drow = np.zeros((nt, 128, 128), np.int64)
    dcol = np.zeros((nt, 128, 128), np.int64)
    for (delta, memb), ti in tiles.items():
        for a in range(2):
            for bq in range(2):
                kc = np.arange(64)[:, None]
                qc = np.arange(64)[None, :]
                ok = memb[a][bq] & (kc >= c0[qc]) & (kc <= c0[qc] + 15)
                dr = 2 * delta + a - bq + 7
                dc = kc - qc + 15
                blk = (slice(a * 64, a * 64 + 64), slice(bq * 64, bq * 64 + 64))
                mask[ti][blk] = np.where(ok, 0.0, -20000.0)
                drow[ti][blk] = np.clip(np.where(ok, dr, 0), 0, 14)
                dcol[ti][blk] = np.clip(np.where(ok, dc, 0), 0, 30)
    return per_m, nt, mask, drow, dcol


def phase3(C):
    for b in range(NB):
        _phase3_b(C, b)


def _phase3_b(C, b):
    nc, S, A = C.nc, C.S, C.A
    A.reset()
    F0, M0, VM, VA, OG = C.dram["F0"], C.dram["M0"], C.dram["VM"], C.dram["VA"], C.dram["OG0"]
    per_m, nt, _, _, _ = C.na_geo
    krT = A.alloc(T, BF16)
    S.op("sp", lambda e: e.dma_start(out=krT[0:64, :], in_=F0[b, 5120:5184, :]), reads=["F0"], writes=["krT"], dma="p3k")
    nam = A.alloc(nt * 128)
    S.op("sp", lambda e: e.dma_start(out=_v3(nam, nt), in_=C.dram["na_mask"].rearrange("t k q -> k t q")), writes=["nam"], dma="p3k")
    hb = []
    for i in range(2):
        hb.append(dict(k=A.alloc(T, BF16), q=A.alloc(T, BF16), qr=A.alloc(T, BF16), v=A.alloc(NTT * 128, BF16), sz=A.alloc(T, BF16),
                       bias=A.alloc(nt * 128)))
    pT = [A.alloc(512, BF16) for _ in range(4)]
    sbf = [A.alloc(128) for _ in range(3)]
    rinv = [A.alloc(512) for _ in range(2)]
    tmp = [A.alloc(512) for _ in range(2)]
    ogs = [A.alloc(T, BF16) for _ in range(2)]
    cnt = dict(s=0, p=0, g=0, sb=0)

    def attend_block(lhsT_list, rhs_list, tn, reads, o_cols, first, last, vlhs, vkey, scale, bias=None, bkey=None, acc=None):
        bank = cnt["s"] % 4
        cnt["s"] += 1
        pS = C.ps[:, bank * 512:bank * 512 + tn]
        pk = "psb%d" % bank
        n = len(lhsT_list)
        for i in range(n):
            S.op("pe", lambda e, i=i: e.matmul(pS[0:128, :], lhsT=lhsT_list[i], rhs=rhs_list[i], start=(i == 0), stop=(i == n - 1)),
                 reads=reads, writes=[pk])
        slot = cnt["p"] % 4
        cnt["p"] += 1
        p_ = pT[slot][:, 0:tn]
        pkey = "pT%d" % slot
        if bias is None:
            S.op("act", lambda e: e.activation(out=p_, in_=pS, func=AF.Exp, scale=scale), reads=[pk], writes=[pkey])
        else:
            sslot = cnt["sb"] % 3
            cnt["sb"] += 1
            sb_ = sbf[sslot][:, 0:tn]
            S.op("dve", lambda e: e.scalar_tensor_tensor(out=sb_, in0=pS, scalar=scale, in1=bias, op0=ALU.mult, op1=ALU.add),
                 reads=[pk, bkey], writes=["sbf%d" % sslot])
            S.op("act", lambda e: e.activation(out=p_, in_=sb_, func=AF.Exp), reads=["sbf%d" % sslot], writes=[pkey])
        po, ps_, ok_, sk_ = acc
        S.op("pe", lambda e: e.matmul(po[:, o_cols[0]:o_cols[0] + tn], lhsT=vlhs, rhs=p_, start=first, stop=last), reads=[pkey, vkey], writes=[ok_])
        S.op("pe", lambda e: e.matmul(ps_[:, o_cols[0]:o_cols[0] + tn], lhsT=C.ones, rhs=p_, start=first, stop=last), reads=[pkey, "ones"], writes=[sk_])

    def finish_group(acc, tn, sz, szkey, og, ogkey, t0):
        po, ps_, ok_, sk_ = acc
        g = cnt["g"] % 2
        cnt["g"] += 1
        S.op("dve", lambda e: e.reciprocal(out=rinv[g][:, 0:tn], in_=ps_[:, 0:tn]), reads=[sk_], writes=["rinv%d" % g])
        S.op("dve", lambda e: e.tensor_tensor(out=tmp[g][:, 0:tn], in0=po[:, 0:tn], in1=rinv[g][:, 0:tn], op=ALU.mult),
             reads=[ok_, "rinv%d" % g], writes=["tmp%d" % g])
        S.op("pool", lambda e: e.tensor_tensor(out=og[:, t0:t0 + tn], in0=tmp[g][:, 0:tn], in1=sz[:, t0:t0 + tn], op=ALU.mult),
             reads=["tmp%d" % g, szkey], writes=[ogkey])

    def acc_banks():
        g = cnt["g"] % 2
        return (C.ps[:, (4 + g) * 512:(5 + g) * 512], C.ps[:, (6 + g) * 512:(7 + g) * 512], "psb%d" % (4 + g), "psb%d" % (6 + g))

    for hh in range(16):
        is_mla = hh >= 8
        h = hh % 8
        B_ = hb[hh % 2]
        hk = "hb%d" % (hh % 2)
        og, ogkey = ogs[hh % 2], "ogs%d" % (hh % 2)
        if is_mla:
            S.op("sp", lambda e, B_=B_, h=h: e.dma_start(out=B_["k"], in_=M0[b, 1536 + h * 128:1536 + (h + 1) * 128, :]), reads=["M0"], writes=[hk + "k"], dma=hk)
            S.op("sp", lambda e, B_=B_, h=h: e.dma_start(out=B_["q"], in_=M0[b, h * 128:(h + 1) * 128, :]), reads=["M0"], writes=[hk + "q"], dma=hk)
            S.op("sp", lambda e, B_=B_, h=h: e.dma_start(out=B_["qr"][0:64, :], in_=M0[b, 1024 + h * 64:1024 + (h + 1) * 64, :]), reads=["M0"], writes=[hk + "qr"], dma=hk)
            S.op("sp", lambda e, B_=B_, h=h: e.dma_start(out=_v3(B_["v"], NTT), in_=VM[b, :, h * 128:(h + 1) * 128].rearrange("(kb p) d -> p kb d", p=128)),
                 reads=["VM"], writes=[hk + "v"], dma=hk)
            S.op("sp", lambda e, B_=B_, h=h: e.dma_start(out=B_["sz"], in_=F0[b, 3072 + 1024 + h * 128:3072 + 1024 + (h + 1) * 128, :]), reads=["F0"], writes=[hk + "sz"], dma=hk)
        else:
            S.op("sp", lambda e, B_=B_, h=h: e.dma_start(out=B_["k"], in_=F0[b, 1024 + h * 128:1024 + (h + 1) * 128, :]), reads=["F0"], writes=[hk + "k"], dma=hk)
            S.op("sp", lambda e, B_=B_, h=h: e.dma_start(out=B_["q"], in_=F0[b, h * 128:(h + 1) * 128, :]), reads=["F0"], writes=[hk + "q"], dma=hk)
            S.op("sp", lambda e, B_=B_, h=h: e.dma_start(out=_v3(B_["v"], NTT), in_=VA[b, :, h * 128:(h + 1) * 128].rearrange("(kb p) d -> p kb d", p=128)),
                 reads=["VA"], writes=[hk + "v"], dma=hk)
            S.op("sp", lambda e, B_=B_, h=h: e.dma_start(out=B_["sz"], in_=F0[b, 3072 + h * 128:3072 + (h + 1) * 128, :]), reads=["F0"], writes=[hk + "sz"], dma=hk)
            S.op("sp", lambda e, B_=B_, h=h: e.dma_start(out=_v3(B_["bias"], nt), in_=C.dram["na_bias"][h].rearrange("t k q -> k t q")), writes=[hk + "b"], dma=hk)
            S.op("pool", lambda e, B_=B_: e.tensor_tensor(out=B_["bias"], in0=B_["bias"], in1=nam, op=ALU.add), reads=[hk + "b", "nam"], writes=[hk + "b"])
        k_, q_, qr_, v3_, sz_ = B_["k"], B_["q"], B_["qr"], _v3(B_["v"], NTT), B_["sz"]
        b3_ = _v3(B_["bias"], nt)
        for gi, (t0, tn) in enumerate(TG):
            acc = acc_banks()
            if is_mla:
                kbs = list(range(18)) if gi < 4 else [16, 17]
                for i, kb in enumerate(kbs):
                    attend_block([k_[:, kb * 128:(kb + 1) * 128], krT[0:64, kb * 128:(kb + 1) * 128]], [q_[:, t0:t0 + tn], qr_[0:64, t0:t0 + tn]], tn,
                                 [hk + "k", hk + "q", hk + "qr", "krT"], (0,), i == 0, i == len(kbs) - 1, v3_[:, kb, :], hk + "v", MLA_SCALE, acc=acc)
            else:
                for i, kb in enumerate((16, 17)):
                    attend_block([k_[:, kb * 128:(kb + 1) * 128]], [q_[:, t0:t0 + tn]], tn, [hk + "k", hk + "q"], (0,), i == 0, (gi == 4 and i == 1),
                                 v3_[:, kb, :], hk + "v", NA_SCALE, acc=acc)
                if gi < 4:
                    for pr in range(4):
                        m = gi * 4 + pr
                        lst = per_m[m]
                        for j, (kb, ti) in enumerate(lst):
                            attend_block([k_[:, kb * 128:(kb + 1) * 128]], [q_[:, m * 128:(m + 1) * 128]], 128, [hk + "k", hk + "q"], (pr * 128,), False,
                                         (pr == 3 and j == len(lst) - 1), v3_[:, kb, :], hk + "v", NA_SCALE, bias=b3_[:, ti, :], bkey=hk + "b", acc=acc)
            finish_group(acc, tn, sz_, hk + "sz", og, ogkey, t0)
        row0 = (1024 if is_mla else 0) + h * 128
        S.op("sp", lambda e, og=og, row0=row0: e.dma_start(out=OG[b, row0:row0 + 128, :], in_=og), reads=[ogkey], writes=["OG0"], dma=ogkey)
    S.barrier()


def out_proj(C, b, l, OGd, nk, w_out, res_tiles, dst_tiles, ntt_list, tag, cbw=512, ntok=T):
    nc, S, A = C.nc, C.S, C.A
    A.reset()
    og = A.alloc(nk * ntok, BF16)
    og3 = _v3(og, nk)
    hk = nk // 2
    S.op("sp", lambda e: e.dma_start(out=og3[:, 0:hk, :], in_=OGd[0:hk * 128, :].rearrange("(k p) t -> p k t", p=128)), reads=["DR_OG"], writes=["og"], dma=tag + "og")
    S.op("sp", lambda e: e.dma_start(out=og3[:, hk:nk, :], in_=OGd[hk * 128:nk * 128, :].rearrange("(k p) t -> p k t", p=128)), reads=["DR_OG"], writes=["og"], dma=tag + "og")
    vs = (b, 2) if len(ntt_list) > 16 else (b,)
    wr = A.alloc(nk * cbw, BF16)
    wx = [A.alloc(nk * cbw, BF16) for _ in vs]
    gt = [A.alloc(cbw) for _ in range(2)]
    xr = [A.alloc(cbw) for _ in range(3)]
    xo = [A.alloc(cbw) for _ in range(3)]
    it = 0
    for cb in range(D // cbw):
        w3 = _v3(wr, nk)
        S.op("pool", lambda e, cb=cb, w3=w3: e.dma_start(out=w3, in_=w_out[:, cb * cbw:(cb + 1) * cbw].rearrange("(k p) c -> p k c", p=128)), writes=["wr"], dma=tag + "w")
        for vi, v in enumerate(vs):
            S.op("sp", lambda e, vi=vi, v=v, cb=cb: e.dma_start(out=gt[vi], in_=C.dram["mod"][l, v, 4096 + cb * cbw:4096 + (cb + 1) * cbw].unsqueeze(0).to_broadcast([128, cbw])),
                 reads=["mod"], writes=["gt%d" % vi], dma=tag + "gt")
            S.op("dve" if vi == 0 else "pool", lambda e, vi=vi, w3=w3: e.tensor_tensor(out=_v3(wx[vi], nk), in0=w3, in1=gt[vi].unsqueeze(1).to_broadcast([128, nk, cbw]), op=ALU.mult),
                 reads=["wr", "gt%d" % vi], writes=["wx%d" % vi])
        for tt in ntt_list:
            vi = 0 if tt < 16 else 1
            wv = _v3(wx[vi], nk)
            bank = 4 + (C.pcnt % 4)
            C.pcnt += 1
            pa = C.ps[:, bank * 512:bank * 512 + cbw]
            for k in range(nk):
                S.op("pe", lambda e, pa=pa, k=k, tt=tt, wv=wv: e.matmul(pa, lhsT=og3[:, k, tt * 128:(tt + 1) * 128], rhs=wv[:, k, :], start=(k == 0), stop=(k == nk - 1)),
                     reads=["og", "wx%d" % vi], writes=["psb%d" % bank])
            sl = it % 3
            it += 1
            S.op("sp", lambda e, sl=sl, tt=tt, cb=cb: e.dma_start(out=xr[sl], in_=res_tiles(tt, cb)), reads=["DR_res"], writes=["xr%d" % sl], dma=tag + "xr%d" % sl)
            S.op("dve", lambda e, sl=sl, pa=pa: e.tensor_tensor(out=xo[sl], in0=pa, in1=xr[sl], op=ALU.add), reads=["psb%d" % bank, "xr%d" % sl], writes=["xo%d" % sl])
            S.op("sp", lambda e, sl=sl, tt=tt, cb=cb: e.dma_start(out=dst_tiles(tt, cb), in_=xo[sl]), reads=["xo%d" % sl], writes=["DR_dst"], dma=tag + "xo%d" % sl)
    S.barrier()


def phase4(C):
    for b in range(NB):
        def res(tt, cb, b=b):
            if tt < 16:
                return C.dram["x"][b, tt * 128:(tt + 1) * 128, cb * 512:(cb + 1) * 512]
            return C.dram["ctx"][b, (tt - 16) * 128:(tt - 15) * 128, cb * 512:(cb + 1) * 512]

        def dst(tt, cb, b=b):
            return C.dram["X1"][b, tt * 128:(tt + 1) * 128, cb * 512:(cb + 1) * 512]
        out_proj(C, b, 0, C.dram["OG0"][b], 16, C.dram["ab_w_out"], res, dst, list(range(NTT)), "p4")


HP = 2052 + 260


def phase5(C):
    for b in range(NB):
        _phase5_b(C, b)


def _phase5_b(C, b):
    nc, S, A = C.nc, C.S, C.A
    A.reset()
    w_in = C.dram["dn_w_in"]
    X1, F1, BA = C.dram["X1"], C.dram["F1"], C.dram["BA"]
    mv = load_modvecs(C, 1, b, C.dram["dn_norm"], "p5m%d" % b)
    xmT = A.alloc(16 * T, BF16)
    xm3 = _v3(xmT, 16)
    xkey = "xmT"
    mark = A.off
    tiles = [X1[b, tt * 128:(tt + 1) * 128, :] for tt in range(NTT)]
    build_xmT(C, xmT, xkey, tiles, mv, "p5b")
    S.barrier()
    A.off = mark
    wr = [A.alloc(16 * 512, BF16) for _ in range(2)]
    stg = [A.alloc(T, BF16) for _ in range(3)]
    hp = [A.alloc(HP, BF16) for _ in range(2)]
    dg = [A.alloc(5 * 128, BF16) for _ in range(2)]
    sT = [A.alloc(512, BF16) for _ in range(3)]
    sq = [A.alloc(512, BF16) for _ in range(2)]
    lnv = [A.alloc(512) for _ in range(2)]
    rst = [A.alloc(512) for _ in range(2)]
    bast = A.alloc(NTT * 128)
    cw3 = C.cw.rearrange("p (blk j) -> p blk j", j=5)
    for i in range(2):
        S.op("pool", lambda e, i=i: e.memset(hp[i], 0.0), writes=["hp%d" % i])
    segs = [("q", i * 512, 512) for i in range(4)] + [("k", 2048 + i * 512, 512) for i in range(4)] + [("v", 4096 + i * 512, 512) for i in range(8)] + \
           [("z", 8192 + i * 512, 512) for i in range(8)] + [("ba", 12288, 128)]
    cnt = dict(st=0, hp=0, c=0, n=0, s=0)
    for wi, (nm, c0, ncw) in enumerate(segs):
        slot = wi % 2
        w3 = _v3(wr[slot], 16)[:, :, 0:ncw]
        wkey = "p5w%d" % slot
        S.op("pool", lambda e, w3=w3, c0=c0, ncw=ncw: e.dma_start(out=w3, in_=w_in[:, c0:c0 + ncw].rearrange("(k p) c -> p k c", p=128)),
             writes=[wkey], dma=wkey)
        if nm == "ba":
            b3 = _v3(bast, NTT)
            for tt in range(NTT):
                bank = 4 + (C.pcnt % 4)
                C.pcnt += 1
                pa = C.ps[:, bank * 512:bank * 512 + 128]
                for k in range(16):
                    S.op("pe", lambda e, pa=pa, k=k, tt=tt, w3=w3: e.matmul(pa, lhsT=xm3[:, k, tt * 128:(tt + 1) * 128], rhs=w3[:, k, :], start=(k == 0), stop=(k == 15)),
                         reads=[wkey, xkey + str(tt)], writes=["psb%d" % bank])
                S.op("dve", lambda e, pa=pa, tt=tt: e.tensor_copy(out=b3[:, tt, :], in_=pa), reads=["psb%d" % bank], writes=["bast"])
            S.op("sp", lambda e: e.dma_start(out=BA[b].rearrange("(tt p) c -> p tt c", p=128), in_=b3), reads=["bast"], writes=["BA"], dma="bast")
            continue

        def evac(sb, gi, t0, tn, pa, pk, m, nm=nm, c0=c0):
            if nm == "z":
                if gi == 0:
                    cnt["st"] += 1
                si = cnt["st"] % 3
                sg, sk = stg[si], "p5st%d" % si
                S.op("act", lambda e: e.activation(out=sg[:, t0:t0 + tn], in_=pa, func=AF.Silu), reads=[pk], writes=[sk])
                if gi == 4:
                    r0 = c0 + sb * 128
                    S.op("sp", lambda e: e.dma_start(out=F1[b, r0:r0 + 128, :], in_=sg), reads=[sk], writes=["F1"], dma=sk)
                return
            if gi == 0:
                cnt["hp"] += 1
            hi = cnt["hp"] % 2
            hb_, hk_ = hp[hi], "hp%d" % hi
            off = 2 + t0 if gi < 4 else 2052 + 2
            if gi % 2 == 0:
                S.op("act", lambda e: e.activation(out=hb_[:, off:off + tn], in_=pa, func=AF.Copy), reads=[pk], writes=[hk_])
            else:
                S.op("dve", lambda e: e.tensor_copy(out=hb_[:, off:off + tn], in_=pa), reads=[pk], writes=[hk_])
            if gi < 4:
                return
            blk = (c0 + sb * 128) // 128
            di = cnt["hp"] % 2
            d3 = _v3(dg[di], 5)
            for j in range(5):
                S.op("pool", lambda e, j=j: e.tensor_scalar(out=d3[:, j, :], in0=C.ident, scalar1=cw3[:, blk, j:j + 1], scalar2=None, op0=ALU.mult),
                     reads=["ident", "cw"], writes=["dg%d" % di])
            cnt["st"] += 1
            si = cnt["st"] % 3
            sg, sk = stg[si], "p5st%d" % si
            outs = []
            for g2, (u0, un) in enumerate(TG):
                bank = cnt["c"] % 4
                cnt["c"] += 1
                pc = C.ps[:, bank * 512:bank * 512 + un]
                base = u0 if g2 < 4 else 2052
                for j in range(5):
                    S.op("pe", lambda e, pc=pc, j=j, base=base, un=un: e.matmul(pc, lhsT=d3[:, j, :], rhs=hb_[:, base + j:base + j + un], start=(j == 0), stop=(j == 4)),
                         reads=["dg%d" % di, hk_], writes=["psb%d" % bank])
                if nm == "v":
                    S.op("act", lambda e, pc=pc, u0=u0, un=un: e.activation(out=sg[:, u0:u0 + un], in_=pc, func=AF.Silu), reads=["psb%d" % bank], writes=[sk])
                else:
                    ssl = cnt["s"] % 3
                    cnt["s"] += 1
                    s_ = sT[ssl]
                    S.op("act", lambda e, pc=pc, s_=s_, un=un: e.activation(out=s_[:, 0:un], in_=pc, func=AF.Silu), reads=["psb%d" % bank], writes=["sT%d" % ssl])
                    outs.append((s_, "sT%d" % ssl, u0, un))
                    if len(outs) == 3 or g2 == 4:
                        pend = []
                        for (s2, s2k, v0, vn) in outs:
                            nsl = cnt["n"] % 2
                            cnt["n"] += 1
                            S.op("pool", lambda e, s2=s2, vn=vn, nsl=nsl: e.tensor_tensor(out=sq[nsl][:, 0:vn], in0=s2[:, 0:vn], in1=s2[:, 0:vn], op=ALU.mult),
                                 reads=[s2k], writes=["sq%d" % nsl])
                            bank2 = cnt["c"] % 4
                            cnt["c"] += 1
                            p3 = C.ps[:, bank2 * 512:bank2 * 512 + vn]
                            S.op("pe", lambda e, p3=p3, nsl=nsl, vn=vn: e.matmul(p3, lhsT=C.ones, rhs=sq[nsl][:, 0:vn], start=True, stop=True),
                                 reads=["ones", "sq%d" % nsl], writes=["psb%d" % bank2])
                            S.op("act", lambda e, p3=p3, nsl=nsl, vn=vn: e.activation(out=lnv[nsl][:, 0:vn], in_=p3, func=AF.Ln, bias=C.eps_t[:, 0:1]),
                                 reads=["psb%d" % bank2], writes=["lnv%d" % nsl])
                            pend.append((s2, s2k, v0, vn, nsl))
                            if len(pend) == 2 or (s2 is outs[-1][0]):
                                for (s3_, s3k, w0, wn, ns2) in pend:
                                    bias_ap = C.lnq[:, 0:1] if nm == "q" else C.zero_t[:, 0:1]
                                    S.op("act", lambda e, ns2=ns2, wn=wn, bias_ap=bias_ap: e.activation(out=rst[ns2][:, 0:wn], in_=lnv[ns2][:, 0:wn], func=AF.Exp, scale=-0.5, bias=bias_ap),
                                         reads=["lnv%d" % ns2], writes=["rst%d" % ns2])
                                    S.op("dve", lambda e, s3_=s3_, w0=w0, wn=wn, ns2=ns2: e.tensor_tensor(out=sg[:, w0:w0 + wn], in0=s3_[:, 0:wn], in1=rst[ns2][:, 0:wn], op=ALU.mult),
                                         reads=[s3k, "rst%d" % ns2], writes=[sk])
                                pend = []
                        outs = []
            r0 = c0 + sb * 128
            S.op("sp", lambda e: e.dma_start(out=F1[b, r0:r0 + 128, :], in_=sg), reads=[sk], writes=["F1"], dma=sk)

        proj_fm(C, xm3, xkey, w3, wkey, ncw, evac, "p5")
    S.barrier()


FSEQ = [16, 17] + list(range(16))
BSEQ = [17, 16] + list(range(15, -1, -1))


def dn_level_masks():
    s_ = np.arange(128)[:, None]
    c_ = np.arange(128)[None, :]
    out = np.zeros((128, 7, 4, 2, 128), np.float32)
    for k in range(1, 8):
        h = 1 << (k - 1)
        same = (s_ // (2 * h)) == (c_ // (2 * h))
        ur = same & ((s_ % (2 * h)) < h) & ((c_ % (2 * h)) >= h)
        ll = ur.T
        for j in range(4):
            fwd = j < 2
            out[:, k - 1, j, 0, :] = -(ur if fwd else ll).astype(np.float32)
            out[:, k - 1, j, 1, :] = -(ll if fwd else ur).astype(np.float32)
    id8 = np.zeros((128, 4, 2, 128), np.float32)
    id8[:, :, :, :] = np.eye(128, dtype=np.float32)[:, None, None, :]
    return out.reshape(128, 7, 1024), id8.reshape(128, 1024)


def dn_level_masks2():
    lm, _ = dn_level_masks()
    lm = lm.reshape(128, 7, 4, 2, 128).copy()
    lm -= np.eye(128, dtype=np.float32)[:, None, None, None, :]
    return np.ascontiguousarray(lm[:, :, 0::2, :, :]).reshape(128, 7, 512)


def dn_masks():
    s = np.arange(128)[:, None]
    c = np.arange(128)[None, :]
    incl = np.stack([(s <= c), (s <= c), (s >= c), (s >= c)], 0).astype(np.float32)
    strict = np.stack([(s < c), (s < c), (s > c), (s > c)], 0).astype(np.float32)
    ident4 = np.stack([np.eye(128, dtype=np.float32)] * 4, 0)
    return incl.transpose(1, 0, 2).copy(), strict.transpose(1, 0, 2).copy(), ident4.transpose(1, 0, 2).copy()


def phase6(C):
    for b in getattr(C, "p6_batches", range(NB)):
        _phase6_b(C, b)


def dump(C, name, ap, readkeys):
    if not getattr(C, "debug", False):
        return
    t = C.nc.dram_tensor("dbg_" + name, list(ap.shape), ap.dtype, kind="ExternalOutput").ap()
    if ap.shape[1] * (4 if ap.dtype == F32 else 2) > 2048:
        C.S.op("sp", lambda e: e.dma_start(out=t, in_=ap), reads=readkeys, writes=["DR_dbg"], dma=1)
        return
    if not hasattr(C, "dbg_stage"):
        C.dbg_stage = C.es.enter_context(C.nc.sbuf_tensor("dbgst", [128, 512], F32))
    st = C.dbg_stage[:, 0:ap.shape[1]] if ap.dtype == F32 else C.dbg_stage[:, 0:(ap.shape[1] + 1) // 2].bitcast(BF16)[:, 0:ap.shape[1]]
    C.S.op("dve", lambda e: e.tensor_copy(out=st, in_=ap), reads=readkeys, writes=["dbgst"])
    C.S.op("sp", lambda e: e.dma_start(out=t, in_=st), reads=["dbgst"], writes=["DR_dbg"], dma=1)


def _phase6_b(C, b):
    nc, S, A = C.nc, C.S, C.A
    A.reset()
    F1, BA, OG1 = C.dram["F1"], C.dram["BA"], C.dram["OG1"]
    mincl = A.alloc(512)
    mstr = A.alloc(512)
    ones_f = A.alloc(128)
    onorm = A.alloc(1)
    S.op("sp", lambda e: e.dma_start(out=_v3(mincl, 4), in_=C.dram["dn_mincl"]), writes=["mincl"], dma=1)
    S.op("sp", lambda e: e.dma_start(out=_v3(mstr, 4), in_=C.dram["dn_mstrict"]), writes=["mstr"], dma=1)
    S.op("sp", lambda e: e.dma_start(out=onorm, in_=C.dram["dn_o_norm"].rearrange("(p o) -> p o", o=1), allow_slow_non_contiguous=True), writes=["onorm"], dma=1)
    S.op("pool", lambda e: e.memset(ones_f, 1.0), writes=["ones_f"])
    mincl3, mstr3 = _v3(mincl, 4), _v3(mstr, 4)
    lmask = A.alloc(7 * 512, BF16)
    lm4 = lmask.rearrange("p (k d x) -> p k d x", k=7, d=2)
    S.op("sp", lambda e: e.dma_start(out=_v3(lmask, 7), in_=C.dram["dn_lmask2"]), writes=["lmask"], dma=1)
    beta = A.alloc(NTT * 64)
    gg = A.alloc(NTT * 64)
    mark6 = A.off
    ba = A.alloc(NTT * 128)
    ba3 = _v3(ba, NTT)
    S.op("sp", lambda e: e.dma_start(out=ba3, in_=BA[b].rearrange("(tt p) c -> p tt c", p=128)), reads=["BA"], writes=["ba"], dma=1)
    tA = A.alloc(NTT * 64)
    tB = A.alloc(NTT * 64)
    alog = A.alloc(64)
    dtb = A.alloc(64)
    one_t = A.alloc(1)
    S.op("pool", lambda e: e.memset(one_t, 1.0), writes=["one_t"])
    S.op("sp", lambda e: e.dma_start(out=alog, in_=C.dram["dn_a_log"].rearrange("d h -> (d h)").unsqueeze(0).to_broadcast([128, 64])), writes=["alog"], dma=1)
    S.op("sp", lambda e: e.dma_start(out=dtb, in_=C.dram["dn_dt_bias"].rearrange("d h -> (d h)").unsqueeze(0).to_broadcast([128, 64])), writes=["dtb"], dma=1)
    beta3, gg3, tA3, tB3 = _v3(beta, NTT), _v3(gg, NTT), _v3(tA, NTT), _v3(tB, NTT)
    S.op("act", lambda e: e.activation(out=tA3, in_=ba3[:, :, 0:64], func=AF.Exp, scale=-1.0), reads=["ba"], writes=["tA"])
    S.op("dve", lambda e: e.tensor_scalar(out=tA, in0=tA, scalar1=1.0, scalar2=None, op0=ALU.add), reads=["tA"], writes=["tA"])
    S.op("dve", lambda e: e.reciprocal(out=beta, in_=tA), reads=["tA"], writes=["beta"])
    S.op("dve", lambda e: e.tensor_tensor(out=tB3, in0=ba3[:, :, 64:128], in1=dtb.unsqueeze(1).to_broadcast([128, NTT, 64]), op=ALU.add), reads=["ba", "dtb"], writes=["tB"])
    S.op("dve", lambda e: e.scalar_tensor_tensor(out=tA, in0=tB, scalar=-1.0, in1=tB, op0=ALU.mult, op1=ALU.max), reads=["tB", "beta"], writes=["tA"])
    S.op("act", lambda e: e.activation(out=tA, in_=tA, func=AF.Exp, scale=-1.0), reads=["tA"], writes=["tA"])
    S.op("act", lambda e: e.activation(out=tA, in_=tA, func=AF.Ln, bias=one_t[:, 0:1]), reads=["tA", "one_t"], writes=["tA"])
    S.op("dve", lambda e: e.scalar_tensor_tensor(out=tB, in0=tB, scalar=0.0, in1=tA, op0=ALU.max, op1=ALU.add), reads=["tA", "tB"], writes=["tB"])
    S.op("act", lambda e: e.activation(out=alog, in_=alog, func=AF.Exp), reads=["alog"], writes=["alog"])
    S.op("dve", lambda e: e.scalar_tensor_tensor(out=gg3, in0=tB3, scalar=-1.0, in1=alog.unsqueeze(1).to_broadcast([128, NTT, 64]), op0=ALU.mult, op1=ALU.mult),
         reads=["tB", "alog"], writes=["gg"])
    S.barrier()
    A.off = mark6
    SHARED = {"gg", "beta", "mincl", "mstr", "lmask", "ident", "ones", "onorm", "ones_f", "eps", "zero", "lnq"}
    S0 = S

    class _SlotSched:
        def __init__(self, si):
            self.si = si

        def op(self, eng, fn, reads=(), writes=(), dma=None):
            f = lambda k: k if (k in SHARED or k in Sched.DRAMKEYS or k.startswith("DR_")) else "s%d_%s" % (self.si, k)
            return S0.op(eng, fn, reads=[f(k) for k in reads], writes=[f(k) for k in writes], dma=dma)

    def run_slot(si, head_list):
        S = _SlotSched(si)
        hbufs = [dict(q=A.alloc(T, BF16), k=A.alloc(T, BF16), v=A.alloc(2 * T, BF16))]
        ktok = A.alloc(NTT * 128, BF16)
        vtok = A.alloc(NTT * 256, BF16)
        oacc = A.alloc(2 * SEQ)
        o3 = _v3(oacc, 2)
        szb1 = A.alloc(SEQ, BF16)
        def mk():
            dec_ = A.alloc(512)
            tmp_ = A.alloc(512)
            tmp2_ = A.alloc(512)
            return dict(grep=A.alloc(512), d1=dec_, dec=dec_, gam=A.alloc(512), bm=A.alloc(512), tmp=tmp_, tmp2=tmp2_, t3=tmp_, t4=tmp2_,
                        x12=A.alloc(12), e12=A.alloc(12), negb=A.alloc(4), xn=A.alloc(1024, BF16), rt=A.alloc(1024, BF16), yy=A.alloc(1024, BF16),
                        xb=dec_, intra=A.alloc(512, BF16), gq=A.alloc(512, BF16), kd=A.alloc(512, BF16), vd=A.alloc(512, BF16), vn=A.alloc(512, BF16))
        stp = [mk()]
        S4 = A.alloc(512)
        S4b = A.alloc(512, BF16)
        sqb = [A.alloc(512, BF16)] * 2
        lnv = [A.alloc(512)] * 2
        rst = lnv
        osum = [A.alloc(512) for _ in range(2)]
        ps = C.ps
        pbase = si * 2048
        kG = kS1 = "psb%d" % (pbase // 512)
        kAB = kS2 = "psb%d" % (pbase // 512 + 1)
        kM = kT = kC = kN = "psM%d" % (pbase // 512)
        psG = ps[:, pbase:pbase + 512]
        psG3 = _v3(psG, 4)
        psS1 = psG
        psA = ps[:, pbase + 512:pbase + 768]
        psB = ps[:, pbase + 768:pbase + 1024]
        psS2 = ps[:, pbase + 512:pbase + 1024]
        psM = ps[:, pbase + 1024:pbase + 2048]
        psM3 = _v3(psM, 4)
        psT = ps[:, pbase + 1024:pbase + 1536]
        psTb = psT.bitcast(BF16)
        psC = ps[:, pbase + 1536:pbase + 1540]
        psN = psT

        for g in head_list:
            H = hbufs[0]
            hk = "h6"
            S.op("sp", lambda e, H=H, g=g: e.dma_start(out=H["q"], in_=F1[b, g * 128:(g + 1) * 128, :]), reads=["F1"], writes=[hk + "q"], dma=1)
            S.op("sp", lambda e, H=H, g=g: e.dma_start(out=H["k"], in_=F1[b, 2048 + g * 128:2048 + (g + 1) * 128, :]), reads=["F1"], writes=[hk + "k"], dma=1)
            S.op("sp", lambda e, H=H, g=g: e.dma_start(out=_v3(H["v"], 2), in_=F1[b, 4096 + 2 * g * 128:4096 + (2 * g + 2) * 128, :].rearrange("(v p) t -> p v t", p=128)),
                 reads=["F1"], writes=[hk + "v"], dma=1)
            QT, KT, VT3 = H["q"], H["k"], _v3(H["v"], 2)
            kt3 = _v3(ktok, NTT)
            vt4 = vtok.rearrange("p (t v d) -> p t v d", t=NTT, v=2)
            jobs = [("k", tt, 0) for tt in range(NTT)] + [("v", tt, vh) for tt in range(NTT) for vh in range(2)]
            groups = [jobs[0:8], jobs[8:16], jobs[16:18]] + [jobs[18 + i:18 + i + 8] for i in range(0, 36, 8)]
            for j0, grp in enumerate(groups):
                yield
                j0 = j0 * 8
                for i, (kind, tt, vh) in enumerate(grp):
                    src = KT[:, tt * 128:(tt + 1) * 128] if kind == "k" else VT3[:, vh, tt * 128:(tt + 1) * 128]
                    S.op("pe", lambda e, i=i, src=src: e.transpose(out=psTb[:, i * 128:(i + 1) * 128], in_=src, identity=C.ident),
                         reads=[hk + "k", hk + "v", "ident"], writes=[kT])
                kind0, tt0, vh0 = grp[0]
                n = len(grp)
                if kind0 == "k":
                    dst = ktok[:, tt0 * 128:(tt0 + n) * 128]
                    dk_ = "ktok"
                else:
                    dst = vtok[:, (tt0 * 2 + vh0) * 128:(tt0 * 2 + vh0 + n) * 128]
                    dk_ = "vtok"
                if (j0 // 8) % 2 == 0:
                    S.op("act", lambda e, dst=dst, n=n: e.activation(out=dst, in_=psTb[:, 0:n * 128], func=AF.Copy), reads=[kT], writes=[dk_])
                else:
                    S.op("dve", lambda e, dst=dst, n=n: e.tensor_copy(out=dst, in_=psTb[:, 0:n * 128]), reads=[kT], writes=[dk_])
            S.op("pool", lambda e: e.memset(S4, 0.0), writes=["S4"])
            S.op("pool", lambda e: e.memset(S4b, 0.0), writes=["S4b"])
            S43, S4b3 = _v3(S4, 4), _v3(S4b, 4)
            c0 = 2 * g
            def step(s, part, g=g, H=H, hk=hk, QT=QT, KT=KT, VT3=VT3, kt3=kt3, vt4=vt4, c0=c0, S43=S43, S4b3=S4b3):
                P = stp[0]
                pk = "st0"
                blks = (FSEQ[s], BSEQ[s])
                cols = [(d * 32 + c0) for d in range(2)]
                grep3, d13, dec3, gam3, bm3, tmp3, tmp23 = [_v3(P[n_], 4) for n_ in ("grep", "d1", "dec", "gam", "bm", "tmp", "tmp2")]
                xb3, intra3, gq3, kd3, vd3, vn3, t33, t43 = [_v3(P[n_], 4) for n_ in ("xb", "intra", "gq", "kd", "vd", "vn", "t3", "t4")]
                x12, e12, negb = P["x12"], P["e12"], P["negb"]
                RT = P["rt"].rearrange("p (j o c) -> p j o c", j=4, o=2)
                krt = pk + "rt"
                xn4 = P["xn"].rearrange("p (j o c) -> p j o c", j=4, o=2)
                kxn = pk + "xn"
                if part == "A":
                    yield
                    for d in range(2):
                        gs = gg3[:, blks[d], cols[d]:cols[d] + 2]
                        S.op("pool", lambda e, d=d, gs=gs: e.tensor_copy(out=grep3[:, 2 * d:2 * d + 2, :], in_=gs.unsqueeze(2).to_broadcast([128, 2, 128])),
                             reads=["gg"], writes=[pk + "grep"])
                        S.op("dve", lambda e, d=d: e.tensor_scalar(out=negb[:, 2 * d:2 * d + 2], in0=beta3[:, blks[d], cols[d]:cols[d] + 2], scalar1=-1.0, scalar2=None, op0=ALU.mult),
                             reads=["beta"], writes=[pk + "negb"])
                        S.op("pool", lambda e, d=d: e.tensor_tensor(out=bm3[:, 2 * d:2 * d + 2, :], in0=mstr3[:, 2 * d:2 * d + 2, :],
                                                                    in1=beta3[:, blks[d], cols[d]:cols[d] + 2].unsqueeze(2).to_broadcast([128, 2, 128]), op=ALU.mult),
                             reads=["beta", "mstr"], writes=[pk + "bm"])
                    yield
                    for j in range(4):
                        d = j // 2
                        S.op("pe", lambda e, j=j, d=d: e.matmul(psG3[:, j, :], lhsT=grep3[:, j, :], rhs=mincl3[:, 2 * d, :], start=True, stop=True),
                             reads=[pk + "grep", "mincl"], writes=[kG])
                    yield
                    for d in range(2):
                        S.op("pe", lambda e, d=d: e.matmul(psC[:, 2 * d:2 * d + 2], lhsT=mincl3[:, 2 * d, :], rhs=gg3[:, blks[d], cols[d]:cols[d] + 2], start=True, stop=True),
                             reads=["gg", "mincl"], writes=[kC])
                    yield
                    for d in range(2):
                        kb_ = KT[:, blks[d] * 128:(blks[d] + 1) * 128]
                        qb_ = QT[:, blks[d] * 128:(blks[d] + 1) * 128]
                        S.op("pe", lambda e, d=d, kb_=kb_: e.matmul(psA[:, d * 128:(d + 1) * 128], lhsT=kb_, rhs=kb_, start=True, stop=True), reads=[hk + "k"], writes=[kAB])
                        S.op("pe", lambda e, d=d, kb_=kb_, qb_=qb_: e.matmul(psB[:, d * 128:(d + 1) * 128], lhsT=kb_, rhs=qb_, start=True, stop=True), reads=[hk + "k", hk + "q"], writes=[kAB])
                    yield
                    S.op("dve", lambda e: e.tensor_copy(out=x12[:, 0:4], in_=psC), reads=[kC], writes=[pk + "x12"])
                    yield
                    for d in range(2):
                        last = 127 if d == 0 else 0
                        S.op("dve", lambda e, d=d, last=last: e.tensor_copy(out=x12[:, 8 + 2 * d:10 + 2 * d], in_=psG3[:, 2 * d:2 * d + 2, last]), reads=[kG], writes=[pk + "x12"])
                    yield
                    S.op("dve", lambda e: e.tensor_tensor(out=x12[:, 4:8], in0=x12[:, 8:12], in1=x12[:, 0:4], op=ALU.subtract), reads=[pk + "x12"], writes=[pk + "x12"])
                    yield
                    S.op("act", lambda e: e.activation(out=e12, in_=x12, func=AF.Exp), reads=[pk + "x12"], writes=[pk + "e12"])
                    yield
                    S.op("dve", lambda e: e.tensor_tensor(out=d13, in0=psG3, in1=x12[:, 0:4].unsqueeze(2).to_broadcast([128, 4, 128]), op=ALU.subtract),
                         reads=[kG, pk + "x12"], writes=[pk + "dec"])
                    yield
                    S.op("pool", lambda e: e.tensor_scalar(out=P["d1"], in0=P["d1"], scalar1=0.0, scalar2=-80.0, op0=ALU.min, op1=ALU.max), reads=[pk + "dec"], writes=[pk + "dec"])
                    yield
                    S.op("act", lambda e: e.activation(out=P["dec"], in_=P["d1"], func=AF.Exp), reads=[pk + "dec"], writes=[pk + "dec"])
                    yield
                    S.op("act", lambda e: e.activation(out=P["gam"], in_=psG, func=AF.Exp), reads=[kG], writes=[pk + "gam"])
                    dec4 = P["dec"].rearrange("p (d v c) -> p d v c", d=2, v=2)
                    psA4 = psA.rearrange("p (d c) -> p d c", d=2).unsqueeze(2).to_broadcast([128, 2, 2, 128])
                    psB4 = psB.rearrange("p (d c) -> p d c", d=2).unsqueeze(2).to_broadcast([128, 2, 2, 128])
                    yield
                    S.op("dve", lambda e, psA4=psA4, dec4=dec4: e.tensor_tensor(out=P["tmp"].rearrange("p (d v c) -> p d v c", d=2, v=2), in0=psA4, in1=dec4, op=ALU.mult),
                         reads=[kAB, pk + "dec"], writes=[pk + "tmp"])
                    yield
                    S.op("pool", lambda e: e.tensor_tensor(out=xn4[:, :, 0, :], in0=tmp3, in1=bm3, op=ALU.mult), reads=[pk + "tmp", pk + "bm"], writes=[kxn])
                    S.op("pool", lambda e: e.tensor_tensor(out=xn4[:, :, 0, :], in0=xn4[:, :, 0, :], in1=C.ident.unsqueeze(1).to_broadcast([128, 4, 128]), op=ALU.subtract),
                         reads=[kxn, "ident"], writes=[kxn])
                    yield
                    S.op("dve", lambda e, psB4=psB4, dec4=dec4: e.tensor_tensor(out=P["tmp2"].rearrange("p (d v c) -> p d v c", d=2, v=2), in0=psB4, in1=dec4, op=ALU.mult),
                         reads=[kAB, pk + "dec"], writes=[pk + "tmp2"])
                    yield
                    S.op("pool", lambda e: e.tensor_tensor(out=intra3, in0=tmp23, in1=mincl3, op=ALU.mult), reads=[pk + "tmp2", "mincl"], writes=[pk + "intra"])
                    yield
                    for d in range(2):
                        qb_ = QT[:, blks[d] * 128:(blks[d] + 1) * 128]
                        S.op("pool", lambda e, d=d, qb_=qb_: e.tensor_tensor(out=gq3[:, 2 * d:2 * d + 2, :], in0=qb_.unsqueeze(1).to_broadcast([128, 2, 128]), in1=gam3[:, 2 * d:2 * d + 2, :], op=ALU.mult),
                             reads=[hk + "q", pk + "gam"], writes=[pk + "gq"])
                        S.op("pool", lambda e, d=d: e.tensor_tensor(out=kd3[:, 2 * d:2 * d + 2, :], in0=kt3[:, blks[d], :].unsqueeze(1).to_broadcast([128, 2, 128]),
                                                                    in1=e12[:, 4 + 2 * d:6 + 2 * d].unsqueeze(2).to_broadcast([128, 2, 128]), op=ALU.mult),
                             reads=["ktok", pk + "e12"], writes=[pk + "kd"])
                    xn4 = P["xn"].rearrange("p (j o c) -> p j o c", j=4, o=2)
                    rt4 = P["rt"].rearrange("p (j o c) -> p j o c", j=4, o=2)
                    yy4 = P["yy"].rearrange("p (j o c) -> p j o c", j=4, o=2)
                    psM4 = psM.rearrange("p (j o c) -> p j o c", j=4, o=2)
                    kxn, krt_, kyy = pk + "xn", pk + "rt", pk + "yy"
                    yield
                    pass
                    yield
                    for j in range(4):
                        S.op("pe", lambda e, j=j: e.transpose(out=psTb[:, j * 128:(j + 1) * 128], in_=xn4[:, j, 0, :], identity=C.ident), reads=[kxn, "ident"], writes=[kT])
                    yield
                    S.op("act", lambda e: e.activation(out=xn4[:, :, 1, :], in_=_v3(psTb[:, 0:512], 4), func=AF.Copy), reads=[kT], writes=[kxn])
                    def lmv(lv):
                        return lm4[:, lv, :, :].unsqueeze(2).to_broadcast([128, 2, 2, 256])
                    v4 = lambda ap: ap.rearrange("p (d v x) -> p d v x", d=2, v=2)
                    yield
                    S.op("dve", lambda e: e.tensor_tensor(out=v4(P["rt"]), in0=v4(P["xn"]), in1=lmv(0), op=ALU.mult), reads=[kxn, "lmask"], writes=[krt_])
                    for lv in range(1, 7):
                        yield
                        for j in range(4):
                            S.op("pe", lambda e, j=j: e.matmul(psM4[:, j, 0, :], lhsT=xn4[:, j, 1, :], rhs=rt4[:, j, 0, :], start=True, stop=True), reads=[kxn, krt_], writes=[kM])
                            S.op("pe", lambda e, j=j: e.matmul(psM4[:, j, 1, :], lhsT=xn4[:, j, 0, :], rhs=rt4[:, j, 1, :], start=True, stop=True), reads=[kxn, krt_], writes=[kM])
                        yield
                        S.op("dve", lambda e, lv=lv: e.tensor_tensor(out=v4(P["yy"]), in0=v4(psM), in1=lmv(lv), op=ALU.mult), reads=[kM, "lmask"], writes=[kyy])
                        yield
                        for j in range(4):
                            S.op("pe", lambda e, j=j: e.matmul(psM4[:, j, 0, :], lhsT=rt4[:, j, 1, :], rhs=yy4[:, j, 0, :], start=True, stop=True), reads=[kyy, krt_], writes=[kM])
                            if lv < 6:
                                S.op("pe", lambda e, j=j: e.matmul(psM4[:, j, 1, :], lhsT=rt4[:, j, 0, :], rhs=yy4[:, j, 1, :], start=True, stop=True), reads=[kyy, krt_], writes=[kM])
                        yield
                        if lv < 6:
                            S.op("act", lambda e: e.activation(out=P["rt"], in_=psM, func=AF.Copy), reads=[kM], writes=[krt_])
                        else:
                            S.op("act", lambda e: e.activation(out=rt4[:, :, 0, :], in_=psM4[:, :, 0, :], func=AF.Copy), reads=[kM], writes=[krt_])
                    RT = rt4
                    krt = krt_
                    return
                yield
                for j in range(4):
                    d = j // 2
                    kb_ = KT[:, blks[d] * 128:(blks[d] + 1) * 128]
                    S.op("pe", lambda e, j=j, kb_=kb_: e.matmul(psS1[:, j * 128:(j + 1) * 128], lhsT=kb_, rhs=S4b3[:, j, :], start=True, stop=True), reads=[hk + "k", "S4b"], writes=[kS1])
                yield
                S.op("dve", lambda e: e.tensor_tensor(out=t33, in0=_v3(psS1, 4), in1=e12[:, 0:4].unsqueeze(2).to_broadcast([128, 4, 128]), op=ALU.mult),
                     reads=[kS1, pk + "e12"], writes=[pk + "tmp"])
                yield
                for d in range(2):
                    S.op("pool" if d == 0 else "dve", lambda e, d=d: e.tensor_tensor(out=vd3[:, 2 * d:2 * d + 2, :], in0=t33[:, 2 * d:2 * d + 2, :], in1=vt4[:, blks[d], :, :], op=ALU.subtract),
                         reads=[pk + "tmp", "vtok"], writes=[pk + "vd"])
                yield
                for j in range(4):
                    S.op("pe", lambda e, j=j: e.matmul(psS2[:, j * 128:(j + 1) * 128], lhsT=RT[:, j, 0, :], rhs=vd3[:, j, :], start=True, stop=True), reads=[krt, pk + "vd"], writes=[kS2])
                yield
                S.op("dve", lambda e: e.tensor_tensor(out=vn3, in0=_v3(psS2, 4), in1=negb.unsqueeze(2).to_broadcast([128, 4, 128]), op=ALU.mult),
                     reads=[kS2, pk + "negb"], writes=[pk + "vn"])
                if s >= 2:
                    for j in range(4):
                        S.op("pe", lambda e, j=j: e.matmul(psS1[:, j * 128:(j + 1) * 128], lhsT=S4b3[:, j, :], rhs=gq3[:, j, :], start=True, stop=False), reads=["S4b", pk + "gq"], writes=[kS1])
                        S.op("pe", lambda e, j=j: e.matmul(psS1[:, j * 128:(j + 1) * 128], lhsT=vn3[:, j, :], rhs=intra3[:, j, :], start=False, stop=True), reads=[pk + "vn", pk + "intra"], writes=[kS1])
                    for d in range(2):
                        dstv = o3[:, :, blks[d] * 128:(blks[d] + 1) * 128]
                        srcv = _v3(psS1[:, d * 256:(d + 1) * 256], 2)
                        if s <= 9:
                            S.op("act", lambda e, dstv=dstv, srcv=srcv: e.activation(out=dstv, in_=srcv, func=AF.Copy), reads=[kS1], writes=["oacc"])
                        else:
                            S.op("dve", lambda e, dstv=dstv, srcv=srcv: e.tensor_tensor(out=dstv, in0=srcv, in1=dstv, op=ALU.add), reads=[kS1, "oacc"], writes=["oacc"])
                if s < NTT - 1:
                    for j in range(4):
                        S.op("pe", lambda e, j=j: e.matmul(psS2[:, j * 128:(j + 1) * 128], lhsT=kd3[:, j, :], rhs=vn3[:, j, :], start=True, stop=True), reads=[pk + "kd", pk + "vn"], writes=[kS2])
                    S.op("pool", lambda e: e.tensor_tensor(out=t43, in0=S43, in1=e12[:, 8:12].unsqueeze(2).to_broadcast([128, 4, 128]), op=ALU.mult),
                         reads=["S4", pk + "e12"], writes=[pk + "tmp2"])
                    S.op("dve", lambda e: e.tensor_tensor(out=S4, in0=psS2, in1=P["t4"], op=ALU.add), reads=[kS2, pk + "tmp2"], writes=["S4"])
                    S.op("act", lambda e: e.activation(out=S4b, in_=S4, func=AF.Copy), reads=["S4"], writes=["S4b"])
            nst_ = getattr(C, "p6_nsteps", NTT)
            for s_ in range(nst_):
                yield from step(s_, "A")
                yield from step(s_, "S")
            for vh in range(2):
                S.op("sp", lambda e, vh=vh, g=g: e.dma_start(out=szb1, in_=F1[b, 8192 + (2 * g + vh) * 128:8192 + (2 * g + vh + 1) * 128, 0:SEQ]),
                     reads=["F1"], writes=["szb1"], dma=1)
                for gi in range(4):
                    yield
                    t0 = gi * 512
                    sl = gi % 2
                    a_ = o3[:, vh, t0:t0 + 512]
                    S.op("pool", lambda e, a_=a_, sl=sl: e.tensor_tensor(out=sqb[sl], in0=a_, in1=a_, op=ALU.mult), reads=["oacc"], writes=["sqb6"])
                    S.op("pe", lambda e, sl=sl: e.matmul(psS1, lhsT=C.ones, rhs=sqb[sl], start=True, stop=True), reads=["ones", "sqb6"], writes=[kS1])
                    S.op("act", lambda e, sl=sl: e.activation(out=lnv[sl], in_=psS1, func=AF.Ln, scale=1.0 / 128, bias=C.eps_t[:, 0:1]), reads=[kS1], writes=["lnv6"])
                    S.op("act", lambda e, sl=sl: e.activation(out=rst[sl], in_=lnv[sl], func=AF.Exp, scale=-0.5), reads=["lnv6"], writes=["lnv6"])
                    S.op("dve", lambda e, sl=sl, a_=a_: e.scalar_tensor_tensor(out=osum[sl], in0=a_, scalar=onorm[:, 0:1], in1=rst[sl], op0=ALU.mult, op1=ALU.mult),
                         reads=["oacc", "lnv6", "onorm"], writes=["osum%d" % sl])
                    S.op("pool", lambda e, sl=sl, t0=t0: e.tensor_tensor(out=szb1[:, t0:t0 + 512], in0=osum[sl], in1=szb1[:, t0:t0 + 512], op=ALU.mult),
                         reads=["osum%d" % sl, "szb1"], writes=["szb1"])
                r0 = (2 * g + vh) * 128
                S.op("sp", lambda e, r0=r0: e.dma_start(out=OG1[b, r0:r0 + 128, :], in_=szb1), reads=["szb1"], writes=["OG1"], dma=1)

    heads_all = list(getattr(C, "p6_heads", range(16)))
    gens = [run_slot(0, heads_all[0::2]), run_slot(1, heads_all[1::2])]
    while gens:
        for g_ in list(gens):
            try:
                next(g_)
            except StopIteration:
                gens.remove(g_)
    S.barrier()


def phase7(C):
    nc, S, A = C.nc, C.S, C.A
    for b in range(NB):
        def res(tt, cb, b=b):
            return C.dram["X1"][b, tt * 128:(tt + 1) * 128, cb * 256:(cb + 1) * 256]

        def dst(tt, cb, b=b):
            return C.dram["X2"][b, tt * 128:(tt + 1) * 128, cb * 256:(cb + 1) * 256]
        out_proj(C, b, 1, C.dram["OG1"][b], 32, C.dram["dn_w_out"], res, dst, list(range(16)), "p7", cbw=256, ntok=SEQ)
    A.reset()
    fn = A.alloc(D)
    S.op("sp", lambda e: e.dma_start(out=fn, in_=C.dram["final_norm"].unsqueeze(0).to_broadcast([128, D])), writes=["fn"], dma=1)
    xr = [A.alloc(D) for _ in range(3)]
    xo = [A.alloc(D) for _ in range(3)]
    junk = A.alloc(D, BF16)
    st = [A.alloc(4) for _ in range(3)]
    it = 0
    for b in range(NB):
        for tt in range(16):
            sl = it % 3
            it += 1
            xt, xo_, s4 = xr[sl], xo[sl], st[sl]
            S.op("sp", lambda e, xt=xt, b=b, tt=tt: e.dma_start(out=xt, in_=C.dram["X2"][b, tt * 128:(tt + 1) * 128, :]), reads=["X2"], writes=["fxr%d" % sl], dma=1)
            S.op("act", lambda e, xt=xt, s4=s4: e.activation(out=junk, in_=xt, func=AF.Square, accum_out=s4[:, 0:1]), reads=["fxr%d" % sl], writes=["fjunk", "fst%d" % sl])
            S.op("act", lambda e, s4=s4: e.activation(out=s4[:, 1:2], in_=s4[:, 0:1], func=AF.Sqrt, scale=1.0 / D, bias=C.eps_t[:, 0:1]), reads=["fst%d" % sl], writes=["fst%d" % sl])
            S.op("dve", lambda e, s4=s4: e.reciprocal(out=s4[:, 2:3], in_=s4[:, 1:2]), reads=["fst%d" % sl], writes=["fst%d" % sl])
            S.op("dve", lambda e, xt=xt, xo_=xo_, s4=s4: e.scalar_tensor_tensor(out=xo_, in0=xt, scalar=s4[:, 2:3], in1=fn, op0=ALU.mult, op1=ALU.mult),
                 reads=["fxr%d" % sl, "fst%d" % sl, "fn"], writes=["fxo%d" % sl])
            S.op("sp", lambda e, xo_=xo_, b=b, tt=tt: e.dma_start(out=C.dram["OUT"][b, tt * 128:(tt + 1) * 128, :], in_=xo_), reads=["fxo%d" % sl], writes=["OUT"], dma=1)
    S.barrier()


NCORES = 8
_PHASES = (phase0, phase1, phase2, phase3, phase4, phase5, phase6, phase7)


def _host_inputs(inp):
    cos, sin = rope_tables()
    per_m, nt, mask, drow, dcol = na_geometry()
    rpb = np.asarray(inp["ab_rpb"][0], np.float32)
    nab = np.stack([rpb[h][drow, dcol] for h in range(8)], 0).astype(np.float32)
    mi, ms, id4 = dn_masks()
    lm, id8 = dn_level_masks()
    f = lambda a: np.ascontiguousarray(np.asarray(a, np.float32))
    shared = {
        "w_mod0": f(inp["ab_w_mod"][0]), "w_mod1": f(inp["dn_w_mod"][0]), "b_mod0": f(inp["ab_b_mod"][0]), "b_mod1": f(inp["dn_b_mod"][0]),
        "ab_norm": f(inp["ab_norm"][0]), "ab_w_in": f(inp["ab_w_in"][0]), "ab_w_qb": f(inp["ab_w_qb"][0]), "ab_w_kvb": f(inp["ab_w_kvb"][0]),
        "ab_q_norm": f(inp["ab_q_norm"][0]), "ab_kv_norm": f(inp["ab_kv_norm"][0]), "ab_w_out": f(inp["ab_w_out"][0]),
        "na_mask": mask, "na_bias": nab, "rope_cos": cos, "rope_sin": sin, "ident": np.eye(128, dtype=np.float32).astype(NPBF),
        "dn_norm": f(inp["dn_norm"][0]), "dn_w_in": f(inp["dn_w_in"][0]), "dn_conv": f(inp["dn_conv"][0]), "dn_a_log": f(inp["dn_a_log"][0]),
        "dn_dt_bias": f(inp["dn_dt_bias"][0]), "dn_o_norm": f(inp["dn_o_norm"][0]), "dn_w_out": f(inp["dn_w_out"][0]), "final_norm": f(inp["final_norm"]),
        "dn_mincl": mi, "dn_mstrict": ms, "dn_ident4": id4.astype(NPBF), "dn_lmask2": dn_level_masks2().astype(NPBF),
    }
    maps = []
    for i in range(NCORES):
        m = dict(shared)
        m["x"] = f(inp["x"][NB * i:NB * (i + 1)])
        m["ctx"] = f(inp["ctx"][NB * i:NB * (i + 1)])
        m["cvec"] = np.concatenate([f(inp["c"][NB * i:NB * (i + 1)]), f(inp["c_ctx"])[None]], 0)
        maps.append(m)
    return maps


_INTERNAL = {
    "mod": ([2, 3, 6144], F32), "F0": ([NB, 5184, T], BF16), "VA": ([NB, T, 1024], BF16), "M0": ([NB, 2560, T], BF16), "VM": ([NB, T, 1024], BF16),
    "OG0": ([NB, 2048, T], BF16), "X1": ([NB, T, D], F32), "F1": ([NB, 12288, T], BF16), "BA": ([NB, T, 128], F32), "OG1": ([NB, 4096, SEQ], BF16),
    "X2": ([NB, SEQ, D], F32),
}


def build_program(maps0, phases=_PHASES):
    nc = bass.Bass("TRN2", target_bir_lowering=False)
    with ExitStack() as es:
        dram = {}
        for nm, a in maps0.items():
            dram[nm] = nc.dram_tensor(nm, list(a.shape), BF16 if a.dtype == NPBF else F32, kind="ExternalInput").ap()
        for nm, (shape, dt_) in _INTERNAL.items():
            dram[nm] = nc.dram_tensor(nm, shape, dt_, kind="Internal").ap()
        dram["OUT"] = nc.dram_tensor("OUT", [NB, SEQ, D], F32, kind="ExternalOutput").ap()
        C = make_ctx(nc, es, dram)
        for p in phases:
            p(C)
        C.S.finalize()
    return nc


def kernel(**inputs):
    maps = _host_inputs(inputs)
    nc = build_program(maps[0])
    res = run_bass_kernel_spmd(nc, maps, core_ids=list(range(NCORES)))
    out = np.concatenate([np.asarray(r["OUT"], np.float32) for r in res.results], axis=0)
    return out
```

```python
import numpy as np
import ml_dtypes
from contextlib import ExitStack
import concourse.bass as bass
import concourse.mybir as mybir
from concourse.bass_utils import run_bass_kernel_spmd

F32 = mybir.dt.float32
BF16 = mybir.dt.bfloat16
AF = mybir.ActivationFunctionType
ALU = mybir.AluOpType
NPBF = ml_dtypes.bfloat16

D = 2048
SEQ = 2048
CTX = 256
T = SEQ + CTX
NTT = T // 128
NB = 2
EPS = 1e-6
TG = [(0, 512), (512, 512), (1024, 512), (1536, 512), (2048, 256)]


class Op:
    __slots__ = ("eng", "fn", "deps", "marked", "val", "sem", "is_dma")


class Buf:
    __slots__ = ("w", "r")

    def __init__(self):
        self.w = None
        self.r = []


class Sched:
    ENGS = ("pe", "act", "dve", "pool", "sp")
    ENGOBJ = {"pe": "tensor", "act": "scalar", "dve": "vector", "pool": "gpsimd", "sp": "sync"}

    def __init__(self, nc, es):
        self.nc = nc
        self.es = es
        self.ops = {e: [] for e in self.ENGS}
        self.bufs = {}
        self.sems = {e: es.enter_context(nc.semaphore("s_" + e)) for e in self.ENGS}
        self.dsems = {}
        self.dpool = []
        self.last_dma = {}
        self.nops = 0

    DRAMKEYS = {"mod", "F0", "VA", "M0", "VM", "OG0", "X1", "X2", "F1", "BA", "OG1", "OUT", "ST"}

    def dsem(self, key):
        if key not in self.dsems:
            i = len(self.dsems)
            if i >= len(self.dpool):
                self.dpool.append([self.es.enter_context(self.nc.semaphore("d_%d" % i)), 0])
            self.dsems[key] = self.dpool[i]
        return self.dsems[key]

    def op(self, eng, fn, reads=(), writes=(), dma=None):
        o = Op()
        o.eng = eng
        o.fn = fn
        o.deps = []
        o.marked = False
        o.val = None
        o.sem = None
        o.is_dma = dma is not None
        self.nops += 1
        if dma is not None:
            dk = None
            for k in list(writes) + list(reads):
                if not (k in self.DRAMKEYS or k.startswith("DR_")):
                    dk = k
                    break
            assert dk is not None, (reads, writes)
            d = self.dsem(dk)
            d[1] += 16
            o.sem = d[0]
            o.val = d[1]
            o.marked = True
            self.last_dma[id(d)] = o
        deps = {}
        for k in reads:
            b = self.bufs.get(k)
            if b is None:
                b = self.bufs[k] = Buf()
            if b.w is not None:
                deps[id(b.w)] = b.w
        for k in writes:
            b = self.bufs.get(k)
            if b is None:
                b = self.bufs[k] = Buf()
            if b.w is not None:
                deps[id(b.w)] = b.w
            for r in b.r:
                deps[id(r)] = r
        for k in reads:
            self.bufs[k].r.append(o)
        for k in writes:
            b = self.bufs[k]
            b.w = o
            b.r = []
        for d in deps.values():
            if d is o:
                continue
            if d.eng == "pe" and eng == "pe" and not d.is_dma:
                continue
            d.marked = True
            o.deps.append(d)
        self.ops[eng].append(o)
        return o

    def barrier(self):
        lasts = []
        for e in self.ENGS:
            for o in reversed(self.ops[e]):
                if not o.is_dma and o.fn is not None:
                    o.marked = True
                    lasts.append(o)
                    break
        lasts += list(self.last_dma.values())
        for e in self.ENGS:
            o = Op()
            o.eng = e
            o.fn = None
            o.deps = list(lasts)
            o.marked = False
            o.val = None
            o.sem = None
            o.is_dma = False
            self.ops[e].append(o)
        self.bufs = {}
        self.dsems = {}

    def finalize(self):
        for e in self.ENGS:
            c = 0
            for o in self.ops[e]:
                if o.is_dma or o.fn is None:
                    continue
                if o.marked:
                    c += 1
                    o.val = c
                    o.sem = self.sems[e]
        nc = self.nc
        with nc.Block() as block:
            for e in self.ENGS:
                ops = self.ops[e]

                def body(engine, ops=ops, e=e):
                    seen = {}
                    for o in ops:
                        for d in o.deps:
                            k = id(d.sem)
                            if seen.get(k, 0) >= d.val:
                                continue
                            seen[k] = d.val
                            engine.wait_ge(d.sem, d.val)
                        if o.fn is None:
                            continue
                        ins = o.fn(engine)
                        if o.is_dma:
                            ins.then_inc(o.sem, 16)
                        elif o.marked:
                            ins.then_inc(o.sem, 1)
                    if e == "sp":
                        for (s, v) in self.dpool:
                            if v > 0:
                                engine.wait_ge(s, v)

                getattr(block, self.ENGOBJ[e])(body)


class Arena:
    def __init__(self, nc, es, nwords=51200):
        self.t = es.enter_context(nc.sbuf_tensor("arena", [128, nwords], F32))
        self.n = nwords
        self.off = 0
        self.uid = 0

    def reset(self):
        self.off = 0

    def alloc(self, nelem, dtype=F32):
        nbytes = nelem * (4 if dtype == F32 else 2)
        words = (nbytes + 31) // 32 * 8
        assert self.off + words <= self.n, "SBUF arena overflow %d+%d" % (self.off, words)
        ap = self.t[:, self.off:self.off + words]
        self.off += words
        if dtype != F32:
            ap = ap.bitcast(dtype)
        return ap[:, 0:nelem]

    def key(self, name):
        self.uid += 1
        return "%s#%d" % (name, self.uid)


class Ctx:
    pass


def _v3(ap, a):
    return ap.rearrange("p (a b) -> p a b", a=a)


def phase0(C):
    nc, S, A = C.nc, C.S, C.A
    A.reset()
    csT = A.alloc(48)
    cs3 = _v3(csT, 16)
    bm = A.alloc(6144)
    osb = [A.alloc(2048), A.alloc(2048)]
    wr = [A.alloc(2048) for _ in range(4)]
    ps = C.ps
    for v in range(3):
        S.op("sp", lambda e, v=v: e.dma_start(out=cs3[:, :, v], in_=C.dram["cvec"][v, :].rearrange("(k p) -> p k", p=128),
                                              allow_slow_non_contiguous=True), writes=["csT"], dma="p0c")
    S.op("act", lambda e: e.activation(out=csT, in_=csT, func=AF.Silu), reads=["csT"], writes=["csT"])
    it = 0
    oi = 0
    for l in range(2):
        wm = C.dram["w_mod%d" % l]
        bmod = C.dram["b_mod%d" % l]
        S.op("sp", lambda e, bmod=bmod: e.dma_start(out=bm[0:3, :], in_=bmod.unsqueeze(0).to_broadcast([3, 6144])),
             writes=["bm"], dma="p0b")
        for g in range(3):
            for k in range(16):
                slot = it % 4
                it += 1
                wt = wr[slot]
                S.op("sp", lambda e, wt=wt, k=k, g=g, wm=wm: e.dma_start(out=wt, in_=wm[k * 128:(k + 1) * 128, g * 2048:(g + 1) * 2048]),
                     writes=["p0w%d" % slot], dma="p0w%d" % slot)
                for n in range(4):
                    S.op("pe", lambda e, wt=wt, k=k, n=n: e.matmul(ps[0:3, n * 512:(n + 1) * 512], lhsT=cs3[:, k, :], rhs=wt[:, n * 512:(n + 1) * 512],
                                                                    start=(k == 0), stop=(k == 15)),
                         reads=["csT", "p0w%d" % slot], writes=["p0ps%d" % n])
            ob = osb[oi % 2]
            okey = "p0o%d" % (oi % 2)
            oi += 1
            for n in range(4):
                S.op("dve", lambda e, ob=ob, n=n, g=g: e.tensor_tensor(out=ob[0:3, n * 512:(n + 1) * 512], in0=ps[0:3, n * 512:(n + 1) * 512],
                                                                       in1=bm[0:3, g * 2048 + n * 512:g * 2048 + (n + 1) * 512], op=ALU.add),
                     reads=["p0ps%d" % n, "bm"], writes=[okey])
            S.op("sp", lambda e, ob=ob, l=l, g=g: e.dma_start(out=C.dram["mod"][l, :, g * 2048:(g + 1) * 2048], in_=ob[0:3, :]),
                 reads=[okey], writes=["mod"], dma="p0o")
    S.barrier()


def load_modvecs(C, l, b, gain, tag):
    S, A = C.S, C.A
    g = A.alloc(16)
    S.op("sp", lambda e: e.dma_start(out=g, in_=gain.rearrange("(k p) -> p k", p=128), allow_slow_non_contiguous=True),
         writes=[tag + "g"], dma=tag + "v")
    res = []
    for vi, v in enumerate((b, 2)):
        sc = A.alloc(16)
        sh = A.alloc(16)
        S.op("sp", lambda e, sc=sc, v=v: e.dma_start(out=sc, in_=C.dram["mod"][l, v, 2048:4096].rearrange("(k p) -> p k", p=128),
                                                     allow_slow_non_contiguous=True), reads=["mod"], writes=[tag + "sc%d" % vi], dma=tag + "v")
        S.op("sp", lambda e, sh=sh, v=v: e.dma_start(out=sh, in_=C.dram["mod"][l, v, 0:2048].rearrange("(k p) -> p k", p=128),
                                                     allow_slow_non_contiguous=True), reads=["mod"], writes=[tag + "sh%d" % vi], dma=tag + "v")
        S.op("dve", lambda e, sc=sc: e.scalar_tensor_tensor(out=sc, in0=sc, scalar=1.0, in1=g, op0=ALU.add, op1=ALU.mult),
             reads=[tag + "sc%d" % vi, tag + "g"], writes=[tag + "sc%d" % vi])
        res.append((sc, sh, tag + "sc%d" % vi, tag + "sh%d" % vi))
    return res


def build_xmT(C, xmT, xkey, src_tiles, mv, tag):
    S, A = C.S, C.A
    xr = [A.alloc(D) for _ in range(2)]
    xh = [A.alloc(D, BF16) for _ in range(2)]
    junk = A.alloc(D, BF16)
    st = [A.alloc(4) for _ in range(2)]
    xm3 = _v3(xmT, 16)
    for tt in range(NTT):
        sl = tt % 2
        xt, xb, s4 = xr[sl], xh[sl], st[sl]
        kx, kb_, ks = tag + "x%d" % sl, tag + "xh%d" % sl, tag + "st%d" % sl
        ms, sh, kms, ksh = mv[0] if tt < 16 else mv[1]
        S.op("sp", lambda e, xt=xt, tt=tt: e.dma_start(out=xt, in_=src_tiles[tt]), writes=[kx], dma=kx)
        S.op("act", lambda e, xt=xt, s4=s4: e.activation(out=junk, in_=xt, func=AF.Square, accum_out=s4[:, 0:1]),
             reads=[kx], writes=[tag + "junk", ks])
        S.op("act", lambda e, s4=s4: e.activation(out=s4[:, 1:2], in_=s4[:, 0:1], func=AF.Sqrt, scale=1.0 / D, bias=C.eps_t[:, 0:1]),
             reads=[ks], writes=[ks])
        S.op("dve", lambda e, s4=s4: e.reciprocal(out=s4[:, 2:3], in_=s4[:, 1:2]), reads=[ks], writes=[ks])
        S.op("dve", lambda e, xt=xt, xb=xb, s4=s4: e.tensor_scalar(out=xb, in0=xt, scalar1=s4[:, 2:3], scalar2=None, op0=ALU.mult),
             reads=[kx, ks], writes=[kb_])
        pb = C.ps[:, (tt % 2) * 1024:(tt % 2) * 1024 + 1024].bitcast(BF16)
        kp = tag + "tp%d" % (tt % 2)
        for k in range(16):
            S.op("pe", lambda e, pb=pb, xb=xb, k=k: e.transpose(out=pb[:, k * 128:(k + 1) * 128], in_=xb[:, k * 128:(k + 1) * 128], identity=C.ident),
                 reads=[kb_, "ident"], writes=[kp])
        for k in range(16):
            o_ = xm3[:, k, tt * 128:(tt + 1) * 128]
            i_ = pb[:, k * 128:(k + 1) * 128]
            if k % 2 == 0:
                S.op("act", lambda e, o_=o_, i_=i_, k=k, ms=ms, sh=sh: e.activation(out=o_, in_=i_, func=AF.Identity, bias=sh[:, k:k + 1], scale=ms[:, k:k + 1]),
                     reads=[kp, kms, ksh], writes=[xkey + str(tt)])
            else:
                S.op("dve", lambda e, o_=o_, i_=i_, k=k, ms=ms, sh=sh: e.tensor_scalar(out=o_, in0=i_, scalar1=ms[:, k:k + 1], scalar2=sh[:, k:k + 1], op0=ALU.mult, op1=ALU.add),
                     reads=[kp, kms, ksh], writes=[xkey + str(tt)])


def proj_fm(C, xm3, xkey, w3, wkey, ncols, evac, tag, m_off=0):
    S = C.S
    for sb in range((ncols + 127) // 128):
        m = min(128, ncols - sb * 128)
        for gi, (t0, tn) in enumerate(TG):
            bank = 4 + (C.pcnt % 4)
            C.pcnt += 1
            pa = C.ps[0:m, bank * 512:bank * 512 + tn]
            pk = "psb%d" % bank
            for k in range(16):
                S.op("pe", lambda e, pa=pa, k=k, sb=sb, m=m, t0=t0, tn=tn: e.matmul(pa, lhsT=w3[:, k, sb * 128:sb * 128 + m], rhs=xm3[:, k, t0:t0 + tn],
                                                                                  start=(k == 0), stop=(k == 15)),
                     reads=[wkey] + [xkey + str(t) for t in range(t0 // 128, (t0 + tn) // 128)], writes=[pk])
            evac(sb, gi, t0, tn, pa, pk, m)


def phase1(C):
    nc, S, A = C.nc, C.S, C.A
    w_in = C.dram["ab_w_in"]
    segs = [("qa", 0, 512), ("qa", 512, 512), ("ka", 1024, 512), ("ka", 1536, 512), ("va", 2048, 512), ("va", 2560, 512),
            ("cq", 3072, 512), ("ckv", 3584, 512), ("kr", 4096, 64), ("krsw", 4096, 64),
            ("z", 4160, 512), ("z", 4672, 512), ("z", 5184, 512), ("z", 5696, 512)]
    frow = {"qa": 0, "ka": 1024 - 1024, "cq": 2048 - 3072, "ckv": 2560 - 3584, "z": 3072 - 4160}
    for b in range(NB):
        _phase1_b(C, b, segs, frow)


def _phase1_b(C, b, segs, frow):
    nc, S, A = C.nc, C.S, C.A
    w_in = C.dram["ab_w_in"]
    if True:
        A.reset()
        mv = load_modvecs(C, 0, b, C.dram["ab_norm"], "p1m%d" % b)
        xmT = A.alloc(16 * T, BF16)
        xm3 = _v3(xmT, 16)
        xkey = "xmT"
        mark = A.off
        tiles = [C.dram["x"][b, tt * 128:(tt + 1) * 128, :] for tt in range(16)] + [C.dram["ctx"][b, tt * 128:(tt + 1) * 128, :] for tt in range(2)]
        build_xmT(C, xmT, xkey, tiles, mv, "p1b")
        pass
        wr = [A.alloc(16 * 512, BF16) for _ in range(2)]
        stg = [A.alloc(T, BF16) for _ in range(3)]
        vst = [A.alloc(512, BF16) for _ in range(2)]
        krp = A.alloc(T)
        kro = A.alloc(T, BF16)
        cos_t = A.alloc(SEQ)
        sin_t = A.alloc(SEQ)
        tmpf = A.alloc(512)
        tmpg = A.alloc(512)
        S.op("sp", lambda e: e.dma_start(out=cos_t[0:64, :], in_=C.dram["rope_cos"]), writes=["cos"], dma="p1c")
        S.op("sp", lambda e: e.dma_start(out=sin_t[0:64, :], in_=C.dram["rope_sin"]), writes=["sin"], dma="p1c")
        F0 = C.dram["F0"]
        VA = C.dram["VA"]
        sc = [0]
        for wi, (nm, c0, ncw) in enumerate(segs):
            slot = wi % 2
            w3 = _v3(wr[slot], 16)[:, :, 0:ncw]
            wkey = "p1w%d" % slot
            if nm == "krsw":
                for (d0, s0) in ((0, 16), (16, 0), (32, 48), (48, 32)):
                    S.op("pool", lambda e, w3=w3, d0=d0, s0=s0: e.dma_start(out=w3[:, :, d0:d0 + 16],
                                                                           in_=w_in[:, 4096 + s0:4096 + s0 + 16].rearrange("(k p) c -> p k c", p=128)),
                         writes=[wkey], dma=wkey)
            else:
                S.op("pool", lambda e, w3=w3, c0=c0, ncw=ncw: e.dma_start(out=w3, in_=w_in[:, c0:c0 + ncw].rearrange("(k p) c -> p k c", p=128)),
                     writes=[wkey], dma=wkey)
            if nm == "va":
                for tt in range(NTT):
                    bank = 4 + (C.pcnt % 4)
                    C.pcnt += 1
                    pa = C.ps[:, bank * 512:bank * 512 + 512]
                    pk = "psb%d" % bank
                    for k in range(16):
                        S.op("pe", lambda e, pa=pa, k=k, tt=tt, w3=w3: e.matmul(pa, lhsT=xm3[:, k, tt * 128:(tt + 1) * 128], rhs=w3[:, k, :], start=(k == 0), stop=(k == 15)),
                             reads=[wkey, xkey + str(tt)], writes=[pk])
                    vs = vst[tt % 2]
                    vk = "p1vs%d" % (tt % 2)
                    eng = "act" if tt % 2 == 0 else "dve"
                    if eng == "act":
                        S.op("act", lambda e, vs=vs, pa=pa: e.activation(out=vs, in_=pa, func=AF.Copy), reads=[pk], writes=[vk])
                    else:
                        S.op("dve", lambda e, vs=vs, pa=pa: e.tensor_copy(out=vs, in_=pa), reads=[pk], writes=[vk])
                    S.op("sp", lambda e, vs=vs, tt=tt, c0=c0: e.dma_start(out=VA[b, tt * 128:(tt + 1) * 128, c0 - 2048:c0 - 2048 + 512], in_=vs),
                         reads=[vk], writes=["VA"], dma=vk)
                continue

            def evac(sb, gi, t0, tn, pa, pk, m, nm=nm, c0=c0):
                if nm in ("kr", "krsw"):
                    if nm == "kr":
                        S.op("dve", lambda e: e.tensor_copy(out=krp[0:64, t0:t0 + tn], in_=pa), reads=[pk], writes=["krp"])
                    else:
                        if gi < 4:
                            S.op("dve", lambda e: e.tensor_tensor(out=tmpf[0:64, 0:tn], in0=pa, in1=sin_t[0:64, t0:t0 + tn], op=ALU.mult),
                                 reads=[pk, "sin"], writes=["tmpf"])
                            S.op("pool", lambda e: e.tensor_tensor(out=tmpg[0:64, 0:tn], in0=krp[0:64, t0:t0 + tn], in1=cos_t[0:64, t0:t0 + tn], op=ALU.mult),
                                 reads=["krp", "cos"], writes=["tmpg"])
                            S.op("dve", lambda e: e.tensor_tensor(out=kro[0:64, t0:t0 + tn], in0=tmpf[0:64, 0:tn], in1=tmpg[0:64, 0:tn], op=ALU.add),
                                 reads=["tmpf", "tmpg"], writes=["kro"])
                        if gi == 4:
                            S.op("dve", lambda e: e.tensor_copy(out=kro[0:64, t0:t0 + tn], in_=krp[0:64, t0:t0 + tn]), reads=["krp"], writes=["kro"])
                            S.op("sp", lambda e: e.dma_start(out=F0[b, 5120:5184, :], in_=kro[0:64, :]), reads=["kro"], writes=["F0"], dma="p1kr")
                    return
                if gi == 0:
                    sc[0] += 1
                si = sc[0] % 3
                sg = stg[si]
                sk = "p1st%d" % si
                func = AF.Silu if nm == "z" else AF.Copy
                if nm == "z" or (gi % 2 == 0):
                    S.op("act", lambda e: e.activation(out=sg[0:m, t0:t0 + tn], in_=pa, func=func), reads=[pk], writes=[sk])
                else:
                    S.op("dve", lambda e: e.tensor_copy(out=sg[0:m, t0:t0 + tn], in_=pa), reads=[pk], writes=[sk])
                if gi == 4:
                    r0 = c0 + sb * 128 + frow[nm]
                    S.op("sp", lambda e: e.dma_start(out=F0[b, r0:r0 + m, :], in_=sg[0:m, :]), reads=[sk], writes=["F0"], dma=sk)

            proj_fm(C, xm3, xkey, w3, wkey, ncw, evac, "p1")
        S.barrier()


def rope_tables():
    quarter = 16
    inv = (10000.0 ** (-np.arange(quarter, dtype=np.float32) / quarter)).astype(np.float32)
    pos = np.arange(SEQ)
    cos = np.zeros((64, SEQ), np.float32)
    sin = np.zeros((64, SEQ), np.float32)
    for half, p in ((0, pos // 64), (1, pos % 64)):
        ang = p.astype(np.float32)[None, :] * inv[:, None]
        c, s = np.cos(ang), np.sin(ang)
        cos[half * 32:half * 32 + 16] = c
        cos[half * 32 + 16:half * 32 + 32] = c
        sin[half * 32:half * 32 + 16] = -s
        sin[half * 32 + 16:half * 32 + 32] = s
    return cos, sin


def make_ctx(nc, es, dram):
    C = Ctx()
    C.nc = nc
    C.S = Sched(nc, es)
    C.es = es
    C.A = Arena(nc, es)
    C.ps = es.enter_context(nc.psum_tensor("ps", [128, 4096], F32))
    C.dram = dram
    C.pcnt = 0
    C.na_geo = na_geometry()
    C.cst = es.enter_context(nc.sbuf_tensor("cst", [128, 512], F32))
    C.ident = C.cst[:, 0:64].bitcast(BF16)
    C.eps_t = C.cst[:, 64:65]
    C.ones = C.cst[:, 72:136].bitcast(BF16)
    C.S.op("sp", lambda e: e.dma_start(out=C.ident, in_=dram["ident"]), writes=["ident"], dma="cst")
    C.S.op("pool", lambda e: e.memset(C.eps_t, EPS), writes=["eps"])
    C.S.op("pool", lambda e: e.memset(C.ones, 1.0), writes=["ones"])
    C.lnq = C.cst[:, 65:66]
    C.zero_t = C.cst[:, 66:67]
    C.cw = C.cst[:, 136:456]
    C.S.op("pool", lambda e: e.memset(C.lnq, float(np.log(128.0 ** -0.5))), writes=["lnq"])
    C.S.op("pool", lambda e: e.memset(C.zero_t, 0.0), writes=["zero"])
    if "dn_conv" in dram:
        cw3 = C.cw.rearrange("p (blk j) -> p blk j", j=5)
        for j in range(5):
            for q4 in range(8):
                C.S.op("sp", lambda e, j=j, q4=q4: e.dma_start(out=cw3[:, q4 * 8:(q4 + 1) * 8, j], in_=dram["dn_conv"][j, q4 * 1024:(q4 + 1) * 1024].rearrange("(blk p) -> p blk", p=128),
                                                          allow_slow_non_contiguous=True), writes=["cw"], dma="cw")
    C.S.barrier()
    return C


MLA_SCALE = 192.0 ** -0.5
NA_SCALE = 128.0 ** -0.5


def phase2(C):
    for b in range(NB):
        _phase2_b(C, b)


def _phase2_b(C, b):
    nc, S, A = C.nc, C.S, C.A
    A.reset()
    F0, M0, VM = C.dram["F0"], C.dram["M0"], C.dram["VM"]
    wqb = A.alloc(4 * 1536, BF16)
    wqb3 = _v3(wqb, 4)
    wqsw = A.alloc(4 * 512, BF16)
    wqsw4 = wqsw.rearrange("p (k h c) -> p k h c", k=4, h=8)
    wkvb = A.alloc(4 * 2048, BF16)
    wkvb3 = _v3(wkvb, 4)
    wkvb4 = wkvb.rearrange("p (k h c) -> p k h c", k=4, h=8)
    cq = A.alloc(4 * T, BF16)
    ckv = A.alloc(4 * T, BF16)
    cqn = A.alloc(4 * T, BF16)
    ckvn = A.alloc(4 * T, BF16)
    sq = [A.alloc(4 * 512, BF16) for _ in range(2)]
    lnv = [A.alloc(512) for _ in range(2)]
    rst = [A.alloc(512) for _ in range(2)]
    qnm = A.alloc(4)
    kvnm = A.alloc(4)
    cos_t = A.alloc(SEQ)
    sin_t = A.alloc(SEQ)
    stg = [A.alloc(T, BF16) for _ in range(3)]
    vst = [A.alloc(1024, BF16) for _ in range(2)]
    t1 = [A.alloc(512) for _ in range(2)]
    t2 = [A.alloc(512) for _ in range(2)]
    wq_d, wkv_d = C.dram["ab_w_qb"], C.dram["ab_w_kvb"]
    S.op("pool", lambda e: e.dma_start(out=wqb3, in_=wq_d.rearrange("(k p) c -> p k c", p=128)), writes=["wqb"], dma="p2w")
    S.op("pool", lambda e: e.dma_start(out=wkvb3, in_=wkv_d.rearrange("(k p) c -> p k c", p=128)), writes=["wkvb"], dma="p2w")
    wq4 = wq_d.rearrange("(k p) (h c) -> p k h c", p=128, h=8)
    for k in range(4):
        for (d0, s0) in ((0, 16), (16, 0), (32, 48), (48, 32)):
            S.op("pool", lambda e, k=k, d0=d0, s0=s0: e.dma_start(out=wqsw4[:, k, :, d0:d0 + 16], in_=wq4[:, k, :, 128 + s0:128 + s0 + 16]),
                 writes=["wqsw"], dma="p2w")
    S.op("sp", lambda e: e.dma_start(out=_v3(cq, 4), in_=F0[b, 2048:2560, :].rearrange("(k p) t -> p k t", p=128)), reads=["F0"], writes=["cq"], dma="p2a")
    S.op("sp", lambda e: e.dma_start(out=_v3(ckv, 4), in_=F0[b, 2560:3072, :].rearrange("(k p) t -> p k t", p=128)), reads=["F0"], writes=["ckv"], dma="p2a")
    S.op("sp", lambda e: e.dma_start(out=qnm, in_=C.dram["ab_q_norm"].rearrange("(k p) -> p k", p=128), allow_slow_non_contiguous=True), writes=["qnm"], dma="p2a")
    S.op("sp", lambda e: e.dma_start(out=kvnm, in_=C.dram["ab_kv_norm"].rearrange("(k p) -> p k", p=128), allow_slow_non_contiguous=True), writes=["kvnm"], dma="p2a")
    S.op("sp", lambda e: e.dma_start(out=cos_t[0:64, :], in_=C.dram["rope_cos"]), writes=["cos"], dma="p2a")
    S.op("sp", lambda e: e.dma_start(out=sin_t[0:64, :], in_=C.dram["rope_sin"]), writes=["sin"], dma="p2a")
    it = 0
    for (src, skey, nrm, nkey, dst, dkey) in ((cq, "cq", qnm, "qnm", cqn, "cqn"), (ckv, "ckv", kvnm, "kvnm", ckvn, "ckvn")):
        s3, d3 = _v3(src, 4), _v3(dst, 4)
        for gi, (t0, tn) in enumerate(TG):
            sl = it % 2
            it += 1
            q3 = _v3(sq[sl], 4)
            S.op("pool", lambda e, q3=q3, s3=s3, t0=t0, tn=tn: e.tensor_tensor(out=q3[:, :, 0:tn], in0=s3[:, :, t0:t0 + tn], in1=s3[:, :, t0:t0 + tn], op=ALU.mult),
                 reads=[skey], writes=["sq%d" % sl])
            bank = 4 + sl
            pa = C.ps[:, bank * 512:bank * 512 + tn]
            for k in range(4):
                S.op("pe", lambda e, pa=pa, q3=q3, k=k, tn=tn: e.matmul(pa, lhsT=C.ones, rhs=q3[:, k, 0:tn], start=(k == 0), stop=(k == 3)),
                     reads=["ones", "sq%d" % sl], writes=["psb%d" % bank])
            lv, rs = lnv[sl], rst[sl]
            S.op("act", lambda e, lv=lv, pa=pa, tn=tn: e.activation(out=lv[:, 0:tn], in_=pa, func=AF.Ln, scale=1.0 / 512, bias=C.eps_t[:, 0:1]),
                 reads=["psb%d" % bank], writes=["lnv%d" % sl])
            S.op("act", lambda e, lv=lv, rs=rs, tn=tn: e.activation(out=rs[:, 0:tn], in_=lv[:, 0:tn], func=AF.Exp, scale=-0.5),
                 reads=["lnv%d" % sl], writes=["rst%d" % sl])
            for k in range(4):
                S.op("dve", lambda e, d3=d3, s3=s3, k=k, t0=t0, tn=tn, nrm=nrm, rs=rs: e.scalar_tensor_tensor(
                    out=d3[:, k, t0:t0 + tn], in0=s3[:, k, t0:t0 + tn], scalar=nrm[:, k:k + 1], in1=rs[:, 0:tn], op0=ALU.mult, op1=ALU.mult),
                    reads=[skey, nkey, "rst%d" % sl], writes=[dkey])
    cqn3, ckvn3 = _v3(cqn, 4), _v3(ckvn, 4)
    sc = [0]

    def small_proj(lhs_fn, rhs3, rkey, wkey, m, gi, t0, tn):
        bank = 4 + (C.pcnt % 4)
        C.pcnt += 1
        pa = C.ps[0:m, bank * 512:bank * 512 + tn]
        for k in range(4):
            S.op("pe", lambda e, pa=pa, k=k: e.matmul(pa, lhsT=lhs_fn(k), rhs=rhs3[:, k, t0:t0 + tn], start=(k == 0), stop=(k == 3)),
                 reads=[wkey, rkey], writes=["psb%d" % bank])
        return pa, "psb%d" % bank

    for h in range(8):
        for (nm, lhs_fn, rhs3, rkey, wkey, row0) in (
                ("qn", lambda k, h=h: wqb3[:, k, h * 192:h * 192 + 128], cqn3, "cqn", "wqb", h * 128),
                ("kn", lambda k, h=h: wkvb3[:, k, h * 256:h * 256 + 128], ckvn3, "ckvn", "wkvb", 1536 + h * 128)):
            sc[0] += 1
            si = sc[0] % 3
            sg, sk = stg[si], "p2st%d" % si
            for gi, (t0, tn) in enumerate(TG):
                pa, pk = small_proj(lhs_fn, rhs3, rkey, wkey, 128, gi, t0, tn)
                if gi % 2 == 0:
                    S.op("act", lambda e, sg=sg, pa=pa, t0=t0, tn=tn: e.activation(out=sg[:, t0:t0 + tn], in_=pa, func=AF.Copy), reads=[pk], writes=[sk])
                else:
                    S.op("dve", lambda e, sg=sg, pa=pa, t0=t0, tn=tn: e.tensor_copy(out=sg[:, t0:t0 + tn], in_=pa), reads=[pk], writes=[sk])
            S.op("sp", lambda e, sg=sg, row0=row0: e.dma_start(out=M0[b, row0:row0 + 128, :], in_=sg), reads=[sk], writes=["M0"], dma=sk)
        sc[0] += 1
        si = sc[0] % 3
        sg, sk = stg[si], "p2st%d" % si
        for gi, (t0, tn) in enumerate(TG):
            pa, pk = small_proj(lambda k, h=h: wqb3[:, k, h * 192 + 128:h * 192 + 192], cqn3, "cqn", "wqb", 64, gi, t0, tn)
            if gi == 4:
                S.op("dve", lambda e, sg=sg, pa=pa, t0=t0, tn=tn: e.tensor_copy(out=sg[0:64, t0:t0 + tn], in_=pa), reads=[pk], writes=[sk])
                continue
            pb, pkb = small_proj(lambda k, h=h: wqsw4[:, k, h, :], cqn3, "cqn", "wqsw", 64, gi, t0, tn)
            a1, a2 = t1[gi % 2], t2[gi % 2]
            S.op("dve", lambda e, a1=a1, pa=pa, t0=t0, tn=tn: e.tensor_tensor(out=a1[0:64, 0:tn], in0=pa, in1=cos_t[0:64, t0:t0 + tn], op=ALU.mult),
                 reads=[pk, "cos"], writes=["t1%d" % (gi % 2)])
            S.op("dve", lambda e, a2=a2, pb=pb, t0=t0, tn=tn: e.tensor_tensor(out=a2[0:64, 0:tn], in0=pb, in1=sin_t[0:64, t0:t0 + tn], op=ALU.mult),
                 reads=[pkb, "sin"], writes=["t2%d" % (gi % 2)])
            S.op("pool", lambda e, a1=a1, a2=a2, sg=sg, t0=t0, tn=tn: e.tensor_tensor(out=sg[0:64, t0:t0 + tn], in0=a1[0:64, 0:tn], in1=a2[0:64, 0:tn], op=ALU.add),
                 reads=["t1%d" % (gi % 2), "t2%d" % (gi % 2)], writes=[sk])
        S.op("sp", lambda e, sg=sg, h=h: e.dma_start(out=M0[b, 1024 + h * 64:1024 + h * 64 + 64, :], in_=sg[0:64, :]), reads=[sk], writes=["M0"], dma=sk)
    for tt in range(NTT):
        vs, vk = vst[tt % 2], "p2vs%d" % (tt % 2)
        for half in range(2):
            bank = 4 + (C.pcnt % 4)
            C.pcnt += 1
            pa = C.ps[:, bank * 512:bank * 512 + 512]
            for k in range(4):
                S.op("pe", lambda e, pa=pa, k=k, tt=tt, half=half: e.matmul(pa.rearrange("p (h c) -> p h c", h=4), lhsT=ckvn3[:, k, tt * 128:(tt + 1) * 128],
                                                                         rhs=wkvb4[:, k, half * 4:half * 4 + 4, 128:256], start=(k == 0), stop=(k == 3)),
                     reads=["wkvb", "ckvn"], writes=["psb%d" % bank])
            if half == 0:
                S.op("act", lambda e, vs=vs, pa=pa: e.activation(out=vs[:, 0:512], in_=pa, func=AF.Copy), reads=["psb%d" % bank], writes=[vk])
            else:
                S.op("dve", lambda e, vs=vs, pa=pa: e.tensor_copy(out=vs[:, 512:1024], in_=pa), reads=["psb%d" % bank], writes=[vk])
        S.op("sp", lambda e, vs=vs, tt=tt: e.dma_start(out=VM[b, tt * 128:(tt + 1) * 128, :], in_=vs), reads=[vk], writes=["VM"], dma=vk)
    S.barrier()


def na_geometry():
    rows = 32
    r = np.arange(rows)
    r0 = np.clip(r - 4, 0, rows - 8)
    col = np.arange(64)
    c0 = np.clip(col - 8, 0, 64 - 16)
    tiles = {}
    per_m = []
    for m in range(16):
        lo = min(r0[2 * m], r0[2 * m + 1])
        hi = max(r0[2 * m], r0[2 * m + 1]) + 7
        lst = []
        for kb in range(lo // 2, hi // 2 + 1):
            memb = tuple(tuple(bool(r0[2 * m + bq] <= 2 * kb + a <= r0[2 * m + bq] + 7) for bq in range(2)) for a in range(2))
            key = (kb - m, memb)
            if key not in tiles:
                tiles[key] = len(tiles)
            lst.append((kb, tiles[key]))
        per_m.append(lst)
    nt = len(tiles)
    mask = np.zeros((nt, 128, 128), np.float32)
    drow = np.zeros((nt, 128, 128), np.int64)
    dcol = np.zeros((nt, 128, 128), np.int64)
    for (delta, memb), ti in tiles.items():
        for a in range(2):
            for bq in range(2):
                kc = np.arange(64)[:, None]
                qc = np.arange(64)[None, :]
                ok = memb[a][bq] & (kc >= c0[qc]) & (kc <= c0[qc] + 15)
                dr = 2 * delta + a - bq + 7
                dc = kc - qc + 15
                blk = (slice(a * 64, a * 64 + 64), slice(bq * 64, bq * 64 + 64))
                mask[ti][blk] = np.where(ok, 0.0, -20000.0)
                drow[ti][blk] = np.clip(np.where(ok, dr, 0), 0, 14)
                dcol[ti][blk] = np.clip(np.where(ok, dc, 0), 0, 30)
    return per_m, nt, mask, drow, dcol


def phase3(C):
    for b in range(NB):
        _phase3_b(C, b)


def _phase3_b(C, b):
    nc, S, A = C.nc, C.S, C.A
    A.reset()
    F0, M0, VM, VA, OG = C.dram["F0"], C.dram["M0"], C.dram["VM"], C.dram["VA"], C.dram["OG0"]
    per_m, nt, _, _, _ = C.na_geo
    krT = A.alloc(T, BF16)
    S.op("sp", lambda e: e.dma_start(out=krT[0:64, :], in_=F0[b, 5120:5184, :]), reads=["F0"], writes=["krT"], dma="p3k")
    nam = A.alloc(nt * 128)
    S.op("sp", lambda e: e.dma_start(out=_v3(nam, nt), in_=C.dram["na_mask"].rearrange("t k q -> k t q")), writes=["nam"], dma="p3k")
    hb = []
    for i in range(2):
        hb.append(dict(k=A.alloc(T, BF16), q=A.alloc(T, BF16), qr=A.alloc(T, BF16), v=A.alloc(NTT * 128, BF16), sz=A.alloc(T, BF16),
                       bias=A.alloc(nt * 128)))
    pT = [A.alloc(512, BF16) for _ in range(4)]
    sbf = [A.alloc(128) for _ in range(3)]
    rinv = [A.alloc(512) for _ in range(2)]
    tmp = [A.alloc(512) for _ in range(2)]
    ogs = [A.alloc(T, BF16) for _ in range(2)]
    cnt = dict(s=0, p=0, g=0, sb=0)

    def attend_block(lhsT_list, rhs_list, tn, reads, o_cols, first, last, vlhs, vkey, scale, bias=None, bkey=None, acc=None):
        bank = cnt["s"] % 4
        cnt["s"] += 1
        pS = C.ps[:, bank * 512:bank * 512 + tn]
        pk = "psb%d" % bank
        n = len(lhsT_list)
        for i in range(n):
            S.op("pe", lambda e, i=i: e.matmul(pS[0:128, :], lhsT=lhsT_list[i], rhs=rhs_list[i], start=(i == 0), stop=(i == n - 1)),
                 reads=reads, writes=[pk])
        slot = cnt["p"] % 4
        cnt["p"] += 1
        p_ = pT[slot][:, 0:tn]
        pkey = "pT%d" % slot
        if bias is None:
            S.op("act", lambda e: e.activation(out=p_, in_=pS, func=AF.Exp, scale=scale), reads=[pk], writes=[pkey])
        else:
            sslot = cnt["sb"] % 3
            cnt["sb"] += 1
            sb_ = sbf[sslot][:, 0:tn]
            S.op("dve", lambda e: e.scalar_tensor_tensor(out=sb_, in0=pS, scalar=scale, in1=bias, op0=ALU.mult, op1=ALU.add),
                 reads=[pk, bkey], writes=["sbf%d" % sslot])
            S.op("act", lambda e: e.activation(out=p_, in_=sb_, func=AF.Exp), reads=["sbf%d" % sslot], writes=[pkey])
        po, ps_, ok_, sk_ = acc

        def part2():
            S.op("pe", lambda e: e.matmul(po[:, o_cols[0]:o_cols[0] + tn], lhsT=vlhs, rhs=p_, start=first, stop=last), reads=[pkey, vkey], writes=[ok_])
            S.op("pe", lambda e: e.matmul(ps_[:, o_cols[0]:o_cols[0] + tn], lhsT=C.ones, rhs=p_, start=first, stop=last), reads=[pkey, "ones"], writes=[sk_])
        pend.append(part2)
        while len(pend) > 2:
            pend.pop(0)()

    pend = []

    def finish_group(acc, tn, sz, szkey, og, ogkey, t0):
        while pend:
            pend.pop(0)()
        po, ps_, ok_, sk_ = acc
        g = cnt["g"] % 2
        cnt["g"] += 1
        S.op("dve", lambda e: e.reciprocal(out=rinv[g][:, 0:tn], in_=ps_[:, 0:tn]), reads=[sk_], writes=["rinv%d" % g])
        S.op("dve", lambda e: e.tensor_tensor(out=tmp[g][:, 0:tn], in0=po[:, 0:tn], in1=rinv[g][:, 0:tn], op=ALU.mult),
             reads=[ok_, "rinv%d" % g], writes=["tmp%d" % g])
        S.op("pool", lambda e: e.tensor_tensor(out=og[:, t0:t0 + tn], in0=tmp[g][:, 0:tn], in1=sz[:, t0:t0 + tn], op=ALU.mult),
             reads=["tmp%d" % g, szkey], writes=[ogkey])

    def acc_banks():
        g = cnt["g"] % 2
        return (C.ps[:, (4 + g) * 512:(5 + g) * 512], C.ps[:, (6 + g) * 512:(7 + g) * 512], "psb%d" % (4 + g), "psb%d" % (6 + g))

    for hh in range(16):
        is_mla = hh >= 8
        h = hh % 8
        B_ = hb[hh % 2]
        hk = "hb%d" % (hh % 2)
        og, ogkey = ogs[hh % 2], "ogs%d" % (hh % 2)
        if is_mla:
            S.op("sp", lambda e, B_=B_, h=h: e.dma_start(out=B_["k"], in_=M0[b, 1536 + h * 128:1536 + (h + 1) * 128, :]), reads=["M0"], writes=[hk + "k"], dma=hk)
            S.op("sp", lambda e, B_=B_, h=h: e.dma_start(out=B_["q"], in_=M0[b, h * 128:(h + 1) * 128, :]), reads=["M0"], writes=[hk + "q"], dma=hk)
            S.op("sp", lambda e, B_=B_, h=h: e.dma_start(out=B_["qr"][0:64, :], in_=M0[b, 1024 + h * 64:1024 + (h + 1) * 64, :]), reads=["M0"], writes=[hk + "qr"], dma=hk)
            S.op("sp", lambda e, B_=B_, h=h: e.dma_start(out=_v3(B_["v"], NTT), in_=VM[b, :, h * 128:(h + 1) * 128].rearrange("(kb p) d -> p kb d", p=128)),
                 reads=["VM"], writes=[hk + "v"], dma=hk)
            S.op("sp", lambda e, B_=B_, h=h: e.dma_start(out=B_["sz"], in_=F0[b, 3072 + 1024 + h * 128:3072 + 1024 + (h + 1) * 128, :]), reads=["F0"], writes=[hk + "sz"], dma=hk)
        else:
            S.op("sp", lambda e, B_=B_, h=h: e.dma_start(out=B_["k"], in_=F0[b, 1024 + h * 128:1024 + (h + 1) * 128, :]), reads=["F0"], writes=[hk + "k"], dma=hk)
            S.op("sp", lambda e, B_=B_, h=h: e.dma_start(out=B_["q"], in_=F0[b, h * 128:(h + 1) * 128, :]), reads=["F0"], writes=[hk + "q"], dma=hk)
            S.op("sp", lambda e, B_=B_, h=h: e.dma_start(out=_v3(B_["v"], NTT), in_=VA[b, :, h * 128:(h + 1) * 128].rearrange("(kb p) d -> p kb d", p=128)),
                 reads=["VA"], writes=[hk + "v"], dma=hk)
            S.op("sp", lambda e, B_=B_, h=h: e.dma_start(out=B_["sz"], in_=F0[b, 3072 + h * 128:3072 + (h + 1) * 128, :]), reads=["F0"], writes=[hk + "sz"], dma=hk)
            S.op("sp", lambda e, B_=B_, h=h: e.dma_start(out=_v3(B_["bias"], nt), in_=C.dram["na_bias"][h].rearrange("t k q -> k t q")), writes=[hk + "b"], dma=hk)
            S.op("pool", lambda e, B_=B_: e.tensor_tensor(out=B_["bias"], in0=B_["bias"], in1=nam, op=ALU.add), reads=[hk + "b", "nam"], writes=[hk + "b"])
        k_, q_, qr_, v3_, sz_ = B_["k"], B_["q"], B_["qr"], _v3(B_["v"], NTT), B_["sz"]
        b3_ = _v3(B_["bias"], nt)
        for gi, (t0, tn) in enumerate(TG):
            acc = acc_banks()
            if is_mla:
                kbs = list(range(18)) if gi < 4 else [16, 17]
                for i, kb in enumerate(kbs):
                    attend_block([k_[:, kb * 128:(kb + 1) * 128], krT[0:64, kb * 128:(kb + 1) * 128]], [q_[:, t0:t0 + tn], qr_[0:64, t0:t0 + tn]], tn,
                                 [hk + "k", hk + "q", hk + "qr", "krT"], (0,), i == 0, i == len(kbs) - 1, v3_[:, kb, :], hk + "v", MLA_SCALE, acc=acc)
            else:
                for i, kb in enumerate((16, 17)):
                    attend_block([k_[:, kb * 128:(kb + 1) * 128]], [q_[:, t0:t0 + tn]], tn, [hk + "k", hk + "q"], (0,), i == 0, (gi == 4 and i == 1),
                                 v3_[:, kb, :], hk + "v", NA_SCALE, acc=acc)
                if gi < 4:
                    for pr in range(4):
                        m = gi * 4 + pr
                        lst = per_m[m]
                        for j, (kb, ti) in enumerate(lst):
                            attend_block([k_[:, kb * 128:(kb + 1) * 128]], [q_[:, m * 128:(m + 1) * 128]], 128, [hk + "k", hk + "q"], (pr * 128,), False,
                                         (pr == 3 and j == len(lst) - 1), v3_[:, kb, :], hk + "v", NA_SCALE, bias=b3_[:, ti, :], bkey=hk + "b", acc=acc)
            finish_group(acc, tn, sz_, hk + "sz", og, ogkey, t0)
        row0 = (1024 if is_mla else 0) + h * 128
        S.op("sp", lambda e, og=og, row0=row0: e.dma_start(out=OG[b, row0:row0 + 128, :], in_=og), reads=[ogkey], writes=["OG0"], dma=ogkey)
    S.barrier()


def out_proj(C, b, l, OGd, nk, w_out, res_tiles, dst_tiles, ntt_list, tag, cbw=512, ntok=T):
    nc, S, A = C.nc, C.S, C.A
    A.reset()
    og = A.alloc(nk * ntok, BF16)
    og3 = _v3(og, nk)
    hk = nk // 2
    S.op("sp", lambda e: e.dma_start(out=og3[:, 0:hk, :], in_=OGd[0:hk * 128, :].rearrange("(k p) t -> p k t", p=128)), reads=["DR_OG"], writes=["og"], dma=tag + "og")
    S.op("sp", lambda e: e.dma_start(out=og3[:, hk:nk, :], in_=OGd[hk * 128:nk * 128, :].rearrange("(k p) t -> p k t", p=128)), reads=["DR_OG"], writes=["og"], dma=tag + "og")
    vs = (b, 2) if len(ntt_list) > 16 else (b,)
    wr = A.alloc(nk * cbw, BF16)
    wx = [A.alloc(nk * cbw, BF16) for _ in vs]
    gt = [A.alloc(cbw) for _ in range(2)]
    xr = [A.alloc(cbw) for _ in range(3)]
    xo = [A.alloc(cbw) for _ in range(3)]
    it = 0
    for cb in range(D // cbw):
        w3 = _v3(wr, nk)
        S.op("pool", lambda e, cb=cb, w3=w3: e.dma_start(out=w3, in_=w_out[:, cb * cbw:(cb + 1) * cbw].rearrange("(k p) c -> p k c", p=128)), writes=["wr"], dma=tag + "w")
        for vi, v in enumerate(vs):
            S.op("sp", lambda e, vi=vi, v=v, cb=cb: e.dma_start(out=gt[vi], in_=C.dram["mod"][l, v, 4096 + cb * cbw:4096 + (cb + 1) * cbw].unsqueeze(0).to_broadcast([128, cbw])),
                 reads=["mod"], writes=["gt%d" % vi], dma=tag + "gt")
            S.op("dve" if vi == 0 else "pool", lambda e, vi=vi, w3=w3: e.tensor_tensor(out=_v3(wx[vi], nk), in0=w3, in1=gt[vi].unsqueeze(1).to_broadcast([128, nk, cbw]), op=ALU.mult),
                 reads=["wr", "gt%d" % vi], writes=["wx%d" % vi])
        for tt in ntt_list:
            vi = 0 if tt < 16 else 1
            wv = _v3(wx[vi], nk)
            bank = 4 + (C.pcnt % 4)
            C.pcnt += 1
            pa = C.ps[:, bank * 512:bank * 512 + cbw]
            for k in range(nk):
                S.op("pe", lambda e, pa=pa, k=k, tt=tt, wv=wv: e.matmul(pa, lhsT=og3[:, k, tt * 128:(tt + 1) * 128], rhs=wv[:, k, :], start=(k == 0), stop=(k == nk - 1)),
                     reads=["og", "wx%d" % vi], writes=["psb%d" % bank])
            sl = it % 3
            it += 1
            S.op("sp", lambda e, sl=sl, tt=tt, cb=cb: e.dma_start(out=xr[sl], in_=res_tiles(tt, cb)), reads=["DR_res"], writes=["xr%d" % sl], dma=tag + "xr%d" % sl)
            S.op("dve", lambda e, sl=sl, pa=pa: e.tensor_tensor(out=xo[sl], in0=pa, in1=xr[sl], op=ALU.add), reads=["psb%d" % bank, "xr%d" % sl], writes=["xo%d" % sl])
            S.op("sp", lambda e, sl=sl, tt=tt, cb=cb: e.dma_start(out=dst_tiles(tt, cb), in_=xo[sl]), reads=["xo%d" % sl], writes=["DR_dst"], dma=tag + "xo%d" % sl)
    S.barrier()


def phase4(C):
    for b in range(NB):
        def res(tt, cb, b=b):
            if tt < 16:
                return C.dram["x"][b, tt * 128:(tt + 1) * 128, cb * 512:(cb + 1) * 512]
            return C.dram["ctx"][b, (tt - 16) * 128:(tt - 15) * 128, cb * 512:(cb + 1) * 512]

        def dst(tt, cb, b=b):
            return C.dram["X1"][b, tt * 128:(tt + 1) * 128, cb * 512:(cb + 1) * 512]
        out_proj(C, b, 0, C.dram["OG0"][b], 16, C.dram["ab_w_out"], res, dst, list(range(NTT)), "p4")


HP = 2052 + 260


def phase5(C):
    for b in range(NB):
        _phase5_b(C, b)


def _phase5_b(C, b):
    nc, S, A = C.nc, C.S, C.A
    A.reset()
    w_in = C.dram["dn_w_in"]
    X1, F1, BA = C.dram["X1"], C.dram["F1"], C.dram["BA"]
    mv = load_modvecs(C, 1, b, C.dram["dn_norm"], "p5m%d" % b)
    xmT = A.alloc(16 * T, BF16)
    xm3 = _v3(xmT, 16)
    xkey = "xmT"
    mark = A.off
    tiles = [X1[b, tt * 128:(tt + 1) * 128, :] for tt in range(NTT)]
    build_xmT(C, xmT, xkey, tiles, mv, "p5b")
    S.barrier()
    A.off = mark
    wr = [A.alloc(16 * 512, BF16) for _ in range(2)]
    stg = [A.alloc(T, BF16) for _ in range(3)]
    hp = [A.alloc(HP, BF16) for _ in range(2)]
    dg = [A.alloc(5 * 128, BF16) for _ in range(2)]
    sT = [A.alloc(512, BF16) for _ in range(3)]
    sq = [A.alloc(512, BF16) for _ in range(2)]
    lnv = [A.alloc(512) for _ in range(2)]
    rst = [A.alloc(512) for _ in range(2)]
    bast = A.alloc(NTT * 128)
    cw3 = C.cw.rearrange("p (blk j) -> p blk j", j=5)
    for i in range(2):
        S.op("pool", lambda e, i=i: e.memset(hp[i], 0.0), writes=["hp%d" % i])
    segs = [("q", i * 512, 512) for i in range(4)] + [("k", 2048 + i * 512, 512) for i in range(4)] + [("v", 4096 + i * 512, 512) for i in range(8)] + \
           [("z", 8192 + i * 512, 512) for i in range(8)] + [("ba", 12288, 128)]
    cnt = dict(st=0, hp=0, c=0, n=0, s=0)
    for wi, (nm, c0, ncw) in enumerate(segs):
        slot = wi % 2
        w3 = _v3(wr[slot], 16)[:, :, 0:ncw]
        wkey = "p5w%d" % slot
        S.op("pool", lambda e, w3=w3, c0=c0, ncw=ncw: e.dma_start(out=w3, in_=w_in[:, c0:c0 + ncw].rearrange("(k p) c -> p k c", p=128)),
             writes=[wkey], dma=wkey)
        if nm == "ba":
            b3 = _v3(bast, NTT)
            for tt in range(NTT):
                bank = 4 + (C.pcnt % 4)
                C.pcnt += 1
                pa = C.ps[:, bank * 512:bank * 512 + 128]
                for k in range(16):
                    S.op("pe", lambda e, pa=pa, k=k, tt=tt, w3=w3: e.matmul(pa, lhsT=xm3[:, k, tt * 128:(tt + 1) * 128], rhs=w3[:, k, :], start=(k == 0), stop=(k == 15)),
                         reads=[wkey, xkey + str(tt)], writes=["psb%d" % bank])
                S.op("dve", lambda e, pa=pa, tt=tt: e.tensor_copy(out=b3[:, tt, :], in_=pa), reads=["psb%d" % bank], writes=["bast"])
            S.op("sp", lambda e: e.dma_start(out=BA[b].rearrange("(tt p) c -> p tt c", p=128), in_=b3), reads=["bast"], writes=["BA"], dma="bast")
            continue

        def evac(sb, gi, t0, tn, pa, pk, m, nm=nm, c0=c0):
            if nm == "z":
                if gi == 0:
                    cnt["st"] += 1
                si = cnt["st"] % 3
                sg, sk = stg[si], "p5st%d" % si
                S.op("act", lambda e: e.activation(out=sg[:, t0:t0 + tn], in_=pa, func=AF.Silu), reads=[pk], writes=[sk])
                if gi == 4:
                    r0 = c0 + sb * 128
                    S.op("sp", lambda e: e.dma_start(out=F1[b, r0:r0 + 128, :], in_=sg), reads=[sk], writes=["F1"], dma=sk)
                return
            if gi == 0:
                cnt["hp"] += 1
            hi = cnt["hp"] % 2
            hb_, hk_ = hp[hi], "hp%d" % hi
            off = 2 + t0 if gi < 4 else 2052 + 2
            if gi % 2 == 0:
                S.op("act", lambda e: e.activation(out=hb_[:, off:off + tn], in_=pa, func=AF.Copy), reads=[pk], writes=[hk_])
            else:
                S.op("dve", lambda e: e.tensor_copy(out=hb_[:, off:off + tn], in_=pa), reads=[pk], writes=[hk_])
            if gi < 4:
                return
            blk = (c0 + sb * 128) // 128
            di = cnt["hp"] % 2
            d3 = _v3(dg[di], 5)
            for j in range(5):
                S.op("pool", lambda e, j=j: e.tensor_scalar(out=d3[:, j, :], in0=C.ident, scalar1=cw3[:, blk, j:j + 1], scalar2=None, op0=ALU.mult),
                     reads=["ident", "cw"], writes=["dg%d" % di])
            cnt["st"] += 1
            si = cnt["st"] % 3
            sg, sk = stg[si], "p5st%d" % si
            outs = []
            for g2, (u0, un) in enumerate(TG):
                bank = cnt["c"] % 4
                cnt["c"] += 1
                pc = C.ps[:, bank * 512:bank * 512 + un]
                base = u0 if g2 < 4 else 2052
                for j in range(5):
                    S.op("pe", lambda e, pc=pc, j=j, base=base, un=un: e.matmul(pc, lhsT=d3[:, j, :], rhs=hb_[:, base + j:base + j + un], start=(j == 0), stop=(j == 4)),
                         reads=["dg%d" % di, hk_], writes=["psb%d" % bank])
                if nm == "v":
                    S.op("act", lambda e, pc=pc, u0=u0, un=un: e.activation(out=sg[:, u0:u0 + un], in_=pc, func=AF.Silu), reads=["psb%d" % bank], writes=[sk])
                else:
                    ssl = cnt["s"] % 3
                    cnt["s"] += 1
                    s_ = sT[ssl]
                    S.op("act", lambda e, pc=pc, s_=s_, un=un: e.activation(out=s_[:, 0:un], in_=pc, func=AF.Silu), reads=["psb%d" % bank], writes=["sT%d" % ssl])
                    outs.append((s_, "sT%d" % ssl, u0, un))
                    if len(outs) == 3 or g2 == 4:
                        pend = []
                        for (s2, s2k, v0, vn) in outs:
                            nsl = cnt["n"] % 2
                            cnt["n"] += 1
                            S.op("pool", lambda e, s2=s2, vn=vn, nsl=nsl: e.tensor_tensor(out=sq[nsl][:, 0:vn], in0=s2[:, 0:vn], in1=s2[:, 0:vn], op=ALU.mult),
                                 reads=[s2k], writes=["sq%d" % nsl])
                            bank2 = cnt["c"] % 4
                            cnt["c"] += 1
                            p3 = C.ps[:, bank2 * 512:bank2 * 512 + vn]
                            S.op("pe", lambda e, p3=p3, nsl=nsl, vn=vn: e.matmul(p3, lhsT=C.ones, rhs=sq[nsl][:, 0:vn], start=True, stop=True),
                                 reads=["ones", "sq%d" % nsl], writes=["psb%d" % bank2])
                            S.op("act", lambda e, p3=p3, nsl=nsl, vn=vn: e.activation(out=lnv[nsl][:, 0:vn], in_=p3, func=AF.Ln, bias=C.eps_t[:, 0:1]),
                                 reads=["psb%d" % bank2], writes=["lnv%d" % nsl])
                            pend.append((s2, s2k, v0, vn, nsl))
                            if len(pend) == 2 or (s2 is outs[-1][0]):
                                for (s3_, s3k, w0, wn, ns2) in pend:
                                    bias_ap = C.lnq[:, 0:1] if nm == "q" else C.zero_t[:, 0:1]
                                    S.op("act", lambda e, ns2=ns2, wn=wn, bias_ap=bias_ap: e.activation(out=rst[ns2][:, 0:wn], in_=lnv[ns2][:, 0:wn], func=AF.Exp, scale=-0.5, bias=bias_ap),
                                         reads=["lnv%d" % ns2], writes=["rst%d" % ns2])
                                    S.op("dve", lambda e, s3_=s3_, w0=w0, wn=wn, ns2=ns2: e.tensor_tensor(out=sg[:, w0:w0 + wn], in0=s3_[:, 0:wn], in1=rst[ns2][:, 0:wn], op=ALU.mult),
                                         reads=[s3k, "rst%d" % ns2], writes=[sk])
                                pend = []
                        outs = []
            r0 = c0 + sb * 128
            S.op("sp", lambda e: e.dma_start(out=F1[b, r0:r0 + 128, :], in_=sg), reads=[sk], writes=["F1"], dma=sk)

        proj_fm(C, xm3, xkey, w3, wkey, ncw, evac, "p5")
    S.barrier()


FSEQ = [16, 17] + list(range(16))
BSEQ = [17, 16] + list(range(15, -1, -1))


def dn_level_masks():
    s_ = np.arange(128)[:, None]
    c_ = np.arange(128)[None, :]
    out = np.zeros((128, 7, 4, 2, 128), np.float32)
    for k in range(1, 8):
        h = 1 << (k - 1)
        same = (s_ // (2 * h)) == (c_ // (2 * h))
        ur = same & ((s_ % (2 * h)) < h) & ((c_ % (2 * h)) >= h)
        ll = ur.T
        for j in range(4):
            fwd = j < 2
            out[:, k - 1, j, 0, :] = -(ur if fwd else ll).astype(np.float32)
            out[:, k - 1, j, 1, :] = -(ll if fwd else ur).astype(np.float32)
    id8 = np.zeros((128, 4, 2, 128), np.float32)
    id8[:, :, :, :] = np.eye(128, dtype=np.float32)[:, None, None, :]
    return out.reshape(128, 7, 1024), id8.reshape(128, 1024)


def dn_level_masks2():
    lm, _ = dn_level_masks()
    lm = lm.reshape(128, 7, 4, 2, 128).copy()
    lm -= np.eye(128, dtype=np.float32)[:, None, None, None, :]
    return np.ascontiguousarray(lm[:, :, 0::2, :, :]).reshape(128, 7, 512)


def dn_masks():
    s = np.arange(128)[:, None]
    c = np.arange(128)[None, :]
    incl = np.stack([(s <= c), (s <= c), (s >= c), (s >= c)], 0).astype(np.float32)
    strict = np.stack([(s < c), (s < c), (s > c), (s > c)], 0).astype(np.float32)
    ident4 = np.stack([np.eye(128, dtype=np.float32)] * 4, 0)
    return incl.transpose(1, 0, 2).copy(), strict.transpose(1, 0, 2).copy(), ident4.transpose(1, 0, 2).copy()


def phase6(C):
    for b in getattr(C, "p6_batches", range(NB)):
        _phase6_b(C, b)


def dump(C, name, ap, readkeys):
    if not getattr(C, "debug", False):
        return
    t = C.nc.dram_tensor("dbg_" + name, list(ap.shape), ap.dtype, kind="ExternalOutput").ap()
    if ap.shape[1] * (4 if ap.dtype == F32 else 2) > 2048:
        C.S.op("sp", lambda e: e.dma_start(out=t, in_=ap), reads=readkeys, writes=["DR_dbg"], dma=1)
        return
    if not hasattr(C, "dbg_stage"):
        C.dbg_stage = C.es.enter_context(C.nc.sbuf_tensor("dbgst", [128, 512], F32))
    st = C.dbg_stage[:, 0:ap.shape[1]] if ap.dtype == F32 else C.dbg_stage[:, 0:(ap.shape[1] + 1) // 2].bitcast(BF16)[:, 0:ap.shape[1]]
    C.S.op("dve", lambda e: e.tensor_copy(out=st, in_=ap), reads=readkeys, writes=["dbgst"])
    C.S.op("sp", lambda e: e.dma_start(out=t, in_=st), reads=["dbgst"], writes=["DR_dbg"], dma=1)


def _phase6_b(C, b):
    nc, S, A = C.nc, C.S, C.A
    A.reset()
    F1, BA, OG1 = C.dram["F1"], C.dram["BA"], C.dram["OG1"]
    mincl = A.alloc(512)
    mstr = A.alloc(512)
    ones_f = A.alloc(128)
    onorm = A.alloc(1)
    S.op("sp", lambda e: e.dma_start(out=_v3(mincl, 4), in_=C.dram["dn_mincl"]), writes=["mincl"], dma=1)
    S.op("sp", lambda e: e.dma_start(out=_v3(mstr, 4), in_=C.dram["dn_mstrict"]), writes=["mstr"], dma=1)
    S.op("sp", lambda e: e.dma_start(out=onorm, in_=C.dram["dn_o_norm"].rearrange("(p o) -> p o", o=1), allow_slow_non_contiguous=True), writes=["onorm"], dma=1)
    S.op("pool", lambda e: e.memset(ones_f, 1.0), writes=["ones_f"])
    mincl3, mstr3 = _v3(mincl, 4), _v3(mstr, 4)
    lmask = A.alloc(7 * 512, BF16)
    lm4 = lmask.rearrange("p (k d x) -> p k d x", k=7, d=2)
    S.op("sp", lambda e: e.dma_start(out=_v3(lmask, 7), in_=C.dram["dn_lmask2"]), writes=["lmask"], dma=1)
    beta = A.alloc(NTT * 64)
    gg = A.alloc(NTT * 64)
    mark6 = A.off
    ba = A.alloc(NTT * 128)
    ba3 = _v3(ba, NTT)
    S.op("sp", lambda e: e.dma_start(out=ba3, in_=BA[b].rearrange("(tt p) c -> p tt c", p=128)), reads=["BA"], writes=["ba"], dma=1)
    tA = A.alloc(NTT * 64)
    tB = A.alloc(NTT * 64)
    alog = A.alloc(64)
    dtb = A.alloc(64)
    one_t = A.alloc(1)
    S.op("pool", lambda e: e.memset(one_t, 1.0), writes=["one_t"])
    S.op("sp", lambda e: e.dma_start(out=alog, in_=C.dram["dn_a_log"].rearrange("d h -> (d h)").unsqueeze(0).to_broadcast([128, 64])), writes=["alog"], dma=1)
    S.op("sp", lambda e: e.dma_start(out=dtb, in_=C.dram["dn_dt_bias"].rearrange("d h -> (d h)").unsqueeze(0).to_broadcast([128, 64])), writes=["dtb"], dma=1)
    beta3, gg3, tA3, tB3 = _v3(beta, NTT), _v3(gg, NTT), _v3(tA, NTT), _v3(tB, NTT)
    S.op("act", lambda e: e.activation(out=tA3, in_=ba3[:, :, 0:64], func=AF.Exp, scale=-1.0), reads=["ba"], writes=["tA"])
    S.op("dve", lambda e: e.tensor_scalar(out=tA, in0=tA, scalar1=1.0, scalar2=None, op0=ALU.add), reads=["tA"], writes=["tA"])
    S.op("dve", lambda e: e.reciprocal(out=beta, in_=tA), reads=["tA"], writes=["beta"])
    S.op("dve", lambda e: e.tensor_tensor(out=tB3, in0=ba3[:, :, 64:128], in1=dtb.unsqueeze(1).to_broadcast([128, NTT, 64]), op=ALU.add), reads=["ba", "dtb"], writes=["tB"])
    S.op("dve", lambda e: e.scalar_tensor_tensor(out=tA, in0=tB, scalar=-1.0, in1=tB, op0=ALU.mult, op1=ALU.max), reads=["tB", "beta"], writes=["tA"])
    S.op("act", lambda e: e.activation(out=tA, in_=tA, func=AF.Exp, scale=-1.0), reads=["tA"], writes=["tA"])
    S.op("act", lambda e: e.activation(out=tA, in_=tA, func=AF.Ln, bias=one_t[:, 0:1]), reads=["tA", "one_t"], writes=["tA"])
    S.op("dve", lambda e: e.scalar_tensor_tensor(out=tB, in0=tB, scalar=0.0, in1=tA, op0=ALU.max, op1=ALU.add), reads=["tA", "tB"], writes=["tB"])
    S.op("act", lambda e: e.activation(out=alog, in_=alog, func=AF.Exp), reads=["alog"], writes=["alog"])
    S.op("dve", lambda e: e.scalar_tensor_tensor(out=gg3, in0=tB3, scalar=-1.0, in1=alog.unsqueeze(1).to_broadcast([128, NTT, 64]), op0=ALU.mult, op1=ALU.mult),
         reads=["tB", "alog"], writes=["gg"])
    S.barrier()
    A.off = mark6
    SHARED = {"gg", "beta", "mincl", "mstr", "lmask", "ident", "ones", "onorm", "ones_f", "eps", "zero", "lnq"}
    S0 = S

    class _SlotSched:
        def __init__(self, si):
            self.si = si

        def op(self, eng, fn, reads=(), writes=(), dma=None):
            f = lambda k: k if (k in SHARED or k in Sched.DRAMKEYS or k.startswith("DR_")) else "s%d_%s" % (self.si, k)
            return S0.op(eng, fn, reads=[f(k) for k in reads], writes=[f(k) for k in writes], dma=dma)

    def run_slot(si, head_list):
        S = _SlotSched(si)
        hbufs = [dict(q=A.alloc(T, BF16), k=A.alloc(T, BF16), v=A.alloc(2 * T, BF16))]
        ktok = A.alloc(NTT * 128, BF16)
        vtok = A.alloc(NTT * 256, BF16)
        oacc = A.alloc(2 * SEQ)
        o3 = _v3(oacc, 2)
        szb1 = A.alloc(SEQ, BF16)
        def mk():
            dec_ = A.alloc(512)
            tmp_ = A.alloc(512)
            tmp2_ = A.alloc(512)
            return dict(grep=A.alloc(512), d1=dec_, dec=dec_, gam=A.alloc(512), bm=A.alloc(512), tmp=tmp_, tmp2=tmp2_, t3=tmp_, t4=tmp2_,
                        x12=A.alloc(12), e12=A.alloc(12), negb=A.alloc(4), xn=A.alloc(1024, BF16), rt=A.alloc(1024, BF16), yy=A.alloc(1024, BF16),
                        xb=dec_, intra=A.alloc(512, BF16), gq=A.alloc(512, BF16), kd=A.alloc(512, BF16), vd=A.alloc(512, BF16), vn=A.alloc(512, BF16))
        stp = [mk()]
        S4 = A.alloc(512)
        S4b = A.alloc(512, BF16)
        sqb = [A.alloc(512, BF16)] * 2
        lnv = [A.alloc(512)] * 2
        rst = lnv
        osum = [A.alloc(512) for _ in range(2)]
        ps = C.ps
        pbase = si * 2048
        kG = kS1 = "psb%d" % (pbase // 512)
        kAB = kS2 = "psb%d" % (pbase // 512 + 1)
        kM = kT = kC = kN = "psM%d" % (pbase // 512)
        psG = ps[:, pbase:pbase + 512]
        psG3 = _v3(psG, 4)
        psS1 = psG
        psA = ps[:, pbase + 512:pbase + 768]
        psB = ps[:, pbase + 768:pbase + 1024]
        psS2 = ps[:, pbase + 512:pbase + 1024]
        psM = ps[:, pbase + 1024:pbase + 2048]
        psM3 = _v3(psM, 4)
        psT = ps[:, pbase + 1024:pbase + 1536]
        psTb = psT.bitcast(BF16)
        psC = ps[:, pbase + 1536:pbase + 1540]
        psN = psT

        for g in head_list:
            H = hbufs[0]
            hk = "h6"
            S.op("sp", lambda e, H=H, g=g: e.dma_start(out=H["q"], in_=F1[b, g * 128:(g + 1) * 128, :]), reads=["F1"], writes=[hk + "q"], dma=1)
            S.op("sp", lambda e, H=H, g=g: e.dma_start(out=H["k"], in_=F1[b, 2048 + g * 128:2048 + (g + 1) * 128, :]), reads=["F1"], writes=[hk + "k"], dma=1)
            S.op("sp", lambda e, H=H, g=g: e.dma_start(out=_v3(H["v"], 2), in_=F1[b, 4096 + 2 * g * 128:4096 + (2 * g + 2) * 128, :].rearrange("(v p) t -> p v t", p=128)),
                 reads=["F1"], writes=[hk + "v"], dma=1)
            QT, KT, VT3 = H["q"], H["k"], _v3(H["v"], 2)
            kt3 = _v3(ktok, NTT)
            vt4 = vtok.rearrange("p (t v d) -> p t v d", t=NTT, v=2)
            jobs = [("k", tt, 0) for tt in range(NTT)] + [("v", tt, vh) for tt in range(NTT) for vh in range(2)]
            groups = [jobs[0:8], jobs[8:16], jobs[16:18]] + [jobs[18 + i:18 + i + 8] for i in range(0, 36, 8)]
            for j0, grp in enumerate(groups):
                yield
                j0 = j0 * 8
                for i, (kind, tt, vh) in enumerate(grp):
                    src = KT[:, tt * 128:(tt + 1) * 128] if kind == "k" else VT3[:, vh, tt * 128:(tt + 1) * 128]
                    S.op("pe", lambda e, i=i, src=src: e.transpose(out=psTb[:, i * 128:(i + 1) * 128], in_=src, identity=C.ident),
                         reads=[hk + "k", hk + "v", "ident"], writes=[kT])
                kind0, tt0, vh0 = grp[0]
                n = len(grp)
                if kind0 == "k":
                    dst = ktok[:, tt0 * 128:(tt0 + n) * 128]
                    dk_ = "ktok"
                else:
                    dst = vtok[:, (tt0 * 2 + vh0) * 128:(tt0 * 2 + vh0 + n) * 128]
                    dk_ = "vtok"
                if (j0 // 8) % 2 == 0:
                    S.op("act", lambda e, dst=dst, n=n: e.activation(out=dst, in_=psTb[:, 0:n * 128], func=AF.Copy), reads=[kT], writes=[dk_])
                else:
                    S.op("dve", lambda e, dst=dst, n=n: e.tensor_copy(out=dst, in_=psTb[:, 0:n * 128]), reads=[kT], writes=[dk_])
            S.op("pool", lambda e: e.memset(S4, 0.0), writes=["S4"])
            S.op("pool", lambda e: e.memset(S4b, 0.0), writes=["S4b"])
            S43, S4b3 = _v3(S4, 4), _v3(S4b, 4)
            c0 = 2 * g
            def step(s, part, g=g, H=H, hk=hk, QT=QT, KT=KT, VT3=VT3, kt3=kt3, vt4=vt4, c0=c0, S43=S43, S4b3=S4b3):
                P = stp[0]
                pk = "st0"
                blks = (FSEQ[s], BSEQ[s])
                cols = [(d * 32 + c0) for d in range(2)]
                grep3, d13, dec3, gam3, bm3, tmp3, tmp23 = [_v3(P[n_], 4) for n_ in ("grep", "d1", "dec", "gam", "bm", "tmp", "tmp2")]
                xb3, intra3, gq3, kd3, vd3, vn3, t33, t43 = [_v3(P[n_], 4) for n_ in ("xb", "intra", "gq", "kd", "vd", "vn", "t3", "t4")]
                x12, e12, negb = P["x12"], P["e12"], P["negb"]
                RT = P["rt"].rearrange("p (j o c) -> p j o c", j=4, o=2)
                krt = pk + "rt"
                xn4 = P["xn"].rearrange("p (j o c) -> p j o c", j=4, o=2)
                kxn = pk + "xn"
                if part == "A":
                    yield
                    for d in range(2):
                        gs = gg3[:, blks[d], cols[d]:cols[d] + 2]
                        S.op("pool", lambda e, d=d, gs=gs: e.tensor_copy(out=grep3[:, 2 * d:2 * d + 2, :], in_=gs.unsqueeze(2).to_broadcast([128, 2, 128])),
                             reads=["gg"], writes=[pk + "grep"])
                        S.op("dve", lambda e, d=d: e.tensor_scalar(out=negb[:, 2 * d:2 * d + 2], in0=beta3[:, blks[d], cols[d]:cols[d] + 2], scalar1=-1.0, scalar2=None, op0=ALU.mult),
                             reads=["beta"], writes=[pk + "negb"])
                        S.op("pool", lambda e, d=d: e.tensor_tensor(out=bm3[:, 2 * d:2 * d + 2, :], in0=mstr3[:, 2 * d:2 * d + 2, :],
                                                                    in1=beta3[:, blks[d], cols[d]:cols[d] + 2].unsqueeze(2).to_broadcast([128, 2, 128]), op=ALU.mult),
                             reads=["beta", "mstr"], writes=[pk + "bm"])
                    yield
                    for j in range(4):
                        d = j // 2
                        S.op("pe", lambda e, j=j, d=d: e.matmul(psG3[:, j, :], lhsT=grep3[:, j, :], rhs=mincl3[:, 2 * d, :], start=True, stop=True),
                             reads=[pk + "grep", "mincl"], writes=[kG])
                    yield
                    for d in range(2):
                        S.op("pe", lambda e, d=d: e.matmul(psC[:, 2 * d:2 * d + 2], lhsT=mincl3[:, 2 * d, :], rhs=gg3[:, blks[d], cols[d]:cols[d] + 2], start=True, stop=True),
                             reads=["gg", "mincl"], writes=[kC])
                    yield
                    for d in range(2):
                        kb_ = KT[:, blks[d] * 128:(blks[d] + 1) * 128]
                        qb_ = QT[:, blks[d] * 128:(blks[d] + 1) * 128]
                        S.op("pe", lambda e, d=d, kb_=kb_: e.matmul(psA[:, d * 128:(d + 1) * 128], lhsT=kb_, rhs=kb_, start=True, stop=True), reads=[hk + "k"], writes=[kAB])
                        S.op("pe", lambda e, d=d, kb_=kb_, qb_=qb_: e.matmul(psB[:, d * 128:(d + 1) * 128], lhsT=kb_, rhs=qb_, start=True, stop=True), reads=[hk + "k", hk + "q"], writes=[kAB])
                    yield
                    S.op("dve", lambda e: e.tensor_copy(out=x12[:, 0:4], in_=psC), reads=[kC], writes=[pk + "x12"])
                    yield
                    for d in range(2):
                        last = 127 if d == 0 else 0
                        S.op("dve", lambda e, d=d, last=last: e.tensor_copy(out=x12[:, 8 + 2 * d:10 + 2 * d], in_=psG3[:, 2 * d:2 * d + 2, last]), reads=[kG], writes=[pk + "x12"])
                    yield
                    S.op("dve", lambda e: e.tensor_tensor(out=x12[:, 4:8], in0=x12[:, 8:12], in1=x12[:, 0:4], op=ALU.subtract), reads=[pk + "x12"], writes=[pk + "x12"])
                    yield
                    S.op("act", lambda e: e.activation(out=e12, in_=x12, func=AF.Exp), reads=[pk + "x12"], writes=[pk + "e12"])
                    yield
                    S.op("dve", lambda e: e.tensor_tensor(out=d13, in0=psG3, in1=x12[:, 0:4].unsqueeze(2).to_broadcast([128, 4, 128]), op=ALU.subtract),
                         reads=[kG, pk + "x12"], writes=[pk + "dec"])
                    yield
                    S.op("pool", lambda e: e.tensor_scalar(out=P["d1"], in0=P["d1"], scalar1=0.0, scalar2=-80.0, op0=ALU.min, op1=ALU.max), reads=[pk + "dec"], writes=[pk + "dec"])
                    yield
                    S.op("act", lambda e: e.activation(out=P["dec"], in_=P["d1"], func=AF.Exp), reads=[pk + "dec"], writes=[pk + "dec"])
                    yield
                    S.op("act", lambda e: e.activation(out=P["gam"], in_=psG, func=AF.Exp), reads=[kG], writes=[pk + "gam"])
                    dec4 = P["dec"].rearrange("p (d v c) -> p d v c", d=2, v=2)
                    psA4 = psA.rearrange("p (d c) -> p d c", d=2).unsqueeze(2).to_broadcast([128, 2, 2, 128])
                    psB4 = psB.rearrange("p (d c) -> p d c", d=2).unsqueeze(2).to_broadcast([128, 2, 2, 128])
                    yield
                    S.op("dve", lambda e, psA4=psA4, dec4=dec4: e.tensor_tensor(out=P["tmp"].rearrange("p (d v c) -> p d v c", d=2, v=2), in0=psA4, in1=dec4, op=ALU.mult),
                         reads=[kAB, pk + "dec"], writes=[pk + "tmp"])
                    yield
                    S.op("pool", lambda e: e.tensor_tensor(out=xn4[:, :, 0, :], in0=tmp3, in1=bm3, op=ALU.mult), reads=[pk + "tmp", pk + "bm"], writes=[kxn])
                    S.op("pool", lambda e: e.tensor_tensor(out=xn4[:, :, 0, :], in0=xn4[:, :, 0, :], in1=C.ident.unsqueeze(1).to_broadcast([128, 4, 128]), op=ALU.subtract),
                         reads=[kxn, "ident"], writes=[kxn])
                    yield
                    S.op("dve", lambda e, psB4=psB4, dec4=dec4: e.tensor_tensor(out=P["tmp2"].rearrange("p (d v c) -> p d v c", d=2, v=2), in0=psB4, in1=dec4, op=ALU.mult),
                         reads=[kAB, pk + "dec"], writes=[pk + "tmp2"])
                    yield
                    S.op("pool", lambda e: e.tensor_tensor(out=intra3, in0=tmp23, in1=mincl3, op=ALU.mult), reads=[pk + "tmp2", "mincl"], writes=[pk + "intra"])
                    yield
                    for d in range(2):
                        qb_ = QT[:, blks[d] * 128:(blks[d] + 1) * 128]
                        S.op("pool", lambda e, d=d, qb_=qb_: e.tensor_tensor(out=gq3[:, 2 * d:2 * d + 2, :], in0=qb_.unsqueeze(1).to_broadcast([128, 2, 128]), in1=gam3[:, 2 * d:2 * d + 2, :], op=ALU.mult),
                             reads=[hk + "q", pk + "gam"], writes=[pk + "gq"])
                        S.op("pool", lambda e, d=d: e.tensor_tensor(out=kd3[:, 2 * d:2 * d + 2, :], in0=kt3[:, blks[d], :].unsqueeze(1).to_broadcast([128, 2, 128]),
                                                                    in1=e12[:, 4 + 2 * d:6 + 2 * d].unsqueeze(2).to_broadcast([128, 2, 128]), op=ALU.mult),
                             reads=["ktok", pk + "e12"], writes=[pk + "kd"])
                    xn4 = P["xn"].rearrange("p (j o c) -> p j o c", j=4, o=2)
                    rt4 = P["rt"].rearrange("p (j o c) -> p j o c", j=4, o=2)
                    yy4 = P["yy"].rearrange("p (j o c) -> p j o c", j=4, o=2)
                    psM4 = psM.rearrange("p (j o c) -> p j o c", j=4, o=2)
                    kxn, krt_, kyy = pk + "xn", pk + "rt", pk + "yy"
                    yield
                    pass
                    yield
                    for j in range(4):
                        S.op("pe", lambda e, j=j: e.transpose(out=psTb[:, j * 128:(j + 1) * 128], in_=xn4[:, j, 0, :], identity=C.ident), reads=[kxn, "ident"], writes=[kT])
                    yield
                    S.op("act", lambda e: e.activation(out=xn4[:, :, 1, :], in_=_v3(psTb[:, 0:512], 4), func=AF.Copy), reads=[kT], writes=[kxn])
                    def lmv(lv):
                        return lm4[:, lv, :, :].unsqueeze(2).to_broadcast([128, 2, 2, 256])
                    v4 = lambda ap: ap.rearrange("p (d v x) -> p d v x", d=2, v=2)
                    yield
                    S.op("dve", lambda e: e.tensor_tensor(out=v4(P["rt"]), in0=v4(P["xn"]), in1=lmv(0), op=ALU.mult), reads=[kxn, "lmask"], writes=[krt_])
                    for lv in range(1, 7):
                        yield
                        for j in range(4):
                            S.op("pe", lambda e, j=j: e.matmul(psM4[:, j, 0, :], lhsT=xn4[:, j, 1, :], rhs=rt4[:, j, 0, :], start=True, stop=True), reads=[kxn, krt_], writes=[kM])
                            S.op("pe", lambda e, j=j: e.matmul(psM4[:, j, 1, :], lhsT=xn4[:, j, 0, :], rhs=rt4[:, j, 1, :], start=True, stop=True), reads=[kxn, krt_], writes=[kM])
                        yield
                        S.op("dve", lambda e, lv=lv: e.tensor_tensor(out=v4(P["yy"]), in0=v4(psM), in1=lmv(lv), op=ALU.mult), reads=[kM, "lmask"], writes=[kyy])
                        yield
                        for j in range(4):
                            S.op("pe", lambda e, j=j: e.matmul(psM4[:, j, 0, :], lhsT=rt4[:, j, 1, :], rhs=yy4[:, j, 0, :], start=True, stop=True), reads=[kyy, krt_], writes=[kM])
                            if lv < 6:
                                S.op("pe", lambda e, j=j: e.matmul(psM4[:, j, 1, :], lhsT=rt4[:, j, 0, :], rhs=yy4[:, j, 1, :], start=True, stop=True), reads=[kyy, krt_], writes=[kM])
                        yield
                        if lv < 6:
                            S.op("act", lambda e: e.activation(out=P["rt"], in_=psM, func=AF.Copy), reads=[kM], writes=[krt_])
                        else:
                            S.op("act", lambda e: e.activation(out=rt4[:, :, 0, :], in_=psM4[:, :, 0, :], func=AF.Copy), reads=[kM], writes=[krt_])
                    RT = rt4
                    krt = krt_
                    return
                yield
                for j in range(4):
                    d = j // 2
                    kb_ = KT[:, blks[d] * 128:(blks[d] + 1) * 128]
                    S.op("pe", lambda e, j=j, kb_=kb_: e.matmul(psS1[:, j * 128:(j + 1) * 128], lhsT=kb_, rhs=S4b3[:, j, :], start=True, stop=True), reads=[hk + "k", "S4b"], writes=[kS1])
                yield
                S.op("dve", lambda e: e.tensor_tensor(out=t33, in0=_v3(psS1, 4), in1=e12[:, 0:4].unsqueeze(2).to_broadcast([128, 4, 128]), op=ALU.mult),
                     reads=[kS1, pk + "e12"], writes=[pk + "tmp"])
                yield
                for d in range(2):
                    S.op("pool" if d == 0 else "dve", lambda e, d=d: e.tensor_tensor(out=vd3[:, 2 * d:2 * d + 2, :], in0=t33[:, 2 * d:2 * d + 2, :], in1=vt4[:, blks[d], :, :], op=ALU.subtract),
                         reads=[pk + "tmp", "vtok"], writes=[pk + "vd"])
                yield
                for j in range(4):
                    S.op("pe", lambda e, j=j: e.matmul(psS2[:, j * 128:(j + 1) * 128], lhsT=RT[:, j, 0, :], rhs=vd3[:, j, :], start=True, stop=True), reads=[krt, pk + "vd"], writes=[kS2])
                yield
                S.op("dve", lambda e: e.tensor_tensor(out=vn3, in0=_v3(psS2, 4), in1=negb.unsqueeze(2).to_broadcast([128, 4, 128]), op=ALU.mult),
                     reads=[kS2, pk + "negb"], writes=[pk + "vn"])
                if s >= 2:
                    for j in range(4):
                        S.op("pe", lambda e, j=j: e.matmul(psS1[:, j * 128:(j + 1) * 128], lhsT=S4b3[:, j, :], rhs=gq3[:, j, :], start=True, stop=False), reads=["S4b", pk + "gq"], writes=[kS1])
                        S.op("pe", lambda e, j=j: e.matmul(psS1[:, j * 128:(j + 1) * 128], lhsT=vn3[:, j, :], rhs=intra3[:, j, :], start=False, stop=True), reads=[pk + "vn", pk + "intra"], writes=[kS1])
                    for d in range(2):
                        dstv = o3[:, :, blks[d] * 128:(blks[d] + 1) * 128]
                        srcv = _v3(psS1[:, d * 256:(d + 1) * 256], 2)
                        if s <= 9:
                            S.op("act", lambda e, dstv=dstv, srcv=srcv: e.activation(out=dstv, in_=srcv, func=AF.Copy), reads=[kS1], writes=["oacc"])
                        else:
                            S.op("dve", lambda e, dstv=dstv, srcv=srcv: e.tensor_tensor(out=dstv, in0=srcv, in1=dstv, op=ALU.add), reads=[kS1, "oacc"], writes=["oacc"])
                if s < NTT - 1:
                    for j in range(4):
                        S.op("pe", lambda e, j=j: e.matmul(psS2[:, j * 128:(j + 1) * 128], lhsT=kd3[:, j, :], rhs=vn3[:, j, :], start=True, stop=True), reads=[pk + "kd", pk + "vn"], writes=[kS2])
                    S.op("pool", lambda e: e.tensor_tensor(out=t43, in0=S43, in1=e12[:, 8:12].unsqueeze(2).to_broadcast([128, 4, 128]), op=ALU.mult),
                         reads=["S4", pk + "e12"], writes=[pk + "tmp2"])
                    S.op("dve", lambda e: e.tensor_tensor(out=S4, in0=psS2, in1=P["t4"], op=ALU.add), reads=[kS2, pk + "tmp2"], writes=["S4"])
                    S.op("act", lambda e: e.activation(out=S4b, in_=S4, func=AF.Copy), reads=["S4"], writes=["S4b"])
            nst_ = getattr(C, "p6_nsteps", NTT)
            for s_ in range(nst_):
                yield from step(s_, "A")
                yield from step(s_, "S")
            for vh in range(2):
                S.op("sp", lambda e, vh=vh, g=g: e.dma_start(out=szb1, in_=F1[b, 8192 + (2 * g + vh) * 128:8192 + (2 * g + vh + 1) * 128, 0:SEQ]),
                     reads=["F1"], writes=["szb1"], dma=1)
                for gi in range(4):
                    yield
                    t0 = gi * 512
                    sl = gi % 2
                    a_ = o3[:, vh, t0:t0 + 512]
                    S.op("pool", lambda e, a_=a_, sl=sl: e.tensor_tensor(out=sqb[sl], in0=a_, in1=a_, op=ALU.mult), reads=["oacc"], writes=["sqb6"])
                    S.op("pe", lambda e, sl=sl: e.matmul(psS1, lhsT=C.ones, rhs=sqb[sl], start=True, stop=True), reads=["ones", "sqb6"], writes=[kS1])
                    S.op("act", lambda e, sl=sl: e.activation(out=lnv[sl], in_=psS1, func=AF.Ln, scale=1.0 / 128, bias=C.eps_t[:, 0:1]), reads=[kS1], writes=["lnv6"])
                    S.op("act", lambda e, sl=sl: e.activation(out=rst[sl], in_=lnv[sl], func=AF.Exp, scale=-0.5), reads=["lnv6"], writes=["lnv6"])
                    S.op("dve", lambda e, sl=sl, a_=a_: e.scalar_tensor_tensor(out=osum[sl], in0=a_, scalar=onorm[:, 0:1], in1=rst[sl], op0=ALU.mult, op1=ALU.mult),
                         reads=["oacc", "lnv6", "onorm"], writes=["osum%d" % sl])
                    S.op("pool", lambda e, sl=sl, t0=t0: e.tensor_tensor(out=szb1[:, t0:t0 + 512], in0=osum[sl], in1=szb1[:, t0:t0 + 512], op=ALU.mult),
                         reads=["osum%d" % sl, "szb1"], writes=["szb1"])
                r0 = (2 * g + vh) * 128
                S.op("sp", lambda e, r0=r0: e.dma_start(out=OG1[b, r0:r0 + 128, :], in_=szb1), reads=["szb1"], writes=["OG1"], dma=1)

    heads_all = list(getattr(C, "p6_heads", range(16)))
    gens = [run_slot(0, heads_all[0::2]), run_slot(1, heads_all[1::2])]
    while gens:
        for g_ in list(gens):
            try:
                next(g_)
            except StopIteration:
                gens.remove(g_)
    S.barrier()


def phase7(C):
    nc, S, A = C.nc, C.S, C.A
    for b in range(NB):
        def res(tt, cb, b=b):
            return C.dram["X1"][b, tt * 128:(tt + 1) * 128, cb * 256:(cb + 1) * 256]

        def dst(tt, cb, b=b):
            return C.dram["X2"][b, tt * 128:(tt + 1) * 128, cb * 256:(cb + 1) * 256]
        out_proj(C, b, 1, C.dram["OG1"][b], 32, C.dram["dn_w_out"], res, dst, list(range(16)), "p7", cbw=256, ntok=SEQ)
    A.reset()
    fn = A.alloc(D)
    S.op("sp", lambda e: e.dma_start(out=fn, in_=C.dram["final_norm"].unsqueeze(0).to_broadcast([128, D])), writes=["fn"], dma=1)
    xr = [A.alloc(D) for _ in range(3)]
    xo = [A.alloc(D) for _ in range(3)]
    junk = A.alloc(D, BF16)
    st = [A.alloc(4) for _ in range(3)]
    it = 0
    for b in range(NB):
        for tt in range(16):
            sl = it % 3
            it += 1
            xt, xo_, s4 = xr[sl], xo[sl], st[sl]
            S.op("sp", lambda e, xt=xt, b=b, tt=tt: e.dma_start(out=xt, in_=C.dram["X2"][b, tt * 128:(tt + 1) * 128, :]), reads=["X2"], writes=["fxr%d" % sl], dma=1)
            S.op("act", lambda e, xt=xt, s4=s4: e.activation(out=junk, in_=xt, func=AF.Square, accum_out=s4[:, 0:1]), reads=["fxr%d" % sl], writes=["fjunk", "fst%d" % sl])
            S.op("act", lambda e, s4=s4: e.activation(out=s4[:, 1:2], in_=s4[:, 0:1], func=AF.Sqrt, scale=1.0 / D, bias=C.eps_t[:, 0:1]), reads=["fst%d" % sl], writes=["fst%d" % sl])
            S.op("dve", lambda e, s4=s4: e.reciprocal(out=s4[:, 2:3], in_=s4[:, 1:2]), reads=["fst%d" % sl], writes=["fst%d" % sl])
            S.op("dve", lambda e, xt=xt, xo_=xo_, s4=s4: e.scalar_tensor_tensor(out=xo_, in0=xt, scalar=s4[:, 2:3], in1=fn, op0=ALU.mult, op1=ALU.mult),
                 reads=["fxr%d" % sl, "fst%d" % sl, "fn"], writes=["fxo%d" % sl])
            S.op("sp", lambda e, xo_=xo_, b=b, tt=tt: e.dma_start(out=C.dram["OUT"][b, tt * 128:(tt + 1) * 128, :], in_=xo_), reads=["fxo%d" % sl], writes=["OUT"], dma=1)
    S.barrier()


NCORES = 8
_PHASES = (phase0, phase1, phase2, phase3, phase4, phase5, phase6, phase7)


def _host_inputs(inp):
    cos, sin = rope_tables()
    per_m, nt, mask, drow, dcol = na_geometry()
    rpb = np.asarray(inp["ab_rpb"][0], np.float32)
    nab = np.stack([rpb[h][drow, dcol] for h in range(8)], 0).astype(np.float32)
    mi, ms, id4 = dn_masks()
    lm, id8 = dn_level_masks()
    f = lambda a: np.ascontiguousarray(np.asarray(a, np.float32))
    shared = {
        "w_mod0": f(inp["ab_w_mod"][0]), "w_mod1": f(inp["dn_w_mod"][0]), "b_mod0": f(inp["ab_b_mod"][0]), "b_mod1": f(inp["dn_b_mod"][0]),
        "ab_norm": f(inp["ab_norm"][0]), "ab_w_in": f(inp["ab_w_in"][0]), "ab_w_qb": f(inp["ab_w_qb"][0]), "ab_w_kvb": f(inp["ab_w_kvb"][0]),
        "ab_q_norm": f(inp["ab_q_norm"][0]), "ab_kv_norm": f(inp["ab_kv_norm"][0]), "ab_w_out": f(inp["ab_w_out"][0]),
        "na_mask": mask, "na_bias": nab, "rope_cos": cos, "rope_sin": sin, "ident": np.eye(128, dtype=np.float32).astype(NPBF),
        "dn_norm": f(inp["dn_norm"][0]), "dn_w_in": f(inp["dn_w_in"][0]), "dn_conv": f(inp["dn_conv"][0]), "dn_a_log": f(inp["dn_a_log"][0]),
        "dn_dt_bias": f(inp["dn_dt_bias"][0]), "dn_o_norm": f(inp["dn_o_norm"][0]), "dn_w_out": f(inp["dn_w_out"][0]), "final_norm": f(inp["final_norm"]),
        "dn_mincl": mi, "dn_mstrict": ms, "dn_ident4": id4.astype(NPBF), "dn_lmask2": dn_level_masks2().astype(NPBF),
    }
    maps = []
    for i in range(NCORES):
        m = dict(shared)
        m["x"] = f(inp["x"][NB * i:NB * (i + 1)])
        m["ctx"] = f(inp["ctx"][NB * i:NB * (i + 1)])
        m["cvec"] = np.concatenate([f(inp["c"][NB * i:NB * (i + 1)]), f(inp["c_ctx"])[None]], 0)
        maps.append(m)
    return maps


_INTERNAL = {
    "mod": ([2, 3, 6144], F32), "F0": ([NB, 5184, T], BF16), "VA": ([NB, T, 1024], BF16), "M0": ([NB, 2560, T], BF16), "VM": ([NB, T, 1024], BF16),
    "OG0": ([NB, 2048, T], BF16), "X1": ([NB, T, D], F32), "F1": ([NB, 12288, T], BF16), "BA": ([NB, T, 128], F32), "OG1": ([NB, 4096, SEQ], BF16),
    "X2": ([NB, SEQ, D], F32),
}


def build_program(maps0, phases=_PHASES):
    nc = bass.Bass("TRN2", target_bir_lowering=False)
    with ExitStack() as es:
        dram = {}
        for nm, a in maps0.items():
            dram[nm] = nc.dram_tensor(nm, list(a.shape), BF16 if a.dtype == NPBF else F32, kind="ExternalInput").ap()
        for nm, (shape, dt_) in _INTERNAL.items():
            dram[nm] = nc.dram_tensor(nm, shape, dt_, kind="Internal").ap()
        dram["OUT"] = nc.dram_tensor("OUT", [NB, SEQ, D], F32, kind="ExternalOutput").ap()
        C = make_ctx(nc, es, dram)
        for p in phases:
            p(C)
        C.S.finalize()
    return nc


def kernel(**inputs):
    maps = _host_inputs(inputs)
    nc = build_program(maps[0])
    res = run_bass_kernel_spmd(nc, maps, core_ids=list(range(NCORES)))
    out = np.concatenate([np.asarray(r["OUT"], np.float32) for r in res.results], axis=0)
    return out
```

```python
import numpy as np
import ml_dtypes
from contextlib import ExitStack
import concourse.bass as bass
import concourse.mybir as mybir
from concourse.bass_utils import run_bass_kernel_spmd

F32 = mybir.dt.float32
BF16 = mybir.dt.bfloat16
AF = mybir.ActivationFunctionType
ALU = mybir.AluOpType
NPBF = ml_dtypes.bfloat16

D = 2048
SEQ = 2048
CTX = 256
T = SEQ + CTX
NTT = T // 128
NB = 2
EPS = 1e-6
TG = [(0, 512), (512, 512), (1024, 512), (1536, 512), (2048, 256)]


class Op:
    __slots__ = ("eng", "fn", "deps", "marked", "val", "sem", "is_dma")


class Buf:
    __slots__ = ("w", "r")

    def __init__(self):
        self.w = None
        self.r = []


class Sched:
    ENGS = ("pe", "act", "dve", "pool", "sp")
    ENGOBJ = {"pe": "tensor", "act": "scalar", "dve": "vector", "pool": "gpsimd", "sp": "sync"}

    def __init__(self, nc, es):
        self.nc = nc
        self.es = es
        self.ops = {e: [] for e in self.ENGS}
        self.bufs = {}
        self.sems = {e: es.enter_context(nc.semaphore("s_" + e)) for e in self.ENGS}
        self.dsems = {}
        self.dpool = []
        self.last_dma = {}
        self.nops = 0

    DRAMKEYS = {"mod", "F0", "VA", "M0", "VM", "OG0", "X1", "X2", "F1", "BA", "OG1", "OUT", "ST"}

    def dsem(self, key):
        if key not in self.dsems:
            i = len(self.dsems)
            if i >= len(self.dpool):
                self.dpool.append([self.es.enter_context(self.nc.semaphore("d_%d" % i)), 0])
            self.dsems[key] = self.dpool[i]
        return self.dsems[key]

    def op(self, eng, fn, reads=(), writes=(), dma=None):
        o = Op()
        o.eng = eng
        o.fn = fn
        o.deps = []
        o.marked = False
        o.val = None
        o.sem = None
        o.is_dma = dma is not None
        self.nops += 1
        if dma is not None:
            dk = None
            for k in list(writes) + list(reads):
                if not (k in self.DRAMKEYS or k.startswith("DR_")):
                    dk = k
                    break
            assert dk is not None, (reads, writes)
            d = self.dsem(dk)
            d[1] += 16
            o.sem = d[0]
            o.val = d[1]
            o.marked = True
            self.last_dma[id(d)] = o
        deps = {}
        for k in reads:
            b = self.bufs.get(k)
            if b is None:
                b = self.bufs[k] = Buf()
            if b.w is not None:
                deps[id(b.w)] = b.w
        for k in writes:
            b = self.bufs.get(k)
            if b is None:
                b = self.bufs[k] = Buf()
            if b.w is not None:
                deps[id(b.w)] = b.w
            for r in b.r:
                deps[id(r)] = r
        for k in reads:
            self.bufs[k].r.append(o)
        for k in writes:
            b = self.bufs[k]
            b.w = o
            b.r = []
        for d in deps.values():
            if d is o:
                continue
            if d.eng == "pe" and eng == "pe" and not d.is_dma:
                continue
            d.marked = True
            o.deps.append(d)
        self.ops[eng].append(o)
        return o

    def barrier(self):
        lasts = []
        for e in self.ENGS:
            for o in reversed(self.ops[e]):
                if not o.is_dma and o.fn is not None:
                    o.marked = True
                    lasts.append(o)
                    break
        lasts += list(self.last_dma.values())
        for e in self.ENGS:
            o = Op()
            o.eng = e
            o.fn = None
            o.deps = list(lasts)
            o.marked = False
            o.val = None
            o.sem = None
            o.is_dma = False
            self.ops[e].append(o)
        self.bufs = {}
        self.dsems = {}

    def finalize(self):
        for e in self.ENGS:
            c = 0
            for o in self.ops[e]:
                if o.is_dma or o.fn is None:
                    continue
                if o.marked:
                    c += 1
                    o.val = c
                    o.sem = self.sems[e]
        nc = self.nc
        with nc.Block() as block:
            for e in self.ENGS:
                ops = self.ops[e]

                def body(engine, ops=ops, e=e):
                    seen = {}
                    for o in ops:
                        for d in o.deps:
                            k = id(d.sem)
                            if seen.get(k, 0) >= d.val:
                                continue
                            seen[k] = d.val
                            engine.wait_ge(d.sem, d.val)
                        if o.fn is None:
                            continue
                        ins = o.fn(engine)
                        if o.is_dma:
                            ins.then_inc(o.sem, 16)
                        elif o.marked:
                            ins.then_inc(o.sem, 1)
                    if e == "sp":
                        for (s, v) in self.dpool:
                            if v > 0:
                                engine.wait_ge(s, v)

                getattr(block, self.ENGOBJ[e])(body)


class Arena:
    def __init__(self, nc, es, nwords=52000):
        self.t = es.enter_context(nc.sbuf_tensor("arena", [128, nwords], F32))
        self.n = nwords
        self.off = 0
        self.uid = 0

    def reset(self):
        self.off = 0

    def alloc(self, nelem, dtype=F32):
        nbytes = nelem * (4 if dtype == F32 else 2)
        words = (nbytes + 31) // 32 * 8
        assert self.off + words <= self.n, "SBUF arena overflow %d+%d" % (self.off, words)
        ap = self.t[:, self.off:self.off + words]
        self.off += words
        if dtype != F32:
            ap = ap.bitcast(dtype)
        return ap[:, 0:nelem]

    def key(self, name):
        self.uid += 1
        return "%s#%d" % (name, self.uid)


class Ctx:
    pass


def _v3(ap, a):
    return ap.rearrange("p (a b) -> p a b", a=a)


def phase0(C):
    nc, S, A = C.nc, C.S, C.A
    A.reset()
    csT = A.alloc(48)
    cs3 = _v3(csT, 16)
    bm = A.alloc(6144)
    osb = [A.alloc(2048), A.alloc(2048)]
    wr = [A.alloc(2048) for _ in range(4)]
    ps = C.ps
    for v in range(3):
        S.op("sp", lambda e, v=v: e.dma_start(out=cs3[:, :, v], in_=C.dram["cvec"][v, :].rearrange("(k p) -> p k", p=128),
                                              allow_slow_non_contiguous=True), writes=["csT"], dma="p0c")
    S.op("act", lambda e: e.activation(out=csT, in_=csT, func=AF.Silu), reads=["csT"], writes=["csT"])
    it = 0
    oi = 0
    for l in range(2):
        wm = C.dram["w_mod%d" % l]
        bmod = C.dram["b_mod%d" % l]
        S.op("sp", lambda e, bmod=bmod: e.dma_start(out=bm[0:3, :], in_=bmod.unsqueeze(0).to_broadcast([3, 6144])),
             writes=["bm"], dma="p0b")
        for g in range(3):
            for k in range(16):
                slot = it % 4
                it += 1
                wt = wr[slot]
                S.op("sp", lambda e, wt=wt, k=k, g=g, wm=wm: e.dma_start(out=wt, in_=wm[k * 128:(k + 1) * 128, g * 2048:(g + 1) * 2048]),
                     writes=["p0w%d" % slot], dma="p0w%d" % slot)
                for n in range(4):
                    S.op("pe", lambda e, wt=wt, k=k, n=n: e.matmul(ps[0:3, n * 512:(n + 1) * 512], lhsT=cs3[:, k, :], rhs=wt[:, n * 512:(n + 1) * 512],
                                                                    start=(k == 0), stop=(k == 15)),
                         reads=["csT", "p0w%d" % slot], writes=["p0ps%d" % n])
            ob = osb[oi % 2]
            okey = "p0o%d" % (oi % 2)
            oi += 1
            for n in range(4):
                S.op("dve", lambda e, ob=ob, n=n, g=g: e.tensor_tensor(out=ob[0:3, n * 512:(n + 1) * 512], in0=ps[0:3, n * 512:(n + 1) * 512],
                                                                       in1=bm[0:3, g * 2048 + n * 512:g * 2048 + (n + 1) * 512], op=ALU.add),
                     reads=["p0ps%d" % n, "bm"], writes=[okey])
            S.op("sp", lambda e, ob=ob, l=l, g=g: e.dma_start(out=C.dram["mod"][l, :, g * 2048:(g + 1) * 2048], in_=ob[0:3, :]),
                 reads=[okey], writes=["mod"], dma="p0o")
    S.barrier()


def load_modvecs(C, l, b, gain, tag):
    S, A = C.S, C.A
    g = A.alloc(16)
    S.op("sp", lambda e: e.dma_start(out=g, in_=gain.rearrange("(k p) -> p k", p=128), allow_slow_non_contiguous=True),
         writes=[tag + "g"], dma=tag + "v")
    res = []
    for vi, v in enumerate((b, 2)):
        sc = A.alloc(16)
        sh = A.alloc(16)
        S.op("sp", lambda e, sc=sc, v=v: e.dma_start(out=sc, in_=C.dram["mod"][l, v, 2048:4096].rearrange("(k p) -> p k", p=128),
                                                     allow_slow_non_contiguous=True), reads=["mod"], writes=[tag + "sc%d" % vi], dma=tag + "v")
        S.op("sp", lambda e, sh=sh, v=v: e.dma_start(out=sh, in_=C.dram["mod"][l, v, 0:2048].rearrange("(k p) -> p k", p=128),
                                                     allow_slow_non_contiguous=True), reads=["mod"], writes=[tag + "sh%d" % vi], dma=tag + "v")
        S.op("dve", lambda e, sc=sc: e.scalar_tensor_tensor(out=sc, in0=sc, scalar=1.0, in1=g, op0=ALU.add, op1=ALU.mult),
             reads=[tag + "sc%d" % vi, tag + "g"], writes=[tag + "sc%d" % vi])
        res.append((sc, sh, tag + "sc%d" % vi, tag + "sh%d" % vi))
    return res


def build_xmT(C, xmT, xkey, src_tiles, mv, tag):
    S, A = C.S, C.A
    xr = [A.alloc(D) for _ in range(2)]
    xh = [A.alloc(D, BF16) for _ in range(2)]
    junk = A.alloc(D, BF16)
    st = [A.alloc(4) for _ in range(2)]
    xm3 = _v3(xmT, 16)
    for tt in range(NTT):
        sl = tt % 2
        xt, xb, s4 = xr[sl], xh[sl], st[sl]
        kx, kb_, ks = tag + "x%d" % sl, tag + "xh%d" % sl, tag + "st%d" % sl
        ms, sh, kms, ksh = mv[0] if tt < 16 else mv[1]
        S.op("sp", lambda e, xt=xt, tt=tt: e.dma_start(out=xt, in_=src_tiles[tt]), writes=[kx], dma=kx)
        S.op("act", lambda e, xt=xt, s4=s4: e.activation(out=junk, in_=xt, func=AF.Square, accum_out=s4[:, 0:1]),
             reads=[kx], writes=[tag + "junk", ks])
        S.op("act", lambda e, s4=s4: e.activation(out=s4[:, 1:2], in_=s4[:, 0:1], func=AF.Sqrt, scale=1.0 / D, bias=C.eps_t[:, 0:1]),
             reads=[ks], writes=[ks])
        S.op("dve", lambda e, s4=s4: e.reciprocal(out=s4[:, 2:3], in_=s4[:, 1:2]), reads=[ks], writes=[ks])
        S.op("dve", lambda e, xt=xt, xb=xb, s4=s4: e.tensor_scalar(out=xb, in0=xt, scalar1=s4[:, 2:3], scalar2=None, op0=ALU.mult),
             reads=[kx, ks], writes=[kb_])
        pb = C.ps[:, (tt % 2) * 1024:(tt % 2) * 1024 + 1024].bitcast(BF16)
        kp = tag + "tp%d" % (tt % 2)
        for k in range(16):
            S.op("pe", lambda e, pb=pb, xb=xb, k=k: e.transpose(out=pb[:, k * 128:(k + 1) * 128], in_=xb[:, k * 128:(k + 1) * 128], identity=C.ident),
                 reads=[kb_, "ident"], writes=[kp])
        for k in range(16):
            o_ = xm3[:, k, tt * 128:(tt + 1) * 128]
            i_ = pb[:, k * 128:(k + 1) * 128]
            if k % 2 == 0:
                S.op("act", lambda e, o_=o_, i_=i_, k=k, ms=ms, sh=sh: e.activation(out=o_, in_=i_, func=AF.Identity, bias=sh[:, k:k + 1], scale=ms[:, k:k + 1]),
                     reads=[kp, kms, ksh], writes=[xkey + str(tt)])
            else:
                S.op("dve", lambda e, o_=o_, i_=i_, k=k, ms=ms, sh=sh: e.tensor_scalar(out=o_, in0=i_, scalar1=ms[:, k:k + 1], scalar2=sh[:, k:k + 1], op0=ALU.mult, op1=ALU.add),
                     reads=[kp, kms, ksh], writes=[xkey + str(tt)])


def proj_fm(C, xm3, xkey, w3, wkey, ncols, evac, tag, m_off=0):
    S = C.S
    for sb in range((ncols + 127) // 128):
        m = min(128, ncols - sb * 128)
        for gi, (t0, tn) in enumerate(TG):
            bank = 4 + (C.pcnt % 4)
            C.pcnt += 1
            pa = C.ps[0:m, bank * 512:bank * 512 + tn]
            pk = "psb%d" % bank
            for k in range(16):
                S.op("pe", lambda e, pa=pa, k=k, sb=sb, m=m, t0=t0, tn=tn: e.matmul(pa, lhsT=w3[:, k, sb * 128:sb * 128 + m], rhs=xm3[:, k, t0:t0 + tn],
                                                                                  start=(k == 0), stop=(k == 15)),
                     reads=[wkey] + [xkey + str(t) for t in range(t0 // 128, (t0 + tn) // 128)], writes=[pk])
            evac(sb, gi, t0, tn, pa, pk, m)


def phase1(C):
    nc, S, A = C.nc, C.S, C.A
    w_in = C.dram["ab_w_in"]
    segs = [("qa", 0, 512), ("qa", 512, 512), ("ka", 1024, 512), ("ka", 1536, 512), ("va", 2048, 512), ("va", 2560, 512),
            ("cq", 3072, 512), ("ckv", 3584, 512), ("kr", 4096, 64), ("krsw", 4096, 64),
            ("z", 4160, 512), ("z", 4672, 512), ("z", 5184, 512), ("z", 5696, 512)]
    frow = {"qa": 0, "ka": 1024 - 1024, "cq": 2048 - 3072, "ckv": 2560 - 3584, "z": 3072 - 4160}
    for b in range(NB):
        _phase1_b(C, b, segs, frow)


def _phase1_b(C, b, segs, frow):
    nc, S, A = C.nc, C.S, C.A
    w_in = C.dram["ab_w_in"]
    if True:
        A.reset()
        mv = load_modvecs(C, 0, b, C.dram["ab_norm"], "p1m%d" % b)
        xmT = A.alloc(16 * T, BF16)
        xm3 = _v3(xmT, 16)
        xkey = "xmT"
        mark = A.off
        tiles = [C.dram["x"][b, tt * 128:(tt + 1) * 128, :] for tt in range(16)] + [C.dram["ctx"][b, tt * 128:(tt + 1) * 128, :] for tt in range(2)]
        build_xmT(C, xmT, xkey, tiles, mv, "p1b")
        pass
        wr = [A.alloc(16 * 512, BF16) for _ in range(2)]
        stg = [A.alloc(T, BF16) for _ in range(3)]
        vst = [A.alloc(512, BF16) for _ in range(2)]
        krp = A.alloc(T)
        kro = A.alloc(T, BF16)
        cos_t = A.alloc(SEQ)
        sin_t = A.alloc(SEQ)
        tmpf = A.alloc(512)
        tmpg = A.alloc(512)
        S.op("sp", lambda e: e.dma_start(out=cos_t[0:64, :], in_=C.dram["rope_cos"]), writes=["cos"], dma="p1c")
        S.op("sp", lambda e: e.dma_start(out=sin_t[0:64, :], in_=C.dram["rope_sin"]), writes=["sin"], dma="p1c")
        F0 = C.dram["F0"]
        VA = C.dram["VA"]
        sc = [0]
        for wi, (nm, c0, ncw) in enumerate(segs):
            slot = wi % 2
            w3 = _v3(wr[slot], 16)[:, :, 0:ncw]
            wkey = "p1w%d" % slot
            if nm == "krsw":
                for (d0, s0) in ((0, 16), (16, 0), (32, 48), (48, 32)):
                    S.op("pool", lambda e, w3=w3, d0=d0, s0=s0: e.dma_start(out=w3[:, :, d0:d0 + 16],
                                                                           in_=w_in[:, 4096 + s0:4096 + s0 + 16].rearrange("(k p) c -> p k c", p=128)),
                         writes=[wkey], dma=wkey)
            else:
                S.op("pool", lambda e, w3=w3, c0=c0, ncw=ncw: e.dma_start(out=w3, in_=w_in[:, c0:c0 + ncw].rearrange("(k p) c -> p k c", p=128)),
                     writes=[wkey], dma=wkey)
            if nm == "va":
                for tt in range(NTT):
                    bank = 4 + (C.pcnt % 4)
                    C.pcnt += 1
                    pa = C.ps[:, bank * 512:bank * 512 + 512]
                    pk = "psb%d" % bank
                    for k in range(16):
                        S.op("pe", lambda e, pa=pa, k=k, tt=tt, w3=w3: e.matmul(pa, lhsT=xm3[:, k, tt * 128:(tt + 1) * 128], rhs=w3[:, k, :], start=(k == 0), stop=(k == 15)),
                             reads=[wkey, xkey + str(tt)], writes=[pk])
                    vs = vst[tt % 2]
                    vk = "p1vs%d" % (tt % 2)
                    eng = "act" if tt % 2 == 0 else "dve"
                    if eng == "act":
                        S.op("act", lambda e, vs=vs, pa=pa: e.activation(out=vs, in_=pa, func=AF.Copy), reads=[pk], writes=[vk])
                    else:
                        S.op("dve", lambda e, vs=vs, pa=pa: e.tensor_copy(out=vs, in_=pa), reads=[pk], writes=[vk])
                    S.op("sp", lambda e, vs=vs, tt=tt, c0=c0: e.dma_start(out=VA[b, tt * 128:(tt + 1) * 128, c0 - 2048:c0 - 2048 + 512], in_=vs),
                         reads=[vk], writes=["VA"], dma=vk)
                continue

            def evac(sb, gi, t0, tn, pa, pk, m, nm=nm, c0=c0):
                if nm in ("kr", "krsw"):
                    if nm == "kr":
                        S.op("dve", lambda e: e.tensor_copy(out=krp[0:64, t0:t0 + tn], in_=pa), reads=[pk], writes=["krp"])
                    else:
                        if gi < 4:
                            S.op("dve", lambda e: e.tensor_tensor(out=tmpf[0:64, 0:tn], in0=pa, in1=sin_t[0:64, t0:t0 + tn], op=ALU.mult),
                                 reads=[pk, "sin"], writes=["tmpf"])
                            S.op("pool", lambda e: e.tensor_tensor(out=tmpg[0:64, 0:tn], in0=krp[0:64, t0:t0 + tn], in1=cos_t[0:64, t0:t0 + tn], op=ALU.mult),
                                 reads=["krp", "cos"], writes=["tmpg"])
                            S.op("dve", lambda e: e.tensor_tensor(out=kro[0:64, t0:t0 + tn], in0=tmpf[0:64, 0:tn], in1=tmpg[0:64, 0:tn], op=ALU.add),
                                 reads=["tmpf", "tmpg"], writes=["kro"])
                        if gi == 4:
                            S.op("dve", lambda e: e.tensor_copy(out=kro[0:64, t0:t0 + tn], in_=krp[0:64, t0:t0 + tn]), reads=["krp"], writes=["kro"])
                            S.op("sp", lambda e: e.dma_start(out=F0[b, 5120:5184, :], in_=kro[0:64, :]), reads=["kro"], writes=["F0"], dma="p1kr")
                    return
                if gi == 0:
                    sc[0] += 1
                si = sc[0] % 3
                sg = stg[si]
                sk = "p1st%d" % si
                func = AF.Silu if nm == "z" else AF.Copy
                if nm == "z" or (gi % 2 == 0):
                    S.op("act", lambda e: e.activation(out=sg[0:m, t0:t0 + tn], in_=pa, func=func), reads=[pk], writes=[sk])
                else:
                    S.op("dve", lambda e: e.tensor_copy(out=sg[0:m, t0:t0 + tn], in_=pa), reads=[pk], writes=[sk])
                if gi == 4:
                    r0 = c0 + sb * 128 + frow[nm]
                    S.op("sp", lambda e: e.dma_start(out=F0[b, r0:r0 + m, :], in_=sg[0:m, :]), reads=[sk], writes=["F0"], dma=sk)

            proj_fm(C, xm3, xkey, w3, wkey, ncw, evac, "p1")
        S.barrier()


def rope_tables():
    quarter = 16
    inv = (10000.0 ** (-np.arange(quarter, dtype=np.float32) / quarter)).astype(np.float32)
    pos = np.arange(SEQ)
    cos = np.zeros((64, SEQ), np.float32)
    sin = np.zeros((64, SEQ), np.float32)
    for half, p in ((0, pos // 64), (1, pos % 64)):
        ang = p.astype(np.float32)[None, :] * inv[:, None]
        c, s = np.cos(ang), np.sin(ang)
        cos[half * 32:half * 32 + 16] = c
        cos[half * 32 + 16:half * 32 + 32] = c
        sin[half * 32:half * 32 + 16] = -s
        sin[half * 32 + 16:half * 32 + 32] = s
    return cos, sin


def make_ctx(nc, es, dram):
    C = Ctx()
    C.nc = nc
    C.S = Sched(nc, es)
    C.es = es
    C.A = Arena(nc, es)
    C.ps = es.enter_context(nc.psum_tensor("ps", [128, 4096], F32))
    C.dram = dram
    C.pcnt = 0
    C.na_geo = na_geometry()
    C.cst = es.enter_context(nc.sbuf_tensor("cst", [128, 512], F32))
    C.ident = C.cst[:, 0:64].bitcast(BF16)
    C.eps_t = C.cst[:, 64:65]
    C.ones = C.cst[:, 72:136].bitcast(BF16)
    C.S.op("sp", lambda e: e.dma_start(out=C.ident, in_=dram["ident"]), writes=["ident"], dma="cst")
    C.S.op("pool", lambda e: e.memset(C.eps_t, EPS), writes=["eps"])
    C.S.op("pool", lambda e: e.memset(C.ones, 1.0), writes=["ones"])
    C.lnq = C.cst[:, 65:66]
    C.zero_t = C.cst[:, 66:67]
    C.cw = C.cst[:, 136:456]
    C.S.op("pool", lambda e: e.memset(C.lnq, float(np.log(128.0 ** -0.5))), writes=["lnq"])
    C.S.op("pool", lambda e: e.memset(C.zero_t, 0.0), writes=["zero"])
    if "dn_conv" in dram:
        cw3 = C.cw.rearrange("p (blk j) -> p blk j", j=5)
        for j in range(5):
            for q4 in range(8):
                C.S.op("sp", lambda e, j=j, q4=q4: e.dma_start(out=cw3[:, q4 * 8:(q4 + 1) * 8, j], in_=dram["dn_conv"][j, q4 * 1024:(q4 + 1) * 1024].rearrange("(blk p) -> p blk", p=128),
                                                          allow_slow_non_contiguous=True), writes=["cw"], dma="cw")
    C.S.barrier()
    return C


MLA_SCALE = 192.0 ** -0.5
NA_SCALE = 128.0 ** -0.5


def phase2(C):
    for b in range(NB):
        _phase2_b(C, b)


def _phase2_b(C, b):
    nc, S, A = C.nc, C.S, C.A
    A.reset()
    F0, M0, VM = C.dram["F0"], C.dram["M0"], C.dram["VM"]
    wqb = A.alloc(4 * 1536, BF16)
    wqb3 = _v3(wqb, 4)
    wqsw = A.alloc(4 * 512, BF16)
    wqsw4 = wqsw.rearrange("p (k h c) -> p k h c", k=4, h=8)
    wkvb = A.alloc(4 * 2048, BF16)
    wkvb3 = _v3(wkvb, 4)
    wkvb4 = wkvb.rearrange("p (k h c) -> p k h c", k=4, h=8)
    cq = A.alloc(4 * T, BF16)
    ckv = A.alloc(4 * T, BF16)
    cqn = A.alloc(4 * T, BF16)
    ckvn = A.alloc(4 * T, BF16)
    sq = [A.alloc(4 * 512, BF16) for _ in range(2)]
    lnv = [A.alloc(512) for _ in range(2)]
    rst = [A.alloc(512) for _ in range(2)]
    qnm = A.alloc(4)
    kvnm = A.alloc(4)
    cos_t = A.alloc(SEQ)
    sin_t = A.alloc(SEQ)
    stg = [A.alloc(T, BF16) for _ in range(3)]
    vst = [A.alloc(1024, BF16) for _ in range(2)]
    t1 = [A.alloc(512) for _ in range(2)]
    t2 = [A.alloc(512) for _ in range(2)]
    wq_d, wkv_d = C.dram["ab_w_qb"], C.dram["ab_w_kvb"]
    S.op("pool", lambda e: e.dma_start(out=wqb3, in_=wq_d.rearrange("(k p) c -> p k c", p=128)), writes=["wqb"], dma="p2w")
    S.op("pool", lambda e: e.dma_start(out=wkvb3, in_=wkv_d.rearrange("(k p) c -> p k c", p=128)), writes=["wkvb"], dma="p2w")
    wq4 = wq_d.rearrange("(k p) (h c) -> p k h c", p=128, h=8)
    for k in range(4):
        for (d0, s0) in ((0, 16), (16, 0), (32, 48), (48, 32)):
            S.op("pool", lambda e, k=k, d0=d0, s0=s0: e.dma_start(out=wqsw4[:, k, :, d0:d0 + 16], in_=wq4[:, k, :, 128 + s0:128 + s0 + 16]),
                 writes=["wqsw"], dma="p2w")
    S.op("sp", lambda e: e.dma_start(out=_v3(cq, 4), in_=F0[b, 2048:2560, :].rearrange("(k p) t -> p k t", p=128)), reads=["F0"], writes=["cq"], dma="p2a")
    S.op("sp", lambda e: e.dma_start(out=_v3(ckv, 4), in_=F0[b, 2560:3072, :].rearrange("(k p) t -> p k t", p=128)), reads=["F0"], writes=["ckv"], dma="p2a")
    S.op("sp", lambda e: e.dma_start(out=qnm, in_=C.dram["ab_q_norm"].rearrange("(k p) -> p k", p=128), allow_slow_non_contiguous=True), writes=["qnm"], dma="p2a")
    S.op("sp", lambda e: e.dma_start(out=kvnm, in_=C.dram["ab_kv_norm"].rearrange("(k p) -> p k", p=128), allow_slow_non_contiguous=True), writes=["kvnm"], dma="p2a")
    S.op("sp", lambda e: e.dma_start(out=cos_t[0:64, :], in_=C.dram["rope_cos"]), writes=["cos"], dma="p2a")
    S.op("sp", lambda e: e.dma_start(out=sin_t[0:64, :], in_=C.dram["rope_sin"]), writes=["sin"], dma="p2a")
    it = 0
    for (src, skey, nrm, nkey, dst, dkey) in ((cq, "cq", qnm, "qnm", cqn, "cqn"), (ckv, "ckv", kvnm, "kvnm", ckvn, "ckvn")):
        s3, d3 = _v3(src, 4), _v3(dst, 4)
        for gi, (t0, tn) in enumerate(TG):
            sl = it % 2
            it += 1
            q3 = _v3(sq[sl], 4)
            S.op("pool", lambda e, q3=q3, s3=s3, t0=t0, tn=tn: e.tensor_tensor(out=q3[:, :, 0:tn], in0=s3[:, :, t0:t0 + tn], in1=s3[:, :, t0:t0 + tn], op=ALU.mult),
                 reads=[skey], writes=["sq%d" % sl])
            bank = 4 + sl
            pa = C.ps[:, bank * 512:bank * 512 + tn]
            for k in range(4):
                S.op("pe", lambda e, pa=pa, q3=q3, k=k, tn=tn: e.matmul(pa, lhsT=C.ones, rhs=q3[:, k, 0:tn], start=(k == 0), stop=(k == 3)),
                     reads=["ones", "sq%d" % sl], writes=["psb%d" % bank])
            lv, rs = lnv[sl], rst[sl]
            S.op("act", lambda e, lv=lv, pa=pa, tn=tn: e.activation(out=lv[:, 0:tn], in_=pa, func=AF.Ln, scale=1.0 / 512, bias=C.eps_t[:, 0:1]),
                 reads=["psb%d" % bank], writes=["lnv%d" % sl])
            S.op("act", lambda e, lv=lv, rs=rs, tn=tn: e.activation(out=rs[:, 0:tn], in_=lv[:, 0:tn], func=AF.Exp, scale=-0.5),
                 reads=["lnv%d" % sl], writes=["rst%d" % sl])
            for k in range(4):
                S.op("dve", lambda e, d3=d3, s3=s3, k=k, t0=t0, tn=tn, nrm=nrm, rs=rs: e.scalar_tensor_tensor(
                    out=d3[:, k, t0:t0 + tn], in0=s3[:, k, t0:t0 + tn], scalar=nrm[:, k:k + 1], in1=rs[:, 0:tn], op0=ALU.mult, op1=ALU.mult),
                    reads=[skey, nkey, "rst%d" % sl], writes=[dkey])
    cqn3, ckvn3 = _v3(cqn, 4), _v3(ckvn, 4)
    sc = [0]

    def small_proj(lhs_fn, rhs3, rkey, wkey, m, gi, t0, tn):
        bank = 4 + (C.pcnt % 4)
        C.pcnt += 1
        pa = C.ps[0:m, bank * 512:bank * 512 + tn]
        for k in range(4):
            S.op("pe", lambda e, pa=pa, k=k: e.matmul(pa, lhsT=lhs_fn(k), rhs=rhs3[:, k, t0:t0 + tn], start=(k == 0), stop=(k == 3)),
                 reads=[wkey, rkey], writes=["psb%d" % bank])
        return pa, "psb%d" % bank

    for h in range(8):
        for (nm, lhs_fn, rhs3, rkey, wkey, row0) in (
                ("qn", lambda k, h=h: wqb3[:, k, h * 192:h * 192 + 128], cqn3, "cqn", "wqb", h * 128),
                ("kn", lambda k, h=h: wkvb3[:, k, h * 256:h * 256 + 128], ckvn3, "ckvn", "wkvb", 1536 + h * 128)):
            sc[0] += 1
            si = sc[0] % 3
            sg, sk = stg[si], "p2st%d" % si
            for gi, (t0, tn) in enumerate(TG):
                pa, pk = small_proj(lhs_fn, rhs3, rkey, wkey, 128, gi, t0, tn)
                if gi % 2 == 0:
                    S.op("act", lambda e, sg=sg, pa=pa, t0=t0, tn=tn: e.activation(out=sg[:, t0:t0 + tn], in_=pa, func=AF.Copy), reads=[pk], writes=[sk])
                else:
                    S.op("dve", lambda e, sg=sg, pa=pa, t0=t0, tn=tn: e.tensor_copy(out=sg[:, t0:t0 + tn], in_=pa), reads=[pk], writes=[sk])
            S.op("sp", lambda e, sg=sg, row0=row0: e.dma_start(out=M0[b, row0:row0 + 128, :], in_=sg), reads=[sk], writes=["M0"], dma=sk)
        sc[0] += 1
        si = sc[0] % 3
        sg, sk = stg[si], "p2st%d" % si
        for gi, (t0, tn) in enumerate(TG):
            pa, pk = small_proj(lambda k, h=h: wqb3[:, k, h * 192 + 128:h * 192 + 192], cqn3, "cqn", "wqb", 64, gi, t0, tn)
            if gi == 4:
                S.op("dve", lambda e, sg=sg, pa=pa, t0=t0, tn=tn: e.tensor_copy(out=sg[0:64, t0:t0 + tn], in_=pa), reads=[pk], writes=[sk])
                continue
            pb, pkb = small_proj(lambda k, h=h: wqsw4[:, k, h, :], cqn3, "cqn", "wqsw", 64, gi, t0, tn)
            a1, a2 = t1[gi % 2], t2[gi % 2]
            S.op("dve", lambda e, a1=a1, pa=pa, t0=t0, tn=tn: e.tensor_tensor(out=a1[0:64, 0:tn], in0=pa, in1=cos_t[0:64, t0:t0 + tn], op=ALU.mult),
                 reads=[pk, "cos"], writes=["t1%d" % (gi % 2)])
            S.op("dve", lambda e, a2=a2, pb=pb, t0=t0, tn=tn: e.tensor_tensor(out=a2[0:64, 0:tn], in0=pb, in1=sin_t[0:64, t0:t0 + tn], op=ALU.mult),
                 reads=[pkb, "sin"], writes=["t2%d" % (gi % 2)])
            S.op("pool", lambda e, a1=a1, a2=a2, sg=sg, t0=t0, tn=tn: e.tensor_tensor(out=sg[0:64, t0:t0 + tn], in0=a1[0:64, 0:tn], in1=a2[0:64, 0:tn], op=ALU.add),
                 reads=["t1%d" % (gi % 2), "t2%d" % (gi % 2)], writes=[sk])
        S.op("sp", lambda e, sg=sg, h=h: e.dma_start(out=M0[b, 1024 + h * 64:1024 + h * 64 + 64, :], in_=sg[0:64, :]), reads=[sk], writes=["M0"], dma=sk)
    for tt in range(NTT):
        vs, vk = vst[tt % 2], "p2vs%d" % (tt % 2)
        for half in range(2):
            bank = 4 + (C.pcnt % 4)
            C.pcnt += 1
            pa = C.ps[:, bank * 512:bank * 512 + 512]
            for k in range(4):
                S.op("pe", lambda e, pa=pa, k=k, tt=tt, half=half: e.matmul(pa.rearrange("p (h c) -> p h c", h=4), lhsT=ckvn3[:, k, tt * 128:(tt + 1) * 128],
                                                                         rhs=wkvb4[:, k, half * 4:half * 4 + 4, 128:256], start=(k == 0), stop=(k == 3)),
                     reads=["wkvb", "ckvn"], writes=["psb%d" % bank])
            if half == 0:
                S.op("act", lambda e, vs=vs, pa=pa: e.activation(out=vs[:, 0:512], in_=pa, func=AF.Copy), reads=["psb%d" % bank], writes=[vk])
            else:
                S.op("dve", lambda e, vs=vs, pa=pa: e.tensor_copy(out=vs[:, 512:1024], in_=pa), reads=["psb%d" % bank], writes=[vk])
        S.op("sp", lambda e, vs=vs, tt=tt: e.dma_start(out=VM[b, tt * 128:(tt + 1) * 128, :], in_=vs), reads=[vk], writes=["VM"], dma=vk)
    S.barrier()


def na_geometry():
    rows = 32
    r = np.arange(rows)
    r0 = np.clip(r - 4, 0, rows - 8)
    col = np.arange(64)
    c0 = np.clip(col - 8, 0, 64 - 16)
    tiles = {}
    per_m = []
    for m in range(16):
        lo = min(r0[2 * m], r0[2 * m + 1])
        hi = max(r0[2 * m], r0[2 * m + 1]) + 7
        lst = []
        for kb in range(lo // 2, hi // 2 + 1):
            memb = tuple(tuple(bool(r0[2 * m + bq] <= 2 * kb + a <= r0[2 * m + bq] + 7) for bq in range(2)) for a in range(2))
            key = (kb - m, memb)
            if key not in tiles:
                tiles[key] = len(tiles)
            lst.append((kb, tiles[key]))
        per_m.append(lst)
    nt = len(tiles)
    mask = np.zeros((nt, 128, 128), np.float32)
    drow = np.zeros((nt, 128, 128), np.int64)
    dcol = np.zeros((nt, 128, 128), np.int64)
    for (delta, memb), ti in tiles.items():
        for a in range(2):
            for bq in range(2):
                kc = np.arange(64)[:, None]
                qc = np.arange(64)[None, :]
                ok = memb[a][bq] & (kc >= c0[qc]) & (kc <= c0[qc] + 15)
                dr = 2 * delta + a - bq + 7
                dc = kc - qc + 15
                blk = (slice(a * 64, a * 64 + 64), slice(bq * 64, bq * 64 + 64))
                mask[ti][blk] = np.where(ok, 0.0, -20000.0)
                drow[ti][blk] = np.clip(np.where(ok, dr, 0), 0, 14)
                dcol[ti][blk] = np.clip(np.where(ok, dc, 0), 0, 30)
    return per_m, nt, mask, drow, dcol


def phase3(C):
    for b in range(NB):
        _phase3_b(C, b)


def _phase3_b(C, b):
    nc, S, A = C.nc, C.S, C.A
    A.reset()
    F0, M0, VM, VA, OG = C.dram["F0"], C.dram["M0"], C.dram["VM"], C.dram["VA"], C.dram["OG0"]
    per_m, nt, _, _, _ = C.na_geo
    krT = A.alloc(T, BF16)
    S.op("sp", lambda e: e.dma_start(out=krT[0:64, :], in_=F0[b, 5120:5184, :]), reads=["F0"], writes=["krT"], dma="p3k")
    nam = A.alloc(nt * 128)
    S.op("sp", lambda e: e.dma_start(out=_v3(nam, nt), in_=C.dram["na_mask"].rearrange("t k q -> k t q")), writes=["nam"], dma="p3k")
    hb = []
    for i in range(2):
        hb.append(dict(k=A.alloc(T, BF16), q=A.alloc(T, BF16), qr=A.alloc(T, BF16), v=A.alloc(NTT * 128, BF16), sz=A.alloc(T, BF16),
                       bias=A.alloc(nt * 128)))
    pT = [A.alloc(512, BF16) for _ in range(4)]
    sbf = [A.alloc(128) for _ in range(3)]
    rinv = [A.alloc(512) for _ in range(2)]
    tmp = [A.alloc(512) for _ in range(2)]
    ogs = [A.alloc(T, BF16) for _ in range(2)]
    cnt = dict(s=0, p=0, g=0, sb=0)

    def attend_block(lhsT_list, rhs_list, tn, reads, o_cols, first, last, vlhs, vkey, scale, bias=None, bkey=None, acc=None):
        bank = cnt["s"] % 4
        cnt["s"] += 1
        pS = C.ps[:, bank * 512:bank * 512 + tn]
        pk = "psb%d" % bank
        n = len(lhsT_list)
        for i in range(n):
            S.op("pe", lambda e, i=i: e.matmul(pS[0:128, :], lhsT=lhsT_list[i], rhs=rhs_list[i], start=(i == 0), stop=(i == n - 1)),
                 reads=reads, writes=[pk])
        slot = cnt["p"] % 4
        cnt["p"] += 1
        p_ = pT[slot][:, 0:tn]
        pkey = "pT%d" % slot
        if bias is None:
            S.op("act", lambda e: e.activation(out=p_, in_=pS, func=AF.Exp, scale=scale), reads=[pk], writes=[pkey])
        else:
            sslot = cnt["sb"] % 3
            cnt["sb"] += 1
            sb_ = sbf[sslot][:, 0:tn]
            S.op("dve", lambda e: e.scalar_tensor_tensor(out=sb_, in0=pS, scalar=scale, in1=bias, op0=ALU.mult, op1=ALU.add),
                 reads=[pk, bkey], writes=["sbf%d" % sslot])
            S.op("act", lambda e: e.activation(out=p_, in_=sb_, func=AF.Exp), reads=["sbf%d" % sslot], writes=[pkey])
        po, ps_, ok_, sk_ = acc

        def part2():
            S.op("pe", lambda e: e.matmul(po[:, o_cols[0]:o_cols[0] + tn], lhsT=vlhs, rhs=p_, start=first, stop=last), reads=[pkey, vkey], writes=[ok_])
            S.op("pe", lambda e: e.matmul(ps_[:, o_cols[0]:o_cols[0] + tn], lhsT=C.ones, rhs=p_, start=first, stop=last), reads=[pkey, "ones"], writes=[sk_])
        pend.append(part2)
        while len(pend) > 2:
            pend.pop(0)()

    pend = []

    def finish_group(acc, tn, sz, szkey, og, ogkey, t0):
        while pend:
            pend.pop(0)()
        po, ps_, ok_, sk_ = acc
        g = cnt["g"] % 2
        cnt["g"] += 1
        S.op("dve", lambda e: e.reciprocal(out=rinv[g][:, 0:tn], in_=ps_[:, 0:tn]), reads=[sk_], writes=["rinv%d" % g])
        S.op("dve", lambda e: e.tensor_tensor(out=tmp[g][:, 0:tn], in0=po[:, 0:tn], in1=rinv[g][:, 0:tn], op=ALU.mult),
             reads=[ok_, "rinv%d" % g], writes=["tmp%d" % g])
        S.op("pool", lambda e: e.tensor_tensor(out=og[:, t0:t0 + tn], in0=tmp[g][:, 0:tn], in1=sz[:, t0:t0 + tn], op=ALU.mult),
             reads=["tmp%d" % g, szkey], writes=[ogkey])

    def acc_banks():
        g = cnt["g"] % 2
        return (C.ps[:, (4 + g) * 512:(5 + g) * 512], C.ps[:, (6 + g) * 512:(7 + g) * 512], "psb%d" % (4 + g), "psb%d" % (6 + g))

    for hh in range(16):
        is_mla = hh >= 8
        h = hh % 8
        B_ = hb[hh % 2]
        hk = "hb%d" % (hh % 2)
        og, ogkey = ogs[hh % 2], "ogs%d" % (hh % 2)
        if is_mla:
            S.op("sp", lambda e, B_=B_, h=h: e.dma_start(out=B_["k"], in_=M0[b, 1536 + h * 128:1536 + (h + 1) * 128, :]), reads=["M0"], writes=[hk + "k"], dma=hk)
            S.op("sp", lambda e, B_=B_, h=h: e.dma_start(out=B_["q"], in_=M0[b, h * 128:(h + 1) * 128, :]), reads=["M0"], writes=[hk + "q"], dma=hk)
            S.op("sp", lambda e, B_=B_, h=h: e.dma_start(out=B_["qr"][0:64, :], in_=M0[b, 1024 + h * 64:1024 + (h + 1) * 64, :]), reads=["M0"], writes=[hk + "qr"], dma=hk)
            S.op("sp", lambda e, B_=B_, h=h: e.dma_start(out=_v3(B_["v"], NTT), in_=VM[b, :, h * 128:(h + 1) * 128].rearrange("(kb p) d -> p kb d", p=128)),
                 reads=["VM"], writes=[hk + "v"], dma=hk)
            S.op("sp", lambda e, B_=B_, h=h: e.dma_start(out=B_["sz"], in_=F0[b, 3072 + 1024 + h * 128:3072 + 1024 + (h + 1) * 128, :]), reads=["F0"], writes=[hk + "sz"], dma=hk)
        else:
            S.op("sp", lambda e, B_=B_, h=h: e.dma_start(out=B_["k"], in_=F0[b, 1024 + h * 128:1024 + (h + 1) * 128, :]), reads=["F0"], writes=[hk + "k"], dma=hk)
            S.op("sp", lambda e, B_=B_, h=h: e.dma_start(out=B_["q"], in_=F0[b, h * 128:(h + 1) * 128, :]), reads=["F0"], writes=[hk + "q"], dma=hk)
            S.op("sp", lambda e, B_=B_, h=h: e.dma_start(out=_v3(B_["v"], NTT), in_=VA[b, :, h * 128:(h + 1) * 128].rearrange("(kb p) d -> p kb d", p=128)),
                 reads=["VA"], writes=[hk + "v"], dma=hk)
            S.op("sp", lambda e, B_=B_, h=h: e.dma_start(out=B_["sz"], in_=F0[b, 3072 + h * 128:3072 + (h + 1) * 128, :]), reads=["F0"], writes=[hk + "sz"], dma=hk)
            S.op("sp", lambda e, B_=B_, h=h: e.dma_start(out=_v3(B_["bias"], nt), in_=C.dram["na_bias"][h].rearrange("t k q -> k t q")), writes=[hk + "b"], dma=hk)
            S.op("pool", lambda e, B_=B_: e.tensor_tensor(out=B_["bias"], in0=B_["bias"], in1=nam, op=ALU.add), reads=[hk + "b", "nam"], writes=[hk + "b"])
        k_, q_, qr_, v3_, sz_ = B_["k"], B_["q"], B_["qr"], _v3(B_["v"], NTT), B_["sz"]
        b3_ = _v3(B_["bias"], nt)
        for gi, (t0, tn) in enumerate(TG):
            acc = acc_banks()
            if is_mla:
                kbs = list(range(18)) if gi < 4 else [16, 17]
                for i, kb in enumerate(kbs):
                    attend_block([k_[:, kb * 128:(kb + 1) * 128], krT[0:64, kb * 128:(kb + 1) * 128]], [q_[:, t0:t0 + tn], qr_[0:64, t0:t0 + tn]], tn,
                                 [hk + "k", hk + "q", hk + "qr", "krT"], (0,), i == 0, i == len(kbs) - 1, v3_[:, kb, :], hk + "v", MLA_SCALE, acc=acc)
            else:
                for i, kb in enumerate((16, 17)):
                    attend_block([k_[:, kb * 128:(kb + 1) * 128]], [q_[:, t0:t0 + tn]], tn, [hk + "k", hk + "q"], (0,), i == 0, (gi == 4 and i == 1),
                                 v3_[:, kb, :], hk + "v", NA_SCALE, acc=acc)
                if gi < 4:
                    for pr in range(4):
                        m = gi * 4 + pr
                        lst = per_m[m]
                        for j, (kb, ti) in enumerate(lst):
                            attend_block([k_[:, kb * 128:(kb + 1) * 128]], [q_[:, m * 128:(m + 1) * 128]], 128, [hk + "k", hk + "q"], (pr * 128,), False,
                                         (pr == 3 and j == len(lst) - 1), v3_[:, kb, :], hk + "v", NA_SCALE, bias=b3_[:, ti, :], bkey=hk + "b", acc=acc)
            finish_group(acc, tn, sz_, hk + "sz", og, ogkey, t0)
        row0 = (1024 if is_mla else 0) + h * 128
        S.op("sp", lambda e, og=og, row0=row0: e.dma_start(out=OG[b, row0:row0 + 128, :], in_=og), reads=[ogkey], writes=["OG0"], dma=ogkey)
    S.barrier()


def out_proj(C, b, l, OGd, nk, w_out, res_tiles, dst_tiles, ntt_list, tag, cbw=512, ntok=T):
    nc, S, A = C.nc, C.S, C.A
    A.reset()
    og = A.alloc(nk * ntok, BF16)
    og3 = _v3(og, nk)
    hk = nk // 2
    S.op("sp", lambda e: e.dma_start(out=og3[:, 0:hk, :], in_=OGd[0:hk * 128, :].rearrange("(k p) t -> p k t", p=128)), reads=["DR_OG"], writes=["og"], dma=tag + "og")
    S.op("sp", lambda e: e.dma_start(out=og3[:, hk:nk, :], in_=OGd[hk * 128:nk * 128, :].rearrange("(k p) t -> p k t", p=128)), reads=["DR_OG"], writes=["og"], dma=tag + "og")
    vs = (b, 2) if len(ntt_list) > 16 else (b,)
    wr = A.alloc(nk * cbw, BF16)
    wx = [A.alloc(nk * cbw, BF16) for _ in vs]
    gt = [A.alloc(cbw) for _ in range(2)]
    xr = [A.alloc(cbw) for _ in range(3)]
    xo = [A.alloc(cbw) for _ in range(3)]
    it = 0
    for cb in range(D // cbw):
        w3 = _v3(wr, nk)
        S.op("pool", lambda e, cb=cb, w3=w3: e.dma_start(out=w3, in_=w_out[:, cb * cbw:(cb + 1) * cbw].rearrange("(k p) c -> p k c", p=128)), writes=["wr"], dma=tag + "w")
        for vi, v in enumerate(vs):
            S.op("sp", lambda e, vi=vi, v=v, cb=cb: e.dma_start(out=gt[vi], in_=C.dram["mod"][l, v, 4096 + cb * cbw:4096 + (cb + 1) * cbw].unsqueeze(0).to_broadcast([128, cbw])),
                 reads=["mod"], writes=["gt%d" % vi], dma=tag + "gt")
            S.op("dve" if vi == 0 else "pool", lambda e, vi=vi, w3=w3: e.tensor_tensor(out=_v3(wx[vi], nk), in0=w3, in1=gt[vi].unsqueeze(1).to_broadcast([128, nk, cbw]), op=ALU.mult),
                 reads=["wr", "gt%d" % vi], writes=["wx%d" % vi])
        for tt in ntt_list:
            vi = 0 if tt < 16 else 1
            wv = _v3(wx[vi], nk)
            bank = 4 + (C.pcnt % 4)
            C.pcnt += 1
            pa = C.ps[:, bank * 512:bank * 512 + cbw]
            for k in range(nk):
                S.op("pe", lambda e, pa=pa, k=k, tt=tt, wv=wv: e.matmul(pa, lhsT=og3[:, k, tt * 128:(tt + 1) * 128], rhs=wv[:, k, :], start=(k == 0), stop=(k == nk - 1)),
                     reads=["og", "wx%d" % vi], writes=["psb%d" % bank])
            sl = it % 3
            it += 1
            S.op("sp", lambda e, sl=sl, tt=tt, cb=cb: e.dma_start(out=xr[sl], in_=res_tiles(tt, cb)), reads=["DR_res"], writes=["xr%d" % sl], dma=tag + "xr%d" % sl)
            S.op("dve", lambda e, sl=sl, pa=pa: e.tensor_tensor(out=xo[sl], in0=pa, in1=xr[sl], op=ALU.add), reads=["psb%d" % bank, "xr%d" % sl], writes=["xo%d" % sl])
            S.op("sp", lambda e, sl=sl, tt=tt, cb=cb: e.dma_start(out=dst_tiles(tt, cb), in_=xo[sl]), reads=["xo%d" % sl], writes=["DR_dst"], dma=tag + "xo%d" % sl)
    S.barrier()


def phase4(C):
    for b in range(NB):
        def res(tt, cb, b=b):
            if tt < 16:
                return C.dram["x"][b, tt * 128:(tt + 1) * 128, cb * 512:(cb + 1) * 512]
            return C.dram["ctx"][b, (tt - 16) * 128:(tt - 15) * 128, cb * 512:(cb + 1) * 512]

        def dst(tt, cb, b=b):
            return C.dram["X1"][b, tt * 128:(tt + 1) * 128, cb * 512:(cb + 1) * 512]
        out_proj(C, b, 0, C.dram["OG0"][b], 16, C.dram["ab_w_out"], res, dst, list(range(NTT)), "p4")


HP = 2052 + 260


def phase5(C):
    for b in range(NB):
        _phase5_b(C, b)


def _phase5_b(C, b):
    nc, S, A = C.nc, C.S, C.A
    A.reset()
    w_in = C.dram["dn_w_in"]
    X1, F1, BA = C.dram["X1"], C.dram["F1"], C.dram["BA"]
    mv = load_modvecs(C, 1, b, C.dram["dn_norm"], "p5m%d" % b)
    xmT = A.alloc(16 * T, BF16)
    xm3 = _v3(xmT, 16)
    xkey = "xmT"
    mark = A.off
    tiles = [X1[b, tt * 128:(tt + 1) * 128, :] for tt in range(NTT)]
    build_xmT(C, xmT, xkey, tiles, mv, "p5b")
    S.barrier()
    A.off = mark
    wr = [A.alloc(16 * 512, BF16) for _ in range(2)]
    stg = [A.alloc(T, BF16) for _ in range(3)]
    hp = [A.alloc(HP, BF16) for _ in range(2)]
    dg = [A.alloc(5 * 128, BF16) for _ in range(2)]
    sT = [A.alloc(512, BF16) for _ in range(3)]
    sq = [A.alloc(512, BF16) for _ in range(2)]
    lnv = [A.alloc(512) for _ in range(2)]
    rst = [A.alloc(512) for _ in range(2)]
    bast = A.alloc(NTT * 128)
    cw3 = C.cw.rearrange("p (blk j) -> p blk j", j=5)
    for i in range(2):
        S.op("pool", lambda e, i=i: e.memset(hp[i], 0.0), writes=["hp%d" % i])
    segs = [("q", i * 512, 512) for i in range(4)] + [("k", 2048 + i * 512, 512) for i in range(4)] + [("v", 4096 + i * 512, 512) for i in range(8)] + \
           [("z", 8192 + i * 512, 512) for i in range(8)] + [("ba", 12288, 128)]
    cnt = dict(st=0, hp=0, c=0, n=0, s=0)
    for wi, (nm, c0, ncw) in enumerate(segs):
        slot = wi % 2
        w3 = _v3(wr[slot], 16)[:, :, 0:ncw]
        wkey = "p5w%d" % slot
        S.op("pool", lambda e, w3=w3, c0=c0, ncw=ncw: e.dma_start(out=w3, in_=w_in[:, c0:c0 + ncw].rearrange("(k p) c -> p k c", p=128)),
             writes=[wkey], dma=wkey)
        if nm == "ba":
            b3 = _v3(bast, NTT)
            for tt in range(NTT):
                bank = 4 + (C.pcnt % 4)
                C.pcnt += 1
                pa = C.ps[:, bank * 512:bank * 512 + 128]
                for k in range(16):
                    S.op("pe", lambda e, pa=pa, k=k, tt=tt, w3=w3: e.matmul(pa, lhsT=xm3[:, k, tt * 128:(tt + 1) * 128], rhs=w3[:, k, :], start=(k == 0), stop=(k == 15)),
                         reads=[wkey, xkey + str(tt)], writes=["psb%d" % bank])
                S.op("dve", lambda e, pa=pa, tt=tt: e.tensor_copy(out=b3[:, tt, :], in_=pa), reads=["psb%d" % bank], writes=["bast"])
            S.op("sp", lambda e: e.dma_start(out=BA[b].rearrange("(tt p) c -> p tt c", p=128), in_=b3), reads=["bast"], writes=["BA"], dma="bast")
            continue

        def evac(sb, gi, t0, tn, pa, pk, m, nm=nm, c0=c0):
            if nm == "z":
                if gi == 0:
                    cnt["st"] += 1
                si = cnt["st"] % 3
                sg, sk = stg[si], "p5st%d" % si
                S.op("act", lambda e: e.activation(out=sg[:, t0:t0 + tn], in_=pa, func=AF.Silu), reads=[pk], writes=[sk])
                if gi == 4:
                    r0 = c0 + sb * 128
                    S.op("sp", lambda e: e.dma_start(out=F1[b, r0:r0 + 128, :], in_=sg), reads=[sk], writes=["F1"], dma=sk)
                return
            if gi == 0:
                cnt["hp"] += 1
            hi = cnt["hp"] % 2
            hb_, hk_ = hp[hi], "hp%d" % hi
            off = 2 + t0 if gi < 4 else 2052 + 2
            if gi % 2 == 0:
                S.op("act", lambda e: e.activation(out=hb_[:, off:off + tn], in_=pa, func=AF.Copy), reads=[pk], writes=[hk_])
            else:
                S.op("dve", lambda e: e.tensor_copy(out=hb_[:, off:off + tn], in_=pa), reads=[pk], writes=[hk_])
            if gi < 4:
                return
            blk = (c0 + sb * 128) // 128
            di = cnt["hp"] % 2
            d3 = _v3(dg[di], 5)
            for j in range(5):
                S.op("pool", lambda e, j=j: e.tensor_scalar(out=d3[:, j, :], in0=C.ident, scalar1=cw3[:, blk, j:j + 1], scalar2=None, op0=ALU.mult),
                     reads=["ident", "cw"], writes=["dg%d" % di])
            cnt["st"] += 1
            si = cnt["st"] % 3
            sg, sk = stg[si], "p5st%d" % si
            outs = []
            for g2, (u0, un) in enumerate(TG):
                bank = cnt["c"] % 4
                cnt["c"] += 1
                pc = C.ps[:, bank * 512:bank * 512 + un]
                base = u0 if g2 < 4 else 2052
                for j in range(5):
                    S.op("pe", lambda e, pc=pc, j=j, base=base, un=un: e.matmul(pc, lhsT=d3[:, j, :], rhs=hb_[:, base + j:base + j + un], start=(j == 0), stop=(j == 4)),
                         reads=["dg%d" % di, hk_], writes=["psb%d" % bank])
                if nm == "v":
                    S.op("act", lambda e, pc=pc, u0=u0, un=un: e.activation(out=sg[:, u0:u0 + un], in_=pc, func=AF.Silu), reads=["psb%d" % bank], writes=[sk])
                else:
                    ssl = cnt["s"] % 3
                    cnt["s"] += 1
                    s_ = sT[ssl]
                    S.op("act", lambda e, pc=pc, s_=s_, un=un: e.activation(out=s_[:, 0:un], in_=pc, func=AF.Silu), reads=["psb%d" % bank], writes=["sT%d" % ssl])
                    outs.append((s_, "sT%d" % ssl, u0, un))
                    if len(outs) == 3 or g2 == 4:
                        pend = []
                        for (s2, s2k, v0, vn) in outs:
                            nsl = cnt["n"] % 2
                            cnt["n"] += 1
                            S.op("pool", lambda e, s2=s2, vn=vn, nsl=nsl: e.tensor_tensor(out=sq[nsl][:, 0:vn], in0=s2[:, 0:vn], in1=s2[:, 0:vn], op=ALU.mult),
                                 reads=[s2k], writes=["sq%d" % nsl])
                            bank2 = cnt["c"] % 4
                            cnt["c"] += 1
                            p3 = C.ps[:, bank2 * 512:bank2 * 512 + vn]
                            S.op("pe", lambda e, p3=p3, nsl=nsl, vn=vn: e.matmul(p3, lhsT=C.ones, rhs=sq[nsl][:, 0:vn], start=True, stop=True),
                                 reads=["ones", "sq%d" % nsl], writes=["psb%d" % bank2])
                            S.op("act", lambda e, p3=p3, nsl=nsl, vn=vn: e.activation(out=lnv[nsl][:, 0:vn], in_=p3, func=AF.Ln, bias=C.eps_t[:, 0:1]),
                                 reads=["psb%d" % bank2], writes=["lnv%d" % nsl])
                            pend.append((s2, s2k, v0, vn, nsl))
                            if len(pend) == 2 or (s2 is outs[-1][0]):
                                for (s3_, s3k, w0, wn, ns2) in pend:
                                    bias_ap = C.lnq[:, 0:1] if nm == "q" else C.zero_t[:, 0:1]
                                    S.op("act", lambda e, ns2=ns2, wn=wn, bias_ap=bias_ap: e.activation(out=rst[ns2][:, 0:wn], in_=lnv[ns2][:, 0:wn], func=AF.Exp, scale=-0.5, bias=bias_ap),
                                         reads=["lnv%d" % ns2], writes=["rst%d" % ns2])
                                    S.op("dve", lambda e, s3_=s3_, w0=w0, wn=wn, ns2=ns2: e.tensor_tensor(out=sg[:, w0:w0 + wn], in0=s3_[:, 0:wn], in1=rst[ns2][:, 0:wn], op=ALU.mult),
                                         reads=[s3k, "rst%d" % ns2], writes=[sk])
                                pend = []
                        outs = []
            r0 = c0 + sb * 128
            S.op("sp", lambda e: e.dma_start(out=F1[b, r0:r0 + 128, :], in_=sg), reads=[sk], writes=["F1"], dma=sk)

        proj_fm(C, xm3, xkey, w3, wkey, ncw, evac, "p5")
    S.barrier()


FSEQ = [16, 17] + list(range(16))
BSEQ = [17, 16] + list(range(15, -1, -1))


def dn_level_masks():
    s_ = np.arange(128)[:, None]
    c_ = np.arange(128)[None, :]
    out = np.zeros((128, 7, 4, 2, 128), np.float32)
    for k in range(1, 8):
        h = 1 << (k - 1)
        same = (s_ // (2 * h)) == (c_ // (2 * h))
        ur = same & ((s_ % (2 * h)) < h) & ((c_ % (2 * h)) >= h)
        ll = ur.T
        for j in range(4):
            fwd = j < 2
            out[:, k - 1, j, 0, :] = -(ur if fwd else ll).astype(np.float32)
            out[:, k - 1, j, 1, :] = -(ll if fwd else ur).astype(np.float32)
    id8 = np.zeros((128, 4, 2, 128), np.float32)
    id8[:, :, :, :] = np.eye(128, dtype=np.float32)[:, None, None, :]
    return out.reshape(128, 7, 1024), id8.reshape(128, 1024)


def dn_level_masks2():
    lm, _ = dn_level_masks()
    lm = lm.reshape(128, 7, 4, 2, 128).copy()
    lm -= np.eye(128, dtype=np.float32)[:, None, None, None, :]
    return np.ascontiguousarray(lm[:, :, 0::2, :, :]).reshape(128, 7, 512)


def dn_masks():
    s = np.arange(128)[:, None]
    c = np.arange(128)[None, :]
    incl = np.stack([(s <= c), (s <= c), (s >= c), (s >= c)], 0).astype(np.float32)
    strict = np.stack([(s < c), (s < c), (s > c), (s > c)], 0).astype(np.float32)
    ident4 = np.stack([np.eye(128, dtype=np.float32)] * 4, 0)
    return incl.transpose(1, 0, 2).copy(), strict.transpose(1, 0, 2).copy(), ident4.transpose(1, 0, 2).copy()


def phase6(C):
    for b in getattr(C, "p6_batches", range(NB)):
        _phase6_b(C, b)


def dump(C, name, ap, readkeys):
    if not getattr(C, "debug", False):
        return
    t = C.nc.dram_tensor("dbg_" + name, list(ap.shape), ap.dtype, kind="ExternalOutput").ap()
    if ap.shape[1] * (4 if ap.dtype == F32 else 2) > 2048:
        C.S.op("sp", lambda e: e.dma_start(out=t, in_=ap), reads=readkeys, writes=["DR_dbg"], dma=1)
        return
    if not hasattr(C, "dbg_stage"):
        C.dbg_stage = C.es.enter_context(C.nc.sbuf_tensor("dbgst", [128, 512], F32))
    st = C.dbg_stage[:, 0:ap.shape[1]] if ap.dtype == F32 else C.dbg_stage[:, 0:(ap.shape[1] + 1) // 2].bitcast(BF16)[:, 0:ap.shape[1]]
    C.S.op("dve", lambda e: e.tensor_copy(out=st, in_=ap), reads=readkeys, writes=["dbgst"])
    C.S.op("sp", lambda e: e.dma_start(out=t, in_=st), reads=["dbgst"], writes=["DR_dbg"], dma=1)


def _phase6_b(C, b):
    nc, S, A = C.nc, C.S, C.A
    A.reset()
    F1, BA, OG1 = C.dram["F1"], C.dram["BA"], C.dram["OG1"]
    mincl = A.alloc(512)
    mstr = A.alloc(512)
    ones_f = A.alloc(128)
    onorm = A.alloc(1)
    S.op("sp", lambda e: e.dma_start(out=_v3(mincl, 4), in_=C.dram["dn_mincl"]), writes=["mincl"], dma=1)
    S.op("sp", lambda e: e.dma_start(out=_v3(mstr, 4), in_=C.dram["dn_mstrict"]), writes=["mstr"], dma=1)
    S.op("sp", lambda e: e.dma_start(out=onorm, in_=C.dram["dn_o_norm"].rearrange("(p o) -> p o", o=1), allow_slow_non_contiguous=True), writes=["onorm"], dma=1)
    S.op("pool", lambda e: e.memset(ones_f, 1.0), writes=["ones_f"])
    mincl3, mstr3 = _v3(mincl, 4), _v3(mstr, 4)
    lmask = A.alloc(7 * 512, BF16)
    lm4 = lmask.rearrange("p (k d x) -> p k d x", k=7, d=2)
    S.op("sp", lambda e: e.dma_start(out=_v3(lmask, 7), in_=C.dram["dn_lmask2"]), writes=["lmask"], dma=1)
    beta = A.alloc(NTT * 64)
    gg = A.alloc(NTT * 64)
    mark6 = A.off
    ba = A.alloc(NTT * 128)
    ba3 = _v3(ba, NTT)
    S.op("sp", lambda e: e.dma_start(out=ba3, in_=BA[b].rearrange("(tt p) c -> p tt c", p=128)), reads=["BA"], writes=["ba"], dma=1)
    tA = A.alloc(NTT * 64)
    tB = A.alloc(NTT * 64)
    alog = A.alloc(64)
    dtb = A.alloc(64)
    one_t = A.alloc(1)
    S.op("pool", lambda e: e.memset(one_t, 1.0), writes=["one_t"])
    S.op("sp", lambda e: e.dma_start(out=alog, in_=C.dram["dn_a_log"].rearrange("d h -> (d h)").unsqueeze(0).to_broadcast([128, 64])), writes=["alog"], dma=1)
    S.op("sp", lambda e: e.dma_start(out=dtb, in_=C.dram["dn_dt_bias"].rearrange("d h -> (d h)").unsqueeze(0).to_broadcast([128, 64])), writes=["dtb"], dma=1)
    beta3, gg3, tA3, tB3 = _v3(beta, NTT), _v3(gg, NTT), _v3(tA, NTT), _v3(tB, NTT)
    S.op("act", lambda e: e.activation(out=tA3, in_=ba3[:, :, 0:64], func=AF.Exp, scale=-1.0), reads=["ba"], writes=["tA"])
    S.op("dve", lambda e: e.tensor_scalar(out=tA, in0=tA, scalar1=1.0, scalar2=None, op0=ALU.add), reads=["tA"], writes=["tA"])
    S.op("dve", lambda e: e.reciprocal(out=beta, in_=tA), reads=["tA"], writes=["beta"])
    S.op("dve", lambda e: e.tensor_tensor(out=tB3, in0=ba3[:, :, 64:128], in1=dtb.unsqueeze(1).to_broadcast([128, NTT, 64]), op=ALU.add), reads=["ba", "dtb"], writes=["tB"])
    S.op("dve", lambda e: e.scalar_tensor_tensor(out=tA, in0=tB, scalar=-1.0, in1=tB, op0=ALU.mult, op1=ALU.max), reads=["tB", "beta"], writes=["tA"])
    S.op("act", lambda e: e.activation(out=tA, in_=tA, func=AF.Exp, scale=-1.0), reads=["tA"], writes=["tA"])
    S.op("act", lambda e: e.activation(out=tA, in_=tA, func=AF.Ln, bias=one_t[:, 0:1]), reads=["tA", "one_t"], writes=["tA"])
    S.op("dve", lambda e: e.scalar_tensor_tensor(out=tB, in0=tB, scalar=0.0, in1=tA, op0=ALU.max, op1=ALU.add), reads=["tA", "tB"], writes=["tB"])
    S.op("act", lambda e: e.activation(out=alog, in_=alog, func=AF.Exp), reads=["alog"], writes=["alog"])
    S.op("dve", lambda e: e.scalar_tensor_tensor(out=gg3, in0=tB3, scalar=-1.0, in1=alog.unsqueeze(1).to_broadcast([128, NTT, 64]), op0=ALU.mult, op1=ALU.mult),
         reads=["tB", "alog"], writes=["gg"])
    S.barrier()
    A.off = mark6
    SHARED = {"gg", "beta", "mincl", "mstr", "lmask", "ident", "ones", "onorm", "ones_f", "eps", "zero", "lnq"}
    S0 = S

    class _SlotSched:
        def __init__(self, si):
            self.si = si

        def op(self, eng, fn, reads=(), writes=(), dma=None):
            f = lambda k: k if (k in SHARED or k in Sched.DRAMKEYS or k.startswith("DR_")) else "s%d_%s" % (self.si, k)
            return S0.op(eng, fn, reads=[f(k) for k in reads], writes=[f(k) for k in writes], dma=dma)

    def run_slot(si, head_list):
        S = _SlotSched(si)
        hbufs = [dict(q=A.alloc(T, BF16), k=A.alloc(T, BF16), v=A.alloc(2 * T, BF16))]
        ktok = A.alloc(NTT * 128, BF16)
        vtok = A.alloc(NTT * 256, BF16)
        oacc = A.alloc(2 * SEQ)
        o3 = _v3(oacc, 2)
        szb1 = A.alloc(SEQ, BF16)
        def mk():
            dec_ = A.alloc(512)
            return dict(grep=A.alloc(512), d1=dec_, dec=dec_, gam=A.alloc(512), bm=A.alloc(512), tmp=A.alloc(512), tmp2=A.alloc(512),
                        x12=A.alloc(12), xn=A.alloc(1024, BF16), rt=A.alloc(1024, BF16), yy=A.alloc(1024, BF16),
                        xb=dec_, vd=A.alloc(512, BF16), vn=A.alloc(512, BF16), t3=A.alloc(512), t4=A.alloc(512))

        def mk_par():
            return dict(e12=A.alloc(12), negb=A.alloc(4), intra=A.alloc(512, BF16), gq=A.alloc(512, BF16), kd=A.alloc(512, BF16))
        PARSET = ("e12", "negb", "intra", "gq", "kd")
        stp_base = mk()
        stp_par = [mk_par(), mk_par()]
        S4 = A.alloc(512)
        S4b = A.alloc(512, BF16)
        sqb = [A.alloc(512, BF16)] * 2
        lnv = [A.alloc(512)] * 2
        rst = lnv
        osum = [A.alloc(512)] * 2
        ps = C.ps
        pbase = si * 2048
        kS1 = "psb%d" % (pbase // 512)
        kS2 = "psb%d" % (pbase // 512 + 1)
        kG = kAB = kM = kT = kC = kN = "psM%d" % (pbase // 512)
        psS1 = ps[:, pbase:pbase + 512]
        psS2 = ps[:, pbase + 512:pbase + 1024]
        psG = ps[:, pbase + 1024:pbase + 1536]
        psG3 = _v3(psG, 4)
        psA = ps[:, pbase + 1536:pbase + 1792]
        psB = ps[:, pbase + 1792:pbase + 2048]
        psM = ps[:, pbase + 1024:pbase + 2048]
        psM3 = _v3(psM, 4)
        psT = ps[:, pbase + 1024:pbase + 1536]
        psTb = psT.bitcast(BF16)
        psC = ps[:, pbase + 1024:pbase + 1028]
        psN = psT

        for g in head_list:
            H = hbufs[0]
            hk = "h6"
            S.op("sp", lambda e, H=H, g=g: e.dma_start(out=H["q"], in_=F1[b, g * 128:(g + 1) * 128, :]), reads=["F1"], writes=[hk + "q"], dma=1)
            S.op("sp", lambda e, H=H, g=g: e.dma_start(out=H["k"], in_=F1[b, 2048 + g * 128:2048 + (g + 1) * 128, :]), reads=["F1"], writes=[hk + "k"], dma=1)
            S.op("sp", lambda e, H=H, g=g: e.dma_start(out=_v3(H["v"], 2), in_=F1[b, 4096 + 2 * g * 128:4096 + (2 * g + 2) * 128, :].rearrange("(v p) t -> p v t", p=128)),
                 reads=["F1"], writes=[hk + "v"], dma=1)
            QT, KT, VT3 = H["q"], H["k"], _v3(H["v"], 2)
            kt3 = _v3(ktok, NTT)
            vt4 = vtok.rearrange("p (t v d) -> p t v d", t=NTT, v=2)
            jobs = [("k", tt, 0) for tt in range(NTT)] + [("v", tt, vh) for tt in range(NTT) for vh in range(2)]
            groups = [jobs[0:8], jobs[8:16], jobs[16:18]] + [jobs[18 + i:18 + i + 8] for i in range(0, 36, 8)]
            for j0, grp in enumerate(groups):
                yield
                j0 = j0 * 8
                for i, (kind, tt, vh) in enumerate(grp):
                    src = KT[:, tt * 128:(tt + 1) * 128] if kind == "k" else VT3[:, vh, tt * 128:(tt + 1) * 128]
                    S.op("pe", lambda e, i=i, src=src: e.transpose(out=psTb[:, i * 128:(i + 1) * 128], in_=src, identity=C.ident),
                         reads=[hk + "k", hk + "v", "ident"], writes=[kT])
                kind0, tt0, vh0 = grp[0]
                n = len(grp)
                if kind0 == "k":
                    dst = ktok[:, tt0 * 128:(tt0 + n) * 128]
                    dk_ = "ktok"
                else:
                    dst = vtok[:, (tt0 * 2 + vh0) * 128:(tt0 * 2 + vh0 + n) * 128]
                    dk_ = "vtok"
                if (j0 // 8) % 2 == 0:
                    S.op("act", lambda e, dst=dst, n=n: e.activation(out=dst, in_=psTb[:, 0:n * 128], func=AF.Copy), reads=[kT], writes=[dk_])
                else:
                    S.op("dve", lambda e, dst=dst, n=n: e.tensor_copy(out=dst, in_=psTb[:, 0:n * 128]), reads=[kT], writes=[dk_])
            S.op("pool", lambda e: e.memset(S4, 0.0), writes=["S4"])
            S.op("pool", lambda e: e.memset(S4b, 0.0), writes=["S4b"])
            S43, S4b3 = _v3(S4, 4), _v3(S4b, 4)
            c0 = 2 * g
            def step(s, part, g=g, H=H, hk=hk, QT=QT, KT=KT, VT3=VT3, kt3=kt3, vt4=vt4, c0=c0, S43=S43, S4b3=S4b3):
                P = dict(stp_base)
                P.update(stp_par[s % 2])
                K = lambda n_: ("st%d" % (s % 2) if n_ in PARSET else "stb") + n_
                blks = (FSEQ[s], BSEQ[s])
                cols = [(d * 32 + c0) for d in range(2)]
                grep3, d13, dec3, gam3, bm3, tmp3, tmp23 = [_v3(P[n_], 4) for n_ in ("grep", "d1", "dec", "gam", "bm", "tmp", "tmp2")]
                xb3, intra3, gq3, kd3, vd3, vn3, t33, t43 = [_v3(P[n_], 4) for n_ in ("xb", "intra", "gq", "kd", "vd", "vn", "t3", "t4")]
                x12, e12, negb = P["x12"], P["e12"], P["negb"]
                RT = P["rt"].rearrange("p (j o c) -> p j o c", j=4, o=2)
                krt = K("rt")
                xn4 = P["xn"].rearrange("p (j o c) -> p j o c", j=4, o=2)
                kxn = K("xn")
                if part == "A":
                    yield
                    for d in range(2):
                        gs = gg3[:, blks[d], cols[d]:cols[d] + 2]
                        S.op("pool", lambda e, d=d, gs=gs: e.tensor_copy(out=grep3[:, 2 * d:2 * d + 2, :], in_=gs.unsqueeze(2).to_broadcast([128, 2, 128])),
                             reads=["gg"], writes=[K("grep")])
                        S.op("dve", lambda e, d=d: e.tensor_scalar(out=negb[:, 2 * d:2 * d + 2], in0=beta3[:, blks[d], cols[d]:cols[d] + 2], scalar1=-1.0, scalar2=None, op0=ALU.mult),
                             reads=["beta"], writes=[K("negb")])
                        S.op("pool", lambda e, d=d: e.tensor_tensor(out=bm3[:, 2 * d:2 * d + 2, :], in0=mstr3[:, 2 * d:2 * d + 2, :],
                                                                    in1=beta3[:, blks[d], cols[d]:cols[d] + 2].unsqueeze(2).to_broadcast([128, 2, 128]), op=ALU.mult),
                             reads=["beta", "mstr"], writes=[K("bm")])
                    yield
                    for d in range(2):
                        S.op("pe", lambda e, d=d: e.matmul(psC[:, 2 * d:2 * d + 2], lhsT=mincl3[:, 2 * d, :], rhs=gg3[:, blks[d], cols[d]:cols[d] + 2], start=True, stop=True),
                             reads=["gg", "mincl"], writes=[kC])
                    S.op("dve", lambda e: e.tensor_copy(out=x12[:, 0:4], in_=psC), reads=[kC], writes=[K("x12")])
                    yield
                    for j in range(4):
                        d = j // 2
                        S.op("pe", lambda e, j=j, d=d: e.matmul(psG3[:, j, :], lhsT=grep3[:, j, :], rhs=mincl3[:, 2 * d, :], start=True, stop=True),
                             reads=[K("grep"), "mincl"], writes=[kG])
                    yield
                    for d in range(2):
                        kb_ = KT[:, blks[d] * 128:(blks[d] + 1) * 128]
                        qb_ = QT[:, blks[d] * 128:(blks[d] + 1) * 128]
                        S.op("pe", lambda e, d=d, kb_=kb_: e.matmul(psA[:, d * 128:(d + 1) * 128], lhsT=kb_, rhs=kb_, start=True, stop=True), reads=[hk + "k"], writes=[kAB])
                        S.op("pe", lambda e, d=d, kb_=kb_, qb_=qb_: e.matmul(psB[:, d * 128:(d + 1) * 128], lhsT=kb_, rhs=qb_, start=True, stop=True), reads=[hk + "k", hk + "q"], writes=[kAB])
                    yield
                    yield
                    for d in range(2):
                        last = 127 if d == 0 else 0
                        S.op("dve", lambda e, d=d, last=last: e.tensor_copy(out=x12[:, 8 + 2 * d:10 + 2 * d], in_=psG3[:, 2 * d:2 * d + 2, last]), reads=[kG], writes=[K("x12")])
                    yield
                    S.op("dve", lambda e: e.tensor_tensor(out=x12[:, 4:8], in0=x12[:, 8:12], in1=x12[:, 0:4], op=ALU.subtract), reads=[K("x12")], writes=[K("x12")])
                    yield
                    S.op("act", lambda e: e.activation(out=e12, in_=x12, func=AF.Exp), reads=[K("x12")], writes=[K("e12")])
                    yield
                    S.op("dve", lambda e: e.tensor_tensor(out=d13, in0=psG3, in1=x12[:, 0:4].unsqueeze(2).to_broadcast([128, 4, 128]), op=ALU.subtract),
                         reads=[kG, K("x12")], writes=[K("dec")])
                    yield
                    S.op("pool", lambda e: e.tensor_scalar(out=P["d1"], in0=P["d1"], scalar1=0.0, scalar2=-80.0, op0=ALU.min, op1=ALU.max), reads=[K("dec")], writes=[K("dec")])
                    yield
                    S.op("act", lambda e: e.activation(out=P["dec"], in_=P["d1"], func=AF.Exp), reads=[K("dec")], writes=[K("dec")])
                    yield
                    S.op("act", lambda e: e.activation(out=P["gam"], in_=psG, func=AF.Exp), reads=[kG], writes=[K("gam")])
                    dec4 = P["dec"].rearrange("p (d v c) -> p d v c", d=2, v=2)
                    psA4 = psA.rearrange("p (d c) -> p d c", d=2).unsqueeze(2).to_broadcast([128, 2, 2, 128])
                    psB4 = psB.rearrange("p (d c) -> p d c", d=2).unsqueeze(2).to_broadcast([128, 2, 2, 128])
                    yield
                    S.op("dve", lambda e, psA4=psA4, dec4=dec4: e.tensor_tensor(out=P["tmp"].rearrange("p (d v c) -> p d v c", d=2, v=2), in0=psA4, in1=dec4, op=ALU.mult),
                         reads=[kAB, K("dec")], writes=[K("tmp")])
                    yield
                    S.op("pool", lambda e: e.tensor_tensor(out=xn4[:, :, 0, :], in0=tmp3, in1=bm3, op=ALU.mult), reads=[K("tmp"), K("bm")], writes=[kxn])
                    S.op("pool", lambda e: e.tensor_tensor(out=xn4[:, :, 0, :], in0=xn4[:, :, 0, :], in1=C.ident.unsqueeze(1).to_broadcast([128, 4, 128]), op=ALU.subtract),
                         reads=[kxn, "ident"], writes=[kxn])
                    yield
                    S.op("dve", lambda e, psB4=psB4, dec4=dec4: e.tensor_tensor(out=P["tmp2"].rearrange("p (d v c) -> p d v c", d=2, v=2), in0=psB4, in1=dec4, op=ALU.mult),
                         reads=[kAB, K("dec")], writes=[K("tmp2")])
                    yield
                    S.op("pool", lambda e: e.tensor_tensor(out=intra3, in0=tmp23, in1=mincl3, op=ALU.mult), reads=[K("tmp2"), "mincl"], writes=[K("intra")])
                    yield
                    for d in range(2):
                        qb_ = QT[:, blks[d] * 128:(blks[d] + 1) * 128]
                        S.op("pool", lambda e, d=d, qb_=qb_: e.tensor_tensor(out=gq3[:, 2 * d:2 * d + 2, :], in0=qb_.unsqueeze(1).to_broadcast([128, 2, 128]), in1=gam3[:, 2 * d:2 * d + 2, :], op=ALU.mult),
                             reads=[hk + "q", K("gam")], writes=[K("gq")])
                        S.op("pool", lambda e, d=d: e.tensor_tensor(out=kd3[:, 2 * d:2 * d + 2, :], in0=kt3[:, blks[d], :].unsqueeze(1).to_broadcast([128, 2, 128]),
                                                                    in1=e12[:, 4 + 2 * d:6 + 2 * d].unsqueeze(2).to_broadcast([128, 2, 128]), op=ALU.mult),
                             reads=["ktok", K("e12")], writes=[K("kd")])
                    xn4 = P["xn"].rearrange("p (j o c) -> p j o c", j=4, o=2)
                    rt4 = P["rt"].rearrange("p (j o c) -> p j o c", j=4, o=2)
                    yy4 = P["yy"].rearrange("p (j o c) -> p j o c", j=4, o=2)
                    psM4 = psM.rearrange("p (j o c) -> p j o c", j=4, o=2)
                    kxn, krt_, kyy = K("xn"), K("rt"), K("yy")
                    yield
                    pass
                    yield
                    for j in range(4):
                        S.op("pe", lambda e, j=j: e.transpose(out=psTb[:, j * 128:(j + 1) * 128], in_=xn4[:, j, 0, :], identity=C.ident), reads=[kxn, "ident"], writes=[kT])
                    yield
                    S.op("act", lambda e: e.activation(out=xn4[:, :, 1, :], in_=_v3(psTb[:, 0:512], 4), func=AF.Copy), reads=[kT], writes=[kxn])
                    def lmv(lv):
                        return lm4[:, lv, :, :].unsqueeze(2).to_broadcast([128, 2, 2, 256])
                    v4 = lambda ap: ap.rearrange("p (d v x) -> p d v x", d=2, v=2)
                    yield
                    S.op("dve", lambda e: e.tensor_tensor(out=v4(P["rt"]), in0=v4(P["xn"]), in1=lmv(0), op=ALU.mult), reads=[kxn, "lmask"], writes=[krt_])
                    for lv in range(1, 7):
                        yield
                        for j in range(4):
                            S.op("pe", lambda e, j=j: e.matmul(psM4[:, j, 0, :], lhsT=xn4[:, j, 1, :], rhs=rt4[:, j, 0, :], start=True, stop=True), reads=[kxn, krt_], writes=[kM])
                            S.op("pe", lambda e, j=j: e.matmul(psM4[:, j, 1, :], lhsT=xn4[:, j, 0, :], rhs=rt4[:, j, 1, :], start=True, stop=True), reads=[kxn, krt_], writes=[kM])
                        yield
                        S.op("dve", lambda e, lv=lv: e.tensor_tensor(out=v4(P["yy"]), in0=v4(psM), in1=lmv(lv), op=ALU.mult), reads=[kM, "lmask"], writes=[kyy])
                        yield
                        for j in range(4):
                            S.op("pe", lambda e, j=j: e.matmul(psM4[:, j, 0, :], lhsT=rt4[:, j, 1, :], rhs=yy4[:, j, 0, :], start=True, stop=True), reads=[kyy, krt_], writes=[kM])
                            if lv < 6:
                                S.op("pe", lambda e, j=j: e.matmul(psM4[:, j, 1, :], lhsT=rt4[:, j, 0, :], rhs=yy4[:, j, 1, :], start=True, stop=True), reads=[kyy, krt_], writes=[kM])
                        yield
                        if lv < 6:
                            S.op("act", lambda e: e.activation(out=P["rt"], in_=psM, func=AF.Copy), reads=[kM], writes=[krt_])
                        else:
                            S.op("act", lambda e: e.activation(out=rt4[:, :, 0, :], in_=psM4[:, :, 0, :], func=AF.Copy), reads=[kM], writes=[krt_])
                    RT = rt4
                    krt = krt_
                    return
                yield
                for j in range(4):
                    d = j // 2
                    kb_ = KT[:, blks[d] * 128:(blks[d] + 1) * 128]
                    S.op("pe", lambda e, j=j, kb_=kb_: e.matmul(psS1[:, j * 128:(j + 1) * 128], lhsT=kb_, rhs=S4b3[:, j, :], start=True, stop=True), reads=[hk + "k", "S4b"], writes=[kS1])
                yield
                S.op("dve", lambda e: e.tensor_tensor(out=t33, in0=_v3(psS1, 4), in1=e12[:, 0:4].unsqueeze(2).to_broadcast([128, 4, 128]), op=ALU.mult),
                     reads=[kS1, K("e12")], writes=[K("t3")])
                yield
                for d in range(2):
                    S.op("pool" if d == 0 else "dve", lambda e, d=d: e.tensor_tensor(out=vd3[:, 2 * d:2 * d + 2, :], in0=t33[:, 2 * d:2 * d + 2, :], in1=vt4[:, blks[d], :, :], op=ALU.subtract),
                         reads=[K("t3"), "vtok"], writes=[K("vd")])
                yield
                for j in range(4):
                    S.op("pe", lambda e, j=j: e.matmul(psS2[:, j * 128:(j + 1) * 128], lhsT=RT[:, j, 0, :], rhs=vd3[:, j, :], start=True, stop=True), reads=[krt, K("vd")], writes=[kS2])
                yield
                S.op("dve", lambda e: e.tensor_tensor(out=vn3, in0=_v3(psS2, 4), in1=negb.unsqueeze(2).to_broadcast([128, 4, 128]), op=ALU.mult),
                     reads=[kS2, K("negb")], writes=[K("vn")])
                if s >= 2:
                    for j in range(4):
                        S.op("pe", lambda e, j=j: e.matmul(psS1[:, j * 128:(j + 1) * 128], lhsT=S4b3[:, j, :], rhs=gq3[:, j, :], start=True, stop=False), reads=["S4b", K("gq")], writes=[kS1])
                        S.op("pe", lambda e, j=j: e.matmul(psS1[:, j * 128:(j + 1) * 128], lhsT=vn3[:, j, :], rhs=intra3[:, j, :], start=False, stop=True), reads=[K("vn"), K("intra")], writes=[kS1])
                    for d in range(2):
                        dstv = o3[:, :, blks[d] * 128:(blks[d] + 1) * 128]
                        srcv = _v3(psS1[:, d * 256:(d + 1) * 256], 2)
                        if s <= 9:
                            S.op("act", lambda e, dstv=dstv, srcv=srcv: e.activation(out=dstv, in_=srcv, func=AF.Copy), reads=[kS1], writes=["oacc"])
                        else:
                            S.op("dve", lambda e, dstv=dstv, srcv=srcv: e.tensor_tensor(out=dstv, in0=srcv, in1=dstv, op=ALU.add), reads=[kS1, "oacc"], writes=["oacc"])
                if s < NTT - 1:
                    for j in range(4):
                        S.op("pe", lambda e, j=j: e.matmul(psS2[:, j * 128:(j + 1) * 128], lhsT=kd3[:, j, :], rhs=vn3[:, j, :], start=True, stop=True), reads=[K("kd"), K("vn")], writes=[kS2])
                    S.op("pool", lambda e: e.tensor_tensor(out=t43, in0=S43, in1=e12[:, 8:12].unsqueeze(2).to_broadcast([128, 4, 128]), op=ALU.mult),
                         reads=["S4", K("e12")], writes=[K("t4")])
                    S.op("dve", lambda e: e.tensor_tensor(out=S4, in0=psS2, in1=P["t4"], op=ALU.add), reads=[kS2, K("t4")], writes=["S4"])
                    S.op("act", lambda e: e.activation(out=S4b, in_=S4, func=AF.Copy), reads=["S4"], writes=["S4b"])
            nst_ = getattr(C, "p6_nsteps", NTT)
            yield from step(0, "A")
            for s_ in range(nst_):
                sub = [step(s_, "S")] + ([step(s_ + 1, "A")] if s_ + 1 < nst_ else [])
                while sub:
                    for g_ in list(sub):
                        try:
                            next(g_)
                            yield
                        except StopIteration:
                            sub.remove(g_)
            for vh in range(2):
                S.op("sp", lambda e, vh=vh, g=g: e.dma_start(out=szb1, in_=F1[b, 8192 + (2 * g + vh) * 128:8192 + (2 * g + vh + 1) * 128, 0:SEQ]),
                     reads=["F1"], writes=["szb1"], dma=1)
                for gi in range(4):
                    yield
                    t0 = gi * 512
                    sl = gi % 2
                    a_ = o3[:, vh, t0:t0 + 512]
                    S.op("pool", lambda e, a_=a_, sl=sl: e.tensor_tensor(out=sqb[sl], in0=a_, in1=a_, op=ALU.mult), reads=["oacc"], writes=["sqb6"])
                    S.op("pe", lambda e, sl=sl: e.matmul(psS1, lhsT=C.ones, rhs=sqb[sl], start=True, stop=True), reads=["ones", "sqb6"], writes=[kS1])
                    S.op("act", lambda e, sl=sl: e.activation(out=lnv[sl], in_=psS1, func=AF.Ln, scale=1.0 / 128, bias=C.eps_t[:, 0:1]), reads=[kS1], writes=["lnv6"])
                    S.op("act", lambda e, sl=sl: e.activation(out=rst[sl], in_=lnv[sl], func=AF.Exp, scale=-0.5), reads=["lnv6"], writes=["lnv6"])
                    S.op("dve", lambda e, sl=sl, a_=a_: e.scalar_tensor_tensor(out=osum[sl], in0=a_, scalar=onorm[:, 0:1], in1=rst[sl], op0=ALU.mult, op1=ALU.mult),
                         reads=["oacc", "lnv6", "onorm"], writes=["osum6"])
                    S.op("pool", lambda e, sl=sl, t0=t0: e.tensor_tensor(out=szb1[:, t0:t0 + 512], in0=osum[sl], in1=szb1[:, t0:t0 + 512], op=ALU.mult),
                         reads=["osum6", "szb1"], writes=["szb1"])
                r0 = (2 * g + vh) * 128
                S.op("sp", lambda e, r0=r0: e.dma_start(out=OG1[b, r0:r0 + 128, :], in_=szb1), reads=["szb1"], writes=["OG1"], dma=1)

    heads_all = list(getattr(C, "p6_heads", range(16)))
    gens = [run_slot(0, heads_all[0::2]), run_slot(1, heads_all[1::2])]
    for _ in range(getattr(C, "p6_offset", 0)):
        try:
            next(gens[0])
        except StopIteration:
            gens.pop(0)
            break
    while gens:
        for g_ in list(gens):
            try:
                next(g_)
            except StopIteration:
                gens.remove(g_)
    S.barrier()


def phase7(C):
    nc, S, A = C.nc, C.S, C.A
    for b in range(NB):
        def res(tt, cb, b=b):
            return C.dram["X1"][b, tt * 128:(tt + 1) * 128, cb * 256:(cb + 1) * 256]

        def dst(tt, cb, b=b):
            return C.dram["X2"][b, tt * 128:(tt + 1) * 128, cb * 256:(cb + 1) * 256]
        out_proj(C, b, 1, C.dram["OG1"][b], 32, C.dram["dn_w_out"], res, dst, list(range(16)), "p7", cbw=256, ntok=SEQ)
    A.reset()
    fn = A.alloc(D)
    S.op("sp", lambda e: e.dma_start(out=fn, in_=C.dram["final_norm"].unsqueeze(0).to_broadcast([128, D])), writes=["fn"], dma=1)
    xr = [A.alloc(D) for _ in range(3)]
    xo = [A.alloc(D) for _ in range(3)]
    junk = A.alloc(D, BF16)
    st = [A.alloc(4) for _ in range(3)]
    it = 0
    for b in range(NB):
        for tt in range(16):
            sl = it % 3
            it += 1
            xt, xo_, s4 = xr[sl], xo[sl], st[sl]
            S.op("sp", lambda e, xt=xt, b=b, tt=tt: e.dma_start(out=xt, in_=C.dram["X2"][b, tt * 128:(tt + 1) * 128, :]), reads=["X2"], writes=["fxr%d" % sl], dma=1)
            S.op("act", lambda e, xt=xt, s4=s4: e.activation(out=junk, in_=xt, func=AF.Square, accum_out=s4[:, 0:1]), reads=["fxr%d" % sl], writes=["fjunk", "fst%d" % sl])
            S.op("act", lambda e, s4=s4: e.activation(out=s4[:, 1:2], in_=s4[:, 0:1], func=AF.Sqrt, scale=1.0 / D, bias=C.eps_t[:, 0:1]), reads=["fst%d" % sl], writes=["fst%d" % sl])
            S.op("dve", lambda e, s4=s4: e.reciprocal(out=s4[:, 2:3], in_=s4[:, 1:2]), reads=["fst%d" % sl], writes=["fst%d" % sl])
            S.op("dve", lambda e, xt=xt, xo_=xo_, s4=s4: e.scalar_tensor_tensor(out=xo_, in0=xt, scalar=s4[:, 2:3], in1=fn, op0=ALU.mult, op1=ALU.mult),
                 reads=["fxr%d" % sl, "fst%d" % sl, "fn"], writes=["fxo%d" % sl])
            S.op("sp", lambda e, xo_=xo_, b=b, tt=tt: e.dma_start(out=C.dram["OUT"][b, tt * 128:(tt + 1) * 128, :], in_=xo_), reads=["fxo%d" % sl], writes=["OUT"], dma=1)
    S.barrier()


NCORES = 8
_PHASES = (phase0, phase1, phase2, phase3, phase4, phase5, phase6, phase7)


def _host_inputs(inp):
    cos, sin = rope_tables()
    per_m, nt, mask, drow, dcol = na_geometry()
    rpb = np.asarray(inp["ab_rpb"][0], np.float32)
    nab = np.stack([rpb[h][drow, dcol] for h in range(8)], 0).astype(np.float32)
    mi, ms, id4 = dn_masks()
    lm, id8 = dn_level_masks()
    f = lambda a: np.ascontiguousarray(np.asarray(a, np.float32))
    shared = {
        "w_mod0": f(inp["ab_w_mod"][0]), "w_mod1": f(inp["dn_w_mod"][0]), "b_mod0": f(inp["ab_b_mod"][0]), "b_mod1": f(inp["dn_b_mod"][0]),
        "ab_norm": f(inp["ab_norm"][0]), "ab_w_in": f(inp["ab_w_in"][0]), "ab_w_qb": f(inp["ab_w_qb"][0]), "ab_w_kvb": f(inp["ab_w_kvb"][0]),
        "ab_q_norm": f(inp["ab_q_norm"][0]), "ab_kv_norm": f(inp["ab_kv_norm"][0]), "ab_w_out": f(inp["ab_w_out"][0]),
        "na_mask": mask, "na_bias": nab, "rope_cos": cos, "rope_sin": sin, "ident": np.eye(128, dtype=np.float32).astype(NPBF),
        "dn_norm": f(inp["dn_norm"][0]), "dn_w_in": f(inp["dn_w_in"][0]), "dn_conv": f(inp["dn_conv"][0]), "dn_a_log": f(inp["dn_a_log"][0]),
        "dn_dt_bias": f(inp["dn_dt_bias"][0]), "dn_o_norm": f(inp["dn_o_norm"][0]), "dn_w_out": f(inp["dn_w_out"][0]), "final_norm": f(inp["final_norm"]),
        "dn_mincl": mi, "dn_mstrict": ms, "dn_ident4": id4.astype(NPBF), "dn_lmask2": dn_level_masks2().astype(NPBF),
    }
    maps = []
    for i in range(NCORES):
        m = dict(shared)
        m["x"] = f(inp["x"][NB * i:NB * (i + 1)])
        m["ctx"] = f(inp["ctx"][NB * i:NB * (i + 1)])
        m["cvec"] = np.concatenate([f(inp["c"][NB * i:NB * (i + 1)]), f(inp["c_ctx"])[None]], 0)
        maps.append(m)
    return maps


_INTERNAL = {
    "mod": ([2, 3, 6144], F32), "F0": ([NB, 5184, T], BF16), "VA": ([NB, T, 1024], BF16), "M0": ([NB, 2560, T], BF16), "VM": ([NB, T, 1024], BF16),
    "OG0": ([NB, 2048, T], BF16), "X1": ([NB, T, D], F32), "F1": ([NB, 12288, T], BF16), "BA": ([NB, T, 128], F32), "OG1": ([NB, 4096, SEQ], BF16),
    "X2": ([NB, SEQ, D], F32),
}


def build_program(maps0, phases=_PHASES):
    nc = bass.Bass("TRN2", target_bir_lowering=False)
    with ExitStack() as es:
        dram = {}
        for nm, a in maps0.items():
            dram[nm] = nc.dram_tensor(nm, list(a.shape), BF16 if a.dtype == NPBF else F32, kind="ExternalInput").ap()
        for nm, (shape, dt_) in _INTERNAL.items():
            dram[nm] = nc.dram_tensor(nm, shape, dt_, kind="Internal").ap()
        dram["OUT"] = nc.dram_tensor("OUT", [NB, SEQ, D], F32, kind="ExternalOutput").ap()
        C = make_ctx(nc, es, dram)
        for p in phases:
            p(C)
        C.S.finalize()
    return nc


def kernel(**inputs):
    maps = _host_inputs(inputs)
    nc = build_program(maps[0])
    res = run_bass_kernel_spmd(nc, maps, core_ids=list(range(NCORES)))
    out = np.concatenate([np.asarray(r["OUT"], np.float32) for r in res.results], axis=0)
    return out
```

```python
import numpy as np
import ml_dtypes
from contextlib import ExitStack
import concourse.bass as bass
import concourse.mybir as mybir
from concourse.bass_utils import run_bass_kernel_spmd

F32 = mybir.dt.float32
BF16 = mybir.dt.bfloat16
AF = mybir.ActivationFunctionType
ALU = mybir.AluOpType
NPBF = ml_dtypes.bfloat16

D = 2048
SEQ = 2048
CTX = 256
T = SEQ + CTX
NTT = T // 128
NB = 2
EPS = 1e-6
TG = [(0, 512), (512, 512), (1024, 512), (1536, 512), (2048, 256)]


class Op:
    __slots__ = ("eng", "fn", "deps", "marked", "val", "sem", "is_dma")


class Buf:
    __slots__ = ("w", "r")

    def __init__(self):
        self.w = None
        self.r = []


class Sched:
    ENGS = ("pe", "act", "dve", "pool", "sp")
    ENGOBJ = {"pe": "tensor", "act": "scalar", "dve": "vector", "pool": "gpsimd", "sp": "sync"}

    def __init__(self, nc, es):
        self.nc = nc
        self.es = es
        self.ops = {e: [] for e in self.ENGS}
        self.bufs = {}
        self.sems = {e: es.enter_context(nc.semaphore("s_" + e)) for e in self.ENGS}
        self.dsems = {}
        self.dpool = []
        self.last_dma = {}
        self.nops = 0

    DRAMKEYS = {"mod", "F0", "VA", "M0", "VM", "OG0", "X1", "X2", "F1", "BA", "OG1", "OUT", "ST"}

    def dsem(self, key):
        if key not in self.dsems:
            i = len(self.dsems)
            if i >= len(self.dpool):
                self.dpool.append([self.es.enter_context(self.nc.semaphore("d_%d" % i)), 0])
            self.dsems[key] = self.dpool[i]
        return self.dsems[key]

    def op(self, eng, fn, reads=(), writes=(), dma=None):
        o = Op()
        o.eng = eng
        o.fn = fn
        o.deps = []
        o.marked = False
        o.val = None
        o.sem = None
        o.is_dma = dma is not None
        self.nops += 1
        if dma is not None:
            dk = None
            for k in list(writes) + list(reads):
                if not (k in self.DRAMKEYS or k.startswith("DR_")):
                    dk = k
                    break
            assert dk is not None, (reads, writes)
            d = self.dsem(dk)
            d[1] += 16
            o.sem = d[0]
            o.val = d[1]
            o.marked = True
            self.last_dma[id(d)] = o
        deps = {}
        for k in reads:
            b = self.bufs.get(k)
            if b is None:
                b = self.bufs[k] = Buf()
            if b.w is not None:
                deps[id(b.w)] = b.w
        for k in writes:
            b = self.bufs.get(k)
            if b is None:
                b = self.bufs[k] = Buf()
            if b.w is not None:
                deps[id(b.w)] = b.w
            for r in b.r:
                deps[id(r)] = r
        for k in reads:
            self.bufs[k].r.append(o)
        for k in writes:
            b = self.bufs[k]
            b.w = o
            b.r = []
        for d in deps.values():
            if d is o:
                continue
            if d.eng == "pe" and eng == "pe" and not d.is_dma:
                continue
            d.marked = True
            o.deps.append(d)
        self.ops[eng].append(o)
        return o

    def barrier(self):
        lasts = []
        for e in self.ENGS:
            for o in reversed(self.ops[e]):
                if not o.is_dma and o.fn is not None:
                    o.marked = True
                    lasts.append(o)
                    break
        lasts += list(self.last_dma.values())
        for e in self.ENGS:
            o = Op()
            o.eng = e
            o.fn = None
            o.deps = list(lasts)
            o.marked = False
            o.val = None
            o.sem = None
            o.is_dma = False
            self.ops[e].append(o)
        self.bufs = {}
        self.dsems = {}

    def finalize(self):
        for e in self.ENGS:
            c = 0
            for o in self.ops[e]:
                if o.is_dma or o.fn is None:
                    continue
                if o.marked:
                    c += 1
                    o.val = c
                    o.sem = self.sems[e]
        nc = self.nc
        with nc.Block() as block:
            for e in self.ENGS:
                ops = self.ops[e]

                def body(engine, ops=ops, e=e):
                    seen = {}
                    for o in ops:
                        for d in o.deps:
                            k = id(d.sem)
                            if seen.get(k, 0) >= d.val:
                                continue
                            seen[k] = d.val
                            engine.wait_ge(d.sem, d.val)
                        if o.fn is None:
                            continue
                        ins = o.fn(engine)
                        if o.is_dma:
                            ins.then_inc(o.sem, 16)
                        elif o.marked:
                            ins.then_inc(o.sem, 1)
                    if e == "sp":
                        for (s, v) in self.dpool:
                            if v > 0:
                                engine.wait_ge(s, v)

                getattr(block, self.ENGOBJ[e])(body)


class Arena:
    def __init__(self, nc, es, nwords=51200):
        self.t = es.enter_context(nc.sbuf_tensor("arena", [128, nwords], F32))
        self.n = nwords
        self.off = 0
        self.uid = 0

    def reset(self):
        self.off = 0

    def alloc(self, nelem, dtype=F32):
        nbytes = nelem * (4 if dtype == F32 else 2)
        words = (nbytes + 31) // 32 * 8
        assert self.off + words <= self.n, "SBUF arena overflow %d+%d" % (self.off, words)
        ap = self.t[:, self.off:self.off + words]
        self.off += words
        if dtype != F32:
            ap = ap.bitcast(dtype)
        return ap[:, 0:nelem]

    def key(self, name):
        self.uid += 1
        return "%s#%d" % (name, self.uid)


class Ctx:
    pass


def _v3(ap, a):
    return ap.rearrange("p (a b) -> p a b", a=a)


def phase0(C):
    nc, S, A = C.nc, C.S, C.A
    A.reset()
    csT = A.alloc(48)
    cs3 = _v3(csT, 16)
    bm = A.alloc(6144)
    osb = [A.alloc(2048), A.alloc(2048)]
    wr = [A.alloc(2048) for _ in range(4)]
    ps = C.ps
    for v in range(3):
        S.op("sp", lambda e, v=v: e.dma_start(out=cs3[:, :, v], in_=C.dram["cvec"][v, :].rearrange("(k p) -> p k", p=128),
                                              allow_slow_non_contiguous=True), writes=["csT"], dma="p0c")
    S.op("act", lambda e: e.activation(out=csT, in_=csT, func=AF.Silu), reads=["csT"], writes=["csT"])
    it = 0
    oi = 0
    for l in range(2):
        wm = C.dram["w_mod%d" % l]
        bmod = C.dram["b_mod%d" % l]
        S.op("sp", lambda e, bmod=bmod: e.dma_start(out=bm[0:3, :], in_=bmod.unsqueeze(0).to_broadcast([3, 6144])),
             writes=["bm"], dma="p0b")
        for g in range(3):
            for k in range(16):
                slot = it % 4
                it += 1
                wt = wr[slot]
                S.op("sp", lambda e, wt=wt, k=k, g=g, wm=wm: e.dma_start(out=wt, in_=wm[k * 128:(k + 1) * 128, g * 2048:(g + 1) * 2048]),
                     writes=["p0w%d" % slot], dma="p0w%d" % slot)
                for n in range(4):
                    S.op("pe", lambda e, wt=wt, k=k, n=n: e.matmul(ps[0:3, n * 512:(n + 1) * 512], lhsT=cs3[:, k, :], rhs=wt[:, n * 512:(n + 1) * 512],
                                                                    start=(k == 0), stop=(k == 15)),
                         reads=["csT", "p0w%d" % slot], writes=["p0ps%d" % n])
            ob = osb[oi % 2]
            okey = "p0o%d" % (oi % 2)
            oi += 1
            for n in range(4):
                S.op("dve", lambda e, ob=ob, n=n, g=g: e.tensor_tensor(out=ob[0:3, n * 512:(n + 1) * 512], in0=ps[0:3, n * 512:(n + 1) * 512],
                                                                       in1=bm[0:3, g * 2048 + n * 512:g * 2048 + (n + 1) * 512], op=ALU.add),
                     reads=["p0ps%d" % n, "bm"], writes=[okey])
            S.op("sp", lambda e, ob=ob, l=l, g=g: e.dma_start(out=C.dram["mod"][l, :, g * 2048:(g + 1) * 2048], in_=ob[0:3, :]),
                 reads=[okey], writes=["mod"], dma="p0o")
    S.barrier()


def load_modvecs(C, l, b, gain, tag):
    S, A = C.S, C.A
    g = A.alloc(16)
    S.op("sp", lambda e: e.dma_start(out=g, in_=gain.rearrange("(k p) -> p k", p=128), allow_slow_non_contiguous=True),
         writes=[tag + "g"], dma=tag + "v")
    res = []
    for vi, v in enumerate((b, 2)):
        sc = A.alloc(16)
        sh = A.alloc(16)
        S.op("sp", lambda e, sc=sc, v=v: e.dma_start(out=sc, in_=C.dram["mod"][l, v, 2048:4096].rearrange("(k p) -> p k", p=128),
                                                     allow_slow_non_contiguous=True), reads=["mod"], writes=[tag + "sc%d" % vi], dma=tag + "v")
        S.op("sp", lambda e, sh=sh, v=v: e.dma_start(out=sh, in_=C.dram["mod"][l, v, 0:2048].rearrange("(k p) -> p k", p=128),
                                                     allow_slow_non_contiguous=True), reads=["mod"], writes=[tag + "sh%d" % vi], dma=tag + "v")
        S.op("dve", lambda e, sc=sc: e.scalar_tensor_tensor(out=sc, in0=sc, scalar=1.0, in1=g, op0=ALU.add, op1=ALU.mult),
             reads=[tag + "sc%d" % vi, tag + "g"], writes=[tag + "sc%d" % vi])
        res.append((sc, sh, tag + "sc%d" % vi, tag + "sh%d" % vi))
    return res


def build_xmT(C, xmT, xkey, src_tiles, mv, tag):
    S, A = C.S, C.A
    xr = [A.alloc(D) for _ in range(2)]
    xh = [A.alloc(D, BF16) for _ in range(2)]
    junk = A.alloc(D, BF16)
    st = [A.alloc(4) for _ in range(2)]
    xm3 = _v3(xmT, 16)
    for tt in range(NTT):
        sl = tt % 2
        xt, xb, s4 = xr[sl], xh[sl], st[sl]
        kx, kb_, ks = tag + "x%d" % sl, tag + "xh%d" % sl, tag + "st%d" % sl
        ms, sh, kms, ksh = mv[0] if tt < 16 else mv[1]
        S.op("sp", lambda e, xt=xt, tt=tt: e.dma_start(out=xt, in_=src_tiles[tt]), writes=[kx], dma=kx)
        S.op("act", lambda e, xt=xt, s4=s4: e.activation(out=junk, in_=xt, func=AF.Square, accum_out=s4[:, 0:1]),
             reads=[kx], writes=[tag + "junk", ks])
        S.op("act", lambda e, s4=s4: e.activation(out=s4[:, 1:2], in_=s4[:, 0:1], func=AF.Sqrt, scale=1.0 / D, bias=C.eps_t[:, 0:1]),
             reads=[ks], writes=[ks])
        S.op("dve", lambda e, s4=s4: e.reciprocal(out=s4[:, 2:3], in_=s4[:, 1:2]), reads=[ks], writes=[ks])
        S.op("dve", lambda e, xt=xt, xb=xb, s4=s4: e.tensor_scalar(out=xb, in0=xt, scalar1=s4[:, 2:3], scalar2=None, op0=ALU.mult),
             reads=[kx, ks], writes=[kb_])
        pb = C.ps[:, (tt % 2) * 1024:(tt % 2) * 1024 + 1024].bitcast(BF16)
        kp = tag + "tp%d" % (tt % 2)
        for k in range(16):
            S.op("pe", lambda e, pb=pb, xb=xb, k=k: e.transpose(out=pb[:, k * 128:(k + 1) * 128], in_=xb[:, k * 128:(k + 1) * 128], identity=C.ident),
                 reads=[kb_, "ident"], writes=[kp])
        for k in range(16):
            o_ = xm3[:, k, tt * 128:(tt + 1) * 128]
            i_ = pb[:, k * 128:(k + 1) * 128]
            if k % 2 == 0:
                S.op("act", lambda e, o_=o_, i_=i_, k=k, ms=ms, sh=sh: e.activation(out=o_, in_=i_, func=AF.Identity, bias=sh[:, k:k + 1], scale=ms[:, k:k + 1]),
                     reads=[kp, kms, ksh], writes=[xkey + str(tt)])
            else:
                S.op("dve", lambda e, o_=o_, i_=i_, k=k, ms=ms, sh=sh: e.tensor_scalar(out=o_, in0=i_, scalar1=ms[:, k:k + 1], scalar2=sh[:, k:k + 1], op0=ALU.mult, op1=ALU.add),
                     reads=[kp, kms, ksh], writes=[xkey + str(tt)])


def proj_fm(C, xm3, xkey, w3, wkey, ncols, evac, tag, m_off=0):
    S = C.S
    for sb in range((ncols + 127) // 128):
        m = min(128, ncols - sb * 128)
        for gi, (t0, tn) in enumerate(TG):
            bank = 4 + (C.pcnt % 4)
            C.pcnt += 1
            pa = C.ps[0:m, bank * 512:bank * 512 + tn]
            pk = "psb%d" % bank
            for k in range(16):
                S.op("pe", lambda e, pa=pa, k=k, sb=sb, m=m, t0=t0, tn=tn: e.matmul(pa, lhsT=w3[:, k, sb * 128:sb * 128 + m], rhs=xm3[:, k, t0:t0 + tn],
                                                                                  start=(k == 0), stop=(k == 15)),
                     reads=[wkey] + [xkey + str(t) for t in range(t0 // 128, (t0 + tn) // 128)], writes=[pk])
            evac(sb, gi, t0, tn, pa, pk, m)


def phase1(C):
    nc, S, A = C.nc, C.S, C.A
    w_in = C.dram["ab_w_in"]
    segs = [("qa", 0, 512), ("qa", 512, 512), ("ka", 1024, 512), ("ka", 1536, 512), ("va", 2048, 512), ("va", 2560, 512),
            ("cq", 3072, 512), ("ckv", 3584, 512), ("kr", 4096, 64), ("krsw", 4096, 64),
            ("z", 4160, 512), ("z", 4672, 512), ("z", 5184, 512), ("z", 5696, 512)]
    frow = {"qa": 0, "ka": 1024 - 1024, "cq": 2048 - 3072, "ckv": 2560 - 3584, "z": 3072 - 4160}
    for b in range(NB):
        _phase1_b(C, b, segs, frow)


def _phase1_b(C, b, segs, frow):
    nc, S, A = C.nc, C.S, C.A
    w_in = C.dram["ab_w_in"]
    if True:
        A.reset()
        mv = load_modvecs(C, 0, b, C.dram["ab_norm"], "p1m%d" % b)
        xmT = A.alloc(16 * T, BF16)
        xm3 = _v3(xmT, 16)
        xkey = "xmT"
        mark = A.off
        tiles = [C.dram["x"][b, tt * 128:(tt + 1) * 128, :] for tt in range(16)] + [C.dram["ctx"][b, tt * 128:(tt + 1) * 128, :] for tt in range(2)]
        build_xmT(C, xmT, xkey, tiles, mv, "p1b")
        pass
        wr = [A.alloc(16 * 512, BF16) for _ in range(2)]
        stg = [A.alloc(T, BF16) for _ in range(3)]
        vst = [A.alloc(512, BF16) for _ in range(2)]
        krp = A.alloc(T)
        kro = A.alloc(T, BF16)
        cos_t = A.alloc(SEQ)
        sin_t = A.alloc(SEQ)
        tmpf = A.alloc(512)
        tmpg = A.alloc(512)
        S.op("sp", lambda e: e.dma_start(out=cos_t[0:64, :], in_=C.dram["rope_cos"]), writes=["cos"], dma="p1c")
        S.op("sp", lambda e: e.dma_start(out=sin_t[0:64, :], in_=C.dram["rope_sin"]), writes=["sin"], dma="p1c")
        F0 = C.dram["F0"]
        VA = C.dram["VA"]
        sc = [0]
        for wi, (nm, c0, ncw) in enumerate(segs):
            slot = wi % 2
            w3 = _v3(wr[slot], 16)[:, :, 0:ncw]
            wkey = "p1w%d" % slot
            if nm == "krsw":
                for (d0, s0) in ((0, 16), (16, 0), (32, 48), (48, 32)):
                    S.op("pool", lambda e, w3=w3, d0=d0, s0=s0: e.dma_start(out=w3[:, :, d0:d0 + 16],
                                                                           in_=w_in[:, 4096 + s0:4096 + s0 + 16].rearrange("(k p) c -> p k c", p=128)),
                         writes=[wkey], dma=wkey)
            else:
                S.op("pool", lambda e, w3=w3, c0=c0, ncw=ncw: e.dma_start(out=w3, in_=w_in[:, c0:c0 + ncw].rearrange("(k p) c -> p k c", p=128)),
                     writes=[wkey], dma=wkey)
            if nm == "va":
                for tt in range(NTT):
                    bank = 4 + (C.pcnt % 4)
                    C.pcnt += 1
                    pa = C.ps[:, bank * 512:bank * 512 + 512]
                    pk = "psb%d" % bank
                    for k in range(16):
                        S.op("pe", lambda e, pa=pa, k=k, tt=tt, w3=w3: e.matmul(pa, lhsT=xm3[:, k, tt * 128:(tt + 1) * 128], rhs=w3[:, k, :], start=(k == 0), stop=(k == 15)),
                             reads=[wkey, xkey + str(tt)], writes=[pk])
                    vs = vst[tt % 2]
                    vk = "p1vs%d" % (tt % 2)
                    eng = "act" if tt % 2 == 0 else "dve"
                    if eng == "act":
                        S.op("act", lambda e, vs=vs, pa=pa: e.activation(out=vs, in_=pa, func=AF.Copy), reads=[pk], writes=[vk])
                    else:
                        S.op("dve", lambda e, vs=vs, pa=pa: e.tensor_copy(out=vs, in_=pa), reads=[pk], writes=[vk])
                    S.op("sp", lambda e, vs=vs, tt=tt, c0=c0: e.dma_start(out=VA[b, tt * 128:(tt + 1) * 128, c0 - 2048:c0 - 2048 + 512], in_=vs),
                         reads=[vk], writes=["VA"], dma=vk)
                continue

            def evac(sb, gi, t0, tn, pa, pk, m, nm=nm, c0=c0):
                if nm in ("kr", "krsw"):
                    if nm == "kr":
                        S.op("dve", lambda e: e.tensor_copy(out=krp[0:64, t0:t0 + tn], in_=pa), reads=[pk], writes=["krp"])
                    else:
                        if gi < 4:
                            S.op("dve", lambda e: e.tensor_tensor(out=tmpf[0:64, 0:tn], in0=pa, in1=sin_t[0:64, t0:t0 + tn], op=ALU.mult),
                                 reads=[pk, "sin"], writes=["tmpf"])
                            S.op("pool", lambda e: e.tensor_tensor(out=tmpg[0:64, 0:tn], in0=krp[0:64, t0:t0 + tn], in1=cos_t[0:64, t0:t0 + tn], op=ALU.mult),
                                 reads=["krp", "cos"], writes=["tmpg"])
                            S.op("dve", lambda e: e.tensor_tensor(out=kro[0:64, t0:t0 + tn], in0=tmpf[0:64, 0:tn], in1=tmpg[0:64, 0:tn], op=ALU.add),
                                 reads=["tmpf", "tmpg"], writes=["kro"])
                        if gi == 4:
                            S.op("dve", lambda e: e.tensor_copy(out=kro[0:64, t0:t0 + tn], in_=krp[0:64, t0:t0 + tn]), reads=["krp"], writes=["kro"])
                            S.op("sp", lambda e: e.dma_start(out=F0[b, 5120:5184, :], in_=kro[0:64, :]), reads=["kro"], writes=["F0"], dma="p1kr")
                    return
                if gi == 0:
                    sc[0] += 1
                si = sc[0] % 3
                sg = stg[si]
                sk = "p1st%d" % si
                func = AF.Silu if nm == "z" else AF.Copy
                if nm == "z" or (gi % 2 == 0):
                    S.op("act", lambda e: e.activation(out=sg[0:m, t0:t0 + tn], in_=pa, func=func), reads=[pk], writes=[sk])
                else:
                    S.op("dve", lambda e: e.tensor_copy(out=sg[0:m, t0:t0 + tn], in_=pa), reads=[pk], writes=[sk])
                if gi == 4:
                    r0 = c0 + sb * 128 + frow[nm]
                    S.op("sp", lambda e: e.dma_start(out=F0[b, r0:r0 + m, :], in_=sg[0:m, :]), reads=[sk], writes=["F0"], dma=sk)

            proj_fm(C, xm3, xkey, w3, wkey, ncw, evac, "p1")
        S.barrier()


def rope_tables():
    quarter = 16
    inv = (10000.0 ** (-np.arange(quarter, dtype=np.float32) / quarter)).astype(np.float32)
    pos = np.arange(SEQ)
    cos = np.zeros((64, SEQ), np.float32)
    sin = np.zeros((64, SEQ), np.float32)
    for half, p in ((0, pos // 64), (1, pos % 64)):
        ang = p.astype(np.float32)[None, :] * inv[:, None]
        c, s = np.cos(ang), np.sin(ang)
        cos[half * 32:half * 32 + 16] = c
        cos[half * 32 + 16:half * 32 + 32] = c
        sin[half * 32:half * 32 + 16] = -s
        sin[half * 32 + 16:half * 32 + 32] = s
    return cos, sin


def make_ctx(nc, es, dram):
    C = Ctx()
    C.nc = nc
    C.S = Sched(nc, es)
    C.es = es
    C.A = Arena(nc, es)
    C.ps = es.enter_context(nc.psum_tensor("ps", [128, 4096], F32))
    C.dram = dram
    C.pcnt = 0
    C.na_geo = na_geometry()
    C.cst = es.enter_context(nc.sbuf_tensor("cst", [128, 512], F32))
    C.ident = C.cst[:, 0:64].bitcast(BF16)
    C.eps_t = C.cst[:, 64:65]
    C.ones = C.cst[:, 72:136].bitcast(BF16)
    C.S.op("sp", lambda e: e.dma_start(out=C.ident, in_=dram["ident"]), writes=["ident"], dma="cst")
    C.S.op("pool", lambda e: e.memset(C.eps_t, EPS), writes=["eps"])
    C.S.op("pool", lambda e: e.memset(C.ones, 1.0), writes=["ones"])
    C.lnq = C.cst[:, 65:66]
    C.zero_t = C.cst[:, 66:67]
    C.cw = C.cst[:, 136:456]
    C.S.op("pool", lambda e: e.memset(C.lnq, float(np.log(128.0 ** -0.5))), writes=["lnq"])
    C.S.op("pool", lambda e: e.memset(C.zero_t, 0.0), writes=["zero"])
    if "dn_conv" in dram:
        cw3 = C.cw.rearrange("p (blk j) -> p blk j", j=5)
        for j in range(5):
            for q4 in range(8):
                C.S.op("sp", lambda e, j=j, q4=q4: e.dma_start(out=cw3[:, q4 * 8:(q4 + 1) * 8, j], in_=dram["dn_conv"][j, q4 * 1024:(q4 + 1) * 1024].rearrange("(blk p) -> p blk", p=128),
                                                          allow_slow_non_contiguous=True), writes=["cw"], dma="cw")
    C.S.barrier()
    return C


MLA_SCALE = 192.0 ** -0.5
NA_SCALE = 128.0 ** -0.5


def phase2(C):
    for b in range(NB):
        _phase2_b(C, b)


def _phase2_b(C, b):
    nc, S, A = C.nc, C.S, C.A
    A.reset()
    F0, M0, VM = C.dram["F0"], C.dram["M0"], C.dram["VM"]
    wqb = A.alloc(4 * 1536, BF16)
    wqb3 = _v3(wqb, 4)
    wqsw = A.alloc(4 * 512, BF16)
    wqsw4 = wqsw.rearrange("p (k h c) -> p k h c", k=4, h=8)
    wkvb = A.alloc(4 * 2048, BF16)
    wkvb3 = _v3(wkvb, 4)
    wkvb4 = wkvb.rearrange("p (k h c) -> p k h c", k=4, h=8)
    cq = A.alloc(4 * T, BF16)
    ckv = A.alloc(4 * T, BF16)
    cqn = A.alloc(4 * T, BF16)
    ckvn = A.alloc(4 * T, BF16)
    sq = [A.alloc(4 * 512, BF16) for _ in range(2)]
    lnv = [A.alloc(512) for _ in range(2)]
    rst = [A.alloc(512) for _ in range(2)]
    qnm = A.alloc(4)
    kvnm = A.alloc(4)
    cos_t = A.alloc(SEQ)
    sin_t = A.alloc(SEQ)
    stg = [A.alloc(T, BF16) for _ in range(3)]
    vst = [A.alloc(1024, BF16) for _ in range(2)]
    t1 = [A.alloc(512) for _ in range(2)]
    t2 = [A.alloc(512) for _ in range(2)]
    wq_d, wkv_d = C.dram["ab_w_qb"], C.dram["ab_w_kvb"]
    S.op("pool", lambda e: e.dma_start(out=wqb3, in_=wq_d.rearrange("(k p) c -> p k c", p=128)), writes=["wqb"], dma="p2w")
    S.op("pool", lambda e: e.dma_start(out=wkvb3, in_=wkv_d.rearrange("(k p) c -> p k c", p=128)), writes=["wkvb"], dma="p2w")
    wq4 = wq_d.rearrange("(k p) (h c) -> p k h c", p=128, h=8)
    for k in range(4):
        for (d0, s0) in ((0, 16), (16, 0), (32, 48), (48, 32)):
            S.op("pool", lambda e, k=k, d0=d0, s0=s0: e.dma_start(out=wqsw4[:, k, :, d0:d0 + 16], in_=wq4[:, k, :, 128 + s0:128 + s0 + 16]),
                 writes=["wqsw"], dma="p2w")
    S.op("sp", lambda e: e.dma_start(out=_v3(cq, 4), in_=F0[b, 2048:2560, :].rearrange("(k p) t -> p k t", p=128)), reads=["F0"], writes=["cq"], dma="p2a")
    S.op("sp", lambda e: e.dma_start(out=_v3(ckv, 4), in_=F0[b, 2560:3072, :].rearrange("(k p) t -> p k t", p=128)), reads=["F0"], writes=["ckv"], dma="p2a")
    S.op("sp", lambda e: e.dma_start(out=qnm, in_=C.dram["ab_q_norm"].rearrange("(k p) -> p k", p=128), allow_slow_non_contiguous=True), writes=["qnm"], dma="p2a")
    S.op("sp", lambda e: e.dma_start(out=kvnm, in_=C.dram["ab_kv_norm"].rearrange("(k p) -> p k", p=128), allow_slow_non_contiguous=True), writes=["kvnm"], dma="p2a")
    S.op("sp", lambda e: e.dma_start(out=cos_t[0:64, :], in_=C.dram["rope_cos"]), writes=["cos"], dma="p2a")
    S.op("sp", lambda e: e.dma_start(out=sin_t[0:64, :], in_=C.dram["rope_sin"]), writes=["sin"], dma="p2a")
    it = 0
    for (src, skey, nrm, nkey, dst, dkey) in ((cq, "cq", qnm, "qnm", cqn, "cqn"), (ckv, "ckv", kvnm, "kvnm", ckvn, "ckvn")):
        s3, d3 = _v3(src, 4), _v3(dst, 4)
        for gi, (t0, tn) in enumerate(TG):
            sl = it % 2
            it += 1
            q3 = _v3(sq[sl], 4)
            S.op("pool", lambda e, q3=q3, s3=s3, t0=t0, tn=tn: e.tensor_tensor(out=q3[:, :, 0:tn], in0=s3[:, :, t0:t0 + tn], in1=s3[:, :, t0:t0 + tn], op=ALU.mult),
                 reads=[skey], writes=["sq%d" % sl])
            bank = 4 + sl
            pa = C.ps[:, bank * 512:bank * 512 + tn]
            for k in range(4):
                S.op("pe", lambda e, pa=pa, q3=q3, k=k, tn=tn: e.matmul(pa, lhsT=C.ones, rhs=q3[:, k, 0:tn], start=(k == 0), stop=(k == 3)),
                     reads=["ones", "sq%d" % sl], writes=["psb%d" % bank])
            lv, rs = lnv[sl], rst[sl]
            S.op("act", lambda e, lv=lv, pa=pa, tn=tn: e.activation(out=lv[:, 0:tn], in_=pa, func=AF.Ln, scale=1.0 / 512, bias=C.eps_t[:, 0:1]),
                 reads=["psb%d" % bank], writes=["lnv%d" % sl])
            S.op("act", lambda e, lv=lv, rs=rs, tn=tn: e.activation(out=rs[:, 0:tn], in_=lv[:, 0:tn], func=AF.Exp, scale=-0.5),
                 reads=["lnv%d" % sl], writes=["rst%d" % sl])
            for k in range(4):
                S.op("dve", lambda e, d3=d3, s3=s3, k=k, t0=t0, tn=tn, nrm=nrm, rs=rs: e.scalar_tensor_tensor(
                    out=d3[:, k, t0:t0 + tn], in0=s3[:, k, t0:t0 + tn], scalar=nrm[:, k:k + 1], in1=rs[:, 0:tn], op0=ALU.mult, op1=ALU.mult),
                    reads=[skey, nkey, "rst%d" % sl], writes=[dkey])
    cqn3, ckvn3 = _v3(cqn, 4), _v3(ckvn, 4)
    sc = [0]

    def small_proj(lhs_fn, rhs3, rkey, wkey, m, gi, t0, tn):
        bank = 4 + (C.pcnt % 4)
        C.pcnt += 1
        pa = C.ps[0:m, bank * 512:bank * 512 + tn]
        for k in range(4):
            S.op("pe", lambda e, pa=pa, k=k: e.matmul(pa, lhsT=lhs_fn(k), rhs=rhs3[:, k, t0:t0 + tn], start=(k == 0), stop=(k == 3)),
                 reads=[wkey, rkey], writes=["psb%d" % bank])
        return pa, "psb%d" % bank

    for h in range(8):
        for (nm, lhs_fn, rhs3, rkey, wkey, row0) in (
                ("qn", lambda k, h=h: wqb3[:, k, h * 192:h * 192 + 128], cqn3, "cqn", "wqb", h * 128),
                ("kn", lambda k, h=h: wkvb3[:, k, h * 256:h * 256 + 128], ckvn3, "ckvn", "wkvb", 1536 + h * 128)):
            sc[0] += 1
            si = sc[0] % 3
            sg, sk = stg[si], "p2st%d" % si
            for gi, (t0, tn) in enumerate(TG):
                pa, pk = small_proj(lhs_fn, rhs3, rkey, wkey, 128, gi, t0, tn)
                if gi % 2 == 0:
                    S.op("act", lambda e, sg=sg, pa=pa, t0=t0, tn=tn: e.activation(out=sg[:, t0:t0 + tn], in_=pa, func=AF.Copy), reads=[pk], writes=[sk])
                else:
                    S.op("dve", lambda e, sg=sg, pa=pa, t0=t0, tn=tn: e.tensor_copy(out=sg[:, t0:t0 + tn], in_=pa), reads=[pk], writes=[sk])
            S.op("sp", lambda e, sg=sg, row0=row0: e.dma_start(out=M0[b, row0:row0 + 128, :], in_=sg), reads=[sk], writes=["M0"], dma=sk)
        sc[0] += 1
        si = sc[0] % 3
        sg, sk = stg[si], "p2st%d" % si
        for gi, (t0, tn) in enumerate(TG):
            pa, pk = small_proj(lambda k, h=h: wqb3[:, k, h * 192 + 128:h * 192 + 192], cqn3, "cqn", "wqb", 64, gi, t0, tn)
            if gi == 4:
                S.op("dve", lambda e, sg=sg, pa=pa, t0=t0, tn=tn: e.tensor_copy(out=sg[0:64, t0:t0 + tn], in_=pa), reads=[pk], writes=[sk])
                continue
            pb, pkb = small_proj(lambda k, h=h: wqsw4[:, k, h, :], cqn3, "cqn", "wqsw", 64, gi, t0, tn)
            a1, a2 = t1[gi % 2], t2[gi % 2]
            S.op("dve", lambda e, a1=a1, pa=pa, t0=t0, tn=tn: e.tensor_tensor(out=a1[0:64, 0:tn], in0=pa, in1=cos_t[0:64, t0:t0 + tn], op=ALU.mult),
                 reads=[pk, "cos"], writes=["t1%d" % (gi % 2)])
            S.op("dve", lambda e, a2=a2, pb=pb, t0=t0, tn=tn: e.tensor_tensor(out=a2[0:64, 0:tn], in0=pb, in1=sin_t[0:64, t0:t0 + tn], op=ALU.mult),
                 reads=[pkb, "sin"], writes=["t2%d" % (gi % 2)])
            S.op("pool", lambda e, a1=a1, a2=a2, sg=sg, t0=t0, tn=tn: e.tensor_tensor(out=sg[0:64, t0:t0 + tn], in0=a1[0:64, 0:tn], in1=a2[0:64, 0:tn], op=ALU.add),
                 reads=["t1%d" % (gi % 2), "t2%d" % (gi % 2)], writes=[sk])
        S.op("sp", lambda e, sg=sg, h=h: e.dma_start(out=M0[b, 1024 + h * 64:1024 + h * 64 + 64, :], in_=sg[0:64, :]), reads=[sk], writes=["M0"], dma=sk)
    for tt in range(NTT):
        vs, vk = vst[tt % 2], "p2vs%d" % (tt % 2)
        for half in range(2):
            bank = 4 + (C.pcnt % 4)
            C.pcnt += 1
            pa = C.ps[:, bank * 512:bank * 512 + 512]
            for k in range(4):
                S.op("pe", lambda e, pa=pa, k=k, tt=tt, half=half: e.matmul(pa.rearrange("p (h c) -> p h c", h=4), lhsT=ckvn3[:, k, tt * 128:(tt + 1) * 128],
                                                                         rhs=wkvb4[:, k, half * 4:half * 4 + 4, 128:256], start=(k == 0), stop=(k == 3)),
                     reads=["wkvb", "ckvn"], writes=["psb%d" % bank])
            if half == 0:
                S.op("act", lambda e, vs=vs, pa=pa: e.activation(out=vs[:, 0:512], in_=pa, func=AF.Copy), reads=["psb%d" % bank], writes=[vk])
            else:
                S.op("dve", lambda e, vs=vs, pa=pa: e.tensor_copy(out=vs[:, 512:1024], in_=pa), reads=["psb%d" % bank], writes=[vk])
        S.op("sp", lambda e, vs=vs, tt=tt: e.dma_start(out=VM[b, tt * 128:(tt + 1) * 128, :], in_=vs), reads=[vk], writes=["VM"], dma=vk)
    S.barrier()


def na_geometry():
    rows = 32
    r = np.arange(rows)
    r0 = np.clip(r - 4, 0, rows - 8)
    col = np.arange(64)
    c0 = np.clip(col - 8, 0, 64 - 16)
    tiles = {}
    per_m = []
    for m in range(16):
        lo = min(r0[2 * m], r0[2 * m + 1])
        hi = max(r0[2 * m], r0[2 * m + 1]) + 7
        lst = []
        for kb in range(lo // 2, hi // 2 + 1):
            memb = tuple(tuple(bool(r0[2 * m + bq] <= 2 * kb + a <= r0[2 * m + bq] + 7) for bq in range(2)) for a in range(2))
            key = (kb - m, memb)
            if key not in tiles:
                tiles[key] = len(tiles)
            lst.append((kb, tiles[key]))
        per_m.append(lst)
    nt = len(tiles)
    mask = np.zeros((nt, 128, 128), np.float32)
    drow = np.zeros((nt, 128, 128), np.int64)
    dcol = np.zeros((nt, 128, 128), np.int64)
    for (delta, memb), ti in tiles.items():
        for a in range(2):
            for bq in range(2):
                kc = np.arange(64)[:, None]
                qc = np.arange(64)[None, :]
                ok = memb[a][bq] & (kc >= c0[qc]) & (kc <= c0[qc] + 15)
                dr = 2 * delta + a - bq + 7
                dc = kc - qc + 15
                blk = (slice(a * 64, a * 64 + 64), slice(bq * 64, bq * 64 + 64))
                mask[ti][blk] = np.where(ok, 0.0, -20000.0)
                drow[ti][blk] = np.clip(np.where(ok, dr, 0), 0, 14)
                dcol[ti][blk] = np.clip(np.where(ok, dc, 0), 0, 30)
    return per_m, nt, mask, drow, dcol


def phase3(C):
    for b in range(NB):
        _phase3_b(C, b)


def _phase3_b(C, b):
    nc, S, A = C.nc, C.S, C.A
    A.reset()
    F0, M0, VM, VA, OG = C.dram["F0"], C.dram["M0"], C.dram["VM"], C.dram["VA"], C.dram["OG0"]
    per_m, nt, _, _, _ = C.na_geo
    krT = A.alloc(T, BF16)
    S.op("sp", lambda e: e.dma_start(out=krT[0:64, :], in_=F0[b, 5120:5184, :]), reads=["F0"], writes=["krT"], dma="p3k")
    nam = A.alloc(nt * 128)
    S.op("sp", lambda e: e.dma_start(out=_v3(nam, nt), in_=C.dram["na_mask"].rearrange("t k q -> k t q")), writes=["nam"], dma="p3k")
    hb = []
    for i in range(2):
        hb.append(dict(k=A.alloc(T, BF16), q=A.alloc(T, BF16), qr=A.alloc(T, BF16), v=A.alloc(NTT * 128, BF16), sz=A.alloc(T, BF16),
                       bias=A.alloc(nt * 128)))
    pT = [A.alloc(512, BF16) for _ in range(4)]
    sbf = [A.alloc(128) for _ in range(3)]
    rinv = [A.alloc(512) for _ in range(2)]
    tmp = [A.alloc(512) for _ in range(2)]
    ogs = [A.alloc(T, BF16) for _ in range(2)]
    cnt = dict(s=0, p=0, g=0, sb=0)

    def attend_block(lhsT_list, rhs_list, tn, reads, o_cols, first, last, vlhs, vkey, scale, bias=None, bkey=None, acc=None):
        bank = cnt["s"] % 4
        cnt["s"] += 1
        pS = C.ps[:, bank * 512:bank * 512 + tn]
        pk = "psb%d" % bank
        n = len(lhsT_list)
        for i in range(n):
            S.op("pe", lambda e, i=i: e.matmul(pS[0:128, :], lhsT=lhsT_list[i], rhs=rhs_list[i], start=(i == 0), stop=(i == n - 1)),
                 reads=reads, writes=[pk])
        slot = cnt["p"] % 4
        cnt["p"] += 1
        p_ = pT[slot][:, 0:tn]
        pkey = "pT%d" % slot
        if bias is None:
            S.op("act", lambda e: e.activation(out=p_, in_=pS, func=AF.Exp, scale=scale), reads=[pk], writes=[pkey])
        else:
            sslot = cnt["sb"] % 3
            cnt["sb"] += 1
            sb_ = sbf[sslot][:, 0:tn]
            S.op("dve", lambda e: e.scalar_tensor_tensor(out=sb_, in0=pS, scalar=scale, in1=bias, op0=ALU.mult, op1=ALU.add),
                 reads=[pk, bkey], writes=["sbf%d" % sslot])
            S.op("act", lambda e: e.activation(out=p_, in_=sb_, func=AF.Exp), reads=["sbf%d" % sslot], writes=[pkey])
        po, ps_, ok_, sk_ = acc

        def part2():
            S.op("pe", lambda e: e.matmul(po[:, o_cols[0]:o_cols[0] + tn], lhsT=vlhs, rhs=p_, start=first, stop=last), reads=[pkey, vkey], writes=[ok_])
            S.op("pe", lambda e: e.matmul(ps_[:, o_cols[0]:o_cols[0] + tn], lhsT=C.ones, rhs=p_, start=first, stop=last), reads=[pkey, "ones"], writes=[sk_])
        pend.append(part2)
        while len(pend) > 2:
            pend.pop(0)()

    pend = []

    def finish_group(acc, tn, sz, szkey, og, ogkey, t0):
        while pend:
            pend.pop(0)()
        po, ps_, ok_, sk_ = acc
        g = cnt["g"] % 2
        cnt["g"] += 1
        S.op("dve", lambda e: e.reciprocal(out=rinv[g][:, 0:tn], in_=ps_[:, 0:tn]), reads=[sk_], writes=["rinv%d" % g])
        S.op("dve", lambda e: e.tensor_tensor(out=tmp[g][:, 0:tn], in0=po[:, 0:tn], in1=rinv[g][:, 0:tn], op=ALU.mult),
             reads=[ok_, "rinv%d" % g], writes=["tmp%d" % g])
        S.op("pool", lambda e: e.tensor_tensor(out=og[:, t0:t0 + tn], in0=tmp[g][:, 0:tn], in1=sz[:, t0:t0 + tn], op=ALU.mult),
             reads=["tmp%d" % g, szkey], writes=[ogkey])

    def acc_banks():
        g = cnt["g"] % 2
        return (C.ps[:, (4 + g) * 512:(5 + g) * 512], C.ps[:, (6 + g) * 512:(7 + g) * 512], "psb%d" % (4 + g), "psb%d" % (6 + g))

    for hh in range(16):
        is_mla = hh >= 8
        h = hh % 8
        B_ = hb[hh % 2]
        hk = "hb%d" % (hh % 2)
        og, ogkey = ogs[hh % 2], "ogs%d" % (hh % 2)
        if is_mla:
            S.op("sp", lambda e, B_=B_, h=h: e.dma_start(out=B_["k"], in_=M0[b, 1536 + h * 128:1536 + (h + 1) * 128, :]), reads=["M0"], writes=[hk + "k"], dma=hk)
            S.op("sp", lambda e, B_=B_, h=h: e.dma_start(out=B_["q"], in_=M0[b, h * 128:(h + 1) * 128, :]), reads=["M0"], writes=[hk + "q"], dma=hk)
            S.op("sp", lambda e, B_=B_, h=h: e.dma_start(out=B_["qr"][0:64, :], in_=M0[b, 1024 + h * 64:1024 + (h + 1) * 64, :]), reads=["M0"], writes=[hk + "qr"], dma=hk)
            S.op("sp", lambda e, B_=B_, h=h: e.dma_start(out=_v3(B_["v"], NTT), in_=VM[b, :, h * 128:(h + 1) * 128].rearrange("(kb p) d -> p kb d", p=128)),
                 reads=["VM"], writes=[hk + "v"], dma=hk)
            S.op("sp", lambda e, B_=B_, h=h: e.dma_start(out=B_["sz"], in_=F0[b, 3072 + 1024 + h * 128:3072 + 1024 + (h + 1) * 128, :]), reads=["F0"], writes=[hk + "sz"], dma=hk)
        else:
            S.op("sp", lambda e, B_=B_, h=h: e.dma_start(out=B_["k"], in_=F0[b, 1024 + h * 128:1024 + (h + 1) * 128, :]), reads=["F0"], writes=[hk + "k"], dma=hk)
            S.op("sp", lambda e, B_=B_, h=h: e.dma_start(out=B_["q"], in_=F0[b, h * 128:(h + 1) * 128, :]), reads=["F0"], writes=[hk + "q"], dma=hk)
            S.op("sp", lambda e, B_=B_, h=h: e.dma_start(out=_v3(B_["v"], NTT), in_=VA[b, :, h * 128:(h + 1) * 128].rearrange("(kb p) d -> p kb d", p=128)),
                 reads=["VA"], writes=[hk + "v"], dma=hk)
            S.op("sp", lambda e, B_=B_, h=h: e.dma_start(out=B_["sz"], in_=F0[b, 3072 + h * 128:3072 + (h + 1) * 128, :]), reads=["F0"], writes=[hk + "sz"], dma=hk)
            S.op("sp", lambda e, B_=B_, h=h: e.dma_start(out=_v3(B_["bias"], nt), in_=C.dram["na_bias"][h].rearrange("t k q -> k t q")), writes=[hk + "b"], dma=hk)
            S.op("pool", lambda e, B_=B_: e.tensor_tensor(out=B_["bias"], in0=B_["bias"], in1=nam, op=ALU.add), reads=[hk + "b", "nam"], writes=[hk + "b"])
        k_, q_, qr_, v3_, sz_ = B_["k"], B_["q"], B_["qr"], _v3(B_["v"], NTT), B_["sz"]
        b3_ = _v3(B_["bias"], nt)
        for gi, (t0, tn) in enumerate(TG):
            acc = acc_banks()
            if is_mla:
                kbs = list(range(18)) if gi < 4 else [16, 17]
                for i, kb in enumerate(kbs):
                    attend_block([k_[:, kb * 128:(kb + 1) * 128], krT[0:64, kb * 128:(kb + 1) * 128]], [q_[:, t0:t0 + tn], qr_[0:64, t0:t0 + tn]], tn,
                                 [hk + "k", hk + "q", hk + "qr", "krT"], (0,), i == 0, i == len(kbs) - 1, v3_[:, kb, :], hk + "v", MLA_SCALE, acc=acc)
            else:
                for i, kb in enumerate((16, 17)):
                    attend_block([k_[:, kb * 128:(kb + 1) * 128]], [q_[:, t0:t0 + tn]], tn, [hk + "k", hk + "q"], (0,), i == 0, (gi == 4 and i == 1),
                                 v3_[:, kb, :], hk + "v", NA_SCALE, acc=acc)
                if gi < 4:
                    for pr in range(4):
                        m = gi * 4 + pr
                        lst = per_m[m]
                        for j, (kb, ti) in enumerate(lst):
                            attend_block([k_[:, kb * 128:(kb + 1) * 128]], [q_[:, m * 128:(m + 1) * 128]], 128, [hk + "k", hk + "q"], (pr * 128,), False,
                                         (pr == 3 and j == len(lst) - 1), v3_[:, kb, :], hk + "v", NA_SCALE, bias=b3_[:, ti, :], bkey=hk + "b", acc=acc)
            finish_group(acc, tn, sz_, hk + "sz", og, ogkey, t0)
        row0 = (1024 if is_mla else 0) + h * 128
        S.op("sp", lambda e, og=og, row0=row0: e.dma_start(out=OG[b, row0:row0 + 128, :], in_=og), reads=[ogkey], writes=["OG0"], dma=ogkey)
    S.barrier()


def out_proj(C, b, l, OGd, nk, w_out, res_tiles, dst_tiles, ntt_list, tag, cbw=512, ntok=T):
    nc, S, A = C.nc, C.S, C.A
    A.reset()
    og = A.alloc(nk * ntok, BF16)
    og3 = _v3(og, nk)
    hk = nk // 2
    S.op("sp", lambda e: e.dma_start(out=og3[:, 0:hk, :], in_=OGd[0:hk * 128, :].rearrange("(k p) t -> p k t", p=128)), reads=["DR_OG"], writes=["og"], dma=tag + "og")
    S.op("sp", lambda e: e.dma_start(out=og3[:, hk:nk, :], in_=OGd[hk * 128:nk * 128, :].rearrange("(k p) t -> p k t", p=128)), reads=["DR_OG"], writes=["og"], dma=tag + "og")
    vs = (b, 2) if len(ntt_list) > 16 else (b,)
    wr = A.alloc(nk * cbw, BF16)
    wx = [A.alloc(nk * cbw, BF16) for _ in vs]
    gt = [A.alloc(cbw) for _ in range(2)]
    xr = [A.alloc(cbw) for _ in range(3)]
    xo = [A.alloc(cbw) for _ in range(3)]
    it = 0
    for cb in range(D // cbw):
        w3 = _v3(wr, nk)
        S.op("pool", lambda e, cb=cb, w3=w3: e.dma_start(out=w3, in_=w_out[:, cb * cbw:(cb + 1) * cbw].rearrange("(k p) c -> p k c", p=128)), writes=["wr"], dma=tag + "w")
        for vi, v in enumerate(vs):
            S.op("sp", lambda e, vi=vi, v=v, cb=cb: e.dma_start(out=gt[vi], in_=C.dram["mod"][l, v, 4096 + cb * cbw:4096 + (cb + 1) * cbw].unsqueeze(0).to_broadcast([128, cbw])),
                 reads=["mod"], writes=["gt%d" % vi], dma=tag + "gt")
            S.op("dve" if vi == 0 else "pool", lambda e, vi=vi, w3=w3: e.tensor_tensor(out=_v3(wx[vi], nk), in0=w3, in1=gt[vi].unsqueeze(1).to_broadcast([128, nk, cbw]), op=ALU.mult),
                 reads=["wr", "gt%d" % vi], writes=["wx%d" % vi])
        for tt in ntt_list:
            vi = 0 if tt < 16 else 1
            wv = _v3(wx[vi], nk)
            bank = 4 + (C.pcnt % 4)
            C.pcnt += 1
            pa = C.ps[:, bank * 512:bank * 512 + cbw]
            for k in range(nk):
                S.op("pe", lambda e, pa=pa, k=k, tt=tt, wv=wv: e.matmul(pa, lhsT=og3[:, k, tt * 128:(tt + 1) * 128], rhs=wv[:, k, :], start=(k == 0), stop=(k == nk - 1)),
                     reads=["og", "wx%d" % vi], writes=["psb%d" % bank])
            sl = it % 3
            it += 1
            S.op("sp", lambda e, sl=sl, tt=tt, cb=cb: e.dma_start(out=xr[sl], in_=res_tiles(tt, cb)), reads=["DR_res"], writes=["xr%d" % sl], dma=tag + "xr%d" % sl)
            S.op("dve", lambda e, sl=sl, pa=pa: e.tensor_tensor(out=xo[sl], in0=pa, in1=xr[sl], op=ALU.add), reads=["psb%d" % bank, "xr%d" % sl], writes=["xo%d" % sl])
            S.op("sp", lambda e, sl=sl, tt=tt, cb=cb: e.dma_start(out=dst_tiles(tt, cb), in_=xo[sl]), reads=["xo%d" % sl], writes=["DR_dst"], dma=tag + "xo%d" % sl)
    S.barrier()


def phase4(C):
    for b in range(NB):
        def res(tt, cb, b=b):
            if tt < 16:
                return C.dram["x"][b, tt * 128:(tt + 1) * 128, cb * 512:(cb + 1) * 512]
            return C.dram["ctx"][b, (tt - 16) * 128:(tt - 15) * 128, cb * 512:(cb + 1) * 512]

        def dst(tt, cb, b=b):
            return C.dram["X1"][b, tt * 128:(tt + 1) * 128, cb * 512:(cb + 1) * 512]
        out_proj(C, b, 0, C.dram["OG0"][b], 16, C.dram["ab_w_out"], res, dst, list(range(NTT)), "p4")


HP = 2052 + 260


def phase5(C):
    for b in range(NB):
        _phase5_b(C, b)


def _phase5_b(C, b):
    nc, S, A = C.nc, C.S, C.A
    A.reset()
    w_in = C.dram["dn_w_in"]
    X1, F1, BA = C.dram["X1"], C.dram["F1"], C.dram["BA"]
    mv = load_modvecs(C, 1, b, C.dram["dn_norm"], "p5m%d" % b)
    xmT = A.alloc(16 * T, BF16)
    xm3 = _v3(xmT, 16)
    xkey = "xmT"
    mark = A.off
    tiles = [X1[b, tt * 128:(tt + 1) * 128, :] for tt in range(NTT)]
    build_xmT(C, xmT, xkey, tiles, mv, "p5b")
    S.barrier()
    A.off = mark
    wr = [A.alloc(16 * 512, BF16) for _ in range(2)]
    stg = [A.alloc(T, BF16) for _ in range(3)]
    hp = [A.alloc(HP, BF16) for _ in range(2)]
    dg = [A.alloc(5 * 128, BF16) for _ in range(2)]
    sT = [A.alloc(512, BF16) for _ in range(3)]
    sq = [A.alloc(512, BF16) for _ in range(2)]
    lnv = [A.alloc(512) for _ in range(2)]
    rst = [A.alloc(512) for _ in range(2)]
    bast = A.alloc(NTT * 128)
    cw3 = C.cw.rearrange("p (blk j) -> p blk j", j=5)
    for i in range(2):
        S.op("pool", lambda e, i=i: e.memset(hp[i], 0.0), writes=["hp%d" % i])
    segs = [("q", i * 512, 512) for i in range(4)] + [("k", 2048 + i * 512, 512) for i in range(4)] + [("v", 4096 + i * 512, 512) for i in range(8)] + \
           [("z", 8192 + i * 512, 512) for i in range(8)] + [("ba", 12288, 128)]
    cnt = dict(st=0, hp=0, c=0, n=0, s=0)
    for wi, (nm, c0, ncw) in enumerate(segs):
        slot = wi % 2
        w3 = _v3(wr[slot], 16)[:, :, 0:ncw]
        wkey = "p5w%d" % slot
        S.op("pool", lambda e, w3=w3, c0=c0, ncw=ncw: e.dma_start(out=w3, in_=w_in[:, c0:c0 + ncw].rearrange("(k p) c -> p k c", p=128)),
             writes=[wkey], dma=wkey)
        if nm == "ba":
            b3 = _v3(bast, NTT)
            for tt in range(NTT):
                bank = 4 + (C.pcnt % 4)
                C.pcnt += 1
                pa = C.ps[:, bank * 512:bank * 512 + 128]
                for k in range(16):
                    S.op("pe", lambda e, pa=pa, k=k, tt=tt, w3=w3: e.matmul(pa, lhsT=xm3[:, k, tt * 128:(tt + 1) * 128], rhs=w3[:, k, :], start=(k == 0), stop=(k == 15)),
                         reads=[wkey, xkey + str(tt)], writes=["psb%d" % bank])
                S.op("dve", lambda e, pa=pa, tt=tt: e.tensor_copy(out=b3[:, tt, :], in_=pa), reads=["psb%d" % bank], writes=["bast"])
            S.op("sp", lambda e: e.dma_start(out=BA[b].rearrange("(tt p) c -> p tt c", p=128), in_=b3), reads=["bast"], writes=["BA"], dma="bast")
            continue

        def evac(sb, gi, t0, tn, pa, pk, m, nm=nm, c0=c0):
            if nm == "z":
                if gi == 0:
                    cnt["st"] += 1
                si = cnt["st"] % 3
                sg, sk = stg[si], "p5st%d" % si
                S.op("act", lambda e: e.activation(out=sg[:, t0:t0 + tn], in_=pa, func=AF.Silu), reads=[pk], writes=[sk])
                if gi == 4:
                    r0 = c0 + sb * 128
                    S.op("sp", lambda e: e.dma_start(out=F1[b, r0:r0 + 128, :], in_=sg), reads=[sk], writes=["F1"], dma=sk)
                return
            if gi == 0:
                cnt["hp"] += 1
            hi = cnt["hp"] % 2
            hb_, hk_ = hp[hi], "hp%d" % hi
            off = 2 + t0 if gi < 4 else 2052 + 2
            if gi % 2 == 0:
                S.op("act", lambda e: e.activation(out=hb_[:, off:off + tn], in_=pa, func=AF.Copy), reads=[pk], writes=[hk_])
            else:
                S.op("dve", lambda e: e.tensor_copy(out=hb_[:, off:off + tn], in_=pa), reads=[pk], writes=[hk_])
            if gi < 4:
                return
            blk = (c0 + sb * 128) // 128
            di = cnt["hp"] % 2
            d3 = _v3(dg[di], 5)
            for j in range(5):
                S.op("pool", lambda e, j=j: e.tensor_scalar(out=d3[:, j, :], in0=C.ident, scalar1=cw3[:, blk, j:j + 1], scalar2=None, op0=ALU.mult),
                     reads=["ident", "cw"], writes=["dg%d" % di])
            cnt["st"] += 1
            si = cnt["st"] % 3
            sg, sk = stg[si], "p5st%d" % si
            outs = []
            for g2, (u0, un) in enumerate(TG):
                bank = cnt["c"] % 4
                cnt["c"] += 1
                pc = C.ps[:, bank * 512:bank * 512 + un]
                base = u0 if g2 < 4 else 2052
                for j in range(5):
                    S.op("pe", lambda e, pc=pc, j=j, base=base, un=un: e.matmul(pc, lhsT=d3[:, j, :], rhs=hb_[:, base + j:base + j + un], start=(j == 0), stop=(j == 4)),
                         reads=["dg%d" % di, hk_], writes=["psb%d" % bank])
                if nm == "v":
                    S.op("act", lambda e, pc=pc, u0=u0, un=un: e.activation(out=sg[:, u0:u0 + un], in_=pc, func=AF.Silu), reads=["psb%d" % bank], writes=[sk])
                else:
                    ssl = cnt["s"] % 3
                    cnt["s"] += 1
                    s_ = sT[ssl]
                    S.op("act", lambda e, pc=pc, s_=s_, un=un: e.activation(out=s_[:, 0:un], in_=pc, func=AF.Silu), reads=["psb%d" % bank], writes=["sT%d" % ssl])
                    outs.append((s_, "sT%d" % ssl, u0, un))
                    if len(outs) == 3 or g2 == 4:
                        pend = []
                        for (s2, s2k, v0, vn) in outs:
                            nsl = cnt["n"] % 2
                            cnt["n"] += 1
                            S.op("pool", lambda e, s2=s2, vn=vn, nsl=nsl: e.tensor_tensor(out=sq[nsl][:, 0:vn], in0=s2[:, 0:vn], in1=s2[:, 0:vn], op=ALU.mult),
                                 reads=[s2k], writes=["sq%d" % nsl])
                            bank2 = cnt["c"] % 4
                            cnt["c"] += 1
                            p3 = C.ps[:, bank2 * 512:bank2 * 512 + vn]
                            S.op("pe", lambda e, p3=p3, nsl=nsl, vn=vn: e.matmul(p3, lhsT=C.ones, rhs=sq[nsl][:, 0:vn], start=True, stop=True),
                                 reads=["ones", "sq%d" % nsl], writes=["psb%d" % bank2])
                            S.op("act", lambda e, p3=p3, nsl=nsl, vn=vn: e.activation(out=lnv[nsl][:, 0:vn], in_=p3, func=AF.Ln, bias=C.eps_t[:, 0:1]),
                                 reads=["psb%d" % bank2], writes=["lnv%d" % nsl])
                            pend.append((s2, s2k, v0, vn, nsl))
                            if len(pend) == 2 or (s2 is outs[-1][0]):
                                for (s3_, s3k, w0, wn, ns2) in pend:
                                    bias_ap = C.lnq[:, 0:1] if nm == "q" else C.zero_t[:, 0:1]
                                    S.op("act", lambda e, ns2=ns2, wn=wn, bias_ap=bias_ap: e.activation(out=rst[ns2][:, 0:wn], in_=lnv[ns2][:, 0:wn], func=AF.Exp, scale=-0.5, bias=bias_ap),
                                         reads=["lnv%d" % ns2], writes=["rst%d" % ns2])
                                    S.op("dve", lambda e, s3_=s3_, w0=w0, wn=wn, ns2=ns2: e.tensor_tensor(out=sg[:, w0:w0 + wn], in0=s3_[:, 0:wn], in1=rst[ns2][:, 0:wn], op=ALU.mult),
                                         reads=[s3k, "rst%d" % ns2], writes=[sk])
                                pend = []
                        outs = []
            r0 = c0 + sb * 128
            S.op("sp", lambda e: e.dma_start(out=F1[b, r0:r0 + 128, :], in_=sg), reads=[sk], writes=["F1"], dma=sk)

        proj_fm(C, xm3, xkey, w3, wkey, ncw, evac, "p5")
    S.barrier()


FSEQ = [16, 17] + list(range(16))
BSEQ = [17, 16] + list(range(15, -1, -1))


def dn_level_masks():
    s_ = np.arange(128)[:, None]
    c_ = np.arange(128)[None, :]
    out = np.zeros((128, 7, 4, 2, 128), np.float32)
    for k in range(1, 8):
        h = 1 << (k - 1)
        same = (s_ // (2 * h)) == (c_ // (2 * h))
        ur = same & ((s_ % (2 * h)) < h) & ((c_ % (2 * h)) >= h)
        ll = ur.T
        for j in range(4):
            fwd = j < 2
            out[:, k - 1, j, 0, :] = -(ur if fwd else ll).astype(np.float32)
            out[:, k - 1, j, 1, :] = -(ll if fwd else ur).astype(np.float32)
    id8 = np.zeros((128, 4, 2, 128), np.float32)
    id8[:, :, :, :] = np.eye(128, dtype=np.float32)[:, None, None, :]
    return out.reshape(128, 7, 1024), id8.reshape(128, 1024)


def dn_level_masks2():
    lm, _ = dn_level_masks()
    lm = lm.reshape(128, 7, 4, 2, 128).copy()
    lm -= np.eye(128, dtype=np.float32)[:, None, None, None, :]
    return np.ascontiguousarray(lm[:, :, 0::2, :, :]).reshape(128, 7, 512)


def dn_masks():
    s = np.arange(128)[:, None]
    c = np.arange(128)[None, :]
    incl = np.stack([(s <= c), (s <= c), (s >= c), (s >= c)], 0).astype(np.float32)
    strict = np.stack([(s < c), (s < c), (s > c), (s > c)], 0).astype(np.float32)
    ident4 = np.stack([np.eye(128, dtype=np.float32)] * 4, 0)
    return incl.transpose(1, 0, 2).copy(), strict.transpose(1, 0, 2).copy(), ident4.transpose(1, 0, 2).copy()


def phase6(C):
    for b in getattr(C, "p6_batches", range(NB)):
        _phase6_b(C, b)


def dump(C, name, ap, readkeys):
    if not getattr(C, "debug", False):
        return
    t = C.nc.dram_tensor("dbg_" + name, list(ap.shape), ap.dtype, kind="ExternalOutput").ap()
    if ap.shape[1] * (4 if ap.dtype == F32 else 2) > 2048:
        C.S.op("sp", lambda e: e.dma_start(out=t, in_=ap), reads=readkeys, writes=["DR_dbg"], dma=1)
        return
    if not hasattr(C, "dbg_stage"):
        C.dbg_stage = C.es.enter_context(C.nc.sbuf_tensor("dbgst", [128, 512], F32))
    st = C.dbg_stage[:, 0:ap.shape[1]] if ap.dtype == F32 else C.dbg_stage[:, 0:(ap.shape[1] + 1) // 2].bitcast(BF16)[:, 0:ap.shape[1]]
    C.S.op("dve", lambda e: e.tensor_copy(out=st, in_=ap), reads=readkeys, writes=["dbgst"])
    C.S.op("sp", lambda e: e.dma_start(out=t, in_=st), reads=["dbgst"], writes=["DR_dbg"], dma=1)


def _phase6_b(C, b):
    nc, S, A = C.nc, C.S, C.A
    A.reset()
    F1, BA, OG1 = C.dram["F1"], C.dram["BA"], C.dram["OG1"]
    mincl = A.alloc(512)
    mstr = A.alloc(512)
    ones_f = A.alloc(128)
    onorm = A.alloc(1)
    S.op("sp", lambda e: e.dma_start(out=_v3(mincl, 4), in_=C.dram["dn_mincl"]), writes=["mincl"], dma=1)
    S.op("sp", lambda e: e.dma_start(out=_v3(mstr, 4), in_=C.dram["dn_mstrict"]), writes=["mstr"], dma=1)
    S.op("sp", lambda e: e.dma_start(out=onorm, in_=C.dram["dn_o_norm"].rearrange("(p o) -> p o", o=1), allow_slow_non_contiguous=True), writes=["onorm"], dma=1)
    S.op("pool", lambda e: e.memset(ones_f, 1.0), writes=["ones_f"])
    mincl3, mstr3 = _v3(mincl, 4), _v3(mstr, 4)
    lmask = A.alloc(7 * 512, BF16)
    lm4 = lmask.rearrange("p (k d x) -> p k d x", k=7, d=2)
    S.op("sp", lambda e: e.dma_start(out=_v3(lmask, 7), in_=C.dram["dn_lmask2"]), writes=["lmask"], dma=1)
    beta = A.alloc(NTT * 64)
    gg = A.alloc(NTT * 64)
    mark6 = A.off
    ba = A.alloc(NTT * 128)
    ba3 = _v3(ba, NTT)
    S.op("sp", lambda e: e.dma_start(out=ba3, in_=BA[b].rearrange("(tt p) c -> p tt c", p=128)), reads=["BA"], writes=["ba"], dma=1)
    tA = A.alloc(NTT * 64)
    tB = A.alloc(NTT * 64)
    alog = A.alloc(64)
    dtb = A.alloc(64)
    one_t = A.alloc(1)
    S.op("pool", lambda e: e.memset(one_t, 1.0), writes=["one_t"])
    S.op("sp", lambda e: e.dma_start(out=alog, in_=C.dram["dn_a_log"].rearrange("d h -> (d h)").unsqueeze(0).to_broadcast([128, 64])), writes=["alog"], dma=1)
    S.op("sp", lambda e: e.dma_start(out=dtb, in_=C.dram["dn_dt_bias"].rearrange("d h -> (d h)").unsqueeze(0).to_broadcast([128, 64])), writes=["dtb"], dma=1)
    beta3, gg3, tA3, tB3 = _v3(beta, NTT), _v3(gg, NTT), _v3(tA, NTT), _v3(tB, NTT)
    S.op("act", lambda e: e.activation(out=tA3, in_=ba3[:, :, 0:64], func=AF.Exp, scale=-1.0), reads=["ba"], writes=["tA"])
    S.op("dve", lambda e: e.tensor_scalar(out=tA, in0=tA, scalar1=1.0, scalar2=None, op0=ALU.add), reads=["tA"], writes=["tA"])
    S.op("dve", lambda e: e.reciprocal(out=beta, in_=tA), reads=["tA"], writes=["beta"])
    S.op("dve", lambda e: e.tensor_tensor(out=tB3, in0=ba3[:, :, 64:128], in1=dtb.unsqueeze(1).to_broadcast([128, NTT, 64]), op=ALU.add), reads=["ba", "dtb"], writes=["tB"])
    S.op("dve", lambda e: e.scalar_tensor_tensor(out=tA, in0=tB, scalar=-1.0, in1=tB, op0=ALU.mult, op1=ALU.max), reads=["tB", "beta"], writes=["tA"])
    S.op("act", lambda e: e.activation(out=tA, in_=tA, func=AF.Exp, scale=-1.0), reads=["tA"], writes=["tA"])
    S.op("act", lambda e: e.activation(out=tA, in_=tA, func=AF.Ln, bias=one_t[:, 0:1]), reads=["tA", "one_t"], writes=["tA"])
    S.op("dve", lambda e: e.scalar_tensor_tensor(out=tB, in0=tB, scalar=0.0, in1=tA, op0=ALU.max, op1=ALU.add), reads=["tA", "tB"], writes=["tB"])
    S.op("act", lambda e: e.activation(out=alog, in_=alog, func=AF.Exp), reads=["alog"], writes=["alog"])
    S.op("dve", lambda e: e.scalar_tensor_tensor(out=gg3, in0=tB3, scalar=-1.0, in1=alog.unsqueeze(1).to_broadcast([128, NTT, 64]), op0=ALU.mult, op1=ALU.mult),
         reads=["tB", "alog"], writes=["gg"])
    S.barrier()
    A.off = mark6
    SHARED = {"gg", "beta", "mincl", "mstr", "lmask", "ident", "ones", "onorm", "ones_f", "eps", "zero", "lnq"}
    S0 = S

    class _SlotSched:
        def __init__(self, si):
            self.si = si

        def op(self, eng, fn, reads=(), writes=(), dma=None):
            f = lambda k: k if (k in SHARED or k in Sched.DRAMKEYS or k.startswith("DR_")) else "s%d_%s" % (self.si, k)
            return S0.op(eng, fn, reads=[f(k) for k in reads], writes=[f(k) for k in writes], dma=dma)

    def run_slot(si, head_list):
        S = _SlotSched(si)
        hbufs = [dict(q=A.alloc(T, BF16), k=A.alloc(T, BF16), v=A.alloc(2 * T, BF16))]
        ktok = A.alloc(NTT * 128, BF16)
        vtok = A.alloc(NTT * 256, BF16)
        oacc = A.alloc(2 * SEQ)
        o3 = _v3(oacc, 2)
        szb1 = A.alloc(SEQ, BF16)
        def mk():
            dec_ = A.alloc(512)
            tmp_ = A.alloc(512)
            tmp2_ = A.alloc(512)
            return dict(grep=A.alloc(512), d1=dec_, dec=dec_, gam=A.alloc(512), bm=A.alloc(512), tmp=tmp_, tmp2=tmp2_, t3=tmp_, t4=tmp2_,
                        x12=A.alloc(12), e12=A.alloc(12), negb=A.alloc(4), xn=A.alloc(1024, BF16), rt=A.alloc(1024, BF16), yy=A.alloc(1024, BF16),
                        xb=dec_, intra=A.alloc(512, BF16), gq=A.alloc(512, BF16), kd=A.alloc(512, BF16), vd=A.alloc(512, BF16), vn=A.alloc(512, BF16))
        stp = [mk()]
        S4 = A.alloc(512)
        S4b = A.alloc(512, BF16)
        sqb = [A.alloc(512, BF16)] * 2
        lnv = [A.alloc(512)] * 2
        rst = lnv
        osum = [A.alloc(512) for _ in range(2)]
        ps = C.ps
        pbase = si * 2048
        kG = kS1 = "psb%d" % (pbase // 512)
        kAB = kS2 = "psb%d" % (pbase // 512 + 1)
        kM = kT = kC = kN = "psM%d" % (pbase // 512)
        psG = ps[:, pbase:pbase + 512]
        psG3 = _v3(psG, 4)
        psS1 = psG
        psA = ps[:, pbase + 512:pbase + 768]
        psB = ps[:, pbase + 768:pbase + 1024]
        psS2 = ps[:, pbase + 512:pbase + 1024]
        psM = ps[:, pbase + 1024:pbase + 2048]
        psM3 = _v3(psM, 4)
        psT = ps[:, pbase + 1024:pbase + 1536]
        psTb = psT.bitcast(BF16)
        psC = ps[:, pbase + 1536:pbase + 1540]
        psN = psT

        for g in head_list:
            H = hbufs[0]
            hk = "h6"
            S.op("sp", lambda e, H=H, g=g: e.dma_start(out=H["q"], in_=F1[b, g * 128:(g + 1) * 128, :]), reads=["F1"], writes=[hk + "q"], dma=1)
            S.op("sp", lambda e, H=H, g=g: e.dma_start(out=H["k"], in_=F1[b, 2048 + g * 128:2048 + (g + 1) * 128, :]), reads=["F1"], writes=[hk + "k"], dma=1)
            S.op("sp", lambda e, H=H, g=g: e.dma_start(out=_v3(H["v"], 2), in_=F1[b, 4096 + 2 * g * 128:4096 + (2 * g + 2) * 128, :].rearrange("(v p) t -> p v t", p=128)),
                 reads=["F1"], writes=[hk + "v"], dma=1)
            QT, KT, VT3 = H["q"], H["k"], _v3(H["v"], 2)
            kt3 = _v3(ktok, NTT)
            vt4 = vtok.rearrange("p (t v d) -> p t v d", t=NTT, v=2)
            jobs = [("k", tt, 0) for tt in range(NTT)] + [("v", tt, vh) for tt in range(NTT) for vh in range(2)]
            groups = [jobs[0:8], jobs[8:16], jobs[16:18]] + [jobs[18 + i:18 + i + 8] for i in range(0, 36, 8)]
            for j0, grp in enumerate(groups):
                yield
                j0 = j0 * 8
                for i, (kind, tt, vh) in enumerate(grp):
                    src = KT[:, tt * 128:(tt + 1) * 128] if kind == "k" else VT3[:, vh, tt * 128:(tt + 1) * 128]
                    S.op("pe", lambda e, i=i, src=src: e.transpose(out=psTb[:, i * 128:(i + 1) * 128], in_=src, identity=C.ident),
                         reads=[hk + "k", hk + "v", "ident"], writes=[kT])
                kind0, tt0, vh0 = grp[0]
                n = len(grp)
                if kind0 == "k":
                    dst = ktok[:, tt0 * 128:(tt0 + n) * 128]
                    dk_ = "ktok"
                else:
                    dst = vtok[:, (tt0 * 2 + vh0) * 128:(tt0 * 2 + vh0 + n) * 128]
                    dk_ = "vtok"
                if (j0 // 8) % 2 == 0:
                    S.op("act", lambda e, dst=dst, n=n: e.activation(out=dst, in_=psTb[:, 0:n * 128], func=AF.Copy), reads=[kT], writes=[dk_])
                else:
                    S.op("dve", lambda e, dst=dst, n=n: e.tensor_copy(out=dst, in_=psTb[:, 0:n * 128]), reads=[kT], writes=[dk_])
            S.op("pool", lambda e: e.memset(S4, 0.0), writes=["S4"])
            S.op("pool", lambda e: e.memset(S4b, 0.0), writes=["S4b"])
            S43, S4b3 = _v3(S4, 4), _v3(S4b, 4)
            c0 = 2 * g
            def step(s, part, g=g, H=H, hk=hk, QT=QT, KT=KT, VT3=VT3, kt3=kt3, vt4=vt4, c0=c0, S43=S43, S4b3=S4b3):
                P = stp[0]
                pk = "st0"
                blks = (FSEQ[s], BSEQ[s])
                cols = [(d * 32 + c0) for d in range(2)]
                grep3, d13, dec3, gam3, bm3, tmp3, tmp23 = [_v3(P[n_], 4) for n_ in ("grep", "d1", "dec", "gam", "bm", "tmp", "tmp2")]
                xb3, intra3, gq3, kd3, vd3, vn3, t33, t43 = [_v3(P[n_], 4) for n_ in ("xb", "intra", "gq", "kd", "vd", "vn", "t3", "t4")]
                x12, e12, negb = P["x12"], P["e12"], P["negb"]
                RT = P["rt"].rearrange("p (j o c) -> p j o c", j=4, o=2)
                krt = pk + "rt"
                xn4 = P["xn"].rearrange("p (j o c) -> p j o c", j=4, o=2)
                kxn = pk + "xn"
                if part == "A":
                    yield
                    for d in range(2):
                        gs = gg3[:, blks[d], cols[d]:cols[d] + 2]
                        S.op("pool", lambda e, d=d, gs=gs: e.tensor_copy(out=grep3[:, 2 * d:2 * d + 2, :], in_=gs.unsqueeze(2).to_broadcast([128, 2, 128])),
                             reads=["gg"], writes=[pk + "grep"])
                        S.op("dve", lambda e, d=d: e.tensor_scalar(out=negb[:, 2 * d:2 * d + 2], in0=beta3[:, blks[d], cols[d]:cols[d] + 2], scalar1=-1.0, scalar2=None, op0=ALU.mult),
                             reads=["beta"], writes=[pk + "negb"])
                        S.op("pool", lambda e, d=d: e.tensor_tensor(out=bm3[:, 2 * d:2 * d + 2, :], in0=mstr3[:, 2 * d:2 * d + 2, :],
                                                                    in1=beta3[:, blks[d], cols[d]:cols[d] + 2].unsqueeze(2).to_broadcast([128, 2, 128]), op=ALU.mult),
                             reads=["beta", "mstr"], writes=[pk + "bm"])
                    yield
                    for j in range(4):
                        d = j // 2
                        S.op("pe", lambda e, j=j, d=d: e.matmul(psG3[:, j, :], lhsT=grep3[:, j, :], rhs=mincl3[:, 2 * d, :], start=True, stop=True),
                             reads=[pk + "grep", "mincl"], writes=[kG])
                    yield
                    for d in range(2):
                        S.op("pe", lambda e, d=d: e.matmul(psC[:, 2 * d:2 * d + 2], lhsT=mincl3[:, 2 * d, :], rhs=gg3[:, blks[d], cols[d]:cols[d] + 2], start=True, stop=True),
                             reads=["gg", "mincl"], writes=[kC])
                    yield
                    for d in range(2):
                        kb_ = KT[:, blks[d] * 128:(blks[d] + 1) * 128]
                        qb_ = QT[:, blks[d] * 128:(blks[d] + 1) * 128]
                        S.op("pe", lambda e, d=d, kb_=kb_: e.matmul(psA[:, d * 128:(d + 1) * 128], lhsT=kb_, rhs=kb_, start=True, stop=True), reads=[hk + "k"], writes=[kAB])
                        S.op("pe", lambda e, d=d, kb_=kb_, qb_=qb_: e.matmul(psB[:, d * 128:(d + 1) * 128], lhsT=kb_, rhs=qb_, start=True, stop=True), reads=[hk + "k", hk + "q"], writes=[kAB])
                    yield
                    S.op("dve", lambda e: e.tensor_copy(out=x12[:, 0:4], in_=psC), reads=[kC], writes=[pk + "x12"])
                    yield
                    for d in range(2):
                        last = 127 if d == 0 else 0
                        S.op("dve", lambda e, d=d, last=last: e.tensor_copy(out=x12[:, 8 + 2 * d:10 + 2 * d], in_=psG3[:, 2 * d:2 * d + 2, last]), reads=[kG], writes=[pk + "x12"])
                    yield
                    S.op("dve", lambda e: e.tensor_tensor(out=x12[:, 4:8], in0=x12[:, 8:12], in1=x12[:, 0:4], op=ALU.subtract), reads=[pk + "x12"], writes=[pk + "x12"])
                    yield
                    S.op("act", lambda e: e.activation(out=e12, in_=x12, func=AF.Exp), reads=[pk + "x12"], writes=[pk + "e12"])
                    yield
                    S.op("dve", lambda e: e.tensor_tensor(out=d13, in0=psG3, in1=x12[:, 0:4].unsqueeze(2).to_broadcast([128, 4, 128]), op=ALU.subtract),
                         reads=[kG, pk + "x12"], writes=[pk + "dec"])
                    yield
                    S.op("pool", lambda e: e.tensor_scalar(out=P["d1"], in0=P["d1"], scalar1=0.0, scalar2=-80.0, op0=ALU.min, op1=ALU.max), reads=[pk + "dec"], writes=[pk + "dec"])
                    yield
                    S.op("act", lambda e: e.activation(out=P["dec"], in_=P["d1"], func=AF.Exp), reads=[pk + "dec"], writes=[pk + "dec"])
                    yield
                    S.op("act", lambda e: e.activation(out=P["gam"], in_=psG, func=AF.Exp), reads=[kG], writes=[pk + "gam"])
                    dec4 = P["dec"].rearrange("p (d v c) -> p d v c", d=2, v=2)
                    psA4 = psA.rearrange("p (d c) -> p d c", d=2).unsqueeze(2).to_broadcast([128, 2, 2, 128])
                    psB4 = psB.rearrange("p (d c) -> p d c", d=2).unsqueeze(2).to_broadcast([128, 2, 2, 128])
                    yield
                    S.op("dve", lambda e, psA4=psA4, dec4=dec4: e.tensor_tensor(out=P["tmp"].rearrange("p (d v c) -> p d v c", d=2, v=2), in0=psA4, in1=dec4, op=ALU.mult),
                         reads=[kAB, pk + "dec"], writes=[pk + "tmp"])
                    yield
                    S.op("pool", lambda e: e.tensor_tensor(out=xn4[:, :, 0, :], in0=tmp3, in1=bm3, op=ALU.mult), reads=[pk + "tmp", pk + "bm"], writes=[kxn])
                    S.op("pool", lambda e: e.tensor_tensor(out=xn4[:, :, 0, :], in0=xn4[:, :, 0, :], in1=C.ident.unsqueeze(1).to_broadcast([128, 4, 128]), op=ALU.subtract),
                         reads=[kxn, "ident"], writes=[kxn])
                    yield
                    S.op("dve", lambda e, psB4=psB4, dec4=dec4: e.tensor_tensor(out=P["tmp2"].rearrange("p (d v c) -> p d v c", d=2, v=2), in0=psB4, in1=dec4, op=ALU.mult),
                         reads=[kAB, pk + "dec"], writes=[pk + "tmp2"])
                    yield
                    S.op("pool", lambda e: e.tensor_tensor(out=intra3, in0=tmp23, in1=mincl3, op=ALU.mult), reads=[pk + "tmp2", "mincl"], writes=[pk + "intra"])
                    yield
                    for d in range(2):
                        qb_ = QT[:, blks[d] * 128:(blks[d] + 1) * 128]
                        S.op("pool", lambda e, d=d, qb_=qb_: e.tensor_tensor(out=gq3[:, 2 * d:2 * d + 2, :], in0=qb_.unsqueeze(1).to_broadcast([128, 2, 128]), in1=gam3[:, 2 * d:2 * d + 2, :], op=ALU.mult),
                             reads=[hk + "q", pk + "gam"], writes=[pk + "gq"])
                        S.op("pool", lambda e, d=d: e.tensor_tensor(out=kd3[:, 2 * d:2 * d + 2, :], in0=kt3[:, blks[d], :].unsqueeze(1).to_broadcast([128, 2, 128]),
                                                                    in1=e12[:, 4 + 2 * d:6 + 2 * d].unsqueeze(2).to_broadcast([128, 2, 128]), op=ALU.mult),
                             reads=["ktok", pk + "e12"], writes=[pk + "kd"])
                    xn4 = P["xn"].rearrange("p (j o c) -> p j o c", j=4, o=2)
                    rt4 = P["rt"].rearrange("p (j o c) -> p j o c", j=4, o=2)
                    yy4 = P["yy"].rearrange("p (j o c) -> p j o c", j=4, o=2)
                    psM4 = psM.rearrange("p (j o c) -> p j o c", j=4, o=2)
                    kxn, krt_, kyy = pk + "xn", pk + "rt", pk + "yy"
                    yield
                    pass
                    yield
                    for j in range(4):
                        S.op("pe", lambda e, j=j: e.transpose(out=psTb[:, j * 128:(j + 1) * 128], in_=xn4[:, j, 0, :], identity=C.ident), reads=[kxn, "ident"], writes=[kT])
                    yield
                    S.op("act", lambda e: e.activation(out=xn4[:, :, 1, :], in_=_v3(psTb[:, 0:512], 4), func=AF.Copy), reads=[kT], writes=[kxn])
                    psTl = psG.bitcast(BF16)

                    def lmv(lv):
                        return lm4[:, lv, :, 0:128].unsqueeze(2).to_broadcast([128, 2, 2, 128])
                    h4 = lambda ap: ap.rearrange("p (d v) c -> p d v c", d=2)
                    yield
                    S.op("dve", lambda e: e.tensor_tensor(out=h4(rt4[:, :, 0, :]), in0=h4(xn4[:, :, 0, :]), in1=lmv(0), op=ALU.mult), reads=[kxn, "lmask"], writes=[krt_])
                    for lv in range(1, 7):
                        yield
                        for j in range(4):
                            S.op("pe", lambda e, j=j: e.matmul(psM4[:, j, 0, :], lhsT=xn4[:, j, 1, :], rhs=rt4[:, j, 0, :], start=True, stop=True), reads=[kxn, krt_], writes=[kM])
                        for j in range(4):
                            S.op("pe", lambda e, j=j: e.transpose(out=psTl[:, j * 128:(j + 1) * 128], in_=rt4[:, j, 0, :], identity=C.ident), reads=[krt_, "ident"], writes=[kG])
                        yield
                        S.op("dve", lambda e, lv=lv: e.tensor_tensor(out=h4(yy4[:, :, 0, :]), in0=h4(psM4[:, :, 0, :]), in1=lmv(lv), op=ALU.mult), reads=[kM, "lmask"], writes=[kyy])
                        S.op("act", lambda e: e.activation(out=rt4[:, :, 1, :], in_=_v3(psTl[:, 0:512], 4), func=AF.Copy), reads=[kG], writes=[pk + "tm"])
                        yield
                        for j in range(4):
                            S.op("pe", lambda e, j=j: e.matmul(psM4[:, j, 0, :], lhsT=rt4[:, j, 1, :], rhs=yy4[:, j, 0, :], start=True, stop=True), reads=[kyy, pk + "tm"], writes=[kM])
                        yield
                        S.op("act", lambda e: e.activation(out=rt4[:, :, 0, :], in_=psM4[:, :, 0, :], func=AF.Copy), reads=[kM], writes=[krt_])
                    RT = rt4
                    krt = krt_
                    return
                yield
                for j in range(4):
                    d = j // 2
                    kb_ = KT[:, blks[d] * 128:(blks[d] + 1) * 128]
                    S.op("pe", lambda e, j=j, kb_=kb_: e.matmul(psS1[:, j * 128:(j + 1) * 128], lhsT=kb_, rhs=S4b3[:, j, :], start=True, stop=True), reads=[hk + "k", "S4b"], writes=[kS1])
                yield
                S.op("dve", lambda e: e.tensor_tensor(out=t33, in0=_v3(psS1, 4), in1=e12[:, 0:4].unsqueeze(2).to_broadcast([128, 4, 128]), op=ALU.mult),
                     reads=[kS1, pk + "e12"], writes=[pk + "tmp"])
                yield
                for d in range(2):
                    S.op("pool" if d == 0 else "dve", lambda e, d=d: e.tensor_tensor(out=vd3[:, 2 * d:2 * d + 2, :], in0=t33[:, 2 * d:2 * d + 2, :], in1=vt4[:, blks[d], :, :], op=ALU.subtract),
                         reads=[pk + "tmp", "vtok"], writes=[pk + "vd"])
                yield
                for j in range(4):
                    S.op("pe", lambda e, j=j: e.matmul(psS2[:, j * 128:(j + 1) * 128], lhsT=RT[:, j, 0, :], rhs=vd3[:, j, :], start=True, stop=True), reads=[krt, pk + "vd"], writes=[kS2])
                yield
                S.op("dve", lambda e: e.tensor_tensor(out=vn3, in0=_v3(psS2, 4), in1=negb.unsqueeze(2).to_broadcast([128, 4, 128]), op=ALU.mult),
                     reads=[kS2, pk + "negb"], writes=[pk + "vn"])
                if s >= 2:
                    for j in range(4):
                        S.op("pe", lambda e, j=j: e.matmul(psS1[:, j * 128:(j + 1) * 128], lhsT=S4b3[:, j, :], rhs=gq3[:, j, :], start=True, stop=False), reads=["S4b", pk + "gq"], writes=[kS1])
                        S.op("pe", lambda e, j=j: e.matmul(psS1[:, j * 128:(j + 1) * 128], lhsT=vn3[:, j, :], rhs=intra3[:, j, :], start=False, stop=True), reads=[pk + "vn", pk + "intra"], writes=[kS1])
                    for d in range(2):
                        dstv = o3[:, :, blks[d] * 128:(blks[d] + 1) * 128]
                        srcv = _v3(psS1[:, d * 256:(d + 1) * 256], 2)
                        if s <= 9:
                            S.op("act", lambda e, dstv=dstv, srcv=srcv: e.activation(out=dstv, in_=srcv, func=AF.Copy), reads=[kS1], writes=["oacc"])
                        else:
                            S.op("dve", lambda e, dstv=dstv, srcv=srcv: e.tensor_tensor(out=dstv, in0=srcv, in1=dstv, op=ALU.add), reads=[kS1, "oacc"], writes=["oacc"])
                if s < NTT - 1:
                    for j in range(4):
                        S.op("pe", lambda e, j=j: e.matmul(psS2[:, j * 128:(j + 1) * 128], lhsT=kd3[:, j, :], rhs=vn3[:, j, :], start=True, stop=True), reads=[pk + "kd", pk + "vn"], writes=[kS2])
                    S.op("pool", lambda e: e.tensor_tensor(out=t43, in0=S43, in1=e12[:, 8:12].unsqueeze(2).to_broadcast([128, 4, 128]), op=ALU.mult),
                         reads=["S4", pk + "e12"], writes=[pk + "tmp2"])
                    S.op("dve", lambda e: e.tensor_tensor(out=S4, in0=psS2, in1=P["t4"], op=ALU.add), reads=[kS2, pk + "tmp2"], writes=["S4"])
                    S.op("act", lambda e: e.activation(out=S4b, in_=S4, func=AF.Copy), reads=["S4"], writes=["S4b"])
            nst_ = getattr(C, "p6_nsteps", NTT)
            for s_ in range(nst_):
                yield from step(s_, "A")
                yield from step(s_, "S")
            for vh in range(2):
                S.op("sp", lambda e, vh=vh, g=g: e.dma_start(out=szb1, in_=F1[b, 8192 + (2 * g + vh) * 128:8192 + (2 * g + vh + 1) * 128, 0:SEQ]),
                     reads=["F1"], writes=["szb1"], dma=1)
                for gi in range(4):
                    yield
                    t0 = gi * 512
                    sl = gi % 2
                    a_ = o3[:, vh, t0:t0 + 512]
                    S.op("pool", lambda e, a_=a_, sl=sl: e.tensor_tensor(out=sqb[sl], in0=a_, in1=a_, op=ALU.mult), reads=["oacc"], writes=["sqb6"])
                    S.op("pe", lambda e, sl=sl: e.matmul(psS1, lhsT=C.ones, rhs=sqb[sl], start=True, stop=True), reads=["ones", "sqb6"], writes=[kS1])
                    S.op("act", lambda e, sl=sl: e.activation(out=lnv[sl], in_=psS1, func=AF.Ln, scale=1.0 / 128, bias=C.eps_t[:, 0:1]), reads=[kS1], writes=["lnv6"])
                    S.op("act", lambda e, sl=sl: e.activation(out=rst[sl], in_=lnv[sl], func=AF.Exp, scale=-0.5), reads=["lnv6"], writes=["lnv6"])
                    S.op("dve", lambda e, sl=sl, a_=a_: e.scalar_tensor_tensor(out=osum[sl], in0=a_, scalar=onorm[:, 0:1], in1=rst[sl], op0=ALU.mult, op1=ALU.mult),
                         reads=["oacc", "lnv6", "onorm"], writes=["osum%d" % sl])
                    S.op("pool", lambda e, sl=sl, t0=t0: e.tensor_tensor(out=szb1[:, t0:t0 + 512], in0=osum[sl], in1=szb1[:, t0:t0 + 512], op=ALU.mult),
                         reads=["osum%d" % sl, "szb1"], writes=["szb1"])
                r0 = (2 * g + vh) * 128
                S.op("sp", lambda e, r0=r0: e.dma_start(out=OG1[b, r0:r0 + 128, :], in_=szb1), reads=["szb1"], writes=["OG1"], dma=1)

    heads_all = list(getattr(C, "p6_heads", range(16)))
    gens = [run_slot(0, heads_all[0::2]), run_slot(1, heads_all[1::2])]
    for _ in range(getattr(C, "p6_offset", 0)):
        try:
            next(gens[0])
        except StopIteration:
            gens.pop(0)
            break
    while gens:
        for g_ in list(gens):
            try:
                next(g_)
            except StopIteration:
                gens.remove(g_)
    S.barrier()


def phase7(C):
    nc, S, A = C.nc, C.S, C.A
    for b in range(NB):
        def res(tt, cb, b=b):
            return C.dram["X1"][b, tt * 128:(tt + 1) * 128, cb * 256:(cb + 1) * 256]

        def dst(tt, cb, b=b):
            return C.dram["X2"][b, tt * 128:(tt + 1) * 128, cb * 256:(cb + 1) * 256]
        out_proj(C, b, 1, C.dram["OG1"][b], 32, C.dram["dn_w_out"], res, dst, list(range(16)), "p7", cbw=256, ntok=SEQ)
    A.reset()
    fn = A.alloc(D)
    S.op("sp", lambda e: e.dma_start(out=fn, in_=C.dram["final_norm"].unsqueeze(0).to_broadcast([128, D])), writes=["fn"], dma=1)
    xr = [A.alloc(D) for _ in range(3)]
    xo = [A.alloc(D) for _ in range(3)]
    junk = A.alloc(D, BF16)
    st = [A.alloc(4) for _ in range(3)]
    it = 0
    for b in range(NB):
        for tt in range(16):
            sl = it % 3
            it += 1
            xt, xo_, s4 = xr[sl], xo[sl], st[sl]
            S.op("sp", lambda e, xt=xt, b=b, tt=tt: e.dma_start(out=xt, in_=C.dram["X2"][b, tt * 128:(tt + 1) * 128, :]), reads=["X2"], writes=["fxr%d" % sl], dma=1)
            S.op("act", lambda e, xt=xt, s4=s4: e.activation(out=junk, in_=xt, func=AF.Square, accum_out=s4[:, 0:1]), reads=["fxr%d" % sl], writes=["fjunk", "fst%d" % sl])
            S.op("act", lambda e, s4=s4: e.activation(out=s4[:, 1:2], in_=s4[:, 0:1], func=AF.Sqrt, scale=1.0 / D, bias=C.eps_t[:, 0:1]), reads=["fst%d" % sl], writes=["fst%d" % sl])
            S.op("dve", lambda e, s4=s4: e.reciprocal(out=s4[:, 2:3], in_=s4[:, 1:2]), reads=["fst%d" % sl], writes=["fst%d" % sl])
            S.op("dve", lambda e, xt=xt, xo_=xo_, s4=s4: e.scalar_tensor_tensor(out=xo_, in0=xt, scalar=s4[:, 2:3], in1=fn, op0=ALU.mult, op1=ALU.mult),
                 reads=["fxr%d" % sl, "fst%d" % sl, "fn"], writes=["fxo%d" % sl])
            S.op("sp", lambda e, xo_=xo_, b=b, tt=tt: e.dma_start(out=C.dram["OUT"][b, tt * 128:(tt + 1) * 128, :], in_=xo_), reads=["fxo%d" % sl], writes=["OUT"], dma=1)
    S.barrier()


NCORES = 8
_PHASES = (phase0, phase1, phase2, phase3, phase4, phase5, phase6, phase7)


def _host_inputs(inp):
    cos, sin = rope_tables()
    per_m, nt, mask, drow, dcol = na_geometry()
    rpb = np.asarray(inp["ab_rpb"][0], np.float32)
    nab = np.stack([rpb[h][drow, dcol] for h in range(8)], 0).astype(np.float32)
    mi, ms, id4 = dn_masks()
    lm, id8 = dn_level_masks()
    f = lambda a: np.ascontiguousarray(np.asarray(a, np.float32))
    shared = {
        "w_mod0": f(inp["ab_w_mod"][0]), "w_mod1": f(inp["dn_w_mod"][0]), "b_mod0": f(inp["ab_b_mod"][0]), "b_mod1": f(inp["dn_b_mod"][0]),
        "ab_norm": f(inp["ab_norm"][0]), "ab_w_in": f(inp["ab_w_in"][0]), "ab_w_qb": f(inp["ab_w_qb"][0]), "ab_w_kvb": f(inp["ab_w_kvb"][0]),
        "ab_q_norm": f(inp["ab_q_norm"][0]), "ab_kv_norm": f(inp["ab_kv_norm"][0]), "ab_w_out": f(inp["ab_w_out"][0]),
        "na_mask": mask, "na_bias": nab, "rope_cos": cos, "rope_sin": sin, "ident": np.eye(128, dtype=np.float32).astype(NPBF),
        "dn_norm": f(inp["dn_norm"][0]), "dn_w_in": f(inp["dn_w_in"][0]), "dn_conv": f(inp["dn_conv"][0]), "dn_a_log": f(inp["dn_a_log"][0]),
        "dn_dt_bias": f(inp["dn_dt_bias"][0]), "dn_o_norm": f(inp["dn_o_norm"][0]), "dn_w_out": f(inp["dn_w_out"][0]), "final_norm": f(inp["final_norm"]),
        "dn_mincl": mi, "dn_mstrict": ms, "dn_ident4": id4.astype(NPBF), "dn_lmask2": dn_level_masks2().astype(NPBF),
    }
    maps = []
    for i in range(NCORES):
        m = dict(shared)
        m["x"] = f(inp["x"][NB * i:NB * (i + 1)])
        m["ctx"] = f(inp["ctx"][NB * i:NB * (i + 1)])
        m["cvec"] = np.concatenate([f(inp["c"][NB * i:NB * (i + 1)]), f(inp["c_ctx"])[None]], 0)
        maps.append(m)
    return maps


_INTERNAL = {
    "mod": ([2, 3, 6144], F32), "F0": ([NB, 5184, T], BF16), "VA": ([NB, T, 1024], BF16), "M0": ([NB, 2560, T], BF16), "VM": ([NB, T, 1024], BF16),
    "OG0": ([NB, 2048, T], BF16), "X1": ([NB, T, D], F32), "F1": ([NB, 12288, T], BF16), "BA": ([NB, T, 128], F32), "OG1": ([NB, 4096, SEQ], BF16),
    "X2": ([NB, SEQ, D], F32),
}


def build_program(maps0, phases=_PHASES):
    nc = bass.Bass("TRN2", target_bir_lowering=False)
    with ExitStack() as es:
        dram = {}
        for nm, a in maps0.items():
            dram[nm] = nc.dram_tensor(nm, list(a.shape), BF16 if a.dtype == NPBF else F32, kind="ExternalInput").ap()
        for nm, (shape, dt_) in _INTERNAL.items():
            dram[nm] = nc.dram_tensor(nm, shape, dt_, kind="Internal").ap()
        dram["OUT"] = nc.dram_tensor("OUT", [NB, SEQ, D], F32, kind="ExternalOutput").ap()
        C = make_ctx(nc, es, dram)
        for p in phases:
            p(C)
        C.S.finalize()
    return nc


def kernel(**inputs):
    maps = _host_inputs(inputs)
    nc = build_program(maps[0])
    res = run_bass_kernel_spmd(nc, maps, core_ids=list(range(NCORES)))
    out = np.concatenate([np.asarray(r["OUT"], np.float32) for r in res.results], axis=0)
    return out
```

```python
import numpy as np
import ml_dtypes
from contextlib import ExitStack
import concourse.bass as bass
import concourse.mybir as mybir
from concourse.bass_utils import run_bass_kernel_spmd

F32 = mybir.dt.float32
BF16 = mybir.dt.bfloat16
AF = mybir.ActivationFunctionType
ALU = mybir.AluOpType
NPBF = ml_dtypes.bfloat16

D = 2048
SEQ = 2048
CTX = 256
T = SEQ + CTX
NTT = T // 128
NB = 2
EPS = 1e-6
TG = [(0, 512), (512, 512), (1024, 512), (1536, 512), (2048, 256)]


class Op:
    __slots__ = ("eng", "fn", "deps", "marked", "val", "sem", "is_dma")


class Buf:
    __slots__ = ("w", "r")

    def __init__(self):
        self.w = None
        self.r = []


class Sched:
    ENGS = ("pe", "act", "dve", "pool", "sp")
    ENGOBJ = {"pe": "tensor", "act": "scalar", "dve": "vector", "pool": "gpsimd", "sp": "sync"}

    def __init__(self, nc, es):
        self.nc = nc
        self.es = es
        self.ops = {e: [] for e in self.ENGS}
        self.bufs = {}
        self.sems = {e: es.enter_context(nc.semaphore("s_" + e)) for e in self.ENGS}
        self.dsems = {}
        self.dpool = []
        self.last_dma = {}
        self.nops = 0

    DRAMKEYS = {"mod", "F0", "VA", "M0", "VM", "OG0", "X1", "X2", "F1", "BA", "OG1", "OUT", "ST"}

    def dsem(self, key):
        if key not in self.dsems:
            i = len(self.dsems)
            if i >= len(self.dpool):
                self.dpool.append([self.es.enter_context(self.nc.semaphore("d_%d" % i)), 0])
            self.dsems[key] = self.dpool[i]
        return self.dsems[key]

    def op(self, eng, fn, reads=(), writes=(), dma=None):
        o = Op()
        o.eng = eng
        o.fn = fn
        o.deps = []
        o.marked = False
        o.val = None
        o.sem = None
        o.is_dma = dma is not None
        self.nops += 1
        if dma is not None:
            dk = None
            for k in list(writes) + list(reads):
                if not (k in self.DRAMKEYS or k.startswith("DR_")):
                    dk = k
                    break
            assert dk is not None, (reads, writes)
            d = self.dsem(dk)
            d[1] += 16
            o.sem = d[0]
            o.val = d[1]
            o.marked = True
            self.last_dma[id(d)] = o
        deps = {}
        for k in reads:
            b = self.bufs.get(k)
            if b is None:
                b = self.bufs[k] = Buf()
            if b.w is not None:
                deps[id(b.w)] = b.w
        for k in writes:
            b = self.bufs.get(k)
            if b is None:
                b = self.bufs[k] = Buf()
            if b.w is not None:
                deps[id(b.w)] = b.w
            for r in b.r:
                deps[id(r)] = r
        for k in reads:
            self.bufs[k].r.append(o)
        for k in writes:
            b = self.bufs[k]
            b.w = o
            b.r = []
        for d in deps.values():
            if d is o:
                continue
            if d.eng == "pe" and eng == "pe" and not d.is_dma:
                continue
            d.marked = True
            o.deps.append(d)
        self.ops[eng].append(o)
        return o

    def barrier(self):
        lasts = []
        for e in self.ENGS:
            for o in reversed(self.ops[e]):
                if not o.is_dma and o.fn is not None:
                    o.marked = True
                    lasts.append(o)
                    break
        lasts += list(self.last_dma.values())
        for e in self.ENGS:
            o = Op()
            o.eng = e
            o.fn = None
            o.deps = list(lasts)
            o.marked = False
            o.val = None
            o.sem = None
            o.is_dma = False
            self.ops[e].append(o)
        self.bufs = {}
        self.dsems = {}

    def finalize(self):
        for e in self.ENGS:
            c = 0
            for o in self.ops[e]:
                if o.is_dma or o.fn is None:
                    continue
                if o.marked:
                    c += 1
                    o.val = c
                    o.sem = self.sems[e]
        nc = self.nc
        with nc.Block() as block:
            for e in self.ENGS:
                ops = self.ops[e]

                def body(engine, ops=ops, e=e):
                    seen = {}
                    for o in ops:
                        for d in o.deps:
                            k = id(d.sem)
                            if seen.get(k, 0) >= d.val:
                                continue
                            seen[k] = d.val
                            engine.wait_ge(d.sem, d.val)
                        if o.fn is None:
                            continue
                        ins = o.fn(engine)
                        if o.is_dma:
                            ins.then_inc(o.sem, 16)
                        elif o.marked:
                            ins.then_inc(o.sem, 1)
                    if e == "sp":
                        for (s, v) in self.dpool:
                            if v > 0:
                                engine.wait_ge(s, v)

                getattr(block, self.ENGOBJ[e])(body)


class Arena:
    def __init__(self, nc, es, nwords=51200):
        self.t = es.enter_context(nc.sbuf_tensor("arena", [128, nwords], F32))
        self.n = nwords
        self.off = 0
        self.uid = 0

    def reset(self):
        self.off = 0

    def alloc(self, nelem, dtype=F32):
        nbytes = nelem * (4 if dtype == F32 else 2)
        words = (nbytes + 31) // 32 * 8
        assert self.off + words <= self.n, "SBUF arena overflow %d+%d" % (self.off, words)
        ap = self.t[:, self.off:self.off + words]
        self.off += words
        if dtype != F32:
            ap = ap.bitcast(dtype)
        return ap[:, 0:nelem]

    def key(self, name):
        self.uid += 1
        return "%s#%d" % (name, self.uid)


class Ctx:
    pass


def _v3(ap, a):
    return ap.rearrange("p (a b) -> p a b", a=a)


def phase0(C):
    nc, S, A = C.nc, C.S, C.A
    A.reset()
    csT = A.alloc(48)
    cs3 = _v3(csT, 16)
    bm = A.alloc(6144)
    osb = [A.alloc(2048), A.alloc(2048)]
    wr = [A.alloc(2048) for _ in range(4)]
    ps = C.ps
    for v in range(3):
        S.op("sp", lambda e, v=v: e.dma_start(out=cs3[:, :, v], in_=C.dram["cvec"][v, :].rearrange("(k p) -> p k", p=128),
                                              allow_slow_non_contiguous=True), writes=["csT"], dma="p0c")
    S.op("act", lambda e: e.activation(out=csT, in_=csT, func=AF.Silu), reads=["csT"], writes=["csT"])
    it = 0
    oi = 0
    for l in range(2):
        wm = C.dram["w_mod%d" % l]
        bmod = C.dram["b_mod%d" % l]
        S.op("sp", lambda e, bmod=bmod: e.dma_start(out=bm[0:3, :], in_=bmod.unsqueeze(0).to_broadcast([3, 6144])),
             writes=["bm"], dma="p0b")
        for g in range(3):
            for k in range(16):
                slot = it % 4
                it += 1
                wt = wr[slot]
                S.op("sp", lambda e, wt=wt, k=k, g=g, wm=wm: e.dma_start(out=wt, in_=wm[k * 128:(k + 1) * 128, g * 2048:(g + 1) * 2048]),
                     writes=["p0w%d" % slot], dma="p0w%d" % slot)
                for n in range(4):
                    S.op("pe", lambda e, wt=wt, k=k, n=n: e.matmul(ps[0:3, n * 512:(n + 1) * 512], lhsT=cs3[:, k, :], rhs=wt[:, n * 512:(n + 1) * 512],
                                                                    start=(k == 0), stop=(k == 15)),
                         reads=["csT", "p0w%d" % slot], writes=["p0ps%d" % n])
            ob = osb[oi % 2]
            okey = "p0o%d" % (oi % 2)
            oi += 1
            for n in range(4):
                S.op("dve", lambda e, ob=ob, n=n, g=g: e.tensor_tensor(out=ob[0:3, n * 512:(n + 1) * 512], in0=ps[0:3, n * 512:(n + 1) * 512],
                                                                       in1=bm[0:3, g * 2048 + n * 512:g * 2048 + (n + 1) * 512], op=ALU.add),
                     reads=["p0ps%d" % n, "bm"], writes=[okey])
            S.op("sp", lambda e, ob=ob, l=l, g=g: e.dma_start(out=C.dram["mod"][l, :, g * 2048:(g + 1) * 2048], in_=ob[0:3, :]),
                 reads=[okey], writes=["mod"], dma="p0o")
    S.barrier()


def load_modvecs(C, l, b, gain, tag):
    S, A = C.S, C.A
    g = A.alloc(16)
    S.op("sp", lambda e: e.dma_start(out=g, in_=gain.rearrange("(k p) -> p k", p=128), allow_slow_non_contiguous=True),
         writes=[tag + "g"], dma=tag + "v")
    res = []
    for vi, v in enumerate((b, 2)):
        sc = A.alloc(16)
        sh = A.alloc(16)
        S.op("sp", lambda e, sc=sc, v=v: e.dma_start(out=sc, in_=C.dram["mod"][l, v, 2048:4096].rearrange("(k p) -> p k", p=128),
                                                     allow_slow_non_contiguous=True), reads=["mod"], writes=[tag + "sc%d" % vi], dma=tag + "v")
        S.op("sp", lambda e, sh=sh, v=v: e.dma_start(out=sh, in_=C.dram["mod"][l, v, 0:2048].rearrange("(k p) -> p k", p=128),
                                                     allow_slow_non_contiguous=True), reads=["mod"], writes=[tag + "sh%d" % vi], dma=tag + "v")
        S.op("dve", lambda e, sc=sc: e.scalar_tensor_tensor(out=sc, in0=sc, scalar=1.0, in1=g, op0=ALU.add, op1=ALU.mult),
             reads=[tag + "sc%d" % vi, tag + "g"], writes=[tag + "sc%d" % vi])
        res.append((sc, sh, tag + "sc%d" % vi, tag + "sh%d" % vi))
    return res


def build_xmT(C, xmT, xkey, src_tiles, mv, tag):
    S, A = C.S, C.A
    xr = [A.alloc(D) for _ in range(2)]
    xh = [A.alloc(D, BF16) for _ in range(2)]
    junk = A.alloc(D, BF16)
    st = [A.alloc(4) for _ in range(2)]
    xm3 = _v3(xmT, 16)
    for tt in range(NTT):
        sl = tt % 2
        xt, xb, s4 = xr[sl], xh[sl], st[sl]
        kx, kb_, ks = tag + "x%d" % sl, tag + "xh%d" % sl, tag + "st%d" % sl
        ms, sh, kms, ksh = mv[0] if tt < 16 else mv[1]
        S.op("sp", lambda e, xt=xt, tt=tt: e.dma_start(out=xt, in_=src_tiles[tt]), writes=[kx], dma=kx)
        S.op("act", lambda e, xt=xt, s4=s4: e.activation(out=junk, in_=xt, func=AF.Square, accum_out=s4[:, 0:1]),
             reads=[kx], writes=[tag + "junk", ks])
        S.op("act", lambda e, s4=s4: e.activation(out=s4[:, 1:2], in_=s4[:, 0:1], func=AF.Sqrt, scale=1.0 / D, bias=C.eps_t[:, 0:1]),
             reads=[ks], writes=[ks])
        S.op("dve", lambda e, s4=s4: e.reciprocal(out=s4[:, 2:3], in_=s4[:, 1:2]), reads=[ks], writes=[ks])
        S.op("dve", lambda e, xt=xt, xb=xb, s4=s4: e.tensor_scalar(out=xb, in0=xt, scalar1=s4[:, 2:3], scalar2=None, op0=ALU.mult),
             reads=[kx, ks], writes=[kb_])
        pb = C.ps[:, (tt % 2) * 1024:(tt % 2) * 1024 + 1024].bitcast(BF16)
        kp = tag + "tp%d" % (tt % 2)
        for k in range(16):
            S.op("pe", lambda e, pb=pb, xb=xb, k=k: e.transpose(out=pb[:, k * 128:(k + 1) * 128], in_=xb[:, k * 128:(k + 1) * 128], identity=C.ident),
                 reads=[kb_, "ident"], writes=[kp])
        for k in range(16):
            o_ = xm3[:, k, tt * 128:(tt + 1) * 128]
            i_ = pb[:, k * 128:(k + 1) * 128]
            if k % 2 == 0:
                S.op("act", lambda e, o_=o_, i_=i_, k=k, ms=ms, sh=sh: e.activation(out=o_, in_=i_, func=AF.Identity, bias=sh[:, k:k + 1], scale=ms[:, k:k + 1]),
                     reads=[kp, kms, ksh], writes=[xkey + str(tt)])
            else:
                S.op("dve", lambda e, o_=o_, i_=i_, k=k, ms=ms, sh=sh: e.tensor_scalar(out=o_, in0=i_, scalar1=ms[:, k:k + 1], scalar2=sh[:, k:k + 1], op0=ALU.mult, op1=ALU.add),
                     reads=[kp, kms, ksh], writes=[xkey + str(tt)])


def proj_fm(C, xm3, xkey, w3, wkey, ncols, evac, tag, m_off=0):
    S = C.S
    for sb in range((ncols + 127) // 128):
        m = min(128, ncols - sb * 128)
        for gi, (t0, tn) in enumerate(TG):
            bank = 4 + (C.pcnt % 4)
            C.pcnt += 1
            pa = C.ps[0:m, bank * 512:bank * 512 + tn]
            pk = "psb%d" % bank
            for k in range(16):
                S.op("pe", lambda e, pa=pa, k=k, sb=sb, m=m, t0=t0, tn=tn: e.matmul(pa, lhsT=w3[:, k, sb * 128:sb * 128 + m], rhs=xm3[:, k, t0:t0 + tn],
                                                                                  start=(k == 0), stop=(k == 15)),
                     reads=[wkey] + [xkey + str(t) for t in range(t0 // 128, (t0 + tn) // 128)], writes=[pk])
            evac(sb, gi, t0, tn, pa, pk, m)


def phase1(C):
    nc, S, A = C.nc, C.S, C.A
    w_in = C.dram["ab_w_in"]
    segs = [("qa", 0, 512), ("qa", 512, 512), ("ka", 1024, 512), ("ka", 1536, 512), ("va", 2048, 512), ("va", 2560, 512),
            ("cq", 3072, 512), ("ckv", 3584, 512), ("kr", 4096, 64), ("krsw", 4096, 64),
            ("z", 4160, 512), ("z", 4672, 512), ("z", 5184, 512), ("z", 5696, 512)]
    frow = {"qa": 0, "ka": 1024 - 1024, "cq": 2048 - 3072, "ckv": 2560 - 3584, "z": 3072 - 4160}
    for b in range(NB):
        _phase1_b(C, b, segs, frow)


def _phase1_b(C, b, segs, frow):
    nc, S, A = C.nc, C.S, C.A
    w_in = C.dram["ab_w_in"]
    if True:
        A.reset()
        mv = load_modvecs(C, 0, b, C.dram["ab_norm"], "p1m%d" % b)
        xmT = A.alloc(16 * T, BF16)
        xm3 = _v3(xmT, 16)
        xkey = "xmT"
        mark = A.off
        tiles = [C.dram["x"][b, tt * 128:(tt + 1) * 128, :] for tt in range(16)] + [C.dram["ctx"][b, tt * 128:(tt + 1) * 128, :] for tt in range(2)]
        build_xmT(C, xmT, xkey, tiles, mv, "p1b")
        pass
        wr = [A.alloc(16 * 512, BF16) for _ in range(2)]
        stg = [A.alloc(T, BF16) for _ in range(3)]
        vst = [A.alloc(512, BF16) for _ in range(2)]
        krp = A.alloc(T)
        kro = A.alloc(T, BF16)
        cos_t = A.alloc(SEQ)
        sin_t = A.alloc(SEQ)
        tmpf = A.alloc(512)
        tmpg = A.alloc(512)
        S.op("sp", lambda e: e.dma_start(out=cos_t[0:64, :], in_=C.dram["rope_cos"]), writes=["cos"], dma="p1c")
        S.op("sp", lambda e: e.dma_start(out=sin_t[0:64, :], in_=C.dram["rope_sin"]), writes=["sin"], dma="p1c")
        F0 = C.dram["F0"]
        VA = C.dram["VA"]
        sc = [0]
        for wi, (nm, c0, ncw) in enumerate(segs):
            slot = wi % 2
            w3 = _v3(wr[slot], 16)[:, :, 0:ncw]
            wkey = "p1w%d" % slot
            if nm == "krsw":
                for (d0, s0) in ((0, 16), (16, 0), (32, 48), (48, 32)):
                    S.op("pool", lambda e, w3=w3, d0=d0, s0=s0: e.dma_start(out=w3[:, :, d0:d0 + 16],
                                                                           in_=w_in[:, 4096 + s0:4096 + s0 + 16].rearrange("(k p) c -> p k c", p=128)),
                         writes=[wkey], dma=wkey)
            else:
                S.op("pool", lambda e, w3=w3, c0=c0, ncw=ncw: e.dma_start(out=w3, in_=w_in[:, c0:c0 + ncw].rearrange("(k p) c -> p k c", p=128)),
                     writes=[wkey], dma=wkey)
            if nm == "va":
                for tt in range(NTT):
                    bank = 4 + (C.pcnt % 4)
                    C.pcnt += 1
                    pa = C.ps[:, bank * 512:bank * 512 + 512]
                    pk = "psb%d" % bank
                    for k in range(16):
                        S.op("pe", lambda e, pa=pa, k=k, tt=tt, w3=w3: e.matmul(pa, lhsT=xm3[:, k, tt * 128:(tt + 1) * 128], rhs=w3[:, k, :], start=(k == 0), stop=(k == 15)),
                             reads=[wkey, xkey + str(tt)], writes=[pk])
                    vs = vst[tt % 2]
                    vk = "p1vs%d" % (tt % 2)
                    eng = "act" if tt % 2 == 0 else "dve"
                    if eng == "act":
                        S.op("act", lambda e, vs=vs, pa=pa: e.activation(out=vs, in_=pa, func=AF.Copy), reads=[pk], writes=[vk])
                    else:
                        S.op("dve", lambda e, vs=vs, pa=pa: e.tensor_copy(out=vs, in_=pa), reads=[pk], writes=[vk])
                    S.op("sp", lambda e, vs=vs, tt=tt, c0=c0: e.dma_start(out=VA[b, tt * 128:(tt + 1) * 128, c0 - 2048:c0 - 2048 + 512], in_=vs),
                         reads=[vk], writes=["VA"], dma=vk)
                continue

            def evac(sb, gi, t0, tn, pa, pk, m, nm=nm, c0=c0):
                if nm in ("kr", "krsw"):
                    if nm == "kr":
                        S.op("dve", lambda e: e.tensor_copy(out=krp[0:64, t0:t0 + tn], in_=pa), reads=[pk], writes=["krp"])
                    else:
                        if gi < 4:
                            S.op("dve", lambda e: e.tensor_tensor(out=tmpf[0:64, 0:tn], in0=pa, in1=sin_t[0:64, t0:t0 + tn], op=ALU.mult),
                                 reads=[pk, "sin"], writes=["tmpf"])
                            S.op("pool", lambda e: e.tensor_tensor(out=tmpg[0:64, 0:tn], in0=krp[0:64, t0:t0 + tn], in1=cos_t[0:64, t0:t0 + tn], op=ALU.mult),
                                 reads=["krp", "cos"], writes=["tmpg"])
                            S.op("dve", lambda e: e.tensor_tensor(out=kro[0:64, t0:t0 + tn], in0=tmpf[0:64, 0:tn], in1=tmpg[0:64, 0:tn], op=ALU.add),
                                 reads=["tmpf", "tmpg"], writes=["kro"])
                        if gi == 4:
                            S.op("dve", lambda e: e.tensor_copy(out=kro[0:64, t0:t0 + tn], in_=krp[0:64, t0:t0 + tn]), reads=["krp"], writes=["kro"])
                            S.op("sp", lambda e: e.dma_start(out=F0[b, 5120:5184, :], in_=kro[0:64, :]), reads=["kro"], writes=["F0"], dma="p1kr")
                    return
                if gi == 0:
                    sc[0] += 1
                si = sc[0] % 3
                sg = stg[si]
                sk = "p1st%d" % si
                func = AF.Silu if nm == "z" else AF.Copy
                if nm == "z" or (gi % 2 == 0):
                    S.op("act", lambda e: e.activation(out=sg[0:m, t0:t0 + tn], in_=pa, func=func), reads=[pk], writes=[sk])
                else:
                    S.op("dve", lambda e: e.tensor_copy(out=sg[0:m, t0:t0 + tn], in_=pa), reads=[pk], writes=[sk])
                if gi == 4:
                    r0 = c0 + sb * 128 + frow[nm]
                    S.op("sp", lambda e: e.dma_start(out=F0[b, r0:r0 + m, :], in_=sg[0:m, :]), reads=[sk], writes=["F0"], dma=sk)

            proj_fm(C, xm3, xkey, w3, wkey, ncw, evac, "p1")
        S.barrier()


def rope_tables():
    quarter = 16
    inv = (10000.0 ** (-np.arange(quarter, dtype=np.float32) / quarter)).astype(np.float32)
    pos = np.arange(SEQ)
    cos = np.zeros((64, SEQ), np.float32)
    sin = np.zeros((64, SEQ), np.float32)
    for half, p in ((0, pos // 64), (1, pos % 64)):
        ang = p.astype(np.float32)[None, :] * inv[:, None]
        c, s = np.cos(ang), np.sin(ang)
        cos[half * 32:half * 32 + 16] = c
        cos[half * 32 + 16:half * 32 + 32] = c
        sin[half * 32:half * 32 + 16] = -s
        sin[half * 32 + 16:half * 32 + 32] = s
    return cos, sin


def make_ctx(nc, es, dram):
    C = Ctx()
    C.nc = nc
    C.S = Sched(nc, es)
    C.es = es
    C.A = Arena(nc, es)
    C.ps = es.enter_context(nc.psum_tensor("ps", [128, 4096], F32))
    C.dram = dram
    C.pcnt = 0
    C.na_geo = na_geometry()
    C.cst = es.enter_context(nc.sbuf_tensor("cst", [128, 512], F32))
    C.ident = C.cst[:, 0:64].bitcast(BF16)
    C.eps_t = C.cst[:, 64:65]
    C.ones = C.cst[:, 72:136].bitcast(BF16)
    C.S.op("sp", lambda e: e.dma_start(out=C.ident, in_=dram["ident"]), writes=["ident"], dma="cst")
    C.S.op("pool", lambda e: e.memset(C.eps_t, EPS), writes=["eps"])
    C.S.op("pool", lambda e: e.memset(C.ones, 1.0), writes=["ones"])
    C.lnq = C.cst[:, 65:66]
    C.zero_t = C.cst[:, 66:67]
    C.cw = C.cst[:, 136:456]
    C.S.op("pool", lambda e: e.memset(C.lnq, float(np.log(128.0 ** -0.5))), writes=["lnq"])
    C.S.op("pool", lambda e: e.memset(C.zero_t, 0.0), writes=["zero"])
    if "dn_conv" in dram:
        cw3 = C.cw.rearrange("p (blk j) -> p blk j", j=5)
        for j in range(5):
            for q4 in range(8):
                C.S.op("sp", lambda e, j=j, q4=q4: e.dma_start(out=cw3[:, q4 * 8:(q4 + 1) * 8, j], in_=dram["dn_conv"][j, q4 * 1024:(q4 + 1) * 1024].rearrange("(blk p) -> p blk", p=128),
                                                          allow_slow_non_contiguous=True), writes=["cw"], dma="cw")
    C.S.barrier()
    return C


MLA_SCALE = 192.0 ** -0.5
NA_SCALE = 128.0 ** -0.5


def phase2(C):
    for b in range(NB):
        _phase2_b(C, b)


def _phase2_b(C, b):
    nc, S, A = C.nc, C.S, C.A
    A.reset()
    F0, M0, VM = C.dram["F0"], C.dram["M0"], C.dram["VM"]
    wqb = A.alloc(4 * 1536, BF16)
    wqb3 = _v3(wqb, 4)
    wqsw = A.alloc(4 * 512, BF16)
    wqsw4 = wqsw.rearrange("p (k h c) -> p k h c", k=4, h=8)
    wkvb = A.alloc(4 * 2048, BF16)
    wkvb3 = _v3(wkvb, 4)
    wkvb4 = wkvb.rearrange("p (k h c) -> p k h c", k=4, h=8)
    cq = A.alloc(4 * T, BF16)
    ckv = A.alloc(4 * T, BF16)
    cqn = A.alloc(4 * T, BF16)
    ckvn = A.alloc(4 * T, BF16)
    sq = [A.alloc(4 * 512, BF16) for _ in range(2)]
    lnv = [A.alloc(512) for _ in range(2)]
    rst = [A.alloc(512) for _ in range(2)]
    qnm = A.alloc(4)
    kvnm = A.alloc(4)
    cos_t = A.alloc(SEQ)
    sin_t = A.alloc(SEQ)
    stg = [A.alloc(T, BF16) for _ in range(3)]
    vst = [A.alloc(1024, BF16) for _ in range(2)]
    t1 = [A.alloc(512) for _ in range(2)]
    t2 = [A.alloc(512) for _ in range(2)]
    wq_d, wkv_d = C.dram["ab_w_qb"], C.dram["ab_w_kvb"]
    S.op("pool", lambda e: e.dma_start(out=wqb3, in_=wq_d.rearrange("(k p) c -> p k c", p=128)), writes=["wqb"], dma="p2w")
    S.op("pool", lambda e: e.dma_start(out=wkvb3, in_=wkv_d.rearrange("(k p) c -> p k c", p=128)), writes=["wkvb"], dma="p2w")
    wq4 = wq_d.rearrange("(k p) (h c) -> p k h c", p=128, h=8)
    for k in range(4):
        for (d0, s0) in ((0, 16), (16, 0), (32, 48), (48, 32)):
            S.op("pool", lambda e, k=k, d0=d0, s0=s0: e.dma_start(out=wqsw4[:, k, :, d0:d0 + 16], in_=wq4[:, k, :, 128 + s0:128 + s0 + 16]),
                 writes=["wqsw"], dma="p2w")
    S.op("sp", lambda e: e.dma_start(out=_v3(cq, 4), in_=F0[b, 2048:2560, :].rearrange("(k p) t -> p k t", p=128)), reads=["F0"], writes=["cq"], dma="p2a")
    S.op("sp", lambda e: e.dma_start(out=_v3(ckv, 4), in_=F0[b, 2560:3072, :].rearrange("(k p) t -> p k t", p=128)), reads=["F0"], writes=["ckv"], dma="p2a")
    S.op("sp", lambda e: e.dma_start(out=qnm, in_=C.dram["ab_q_norm"].rearrange("(k p) -> p k", p=128), allow_slow_non_contiguous=True), writes=["qnm"], dma="p2a")
    S.op("sp", lambda e: e.dma_start(out=kvnm, in_=C.dram["ab_kv_norm"].rearrange("(k p) -> p k", p=128), allow_slow_non_contiguous=True), writes=["kvnm"], dma="p2a")
    S.op("sp", lambda e: e.dma_start(out=cos_t[0:64, :], in_=C.dram["rope_cos"]), writes=["cos"], dma="p2a")
    S.op("sp", lambda e: e.dma_start(out=sin_t[0:64, :], in_=C.dram["rope_sin"]), writes=["sin"], dma="p2a")
    it = 0
    for (src, skey, nrm, nkey, dst, dkey) in ((cq, "cq", qnm, "qnm", cqn, "cqn"), (ckv, "ckv", kvnm, "kvnm", ckvn, "ckvn")):
        s3, d3 = _v3(src, 4), _v3(dst, 4)
        for gi, (t0, tn) in enumerate(TG):
            sl = it % 2
            it += 1
            q3 = _v3(sq[sl], 4)
            S.op("pool", lambda e, q3=q3, s3=s3, t0=t0, tn=tn: e.tensor_tensor(out=q3[:, :, 0:tn], in0=s3[:, :, t0:t0 + tn], in1=s3[:, :, t0:t0 + tn], op=ALU.mult),
                 reads=[skey], writes=["sq%d" % sl])
            bank = 4 + sl
            pa = C.ps[:, bank * 512:bank * 512 + tn]
            for k in range(4):
                S.op("pe", lambda e, pa=pa, q3=q3, k=k, tn=tn: e.matmul(pa, lhsT=C.ones, rhs=q3[:, k, 0:tn], start=(k == 0), stop=(k == 3)),
                     reads=["ones", "sq%d" % sl], writes=["psb%d" % bank])
            lv, rs = lnv[sl], rst[sl]
            S.op("act", lambda e, lv=lv, pa=pa, tn=tn: e.activation(out=lv[:, 0:tn], in_=pa, func=AF.Ln, scale=1.0 / 512, bias=C.eps_t[:, 0:1]),
                 reads=["psb%d" % bank], writes=["lnv%d" % sl])
            S.op("act", lambda e, lv=lv, rs=rs, tn=tn: e.activation(out=rs[:, 0:tn], in_=lv[:, 0:tn], func=AF.Exp, scale=-0.5),
                 reads=["lnv%d" % sl], writes=["rst%d" % sl])
            for k in range(4):
                S.op("dve", lambda e, d3=d3, s3=s3, k=k, t0=t0, tn=tn, nrm=nrm, rs=rs: e.scalar_tensor_tensor(
                    out=d3[:, k, t0:t0 + tn], in0=s3[:, k, t0:t0 + tn], scalar=nrm[:, k:k + 1], in1=rs[:, 0:tn], op0=ALU.mult, op1=ALU.mult),
                    reads=[skey, nkey, "rst%d" % sl], writes=[dkey])
    cqn3, ckvn3 = _v3(cqn, 4), _v3(ckvn, 4)
    sc = [0]

    def small_proj(lhs_fn, rhs3, rkey, wkey, m, gi, t0, tn):
        bank = 4 + (C.pcnt % 4)
        C.pcnt += 1
        pa = C.ps[0:m, bank * 512:bank * 512 + tn]
        for k in range(4):
            S.op("pe", lambda e, pa=pa, k=k: e.matmul(pa, lhsT=lhs_fn(k), rhs=rhs3[:, k, t0:t0 + tn], start=(k == 0), stop=(k == 3)),
                 reads=[wkey, rkey], writes=["psb%d" % bank])
        return pa, "psb%d" % bank

    for h in range(8):
        for (nm, lhs_fn, rhs3, rkey, wkey, row0) in (
                ("qn", lambda k, h=h: wqb3[:, k, h * 192:h * 192 + 128], cqn3, "cqn", "wqb", h * 128),
                ("kn", lambda k, h=h: wkvb3[:, k, h * 256:h * 256 + 128], ckvn3, "ckvn", "wkvb", 1536 + h * 128)):
            sc[0] += 1
            si = sc[0] % 3
            sg, sk = stg[si], "p2st%d" % si
            for gi, (t0, tn) in enumerate(TG):
                pa, pk = small_proj(lhs_fn, rhs3, rkey, wkey, 128, gi, t0, tn)
                if gi % 2 == 0:
                    S.op("act", lambda e, sg=sg, pa=pa, t0=t0, tn=tn: e.activation(out=sg[:, t0:t0 + tn], in_=pa, func=AF.Copy), reads=[pk], writes=[sk])
                else:
                    S.op("dve", lambda e, sg=sg, pa=pa, t0=t0, tn=tn: e.tensor_copy(out=sg[:, t0:t0 + tn], in_=pa), reads=[pk], writes=[sk])
            S.op("sp", lambda e, sg=sg, row0=row0: e.dma_start(out=M0[b, row0:row0 + 128, :], in_=sg), reads=[sk], writes=["M0"], dma=sk)
        sc[0] += 1
        si = sc[0] % 3
        sg, sk = stg[si], "p2st%d" % si
        for gi, (t0, tn) in enumerate(TG):
            pa, pk = small_proj(lambda k, h=h: wqb3[:, k, h * 192 + 128:h * 192 + 192], cqn3, "cqn", "wqb", 64, gi, t0, tn)
            if gi == 4:
                S.op("dve", lambda e, sg=sg, pa=pa, t0=t0, tn=tn: e.tensor_copy(out=sg[0:64, t0:t0 + tn], in_=pa), reads=[pk], writes=[sk])
                continue
            pb, pkb = small_proj(lambda k, h=h: wqsw4[:, k, h, :], cqn3, "cqn", "wqsw", 64, gi, t0, tn)
            a1, a2 = t1[gi % 2], t2[gi % 2]
            S.op("dve", lambda e, a1=a1, pa=pa, t0=t0, tn=tn: e.tensor_tensor(out=a1[0:64, 0:tn], in0=pa, in1=cos_t[0:64, t0:t0 + tn], op=ALU.mult),
                 reads=[pk, "cos"], writes=["t1%d" % (gi % 2)])
            S.op("dve", lambda e, a2=a2, pb=pb, t0=t0, tn=tn: e.tensor_tensor(out=a2[0:64, 0:tn], in0=pb, in1=sin_t[0:64, t0:t0 + tn], op=ALU.mult),
                 reads=[pkb, "sin"], writes=["t2%d" % (gi % 2)])
            S.op("pool", lambda e, a1=a1, a2=a2, sg=sg, t0=t0, tn=tn: e.tensor_tensor(out=sg[0:64, t0:t0 + tn], in0=a1[0:64, 0:tn], in1=a2[0:64, 0:tn], op=ALU.add),
                 reads=["t1%d" % (gi % 2), "t2%d" % (gi % 2)], writes=[sk])
        S.op("sp", lambda e, sg=sg, h=h: e.dma_start(out=M0[b, 1024 + h * 64:1024 + h * 64 + 64, :], in_=sg[0:64, :]), reads=[sk], writes=["M0"], dma=sk)
    for tt in range(NTT):
        vs, vk = vst[tt % 2], "p2vs%d" % (tt % 2)
        for half in range(2):
            bank = 4 + (C.pcnt % 4)
            C.pcnt += 1
            pa = C.ps[:, bank * 512:bank * 512 + 512]
            for k in range(4):
                S.op("pe", lambda e, pa=pa, k=k, tt=tt, half=half: e.matmul(pa.rearrange("p (h c) -> p h c", h=4), lhsT=ckvn3[:, k, tt * 128:(tt + 1) * 128],
                                                                         rhs=wkvb4[:, k, half * 4:half * 4 + 4, 128:256], start=(k == 0), stop=(k == 3)),
                     reads=["wkvb", "ckvn"], writes=["psb%d" % bank])
            if half == 0:
                S.op("act", lambda e, vs=vs, pa=pa: e.activation(out=vs[:, 0:512], in_=pa, func=AF.Copy), reads=["psb%d" % bank], writes=[vk])
            else:
                S.op("dve", lambda e, vs=vs, pa=pa: e.tensor_copy(out=vs[:, 512:1024], in_=pa), reads=["psb%d" % bank], writes=[vk])
        S.op("sp", lambda e, vs=vs, tt=tt: e.dma_start(out=VM[b, tt * 128:(tt + 1) * 128, :], in_=vs), reads=[vk], writes=["VM"], dma=vk)
    S.barrier()


def na_geometry():
    rows = 32
    r = np.arange(rows)
    r0 = np.clip(r - 4, 0, rows - 8)
    col = np.arange(64)
    c0 = np.clip(col - 8, 0, 64 - 16)
    tiles = {}
    per_m = []
    for m in range(16):
        lo = min(r0[2 * m], r0[2 * m + 1])
        hi = max(r0[2 * m], r0[2 * m + 1]) + 7
        lst = []
        for kb in range(lo // 2, hi // 2 + 1):
            memb = tuple(tuple(bool(r0[2 * m + bq] <= 2 * kb + a <= r0[2 * m + bq] + 7) for bq in range(2)) for a in range(2))
            key = (kb - m, memb)
            if key not in tiles:
                tiles[key] = len(tiles)
            lst.append((kb, tiles[key]))
        per_m.append(lst)
    nt = len(tiles)
    mask = np.zeros((nt, 128, 128), np.float32)
    drow = np.zeros((nt, 128, 128), np.int64)
    dcol = np.zeros((nt, 128, 128), np.int64)
    for (delta, memb), ti in tiles.items():
        for a in range(2):
            for bq in range(2):
                kc = np.arange(64)[:, None]
                qc = np.arange(64)[None, :]
                ok = memb[a][bq] & (kc >= c0[qc]) & (kc <= c0[qc] + 15)
                dr = 2 * delta + a - bq + 7
                dc = kc - qc + 15
                blk = (slice(a * 64, a * 64 + 64), slice(bq * 64, bq * 64 + 64))
                mask[ti][blk] = np.where(ok, 0.0, -20000.0)
                drow[ti][blk] = np.clip(np.where(ok, dr, 0), 0, 14)
                dcol[ti][blk] = np.clip(np.where(ok, dc, 0), 0, 30)
    return per_m, nt, mask, drow, dcol


def phase3(C):
    for b in range(NB):
        _phase3_b(C, b)


def _phase3_b(C, b):
    nc, S, A = C.nc, C.S, C.A
    A.reset()
    F0, M0, VM, VA, OG = C.dram["F0"], C.dram["M0"], C.dram["VM"], C.dram["VA"], C.dram["OG0"]
    per_m, nt, _, _, _ = C.na_geo
    krT = A.alloc(T, BF16)
    S.op("sp", lambda e: e.dma_start(out=krT[0:64, :], in_=F0[b, 5120:5184, :]), reads=["F0"], writes=["krT"], dma="p3k")
    nam = A.alloc(nt * 128)
    S.op("sp", lambda e: e.dma_start(out=_v3(nam, nt), in_=C.dram["na_mask"].rearrange("t k q -> k t q")), writes=["nam"], dma="p3k")
    hb = []
    for i in range(2):
        hb.append(dict(k=A.alloc(T, BF16), q=A.alloc(T, BF16), qr=A.alloc(T, BF16), v=A.alloc(NTT * 128, BF16), sz=A.alloc(T, BF16),
                       bias=A.alloc(nt * 128)))
    pT = [A.alloc(512, BF16) for _ in range(4)]
    sbf = [A.alloc(128) for _ in range(3)]
    rinv = [A.alloc(512) for _ in range(2)]
    tmp = [A.alloc(512) for _ in range(2)]
    ogs = [A.alloc(T, BF16) for _ in range(2)]
    cnt = dict(s=0, p=0, g=0, sb=0)

    def attend_block(lhsT_list, rhs_list, tn, reads, o_cols, first, last, vlhs, vkey, scale, bias=None, bkey=None, acc=None):
        bank = cnt["s"] % 4
        cnt["s"] += 1
        pS = C.ps[:, bank * 512:bank * 512 + tn]
        pk = "psb%d" % bank
        n = len(lhsT_list)
        for i in range(n):
            S.op("pe", lambda e, i=i: e.matmul(pS[0:128, :], lhsT=lhsT_list[i], rhs=rhs_list[i], start=(i == 0), stop=(i == n - 1)),
                 reads=reads, writes=[pk])
        slot = cnt["p"] % 4
        cnt["p"] += 1
        p_ = pT[slot][:, 0:tn]
        pkey = "pT%d" % slot
        if bias is None:
            S.op("act", lambda e: e.activation(out=p_, in_=pS, func=AF.Exp, scale=scale), reads=[pk], writes=[pkey])
        else:
            sslot = cnt["sb"] % 3
            cnt["sb"] += 1
            sb_ = sbf[sslot][:, 0:tn]
            S.op("dve", lambda e: e.scalar_tensor_tensor(out=sb_, in0=pS, scalar=scale, in1=bias, op0=ALU.mult, op1=ALU.add),
                 reads=[pk, bkey], writes=["sbf%d" % sslot])
            S.op("act", lambda e: e.activation(out=p_, in_=sb_, func=AF.Exp), reads=["sbf%d" % sslot], writes=[pkey])
        po, ps_, ok_, sk_ = acc

        def part2():
            S.op("pe", lambda e: e.matmul(po[:, o_cols[0]:o_cols[0] + tn], lhsT=vlhs, rhs=p_, start=first, stop=last), reads=[pkey, vkey], writes=[ok_])
            S.op("pe", lambda e: e.matmul(ps_[:, o_cols[0]:o_cols[0] + tn], lhsT=C.ones, rhs=p_, start=first, stop=last), reads=[pkey, "ones"], writes=[sk_])
        pend.append(part2)
        while len(pend) > 3:
            pend.pop(0)()

    pend = []

    def finish_group(acc, tn, sz, szkey, og, ogkey, t0):
        while pend:
            pend.pop(0)()
        po, ps_, ok_, sk_ = acc
        g = cnt["g"] % 2
        cnt["g"] += 1
        S.op("dve", lambda e: e.reciprocal(out=rinv[g][:, 0:tn], in_=ps_[:, 0:tn]), reads=[sk_], writes=["rinv%d" % g])
        S.op("dve", lambda e: e.tensor_tensor(out=tmp[g][:, 0:tn], in0=po[:, 0:tn], in1=rinv[g][:, 0:tn], op=ALU.mult),
             reads=[ok_, "rinv%d" % g], writes=["tmp%d" % g])
        S.op("pool", lambda e: e.tensor_tensor(out=og[:, t0:t0 + tn], in0=tmp[g][:, 0:tn], in1=sz[:, t0:t0 + tn], op=ALU.mult),
             reads=["tmp%d" % g, szkey], writes=[ogkey])

    def acc_banks():
        g = cnt["g"] % 2
        return (C.ps[:, (4 + g) * 512:(5 + g) * 512], C.ps[:, (6 + g) * 512:(7 + g) * 512], "psb%d" % (4 + g), "psb%d" % (6 + g))

    for hh in range(16):
        is_mla = hh >= 8
        h = hh % 8
        B_ = hb[hh % 2]
        hk = "hb%d" % (hh % 2)
        og, ogkey = ogs[hh % 2], "ogs%d" % (hh % 2)
        if is_mla:
            S.op("sp", lambda e, B_=B_, h=h: e.dma_start(out=B_["k"], in_=M0[b, 1536 + h * 128:1536 + (h + 1) * 128, :]), reads=["M0"], writes=[hk + "k"], dma=hk)
            S.op("sp", lambda e, B_=B_, h=h: e.dma_start(out=B_["q"], in_=M0[b, h * 128:(h + 1) * 128, :]), reads=["M0"], writes=[hk + "q"], dma=hk)
            S.op("sp", lambda e, B_=B_, h=h: e.dma_start(out=B_["qr"][0:64, :], in_=M0[b, 1024 + h * 64:1024 + (h + 1) * 64, :]), reads=["M0"], writes=[hk + "qr"], dma=hk)
            S.op("sp", lambda e, B_=B_, h=h: e.dma_start(out=_v3(B_["v"], NTT), in_=VM[b, :, h * 128:(h + 1) * 128].rearrange("(kb p) d -> p kb d", p=128)),
                 reads=["VM"], writes=[hk + "v"], dma=hk)
            S.op("sp", lambda e, B_=B_, h=h: e.dma_start(out=B_["sz"], in_=F0[b, 3072 + 1024 + h * 128:3072 + 1024 + (h + 1) * 128, :]), reads=["F0"], writes=[hk + "sz"], dma=hk)
        else:
            S.op("sp", lambda e, B_=B_, h=h: e.dma_start(out=B_["k"], in_=F0[b, 1024 + h * 128:1024 + (h + 1) * 128, :]), reads=["F0"], writes=[hk + "k"], dma=hk)
            S.op("sp", lambda e, B_=B_, h=h: e.dma_start(out=B_["q"], in_=F0[b, h * 128:(h + 1) * 128, :]), reads=["F0"], writes=[hk + "q"], dma=hk)
            S.op("sp", lambda e, B_=B_, h=h: e.dma_start(out=_v3(B_["v"], NTT), in_=VA[b, :, h * 128:(h + 1) * 128].rearrange("(kb p) d -> p kb d", p=128)),
                 reads=["VA"], writes=[hk + "v"], dma=hk)
            S.op("sp", lambda e, B_=B_, h=h: e.dma_start(out=B_["sz"], in_=F0[b, 3072 + h * 128:3072 + (h + 1) * 128, :]), reads=["F0"], writes=[hk + "sz"], dma=hk)
            S.op("sp", lambda e, B_=B_, h=h: e.dma_start(out=_v3(B_["bias"], nt), in_=C.dram["na_bias"][h].rearrange("t k q -> k t q")), writes=[hk + "b"], dma=hk)
            S.op("pool", lambda e, B_=B_: e.tensor_tensor(out=B_["bias"], in0=B_["bias"], in1=nam, op=ALU.add), reads=[hk + "b", "nam"], writes=[hk + "b"])
        k_, q_, qr_, v3_, sz_ = B_["k"], B_["q"], B_["qr"], _v3(B_["v"], NTT), B_["sz"]
        b3_ = _v3(B_["bias"], nt)
        for gi, (t0, tn) in enumerate(TG):
            acc = acc_banks()
            if is_mla:
                kbs = list(range(18)) if gi < 4 else [16, 17]
                for i, kb in enumerate(kbs):
                    attend_block([k_[:, kb * 128:(kb + 1) * 128], krT[0:64, kb * 128:(kb + 1) * 128]], [q_[:, t0:t0 + tn], qr_[0:64, t0:t0 + tn]], tn,
                                 [hk + "k", hk + "q", hk + "qr", "krT"], (0,), i == 0, i == len(kbs) - 1, v3_[:, kb, :], hk + "v", MLA_SCALE, acc=acc)
            else:
                for i, kb in enumerate((16, 17)):
                    attend_block([k_[:, kb * 128:(kb + 1) * 128]], [q_[:, t0:t0 + tn]], tn, [hk + "k", hk + "q"], (0,), i == 0, (gi == 4 and i == 1),
                                 v3_[:, kb, :], hk + "v", NA_SCALE, acc=acc)
                if gi < 4:
                    for pr in range(4):
                        m = gi * 4 + pr
                        lst = per_m[m]
                        for j, (kb, ti) in enumerate(lst):
                            attend_block([k_[:, kb * 128:(kb + 1) * 128]], [q_[:, m * 128:(m + 1) * 128]], 128, [hk + "k", hk + "q"], (pr * 128,), False,
                                         (pr == 3 and j == len(lst) - 1), v3_[:, kb, :], hk + "v", NA_SCALE, bias=b3_[:, ti, :], bkey=hk + "b", acc=acc)
            finish_group(acc, tn, sz_, hk + "sz", og, ogkey, t0)
        row0 = (1024 if is_mla else 0) + h * 128
        S.op("sp", lambda e, og=og, row0=row0: e.dma_start(out=OG[b, row0:row0 + 128, :], in_=og), reads=[ogkey], writes=["OG0"], dma=ogkey)
    S.barrier()


def out_proj(C, b, l, OGd, nk, w_out, res_tiles, dst_tiles, ntt_list, tag, cbw=512, ntok=T):
    nc, S, A = C.nc, C.S, C.A
    A.reset()
    og = A.alloc(nk * ntok, BF16)
    og3 = _v3(og, nk)
    hk = nk // 2
    S.op("sp", lambda e: e.dma_start(out=og3[:, 0:hk, :], in_=OGd[0:hk * 128, :].rearrange("(k p) t -> p k t", p=128)), reads=["DR_OG"], writes=["og"], dma=tag + "og")
    S.op("sp", lambda e: e.dma_start(out=og3[:, hk:nk, :], in_=OGd[hk * 128:nk * 128, :].rearrange("(k p) t -> p k t", p=128)), reads=["DR_OG"], writes=["og"], dma=tag + "og")
    vs = (b, 2) if len(ntt_list) > 16 else (b,)
    wr = A.alloc(nk * cbw, BF16)
    wx = [A.alloc(nk * cbw, BF16) for _ in vs]
    gt = [A.alloc(cbw) for _ in range(2)]
    xr = [A.alloc(cbw) for _ in range(3)]
    xo = [A.alloc(cbw) for _ in range(3)]
    it = 0
    for cb in range(D // cbw):
        w3 = _v3(wr, nk)
        S.op("pool", lambda e, cb=cb, w3=w3: e.dma_start(out=w3, in_=w_out[:, cb * cbw:(cb + 1) * cbw].rearrange("(k p) c -> p k c", p=128)), writes=["wr"], dma=tag + "w")
        for vi, v in enumerate(vs):
            S.op("sp", lambda e, vi=vi, v=v, cb=cb: e.dma_start(out=gt[vi], in_=C.dram["mod"][l, v, 4096 + cb * cbw:4096 + (cb + 1) * cbw].unsqueeze(0).to_broadcast([128, cbw])),
                 reads=["mod"], writes=["gt%d" % vi], dma=tag + "gt")
            S.op("dve" if vi == 0 else "pool", lambda e, vi=vi, w3=w3: e.tensor_tensor(out=_v3(wx[vi], nk), in0=w3, in1=gt[vi].unsqueeze(1).to_broadcast([128, nk, cbw]), op=ALU.mult),
                 reads=["wr", "gt%d" % vi], writes=["wx%d" % vi])
        for tt in ntt_list:
            vi = 0 if tt < 16 else 1
            wv = _v3(wx[vi], nk)
            bank = 4 + (C.pcnt % 4)
            C.pcnt += 1
            pa = C.ps[:, bank * 512:bank * 512 + cbw]
            for k in range(nk):
                S.op("pe", lambda e, pa=pa, k=k, tt=tt, wv=wv: e.matmul(pa, lhsT=og3[:, k, tt * 128:(tt + 1) * 128], rhs=wv[:, k, :], start=(k == 0), stop=(k == nk - 1)),
                     reads=["og", "wx%d" % vi], writes=["psb%d" % bank])
            sl = it % 3
            it += 1
            S.op("sp", lambda e, sl=sl, tt=tt, cb=cb: e.dma_start(out=xr[sl], in_=res_tiles(tt, cb)), reads=["DR_res"], writes=["xr%d" % sl], dma=tag + "xr%d" % sl)
            S.op("dve", lambda e, sl=sl, pa=pa: e.tensor_tensor(out=xo[sl], in0=pa, in1=xr[sl], op=ALU.add), reads=["psb%d" % bank, "xr%d" % sl], writes=["xo%d" % sl])
            S.op("sp", lambda e, sl=sl, tt=tt, cb=cb: e.dma_start(out=dst_tiles(tt, cb), in_=xo[sl]), reads=["xo%d" % sl], writes=["DR_dst"], dma=tag + "xo%d" % sl)
    S.barrier()


def phase4(C):
    for b in range(NB):
        def res(tt, cb, b=b):
            if tt < 16:
                return C.dram["x"][b, tt * 128:(tt + 1) * 128, cb * 512:(cb + 1) * 512]
            return C.dram["ctx"][b, (tt - 16) * 128:(tt - 15) * 128, cb * 512:(cb + 1) * 512]

        def dst(tt, cb, b=b):
            return C.dram["X1"][b, tt * 128:(tt + 1) * 128, cb * 512:(cb + 1) * 512]
        out_proj(C, b, 0, C.dram["OG0"][b], 16, C.dram["ab_w_out"], res, dst, list(range(NTT)), "p4")


HP = 2052 + 260


def phase5(C):
    for b in range(NB):
        _phase5_b(C, b)


def _phase5_b(C, b):
    nc, S, A = C.nc, C.S, C.A
    A.reset()
    w_in = C.dram["dn_w_in"]
    X1, F1, BA = C.dram["X1"], C.dram["F1"], C.dram["BA"]
    mv = load_modvecs(C, 1, b, C.dram["dn_norm"], "p5m%d" % b)
    xmT = A.alloc(16 * T, BF16)
    xm3 = _v3(xmT, 16)
    xkey = "xmT"
    mark = A.off
    tiles = [X1[b, tt * 128:(tt + 1) * 128, :] for tt in range(NTT)]
    build_xmT(C, xmT, xkey, tiles, mv, "p5b")
    S.barrier()
    A.off = mark
    wr = [A.alloc(16 * 512, BF16) for _ in range(2)]
    stg = [A.alloc(T, BF16) for _ in range(3)]
    hp = [A.alloc(HP, BF16) for _ in range(2)]
    dg = [A.alloc(5 * 128, BF16) for _ in range(2)]
    sT = [A.alloc(512, BF16) for _ in range(3)]
    sq = [A.alloc(512, BF16) for _ in range(2)]
    lnv = [A.alloc(512) for _ in range(2)]
    rst = [A.alloc(512) for _ in range(2)]
    bast = A.alloc(NTT * 128)
    cw3 = C.cw.rearrange("p (blk j) -> p blk j", j=5)
    for i in range(2):
        S.op("pool", lambda e, i=i: e.memset(hp[i], 0.0), writes=["hp%d" % i])
    segs = [("q", i * 512, 512) for i in range(4)] + [("k", 2048 + i * 512, 512) for i in range(4)] + [("v", 4096 + i * 512, 512) for i in range(8)] + \
           [("z", 8192 + i * 512, 512) for i in range(8)] + [("ba", 12288, 128)]
    cnt = dict(st=0, hp=0, c=0, n=0, s=0)
    for wi, (nm, c0, ncw) in enumerate(segs):
        slot = wi % 2
        w3 = _v3(wr[slot], 16)[:, :, 0:ncw]
        wkey = "p5w%d" % slot
        S.op("pool", lambda e, w3=w3, c0=c0, ncw=ncw: e.dma_start(out=w3, in_=w_in[:, c0:c0 + ncw].rearrange("(k p) c -> p k c", p=128)),
             writes=[wkey], dma=wkey)
        if nm == "ba":
            b3 = _v3(bast, NTT)
            for tt in range(NTT):
                bank = 4 + (C.pcnt % 4)
                C.pcnt += 1
                pa = C.ps[:, bank * 512:bank * 512 + 128]
                for k in range(16):
                    S.op("pe", lambda e, pa=pa, k=k, tt=tt, w3=w3: e.matmul(pa, lhsT=xm3[:, k, tt * 128:(tt + 1) * 128], rhs=w3[:, k, :], start=(k == 0), stop=(k == 15)),
                         reads=[wkey, xkey + str(tt)], writes=["psb%d" % bank])
                S.op("dve", lambda e, pa=pa, tt=tt: e.tensor_copy(out=b3[:, tt, :], in_=pa), reads=["psb%d" % bank], writes=["bast"])
            S.op("sp", lambda e: e.dma_start(out=BA[b].rearrange("(tt p) c -> p tt c", p=128), in_=b3), reads=["bast"], writes=["BA"], dma="bast")
            continue

        def evac(sb, gi, t0, tn, pa, pk, m, nm=nm, c0=c0):
            if nm == "z":
                if gi == 0:
                    cnt["st"] += 1
                si = cnt["st"] % 3
                sg, sk = stg[si], "p5st%d" % si
                S.op("act", lambda e: e.activation(out=sg[:, t0:t0 + tn], in_=pa, func=AF.Silu), reads=[pk], writes=[sk])
                if gi == 4:
                    r0 = c0 + sb * 128
                    S.op("sp", lambda e: e.dma_start(out=F1[b, r0:r0 + 128, :], in_=sg), reads=[sk], writes=["F1"], dma=sk)
                return
            if gi == 0:
                cnt["hp"] += 1
            hi = cnt["hp"] % 2
            hb_, hk_ = hp[hi], "hp%d" % hi
            off = 2 + t0 if gi < 4 else 2052 + 2
            if gi % 2 == 0:
                S.op("act", lambda e: e.activation(out=hb_[:, off:off + tn], in_=pa, func=AF.Copy), reads=[pk], writes=[hk_])
            else:
                S.op("dve", lambda e: e.tensor_copy(out=hb_[:, off:off + tn], in_=pa), reads=[pk], writes=[hk_])
            if gi < 4:
                return
            blk = (c0 + sb * 128) // 128
            di = cnt["hp"] % 2
            d3 = _v3(dg[di], 5)
            for j in range(5):
                S.op("pool", lambda e, j=j: e.tensor_scalar(out=d3[:, j, :], in0=C.ident, scalar1=cw3[:, blk, j:j + 1], scalar2=None, op0=ALU.mult),
                     reads=["ident", "cw"], writes=["dg%d" % di])
            cnt["st"] += 1
            si = cnt["st"] % 3
            sg, sk = stg[si], "p5st%d" % si
            outs = []
            for g2, (u0, un) in enumerate(TG):
                bank = cnt["c"] % 4
                cnt["c"] += 1
                pc = C.ps[:, bank * 512:bank * 512 + un]
                base = u0 if g2 < 4 else 2052
                for j in range(5):
                    S.op("pe", lambda e, pc=pc, j=j, base=base, un=un: e.matmul(pc, lhsT=d3[:, j, :], rhs=hb_[:, base + j:base + j + un], start=(j == 0), stop=(j == 4)),
                         reads=["dg%d" % di, hk_], writes=["psb%d" % bank])
                if nm == "v":
                    S.op("act", lambda e, pc=pc, u0=u0, un=un: e.activation(out=sg[:, u0:u0 + un], in_=pc, func=AF.Silu), reads=["psb%d" % bank], writes=[sk])
                else:
                    ssl = cnt["s"] % 3
                    cnt["s"] += 1
                    s_ = sT[ssl]
                    S.op("act", lambda e, pc=pc, s_=s_, un=un: e.activation(out=s_[:, 0:un], in_=pc, func=AF.Silu), reads=["psb%d" % bank], writes=["sT%d" % ssl])
                    outs.append((s_, "sT%d" % ssl, u0, un))
                    if len(outs) == 3 or g2 == 4:
                        pend = []
                        for (s2, s2k, v0, vn) in outs:
                            nsl = cnt["n"] % 2
                            cnt["n"] += 1
                            S.op("pool", lambda e, s2=s2, vn=vn, nsl=nsl: e.tensor_tensor(out=sq[nsl][:, 0:vn], in0=s2[:, 0:vn], in1=s2[:, 0:vn], op=ALU.mult),
                                 reads=[s2k], writes=["sq%d" % nsl])
                            bank2 = cnt["c"] % 4
                            cnt["c"] += 1
                            p3 = C.ps[:, bank2 * 512:bank2 * 512 + vn]
                            S.op("pe", lambda e, p3=p3, nsl=nsl, vn=vn: e.matmul(p3, lhsT=C.ones, rhs=sq[nsl][:, 0:vn], start=True, stop=True),
                                 reads=["ones", "sq%d" % nsl], writes=["psb%d" % bank2])
                            S.op("act", lambda e, p3=p3, nsl=nsl, vn=vn: e.activation(out=lnv[nsl][:, 0:vn], in_=p3, func=AF.Ln, bias=C.eps_t[:, 0:1]),
                                 reads=["psb%d" % bank2], writes=["lnv%d" % nsl])
                            pend.append((s2, s2k, v0, vn, nsl))
                            if len(pend) == 2 or (s2 is outs[-1][0]):
                                for (s3_, s3k, w0, wn, ns2) in pend:
                                    bias_ap = C.lnq[:, 0:1] if nm == "q" else C.zero_t[:, 0:1]
                                    S.op("act", lambda e, ns2=ns2, wn=wn, bias_ap=bias_ap: e.activation(out=rst[ns2][:, 0:wn], in_=lnv[ns2][:, 0:wn], func=AF.Exp, scale=-0.5, bias=bias_ap),
                                         reads=["lnv%d" % ns2], writes=["rst%d" % ns2])
                                    S.op("dve", lambda e, s3_=s3_, w0=w0, wn=wn, ns2=ns2: e.tensor_tensor(out=sg[:, w0:w0 + wn], in0=s3_[:, 0:wn], in1=rst[ns2][:, 0:wn], op=ALU.mult),
                                         reads=[s3k, "rst%d" % ns2], writes=[sk])
                                pend = []
                        outs = []
            r0 = c0 + sb * 128
            S.op("sp", lambda e: e.dma_start(out=F1[b, r0:r0 + 128, :], in_=sg), reads=[sk], writes=["F1"], dma=sk)

        proj_fm(C, xm3, xkey, w3, wkey, ncw, evac, "p5")
    S.barrier()


FSEQ = [16, 17] + list(range(16))
BSEQ = [17, 16] + list(range(15, -1, -1))


def dn_level_masks():
    s_ = np.arange(128)[:, None]
    c_ = np.arange(128)[None, :]
    out = np.zeros((128, 7, 4, 2, 128), np.float32)
    for k in range(1, 8):
        h = 1 << (k - 1)
        same = (s_ // (2 * h)) == (c_ // (2 * h))
        ur = same & ((s_ % (2 * h)) < h) & ((c_ % (2 * h)) >= h)
        ll = ur.T
        for j in range(4):
            fwd = j < 2
            out[:, k - 1, j, 0, :] = -(ur if fwd else ll).astype(np.float32)
            out[:, k - 1, j, 1, :] = -(ll if fwd else ur).astype(np.float32)
    id8 = np.zeros((128, 4, 2, 128), np.float32)
    id8[:, :, :, :] = np.eye(128, dtype=np.float32)[:, None, None, :]
    return out.reshape(128, 7, 1024), id8.reshape(128, 1024)


def dn_level_masks2():
    lm, _ = dn_level_masks()
    lm = lm.reshape(128, 7, 4, 2, 128).copy()
    lm -= np.eye(128, dtype=np.float32)[:, None, None, None, :]
    return np.ascontiguousarray(lm[:, :, 0::2, :, :]).reshape(128, 7, 512)


def dn_masks():
    s = np.arange(128)[:, None]
    c = np.arange(128)[None, :]
    incl = np.stack([(s <= c), (s <= c), (s >= c), (s >= c)], 0).astype(np.float32)
    strict = np.stack([(s < c), (s < c), (s > c), (s > c)], 0).astype(np.float32)
    ident4 = np.stack([np.eye(128, dtype=np.float32)] * 4, 0)
    return incl.transpose(1, 0, 2).copy(), strict.transpose(1, 0, 2).copy(), ident4.transpose(1, 0, 2).copy()


def phase6(C):
    for b in getattr(C, "p6_batches", range(NB)):
        _phase6_b(C, b)


def dump(C, name, ap, readkeys):
    if not getattr(C, "debug", False):
        return
    t = C.nc.dram_tensor("dbg_" + name, list(ap.shape), ap.dtype, kind="ExternalOutput").ap()
    if ap.shape[1] * (4 if ap.dtype == F32 else 2) > 2048:
        C.S.op("sp", lambda e: e.dma_start(out=t, in_=ap), reads=readkeys, writes=["DR_dbg"], dma=1)
        return
    if not hasattr(C, "dbg_stage"):
        C.dbg_stage = C.es.enter_context(C.nc.sbuf_tensor("dbgst", [128, 512], F32))
    st = C.dbg_stage[:, 0:ap.shape[1]] if ap.dtype == F32 else C.dbg_stage[:, 0:(ap.shape[1] + 1) // 2].bitcast(BF16)[:, 0:ap.shape[1]]
    C.S.op("dve", lambda e: e.tensor_copy(out=st, in_=ap), reads=readkeys, writes=["dbgst"])
    C.S.op("sp", lambda e: e.dma_start(out=t, in_=st), reads=["dbgst"], writes=["DR_dbg"], dma=1)


def _phase6_b(C, b):
    nc, S, A = C.nc, C.S, C.A
    A.reset()
    F1, BA, OG1 = C.dram["F1"], C.dram["BA"], C.dram["OG1"]
    mincl = A.alloc(512)
    mstr = A.alloc(512)
    ones_f = A.alloc(128)
    onorm = A.alloc(1)
    S.op("sp", lambda e: e.dma_start(out=_v3(mincl, 4), in_=C.dram["dn_mincl"]), writes=["mincl"], dma=1)
    S.op("sp", lambda e: e.dma_start(out=_v3(mstr, 4), in_=C.dram["dn_mstrict"]), writes=["mstr"], dma=1)
    S.op("sp", lambda e: e.dma_start(out=onorm, in_=C.dram["dn_o_norm"].rearrange("(p o) -> p o", o=1), allow_slow_non_contiguous=True), writes=["onorm"], dma=1)
    S.op("pool", lambda e: e.memset(ones_f, 1.0), writes=["ones_f"])
    mincl3, mstr3 = _v3(mincl, 4), _v3(mstr, 4)
    lmask = A.alloc(7 * 512, BF16)
    lm4 = lmask.rearrange("p (k d x) -> p k d x", k=7, d=2)
    S.op("sp", lambda e: e.dma_start(out=_v3(lmask, 7), in_=C.dram["dn_lmask2"]), writes=["lmask"], dma=1)
    beta = A.alloc(NTT * 64)
    gg = A.alloc(NTT * 64)
    mark6 = A.off
    ba = A.alloc(NTT * 128)
    ba3 = _v3(ba, NTT)
    S.op("sp", lambda e: e.dma_start(out=ba3, in_=BA[b].rearrange("(tt p) c -> p tt c", p=128)), reads=["BA"], writes=["ba"], dma=1)
    tA = A.alloc(NTT * 64)
    tB = A.alloc(NTT * 64)
    alog = A.alloc(64)
    dtb = A.alloc(64)
    one_t = A.alloc(1)
    S.op("pool", lambda e: e.memset(one_t, 1.0), writes=["one_t"])
    S.op("sp", lambda e: e.dma_start(out=alog, in_=C.dram["dn_a_log"].rearrange("d h -> (d h)").unsqueeze(0).to_broadcast([128, 64])), writes=["alog"], dma=1)
    S.op("sp", lambda e: e.dma_start(out=dtb, in_=C.dram["dn_dt_bias"].rearrange("d h -> (d h)").unsqueeze(0).to_broadcast([128, 64])), writes=["dtb"], dma=1)
    beta3, gg3, tA3, tB3 = _v3(beta, NTT), _v3(gg, NTT), _v3(tA, NTT), _v3(tB, NTT)
    S.op("act", lambda e: e.activation(out=tA3, in_=ba3[:, :, 0:64], func=AF.Exp, scale=-1.0), reads=["ba"], writes=["tA"])
    S.op("dve", lambda e: e.tensor_scalar(out=tA, in0=tA, scalar1=1.0, scalar2=None, op0=ALU.add), reads=["tA"], writes=["tA"])
    S.op("dve", lambda e: e.reciprocal(out=beta, in_=tA), reads=["tA"], writes=["beta"])
    S.op("dve", lambda e: e.tensor_tensor(out=tB3, in0=ba3[:, :, 64:128], in1=dtb.unsqueeze(1).to_broadcast([128, NTT, 64]), op=ALU.add), reads=["ba", "dtb"], writes=["tB"])
    S.op("dve", lambda e: e.scalar_tensor_tensor(out=tA, in0=tB, scalar=-1.0, in1=tB, op0=ALU.mult, op1=ALU.max), reads=["tB", "beta"], writes=["tA"])
    S.op("act", lambda e: e.activation(out=tA, in_=tA, func=AF.Exp, scale=-1.0), reads=["tA"], writes=["tA"])
    S.op("act", lambda e: e.activation(out=tA, in_=tA, func=AF.Ln, bias=one_t[:, 0:1]), reads=["tA", "one_t"], writes=["tA"])
    S.op("dve", lambda e: e.scalar_tensor_tensor(out=tB, in0=tB, scalar=0.0, in1=tA, op0=ALU.max, op1=ALU.add), reads=["tA", "tB"], writes=["tB"])
    S.op("act", lambda e: e.activation(out=alog, in_=alog, func=AF.Exp), reads=["alog"], writes=["alog"])
    S.op("dve", lambda e: e.scalar_tensor_tensor(out=gg3, in0=tB3, scalar=-1.0, in1=alog.unsqueeze(1).to_broadcast([128, NTT, 64]), op0=ALU.mult, op1=ALU.mult),
         reads=["tB", "alog"], writes=["gg"])
    S.barrier()
    A.off = mark6
    SHARED = {"gg", "beta", "mincl", "mstr", "lmask", "ident", "ones", "onorm", "ones_f", "eps", "zero", "lnq"}
    S0 = S

    class _SlotSched:
        def __init__(self, si):
            self.si = si

        def op(self, eng, fn, reads=(), writes=(), dma=None):
            f = lambda k: k if (k in SHARED or k in Sched.DRAMKEYS or k.startswith("DR_")) else "s%d_%s" % (self.si, k)
            return S0.op(eng, fn, reads=[f(k) for k in reads], writes=[f(k) for k in writes], dma=dma)

    def run_slot(si, head_list):
        S = _SlotSched(si)
        engA = "pool" if si == 0 else "dve"
        hbufs = [dict(q=A.alloc(T, BF16), k=A.alloc(T, BF16), v=A.alloc(2 * T, BF16))]
        ktok = A.alloc(NTT * 128, BF16)
        vtok = A.alloc(NTT * 256, BF16)
        oacc = A.alloc(2 * SEQ)
        o3 = _v3(oacc, 2)
        szb1 = A.alloc(SEQ, BF16)
        def mk():
            dec_ = A.alloc(512)
            tmp_ = A.alloc(512)
            tmp2_ = A.alloc(512)
            return dict(grep=A.alloc(512), d1=dec_, dec=dec_, gam=A.alloc(512), bm=A.alloc(512), tmp=tmp_, tmp2=tmp2_, t3=tmp_, t4=tmp2_,
                        x12=A.alloc(12), e12=A.alloc(12), negb=A.alloc(4), xn=A.alloc(1024, BF16), rt=A.alloc(1024, BF16), yy=A.alloc(1024, BF16),
                        xb=dec_, intra=A.alloc(512, BF16), gq=A.alloc(512, BF16), kd=A.alloc(512, BF16), vd=A.alloc(512, BF16), vn=A.alloc(512, BF16))
        stp = [mk()]
        S4 = A.alloc(512)
        S4b = A.alloc(512, BF16)
        sqb = [A.alloc(512, BF16)] * 2
        lnv = [A.alloc(512)] * 2
        rst = lnv
        osum = [A.alloc(512) for _ in range(2)]
        ps = C.ps
        pbase = si * 2048
        kG = kS1 = "psb%d" % (pbase // 512)
        kAB = kS2 = "psb%d" % (pbase // 512 + 1)
        kM = kT = kC = kN = "psM%d" % (pbase // 512)
        psG = ps[:, pbase:pbase + 512]
        psG3 = _v3(psG, 4)
        psS1 = psG
        psA = ps[:, pbase + 512:pbase + 768]
        psB = ps[:, pbase + 768:pbase + 1024]
        psS2 = ps[:, pbase + 512:pbase + 1024]
        psM = ps[:, pbase + 1024:pbase + 2048]
        psM3 = _v3(psM, 4)
        psT = ps[:, pbase + 1024:pbase + 1536]
        psTb = psT.bitcast(BF16)
        psC = ps[:, pbase + 1536:pbase + 1540]
        psN = psT

        for g in head_list:
            H = hbufs[0]
            hk = "h6"
            S.op("sp", lambda e, H=H, g=g: e.dma_start(out=H["q"], in_=F1[b, g * 128:(g + 1) * 128, :]), reads=["F1"], writes=[hk + "q"], dma=1)
            S.op("sp", lambda e, H=H, g=g: e.dma_start(out=H["k"], in_=F1[b, 2048 + g * 128:2048 + (g + 1) * 128, :]), reads=["F1"], writes=[hk + "k"], dma=1)
            S.op("sp", lambda e, H=H, g=g: e.dma_start(out=_v3(H["v"], 2), in_=F1[b, 4096 + 2 * g * 128:4096 + (2 * g + 2) * 128, :].rearrange("(v p) t -> p v t", p=128)),
                 reads=["F1"], writes=[hk + "v"], dma=1)
            QT, KT, VT3 = H["q"], H["k"], _v3(H["v"], 2)
            kt3 = _v3(ktok, NTT)
            vt4 = vtok.rearrange("p (t v d) -> p t v d", t=NTT, v=2)
            jobs = [("k", tt, 0) for tt in range(NTT)] + [("v", tt, vh) for tt in range(NTT) for vh in range(2)]
            groups = [jobs[0:8], jobs[8:16], jobs[16:18]] + [jobs[18 + i:18 + i + 8] for i in range(0, 36, 8)]
            for j0, grp in enumerate(groups):
                yield
                j0 = j0 * 8
                for i, (kind, tt, vh) in enumerate(grp):
                    src = KT[:, tt * 128:(tt + 1) * 128] if kind == "k" else VT3[:, vh, tt * 128:(tt + 1) * 128]
                    S.op("pe", lambda e, i=i, src=src: e.transpose(out=psTb[:, i * 128:(i + 1) * 128], in_=src, identity=C.ident),
                         reads=[hk + "k", hk + "v", "ident"], writes=[kT])
                kind0, tt0, vh0 = grp[0]
                n = len(grp)
                if kind0 == "k":
                    dst = ktok[:, tt0 * 128:(tt0 + n) * 128]
                    dk_ = "ktok"
                else:
                    dst = vtok[:, (tt0 * 2 + vh0) * 128:(tt0 * 2 + vh0 + n) * 128]
                    dk_ = "vtok"
                if (j0 // 8) % 2 == 0:
                    S.op("act", lambda e, dst=dst, n=n: e.activation(out=dst, in_=psTb[:, 0:n * 128], func=AF.Copy), reads=[kT], writes=[dk_])
                else:
                    S.op("dve", lambda e, dst=dst, n=n: e.tensor_copy(out=dst, in_=psTb[:, 0:n * 128]), reads=[kT], writes=[dk_])
            S.op("pool", lambda e: e.memset(S4, 0.0), writes=["S4"])
            S.op("pool", lambda e: e.memset(S4b, 0.0), writes=["S4b"])
            S43, S4b3 = _v3(S4, 4), _v3(S4b, 4)
            c0 = 2 * g
            def step(s, part, g=g, H=H, hk=hk, QT=QT, KT=KT, VT3=VT3, kt3=kt3, vt4=vt4, c0=c0, S43=S43, S4b3=S4b3):
                P = stp[0]
                pk = "st0"
                blks = (FSEQ[s], BSEQ[s])
                cols = [(d * 32 + c0) for d in range(2)]
                grep3, d13, dec3, gam3, bm3, tmp3, tmp23 = [_v3(P[n_], 4) for n_ in ("grep", "d1", "dec", "gam", "bm", "tmp", "tmp2")]
                xb3, intra3, gq3, kd3, vd3, vn3, t33, t43 = [_v3(P[n_], 4) for n_ in ("xb", "intra", "gq", "kd", "vd", "vn", "t3", "t4")]
                x12, e12, negb = P["x12"], P["e12"], P["negb"]
                RT = P["rt"].rearrange("p (j o c) -> p j o c", j=4, o=2)
                krt = pk + "rt"
                xn4 = P["xn"].rearrange("p (j o c) -> p j o c", j=4, o=2)
                kxn = pk + "xn"
                if part == "A":
                    yield
                    for d in range(2):
                        gs = gg3[:, blks[d], cols[d]:cols[d] + 2]
                        S.op("pool", lambda e, d=d, gs=gs: e.tensor_copy(out=grep3[:, 2 * d:2 * d + 2, :], in_=gs.unsqueeze(2).to_broadcast([128, 2, 128])),
                             reads=["gg"], writes=[pk + "grep"])
                        S.op("dve", lambda e, d=d: e.tensor_scalar(out=negb[:, 2 * d:2 * d + 2], in0=beta3[:, blks[d], cols[d]:cols[d] + 2], scalar1=-1.0, scalar2=None, op0=ALU.mult),
                             reads=["beta"], writes=[pk + "negb"])
                        S.op("pool", lambda e, d=d: e.tensor_tensor(out=bm3[:, 2 * d:2 * d + 2, :], in0=mstr3[:, 2 * d:2 * d + 2, :],
                                                                    in1=beta3[:, blks[d], cols[d]:cols[d] + 2].unsqueeze(2).to_broadcast([128, 2, 128]), op=ALU.mult),
                             reads=["beta", "mstr"], writes=[pk + "bm"])
                    yield
                    for j in range(4):
                        d = j // 2
                        S.op("pe", lambda e, j=j, d=d: e.matmul(psG3[:, j, :], lhsT=grep3[:, j, :], rhs=mincl3[:, 2 * d, :], start=True, stop=True),
                             reads=[pk + "grep", "mincl"], writes=[kG])
                    yield
                    for d in range(2):
                        S.op("pe", lambda e, d=d: e.matmul(psC[:, 2 * d:2 * d + 2], lhsT=mincl3[:, 2 * d, :], rhs=gg3[:, blks[d], cols[d]:cols[d] + 2], start=True, stop=True),
                             reads=["gg", "mincl"], writes=[kC])
                    yield
                    for d in range(2):
                        kb_ = KT[:, blks[d] * 128:(blks[d] + 1) * 128]
                        qb_ = QT[:, blks[d] * 128:(blks[d] + 1) * 128]
                        S.op("pe", lambda e, d=d, kb_=kb_: e.matmul(psA[:, d * 128:(d + 1) * 128], lhsT=kb_, rhs=kb_, start=True, stop=True), reads=[hk + "k"], writes=[kAB])
                        S.op("pe", lambda e, d=d, kb_=kb_, qb_=qb_: e.matmul(psB[:, d * 128:(d + 1) * 128], lhsT=kb_, rhs=qb_, start=True, stop=True), reads=[hk + "k", hk + "q"], writes=[kAB])
                    yield
                    S.op("dve", lambda e: e.tensor_copy(out=x12[:, 0:4], in_=psC), reads=[kC], writes=[pk + "x12"])
                    yield
                    for d in range(2):
                        last = 127 if d == 0 else 0
                        S.op("dve", lambda e, d=d, last=last: e.tensor_copy(out=x12[:, 8 + 2 * d:10 + 2 * d], in_=psG3[:, 2 * d:2 * d + 2, last]), reads=[kG], writes=[pk + "x12"])
                    yield
                    S.op("dve", lambda e: e.tensor_tensor(out=x12[:, 4:8], in0=x12[:, 8:12], in1=x12[:, 0:4], op=ALU.subtract), reads=[pk + "x12"], writes=[pk + "x12"])
                    yield
                    S.op("act", lambda e: e.activation(out=e12, in_=x12, func=AF.Exp), reads=[pk + "x12"], writes=[pk + "e12"])
                    yield
                    S.op("dve", lambda e: e.tensor_tensor(out=d13, in0=psG3, in1=x12[:, 0:4].unsqueeze(2).to_broadcast([128, 4, 128]), op=ALU.subtract),
                         reads=[kG, pk + "x12"], writes=[pk + "dec"])
                    yield
                    S.op("pool", lambda e: e.tensor_scalar(out=P["d1"], in0=P["d1"], scalar1=0.0, scalar2=-80.0, op0=ALU.min, op1=ALU.max), reads=[pk + "dec"], writes=[pk + "dec"])
                    yield
                    S.op("act", lambda e: e.activation(out=P["dec"], in_=P["d1"], func=AF.Exp), reads=[pk + "dec"], writes=[pk + "dec"])
                    yield
                    S.op("act", lambda e: e.activation(out=P["gam"], in_=psG, func=AF.Exp), reads=[kG], writes=[pk + "gam"])
                    dec4 = P["dec"].rearrange("p (d v c) -> p d v c", d=2, v=2)
                    psA4 = psA.rearrange("p (d c) -> p d c", d=2).unsqueeze(2).to_broadcast([128, 2, 2, 128])
                    psB4 = psB.rearrange("p (d c) -> p d c", d=2).unsqueeze(2).to_broadcast([128, 2, 2, 128])
                    yield
                    S.op("dve", lambda e, psA4=psA4, dec4=dec4: e.tensor_tensor(out=P["tmp"].rearrange("p (d v c) -> p d v c", d=2, v=2), in0=psA4, in1=dec4, op=ALU.mult),
                         reads=[kAB, pk + "dec"], writes=[pk + "tmp"])
                    yield
                    S.op(engA, lambda e: e.tensor_tensor(out=xn4[:, :, 0, :], in0=tmp3, in1=bm3, op=ALU.mult), reads=[pk + "tmp", pk + "bm"], writes=[kxn])
                    S.op(engA, lambda e: e.tensor_tensor(out=xn4[:, :, 0, :], in0=xn4[:, :, 0, :], in1=C.ident.unsqueeze(1).to_broadcast([128, 4, 128]), op=ALU.subtract),
                         reads=[kxn, "ident"], writes=[kxn])
                    yield
                    S.op("dve", lambda e, psB4=psB4, dec4=dec4: e.tensor_tensor(out=P["tmp2"].rearrange("p (d v c) -> p d v c", d=2, v=2), in0=psB4, in1=dec4, op=ALU.mult),
                         reads=[kAB, pk + "dec"], writes=[pk + "tmp2"])
                    yield
                    S.op(engA, lambda e: e.tensor_tensor(out=intra3, in0=tmp23, in1=mincl3, op=ALU.mult), reads=[pk + "tmp2", "mincl"], writes=[pk + "intra"])
                    yield
                    for d in range(2):
                        qb_ = QT[:, blks[d] * 128:(blks[d] + 1) * 128]
                        S.op(engA, lambda e, d=d, qb_=qb_: e.tensor_tensor(out=gq3[:, 2 * d:2 * d + 2, :], in0=qb_.unsqueeze(1).to_broadcast([128, 2, 128]), in1=gam3[:, 2 * d:2 * d + 2, :], op=ALU.mult),
                             reads=[hk + "q", pk + "gam"], writes=[pk + "gq"])
                        S.op("pool", lambda e, d=d: e.tensor_tensor(out=kd3[:, 2 * d:2 * d + 2, :], in0=kt3[:, blks[d], :].unsqueeze(1).to_broadcast([128, 2, 128]),
                                                                    in1=e12[:, 4 + 2 * d:6 + 2 * d].unsqueeze(2).to_broadcast([128, 2, 128]), op=ALU.mult),
                             reads=["ktok", pk + "e12"], writes=[pk + "kd"])
                    xn4 = P["xn"].rearrange("p (j o c) -> p j o c", j=4, o=2)
                    rt4 = P["rt"].rearrange("p (j o c) -> p j o c", j=4, o=2)
                    yy4 = P["yy"].rearrange("p (j o c) -> p j o c", j=4, o=2)
                    psM4 = psM.rearrange("p (j o c) -> p j o c", j=4, o=2)
                    kxn, krt_, kyy = pk + "xn", pk + "rt", pk + "yy"
                    yield
                    pass
                    yield
                    for j in range(4):
                        S.op("pe", lambda e, j=j: e.transpose(out=psTb[:, j * 128:(j + 1) * 128], in_=xn4[:, j, 0, :], identity=C.ident), reads=[kxn, "ident"], writes=[kT])
                    yield
                    S.op("act", lambda e: e.activation(out=xn4[:, :, 1, :], in_=_v3(psTb[:, 0:512], 4), func=AF.Copy), reads=[kT], writes=[kxn])
                    psTl = psG.bitcast(BF16)

                    def lmv(lv):
                        return lm4[:, lv, :, 0:128].unsqueeze(2).to_broadcast([128, 2, 2, 128])
                    h4 = lambda ap: ap.rearrange("p (d v) c -> p d v c", d=2)
                    yield
                    S.op("dve", lambda e: e.tensor_tensor(out=h4(rt4[:, :, 0, :]), in0=h4(xn4[:, :, 0, :]), in1=lmv(0), op=ALU.mult), reads=[kxn, "lmask"], writes=[krt_])
                    for lv in range(1, 7):
                        yield
                        for j in range(4):
                            S.op("pe", lambda e, j=j: e.matmul(psM4[:, j, 0, :], lhsT=xn4[:, j, 1, :], rhs=rt4[:, j, 0, :], start=True, stop=True), reads=[kxn, krt_], writes=[kM])
                        for j in range(4):
                            S.op("pe", lambda e, j=j: e.transpose(out=psTl[:, j * 128:(j + 1) * 128], in_=rt4[:, j, 0, :], identity=C.ident), reads=[krt_, "ident"], writes=[kG])
                        yield
                        S.op("dve", lambda e, lv=lv: e.tensor_tensor(out=h4(yy4[:, :, 0, :]), in0=h4(psM4[:, :, 0, :]), in1=lmv(lv), op=ALU.mult), reads=[kM, "lmask"], writes=[kyy])
                        S.op("act", lambda e: e.activation(out=rt4[:, :, 1, :], in_=_v3(psTl[:, 0:512], 4), func=AF.Copy), reads=[kG], writes=[pk + "tm"])
                        yield
                        for j in range(4):
                            S.op("pe", lambda e, j=j: e.matmul(psM4[:, j, 0, :], lhsT=rt4[:, j, 1, :], rhs=yy4[:, j, 0, :], start=True, stop=True), reads=[kyy, pk + "tm"], writes=[kM])
                        yield
                        S.op("act", lambda e: e.activation(out=rt4[:, :, 0, :], in_=psM4[:, :, 0, :], func=AF.Copy), reads=[kM], writes=[krt_])
                    RT = rt4
                    krt = krt_
                    return
                yield
                for j in range(4):
                    d = j // 2
                    kb_ = KT[:, blks[d] * 128:(blks[d] + 1) * 128]
                    S.op("pe", lambda e, j=j, kb_=kb_: e.matmul(psS1[:, j * 128:(j + 1) * 128], lhsT=kb_, rhs=S4b3[:, j, :], start=True, stop=True), reads=[hk + "k", "S4b"], writes=[kS1])
                yield
                S.op("dve", lambda e: e.tensor_tensor(out=t33, in0=_v3(psS1, 4), in1=e12[:, 0:4].unsqueeze(2).to_broadcast([128, 4, 128]), op=ALU.mult),
                     reads=[kS1, pk + "e12"], writes=[pk + "tmp"])
                yield
                for d in range(2):
                    S.op("pool" if d == 0 else "dve", lambda e, d=d: e.tensor_tensor(out=vd3[:, 2 * d:2 * d + 2, :], in0=t33[:, 2 * d:2 * d + 2, :], in1=vt4[:, blks[d], :, :], op=ALU.subtract),
                         reads=[pk + "tmp", "vtok"], writes=[pk + "vd"])
                yield
                for j in range(4):
                    S.op("pe", lambda e, j=j: e.matmul(psS2[:, j * 128:(j + 1) * 128], lhsT=RT[:, j, 0, :], rhs=vd3[:, j, :], start=True, stop=True), reads=[krt, pk + "vd"], writes=[kS2])
                yield
                S.op("dve", lambda e: e.tensor_tensor(out=vn3, in0=_v3(psS2, 4), in1=negb.unsqueeze(2).to_broadcast([128, 4, 128]), op=ALU.mult),
                     reads=[kS2, pk + "negb"], writes=[pk + "vn"])
                if s >= 2:
                    for j in range(4):
                        S.op("pe", lambda e, j=j: e.matmul(psS1[:, j * 128:(j + 1) * 128], lhsT=S4b3[:, j, :], rhs=gq3[:, j, :], start=True, stop=False), reads=["S4b", pk + "gq"], writes=[kS1])
                        S.op("pe", lambda e, j=j: e.matmul(psS1[:, j * 128:(j + 1) * 128], lhsT=vn3[:, j, :], rhs=intra3[:, j, :], start=False, stop=True), reads=[pk + "vn", pk + "intra"], writes=[kS1])
                    for d in range(2):
                        dstv = o3[:, :, blks[d] * 128:(blks[d] + 1) * 128]
                        srcv = _v3(psS1[:, d * 256:(d + 1) * 256], 2)
                        if s <= 9:
                            S.op("act", lambda e, dstv=dstv, srcv=srcv: e.activation(out=dstv, in_=srcv, func=AF.Copy), reads=[kS1], writes=["oacc"])
                        else:
                            S.op("dve", lambda e, dstv=dstv, srcv=srcv: e.tensor_tensor(out=dstv, in0=srcv, in1=dstv, op=ALU.add), reads=[kS1, "oacc"], writes=["oacc"])
                if s < NTT - 1:
                    for j in range(4):
                        S.op("pe", lambda e, j=j: e.matmul(psS2[:, j * 128:(j + 1) * 128], lhsT=kd3[:, j, :], rhs=vn3[:, j, :], start=True, stop=True), reads=[pk + "kd", pk + "vn"], writes=[kS2])
                    S.op("pool", lambda e: e.tensor_tensor(out=t43, in0=S43, in1=e12[:, 8:12].unsqueeze(2).to_broadcast([128, 4, 128]), op=ALU.mult),
                         reads=["S4", pk + "e12"], writes=[pk + "tmp2"])
                    S.op("dve", lambda e: e.tensor_tensor(out=S4, in0=psS2, in1=P["t4"], op=ALU.add), reads=[kS2, pk + "tmp2"], writes=["S4"])
                    S.op("act", lambda e: e.activation(out=S4b, in_=S4, func=AF.Copy), reads=["S4"], writes=["S4b"])
            nst_ = getattr(C, "p6_nsteps", NTT)
            for s_ in range(nst_):
                yield from step(s_, "A")
                yield from step(s_, "S")
            for vh in range(2):
                S.op("sp", lambda e, vh=vh, g=g: e.dma_start(out=szb1, in_=F1[b, 8192 + (2 * g + vh) * 128:8192 + (2 * g + vh + 1) * 128, 0:SEQ]),
                     reads=["F1"], writes=["szb1"], dma=1)
                for gi in range(4):
                    yield
                    t0 = gi * 512
                    sl = gi % 2
                    a_ = o3[:, vh, t0:t0 + 512]
                    S.op("pool", lambda e, a_=a_, sl=sl: e.tensor_tensor(out=sqb[sl], in0=a_, in1=a_, op=ALU.mult), reads=["oacc"], writes=["sqb6"])
                    S.op("pe", lambda e, sl=sl: e.matmul(psS1, lhsT=C.ones, rhs=sqb[sl], start=True, stop=True), reads=["ones", "sqb6"], writes=[kS1])
                    S.op("act", lambda e, sl=sl: e.activation(out=lnv[sl], in_=psS1, func=AF.Ln, scale=1.0 / 128, bias=C.eps_t[:, 0:1]), reads=[kS1], writes=["lnv6"])
                    S.op("act", lambda e, sl=sl: e.activation(out=rst[sl], in_=lnv[sl], func=AF.Exp, scale=-0.5), reads=["lnv6"], writes=["lnv6"])
                    S.op("dve", lambda e, sl=sl, a_=a_: e.scalar_tensor_tensor(out=osum[sl], in0=a_, scalar=onorm[:, 0:1], in1=rst[sl], op0=ALU.mult, op1=ALU.mult),
                         reads=["oacc", "lnv6", "onorm"], writes=["osum%d" % sl])
                    S.op("pool", lambda e, sl=sl, t0=t0: e.tensor_tensor(out=szb1[:, t0:t0 + 512], in0=osum[sl], in1=szb1[:, t0:t0 + 512], op=ALU.mult),
                         reads=["osum%d" % sl, "szb1"], writes=["szb1"])
                r0 = (2 * g + vh) * 128
                S.op("sp", lambda e, r0=r0: e.dma_start(out=OG1[b, r0:r0 + 128, :], in_=szb1), reads=["szb1"], writes=["OG1"], dma=1)

    heads_all = list(getattr(C, "p6_heads", range(16)))
    gens = [run_slot(0, heads_all[0::2]), run_slot(1, heads_all[1::2])]
    for _ in range(getattr(C, "p6_offset", 0)):
        try:
            next(gens[0])
        except StopIteration:
            gens.pop(0)
            break
    while gens:
        for g_ in list(gens):
            try:
                next(g_)
            except StopIteration:
                gens.remove(g_)
    S.barrier()


def phase7(C):
    nc, S, A = C.nc, C.S, C.A
    for b in range(NB):
        def res(tt, cb, b=b):
            return C.dram["X1"][b, tt * 128:(tt + 1) * 128, cb * 256:(cb + 1) * 256]

        def dst(tt, cb, b=b):
            return C.dram["X2"][b, tt * 128:(tt + 1) * 128, cb * 256:(cb + 1) * 256]
        out_proj(C, b, 1, C.dram["OG1"][b], 32, C.dram["dn_w_out"], res, dst, list(range(16)), "p7", cbw=256, ntok=SEQ)
    A.reset()
    fn = A.alloc(D)
    S.op("sp", lambda e: e.dma_start(out=fn, in_=C.dram["final_norm"].unsqueeze(0).to_broadcast([128, D])), writes=["fn"], dma=1)
    xr = [A.alloc(D) for _ in range(3)]
    xo = [A.alloc(D) for _ in range(3)]
    junk = A.alloc(D, BF16)
    st = [A.alloc(4) for _ in range(3)]
    it = 0
    for b in range(NB):
        for tt in range(16):
            sl = it % 3
            it += 1
            xt, xo_, s4 = xr[sl], xo[sl], st[sl]
            S.op("sp", lambda e, xt=xt, b=b, tt=tt: e.dma_start(out=xt, in_=C.dram["X2"][b, tt * 128:(tt + 1) * 128, :]), reads=["X2"], writes=["fxr%d" % sl], dma=1)
            S.op("act", lambda e, xt=xt, s4=s4: e.activation(out=junk, in_=xt, func=AF.Square, accum_out=s4[:, 0:1]), reads=["fxr%d" % sl], writes=["fjunk", "fst%d" % sl])
            S.op("act", lambda e, s4=s4: e.activation(out=s4[:, 1:2], in_=s4[:, 0:1], func=AF.Sqrt, scale=1.0 / D, bias=C.eps_t[:, 0:1]), reads=["fst%d" % sl], writes=["fst%d" % sl])
            S.op("dve", lambda e, s4=s4: e.reciprocal(out=s4[:, 2:3], in_=s4[:, 1:2]), reads=["fst%d" % sl], writes=["fst%d" % sl])
            S.op("dve", lambda e, xt=xt, xo_=xo_, s4=s4: e.scalar_tensor_tensor(out=xo_, in0=xt, scalar=s4[:, 2:3], in1=fn, op0=ALU.mult, op1=ALU.mult),
                 reads=["fxr%d" % sl, "fst%d" % sl, "fn"], writes=["fxo%d" % sl])
            S.op("sp", lambda e, xo_=xo_, b=b, tt=tt: e.dma_start(out=C.dram["OUT"][b, tt * 128:(tt + 1) * 128, :], in_=xo_), reads=["fxo%d" % sl], writes=["OUT"], dma=1)
    S.barrier()


NCORES = 8
_PHASES = (phase0, phase1, phase2, phase3, phase4, phase5, phase6, phase7)


def _host_inputs(inp):
    cos, sin = rope_tables()
    per_m, nt, mask, drow, dcol = na_geometry()
    rpb = np.asarray(inp["ab_rpb"][0], np.float32)
    nab = np.stack([rpb[h][drow, dcol] for h in range(8)], 0).astype(np.float32)
    mi, ms, id4 = dn_masks()
    lm, id8 = dn_level_masks()
    f = lambda a: np.ascontiguousarray(np.asarray(a, np.float32))
    shared = {
        "w_mod0": f(inp["ab_w_mod"][0]), "w_mod1": f(inp["dn_w_mod"][0]), "b_mod0": f(inp["ab_b_mod"][0]), "b_mod1": f(inp["dn_b_mod"][0]),
        "ab_norm": f(inp["ab_norm"][0]), "ab_w_in": f(inp["ab_w_in"][0]), "ab_w_qb": f(inp["ab_w_qb"][0]), "ab_w_kvb": f(inp["ab_w_kvb"][0]),
        "ab_q_norm": f(inp["ab_q_norm"][0]), "ab_kv_norm": f(inp["ab_kv_norm"][0]), "ab_w_out": f(inp["ab_w_out"][0]),
        "na_mask": mask, "na_bias": nab, "rope_cos": cos, "rope_sin": sin, "ident": np.eye(128, dtype=np.float32).astype(NPBF),
        "dn_norm": f(inp["dn_norm"][0]), "dn_w_in": f(inp["dn_w_in"][0]), "dn_conv": f(inp["dn_conv"][0]), "dn_a_log": f(inp["dn_a_log"][0]),
        "dn_dt_bias": f(inp["dn_dt_bias"][0]), "dn_o_norm": f(inp["dn_o_norm"][0]), "dn_w_out": f(inp["dn_w_out"][0]), "final_norm": f(inp["final_norm"]),
        "dn_mincl": mi, "dn_mstrict": ms, "dn_ident4": id4.astype(NPBF), "dn_lmask2": dn_level_masks2().astype(NPBF),
    }
    maps = []
    for i in range(NCORES):
        m = dict(shared)
        m["x"] = f(inp["x"][NB * i:NB * (i + 1)])
        m["ctx"] = f(inp["ctx"][NB * i:NB * (i + 1)])
        m["cvec"] = np.concatenate([f(inp["c"][NB * i:NB * (i + 1)]), f(inp["c_ctx"])[None]], 0)
        maps.append(m)
    return maps


_INTERNAL = {
    "mod": ([2, 3, 6144], F32), "F0": ([NB, 5184, T], BF16), "VA": ([NB, T, 1024], BF16), "M0": ([NB, 2560, T], BF16), "VM": ([NB, T, 1024], BF16),
    "OG0": ([NB, 2048, T], BF16), "X1": ([NB, T, D], F32), "F1": ([NB, 12288, T], BF16), "BA": ([NB, T, 128], F32), "OG1": ([NB, 4096, SEQ], BF16),
    "X2": ([NB, SEQ, D], F32),
}


def build_program(maps0, phases=_PHASES):
    nc = bass.Bass("TRN2", target_bir_lowering=False)
    with ExitStack() as es:
        dram = {}
        for nm, a in maps0.items():
            dram[nm] = nc.dram_tensor(nm, list(a.shape), BF16 if a.dtype == NPBF else F32, kind="ExternalInput").ap()
        for nm, (shape, dt_) in _INTERNAL.items():
            dram[nm] = nc.dram_tensor(nm, shape, dt_, kind="Internal").ap()
        dram["OUT"] = nc.dram_tensor("OUT", [NB, SEQ, D], F32, kind="ExternalOutput").ap()
        C = make_ctx(nc, es, dram)
        for p in phases:
            p(C)
        C.S.finalize()
    return nc


def kernel(**inputs):
    maps = _host_inputs(inputs)
    nc = build_program(maps[0])
    res = run_bass_kernel_spmd(nc, maps, core_ids=list(range(NCORES)))
    out = np.concatenate([np.asarray(r["OUT"], np.float32) for r in res.results], axis=0)
    return out
```

```python
import numpy as np
import ml_dtypes
from contextlib import ExitStack
import concourse.bass as bass
import concourse.mybir as mybir
from concourse.bass_utils import run_bass_kernel_spmd

F32 = mybir.dt.float32
BF16 = mybir.dt.bfloat16
AF = mybir.ActivationFunctionType
ALU = mybir.AluOpType
NPBF = ml_dtypes.bfloat16

D = 2048
SEQ = 2048
CTX = 256
T = SEQ + CTX
NTT = T // 128
NB = 2
EPS = 1e-6
TG = [(0, 512), (512, 512), (1024, 512), (1536, 512), (2048, 256)]


class Op:
    __slots__ = ("eng", "fn", "deps", "marked", "val", "sem", "is_dma")


class Buf:
    __slots__ = ("w", "r")

    def __init__(self):
        self.w = None
        self.r = []


class Sched:
    ENGS = ("pe", "act", "dve", "pool", "sp")
    ENGOBJ = {"pe": "tensor", "act": "scalar", "dve": "vector", "pool": "gpsimd", "sp": "sync"}

    def __init__(self, nc, es):
        self.nc = nc
        self.es = es
        self.ops = {e: [] for e in self.ENGS}
        self.bufs = {}
        self.sems = {e: es.enter_context(nc.semaphore("s_" + e)) for e in self.ENGS}
        self.dsems = {}
        self.dpool = []
        self.last_dma = {}
        self.nops = 0

    DRAMKEYS = {"mod", "F0", "VA", "M0", "VM", "OG0", "X1", "X2", "F1", "BA", "OG1", "OUT", "ST"}

    def dsem(self, key):
        if key not in self.dsems:
            i = len(self.dsems)
            if i >= len(self.dpool):
                self.dpool.append([self.es.enter_context(self.nc.semaphore("d_%d" % i)), 0])
            self.dsems[key] = self.dpool[i]
        return self.dsems[key]

    def op(self, eng, fn, reads=(), writes=(), dma=None):
        o = Op()
        o.eng = eng
        o.fn = fn
        o.deps = []
        o.marked = False
        o.val = None
        o.sem = None
        o.is_dma = dma is not None
        self.nops += 1
        if dma is not None:
            dk = None
            for k in list(writes) + list(reads):
                if not (k in self.DRAMKEYS or k.startswith("DR_")):
                    dk = k
                    break
            assert dk is not None, (reads, writes)
            d = self.dsem(dk)
            d[1] += 16
            o.sem = d[0]
            o.val = d[1]
            o.marked = True
            self.last_dma[id(d)] = o
        deps = {}
        for k in reads:
            b = self.bufs.get(k)
            if b is None:
                b = self.bufs[k] = Buf()
            if b.w is not None:
                deps[id(b.w)] = b.w
        for k in writes:
            b = self.bufs.get(k)
            if b is None:
                b = self.bufs[k] = Buf()
            if b.w is not None:
                deps[id(b.w)] = b.w
            for r in b.r:
                deps[id(r)] = r
        for k in reads:
            self.bufs[k].r.append(o)
        for k in writes:
            b = self.bufs[k]
            b.w = o
            b.r = []
        for d in deps.values():
            if d is o:
                continue
            if d.eng == "pe" and eng == "pe" and not d.is_dma:
                continue
            d.marked = True
            o.deps.append(d)
        self.ops[eng].append(o)
        return o

    def barrier(self):
        lasts = []
        for e in self.ENGS:
            for o in reversed(self.ops[e]):
                if not o.is_dma and o.fn is not None:
                    o.marked = True
                    lasts.append(o)
                    break
        lasts += list(self.last_dma.values())
        for e in self.ENGS:
            o = Op()
            o.eng = e
            o.fn = None
            o.deps = list(lasts)
            o.marked = False
            o.val = None
            o.sem = None
            o.is_dma = False
            self.ops[e].append(o)
        self.bufs = {}
        self.dsems = {}

    def finalize(self):
        for e in self.ENGS:
            c = 0
            for o in self.ops[e]:
                if o.is_dma or o.fn is None:
                    continue
                if o.marked:
                    c += 1
                    o.val = c
                    o.sem = self.sems[e]
        nc = self.nc
        with nc.Block() as block:
            for e in self.ENGS:
                ops = self.ops[e]

                def body(engine, ops=ops, e=e):
                    seen = {}
                    for o in ops:
                        for d in o.deps:
                            k = id(d.sem)
                            if seen.get(k, 0) >= d.val:
                                continue
                            seen[k] = d.val
                            engine.wait_ge(d.sem, d.val)
                        if o.fn is None:
                            continue
                        ins = o.fn(engine)
                        if o.is_dma:
                            ins.then_inc(o.sem, 16)
                        elif o.marked:
                            ins.then_inc(o.sem, 1)
                    if e == "sp":
                        for (s, v) in self.dpool:
                            if v > 0:
                                engine.wait_ge(s, v)

                getattr(block, self.ENGOBJ[e])(body)


class Arena:
    def __init__(self, nc, es, nwords=51200):
        self.t = es.enter_context(nc.sbuf_tensor("arena", [128, nwords], F32))
        self.n = nwords
        self.off = 0
        self.uid = 0

    def reset(self):
        self.off = 0

    def alloc(self, nelem, dtype=F32):
        nbytes = nelem * (4 if dtype == F32 else 2)
        words = (nbytes + 31) // 32 * 8
        assert self.off + words <= self.n, "SBUF arena overflow %d+%d" % (self.off, words)
        ap = self.t[:, self.off:self.off + words]
        self.off += words
        if dtype != F32:
            ap = ap.bitcast(dtype)
        return ap[:, 0:nelem]

    def key(self, name):
        self.uid += 1
        return "%s#%d" % (name, self.uid)


class Ctx:
    pass


def _v3(ap, a):
    return ap.rearrange("p (a b) -> p a b", a=a)


def phase0(C):
    nc, S, A = C.nc, C.S, C.A
    A.reset()
    csT = A.alloc(48)
    cs3 = _v3(csT, 16)
    bm = A.alloc(6144)
    osb = [A.alloc(2048), A.alloc(2048)]
    wr = [A.alloc(2048) for _ in range(4)]
    ps = C.ps
    for v in range(3):
        S.op("sp", lambda e, v=v: e.dma_start(out=cs3[:, :, v], in_=C.dram["cvec"][v, :].rearrange("(k p) -> p k", p=128),
                                              allow_slow_non_contiguous=True), writes=["csT"], dma="p0c")
    S.op("act", lambda e: e.activation(out=csT, in_=csT, func=AF.Silu), reads=["csT"], writes=["csT"])
    it = 0
    oi = 0
    for l in range(2):
        wm = C.dram["w_mod%d" % l]
        bmod = C.dram["b_mod%d" % l]
        S.op("sp", lambda e, bmod=bmod: e.dma_start(out=bm[0:3, :], in_=bmod.unsqueeze(0).to_broadcast([3, 6144])),
             writes=["bm"], dma="p0b")
        for g in range(3):
            for k in range(16):
                slot = it % 4
                it += 1
                wt = wr[slot]
                S.op("sp", lambda e, wt=wt, k=k, g=g, wm=wm: e.dma_start(out=wt, in_=wm[k * 128:(k + 1) * 128, g * 2048:(g + 1) * 2048]),
                     writes=["p0w%d" % slot], dma="p0w%d" % slot)
                for n in range(4):
                    S.op("pe", lambda e, wt=wt, k=k, n=n: e.matmul(ps[0:3, n * 512:(n + 1) * 512], lhsT=cs3[:, k, :], rhs=wt[:, n * 512:(n + 1) * 512],
                                                                    start=(k == 0), stop=(k == 15)),
                         reads=["csT", "p0w%d" % slot], writes=["p0ps%d" % n])
            ob = osb[oi % 2]
            okey = "p0o%d" % (oi % 2)
            oi += 1
            for n in range(4):
                S.op("dve", lambda e, ob=ob, n=n, g=g: e.tensor_tensor(out=ob[0:3, n * 512:(n + 1) * 512], in0=ps[0:3, n * 512:(n + 1) * 512],
                                                                       in1=bm[0:3, g * 2048 + n * 512:g * 2048 + (n + 1) * 512], op=ALU.add),
                     reads=["p0ps%d" % n, "bm"], writes=[okey])
            S.op("sp", lambda e, ob=ob, l=l, g=g: e.dma_start(out=C.dram["mod"][l, :, g * 2048:(g + 1) * 2048], in_=ob[0:3, :]),
                 reads=[okey], writes=["mod"], dma="p0o")
    S.barrier()


def load_modvecs(C, l, b, gain, tag):
    S, A = C.S, C.A
    g = A.alloc(16)
    S.op("sp", lambda e: e.dma_start(out=g, in_=gain.rearrange("(k p) -> p k", p=128), allow_slow_non_contiguous=True),
         writes=[tag + "g"], dma=tag + "v")
    res = []
    for vi, v in enumerate((b, 2)):
        sc = A.alloc(16)
        sh = A.alloc(16)
        S.op("sp", lambda e, sc=sc, v=v: e.dma_start(out=sc, in_=C.dram["mod"][l, v, 2048:4096].rearrange("(k p) -> p k", p=128),
                                                     allow_slow_non_contiguous=True), reads=["mod"], writes=[tag + "sc%d" % vi], dma=tag + "v")
        S.op("sp", lambda e, sh=sh, v=v: e.dma_start(out=sh, in_=C.dram["mod"][l, v, 0:2048].rearrange("(k p) -> p k", p=128),
                                                     allow_slow_non_contiguous=True), reads=["mod"], writes=[tag + "sh%d" % vi], dma=tag + "v")
        S.op("dve", lambda e, sc=sc: e.scalar_tensor_tensor(out=sc, in0=sc, scalar=1.0, in1=g, op0=ALU.add, op1=ALU.mult),
             reads=[tag + "sc%d" % vi, tag + "g"], writes=[tag + "sc%d" % vi])
        res.append((sc, sh, tag + "sc%d" % vi, tag + "sh%d" % vi))
    return res


def build_xmT(C, xmT, xkey, src_tiles, mv, tag):
    S, A = C.S, C.A
    xr = [A.alloc(D) for _ in range(2)]
    xh = [A.alloc(D, BF16) for _ in range(2)]
    junk = A.alloc(D, BF16)
    st = [A.alloc(4) for _ in range(2)]
    xm3 = _v3(xmT, 16)
    for tt in range(NTT):
        sl = tt % 2
        xt, xb, s4 = xr[sl], xh[sl], st[sl]
        kx, kb_, ks = tag + "x%d" % sl, tag + "xh%d" % sl, tag + "st%d" % sl
        ms, sh, kms, ksh = mv[0] if tt < 16 else mv[1]
        S.op("sp", lambda e, xt=xt, tt=tt: e.dma_start(out=xt, in_=src_tiles[tt]), writes=[kx], dma=kx)
        S.op("act", lambda e, xt=xt, s4=s4: e.activation(out=junk, in_=xt, func=AF.Square, accum_out=s4[:, 0:1]),
             reads=[kx], writes=[tag + "junk", ks])
        S.op("act", lambda e, s4=s4: e.activation(out=s4[:, 1:2], in_=s4[:, 0:1], func=AF.Sqrt, scale=1.0 / D, bias=C.eps_t[:, 0:1]),
             reads=[ks], writes=[ks])
        S.op("dve", lambda e, s4=s4: e.reciprocal(out=s4[:, 2:3], in_=s4[:, 1:2]), reads=[ks], writes=[ks])
        S.op("dve", lambda e, xt=xt, xb=xb, s4=s4: e.tensor_scalar(out=xb, in0=xt, scalar1=s4[:, 2:3], scalar2=None, op0=ALU.mult),
             reads=[kx, ks], writes=[kb_])
        pb = C.ps[:, (tt % 2) * 1024:(tt % 2) * 1024 + 1024].bitcast(BF16)
        kp = tag + "tp%d" % (tt % 2)
        for k in range(16):
            S.op("pe", lambda e, pb=pb, xb=xb, k=k: e.transpose(out=pb[:, k * 128:(k + 1) * 128], in_=xb[:, k * 128:(k + 1) * 128], identity=C.ident),
                 reads=[kb_, "ident"], writes=[kp])
        for k in range(16):
            o_ = xm3[:, k, tt * 128:(tt + 1) * 128]
            i_ = pb[:, k * 128:(k + 1) * 128]
            if k % 2 == 0:
                S.op("act", lambda e, o_=o_, i_=i_, k=k, ms=ms, sh=sh: e.activation(out=o_, in_=i_, func=AF.Identity, bias=sh[:, k:k + 1], scale=ms[:, k:k + 1]),
                     reads=[kp, kms, ksh], writes=[xkey + str(tt)])
            else:
                S.op("dve", lambda e, o_=o_, i_=i_, k=k, ms=ms, sh=sh: e.tensor_scalar(out=o_, in0=i_, scalar1=ms[:, k:k + 1], scalar2=sh[:, k:k + 1], op0=ALU.mult, op1=ALU.add),
                     reads=[kp, kms, ksh], writes=[xkey + str(tt)])


def proj_fm(C, xm3, xkey, w3, wkey, ncols, evac, tag, m_off=0):
    S = C.S
    for sb in range((ncols + 127) // 128):
        m = min(128, ncols - sb * 128)
        for gi, (t0, tn) in enumerate(TG):
            bank = 4 + (C.pcnt % 4)
            C.pcnt += 1
            pa = C.ps[0:m, bank * 512:bank * 512 + tn]
            pk = "psb%d" % bank
            for k in range(16):
                S.op("pe", lambda e, pa=pa, k=k, sb=sb, m=m, t0=t0, tn=tn: e.matmul(pa, lhsT=w3[:, k, sb * 128:sb * 128 + m], rhs=xm3[:, k, t0:t0 + tn],
                                                                                  start=(k == 0), stop=(k == 15)),
                     reads=[wkey] + [xkey + str(t) for t in range(t0 // 128, (t0 + tn) // 128)], writes=[pk])
            evac(sb, gi, t0, tn, pa, pk, m)


def phase1(C):
    nc, S, A = C.nc, C.S, C.A
    w_in = C.dram["ab_w_in"]
    segs = [("qa", 0, 512), ("qa", 512, 512), ("ka", 1024, 512), ("ka", 1536, 512), ("va", 2048, 512), ("va", 2560, 512),
            ("cq", 3072, 512), ("ckv", 3584, 512), ("kr", 4096, 64), ("krsw", 4096, 64),
            ("z", 4160, 512), ("z", 4672, 512), ("z", 5184, 512), ("z", 5696, 512)]
    frow = {"qa": 0, "ka": 1024 - 1024, "cq": 2048 - 3072, "ckv": 2560 - 3584, "z": 3072 - 4160}
    for b in range(NB):
        _phase1_b(C, b, segs, frow)


def _phase1_b(C, b, segs, frow):
    nc, S, A = C.nc, C.S, C.A
    w_in = C.dram["ab_w_in"]
    if True:
        A.reset()
        mv = load_modvecs(C, 0, b, C.dram["ab_norm"], "p1m%d" % b)
        xmT = A.alloc(16 * T, BF16)
        xm3 = _v3(xmT, 16)
        xkey = "xmT"
        mark = A.off
        tiles = [C.dram["x"][b, tt * 128:(tt + 1) * 128, :] for tt in range(16)] + [C.dram["ctx"][b, tt * 128:(tt + 1) * 128, :] for tt in range(2)]
        build_xmT(C, xmT, xkey, tiles, mv, "p1b")
        pass
        wr = [A.alloc(16 * 512, BF16) for _ in range(2)]
        stg = [A.alloc(T, BF16) for _ in range(3)]
        vst = [A.alloc(512, BF16) for _ in range(2)]
        krp = A.alloc(T)
        kro = A.alloc(T, BF16)
        cos_t = A.alloc(SEQ)
        sin_t = A.alloc(SEQ)
        tmpf = A.alloc(512)
        tmpg = A.alloc(512)
        S.op("sp", lambda e: e.dma_start(out=cos_t[0:64, :], in_=C.dram["rope_cos"]), writes=["cos"], dma="p1c")
        S.op("sp", lambda e: e.dma_start(out=sin_t[0:64, :], in_=C.dram["rope_sin"]), writes=["sin"], dma="p1c")
        F0 = C.dram["F0"]
        VA = C.dram["VA"]
        sc = [0]
        for wi, (nm, c0, ncw) in enumerate(segs):
            slot = wi % 2
            w3 = _v3(wr[slot], 16)[:, :, 0:ncw]
            wkey = "p1w%d" % slot
            if nm == "krsw":
                for (d0, s0) in ((0, 16), (16, 0), (32, 48), (48, 32)):
                    S.op("pool", lambda e, w3=w3, d0=d0, s0=s0: e.dma_start(out=w3[:, :, d0:d0 + 16],
                                                                           in_=w_in[:, 4096 + s0:4096 + s0 + 16].rearrange("(k p) c -> p k c", p=128)),
                         writes=[wkey], dma=wkey)
            else:
                S.op("pool", lambda e, w3=w3, c0=c0, ncw=ncw: e.dma_start(out=w3, in_=w_in[:, c0:c0 + ncw].rearrange("(k p) c -> p k c", p=128)),
                     writes=[wkey], dma=wkey)
            if nm == "va":
                for tt in range(NTT):
                    bank = 4 + (C.pcnt % 4)
                    C.pcnt += 1
                    pa = C.ps[:, bank * 512:bank * 512 + 512]
                    pk = "psb%d" % bank
                    for k in range(16):
                        S.op("pe", lambda e, pa=pa, k=k, tt=tt, w3=w3: e.matmul(pa, lhsT=xm3[:, k, tt * 128:(tt + 1) * 128], rhs=w3[:, k, :], start=(k == 0), stop=(k == 15)),
                             reads=[wkey, xkey + str(tt)], writes=[pk])
                    vs = vst[tt % 2]
                    vk = "p1vs%d" % (tt % 2)
                    eng = "act" if tt % 2 == 0 else "dve"
                    if eng == "act":
                        S.op("act", lambda e, vs=vs, pa=pa: e.activation(out=vs, in_=pa, func=AF.Copy), reads=[pk], writes=[vk])
                    else:
                        S.op("dve", lambda e, vs=vs, pa=pa: e.tensor_copy(out=vs, in_=pa), reads=[pk], writes=[vk])
                    S.op("sp", lambda e, vs=vs, tt=tt, c0=c0: e.dma_start(out=VA[b, tt * 128:(tt + 1) * 128, c0 - 2048:c0 - 2048 + 512], in_=vs),
                         reads=[vk], writes=["VA"], dma=vk)
                continue

            def evac(sb, gi, t0, tn, pa, pk, m, nm=nm, c0=c0):
                if nm in ("kr", "krsw"):
                    if nm == "kr":
                        S.op("dve", lambda e: e.tensor_copy(out=krp[0:64, t0:t0 + tn], in_=pa), reads=[pk], writes=["krp"])
                    else:
                        if gi < 4:
                            S.op("dve", lambda e: e.tensor_tensor(out=tmpf[0:64, 0:tn], in0=pa, in1=sin_t[0:64, t0:t0 + tn], op=ALU.mult),
                                 reads=[pk, "sin"], writes=["tmpf"])
                            S.op("pool", lambda e: e.tensor_tensor(out=tmpg[0:64, 0:tn], in0=krp[0:64, t0:t0 + tn], in1=cos_t[0:64, t0:t0 + tn], op=ALU.mult),
                                 reads=["krp", "cos"], writes=["tmpg"])
                            S.op("dve", lambda e: e.tensor_tensor(out=kro[0:64, t0:t0 + tn], in0=tmpf[0:64, 0:tn], in1=tmpg[0:64, 0:tn], op=ALU.add),
                                 reads=["tmpf", "tmpg"], writes=["kro"])
                        if gi == 4:
                            S.op("dve", lambda e: e.tensor_copy(out=kro[0:64, t0:t0 + tn], in_=krp[0:64, t0:t0 + tn]), reads=["krp"], writes=["kro"])
                            S.op("sp", lambda e: e.dma_start(out=F0[b, 5120:5184, :], in_=kro[0:64, :]), reads=["kro"], writes=["F0"], dma="p1kr")
                    return
                if gi == 0:
                    sc[0] += 1
                si = sc[0] % 3
                sg = stg[si]
                sk = "p1st%d" % si
                func = AF.Silu if nm == "z" else AF.Copy
                if nm == "z" or (gi % 2 == 0):
                    S.op("act", lambda e: e.activation(out=sg[0:m, t0:t0 + tn], in_=pa, func=func), reads=[pk], writes=[sk])
                else:
                    S.op("dve", lambda e: e.tensor_copy(out=sg[0:m, t0:t0 + tn], in_=pa), reads=[pk], writes=[sk])
                if gi == 4:
                    r0 = c0 + sb * 128 + frow[nm]
                    S.op("sp", lambda e: e.dma_start(out=F0[b, r0:r0 + m, :], in_=sg[0:m, :]), reads=[sk], writes=["F0"], dma=sk)

            proj_fm(C, xm3, xkey, w3, wkey, ncw, evac, "p1")
        S.barrier()


def rope_tables():
    quarter = 16
    inv = (10000.0 ** (-np.arange(quarter, dtype=np.float32) / quarter)).astype(np.float32)
    pos = np.arange(SEQ)
    cos = np.zeros((64, SEQ), np.float32)
    sin = np.zeros((64, SEQ), np.float32)
    for half, p in ((0, pos // 64), (1, pos % 64)):
        ang = p.astype(np.float32)[None, :] * inv[:, None]
        c, s = np.cos(ang), np.sin(ang)
        cos[half * 32:half * 32 + 16] = c
        cos[half * 32 + 16:half * 32 + 32] = c
        sin[half * 32:half * 32 + 16] = -s
        sin[half * 32 + 16:half * 32 + 32] = s
    return cos, sin


def make_ctx(nc, es, dram):
    C = Ctx()
    C.nc = nc
    C.S = Sched(nc, es)
    C.es = es
    C.A = Arena(nc, es)
    C.ps = es.enter_context(nc.psum_tensor("ps", [128, 4096], F32))
    C.dram = dram
    C.pcnt = 0
    C.na_geo = na_geometry()
    C.cst = es.enter_context(nc.sbuf_tensor("cst", [128, 512], F32))
    C.ident = C.cst[:, 0:64].bitcast(BF16)
    C.eps_t = C.cst[:, 64:65]
    C.ones = C.cst[:, 72:136].bitcast(BF16)
    C.S.op("sp", lambda e: e.dma_start(out=C.ident, in_=dram["ident"]), writes=["ident"], dma="cst")
    C.S.op("pool", lambda e: e.memset(C.eps_t, EPS), writes=["eps"])
    C.S.op("pool", lambda e: e.memset(C.ones, 1.0), writes=["ones"])
    C.lnq = C.cst[:, 65:66]
    C.zero_t = C.cst[:, 66:67]
    C.cw = C.cst[:, 136:456]
    C.S.op("pool", lambda e: e.memset(C.lnq, float(np.log(128.0 ** -0.5))), writes=["lnq"])
    C.S.op("pool", lambda e: e.memset(C.zero_t, 0.0), writes=["zero"])
    if "dn_conv" in dram:
        cw3 = C.cw.rearrange("p (blk j) -> p blk j", j=5)
        for j in range(5):
            for q4 in range(8):
                C.S.op("sp", lambda e, j=j, q4=q4: e.dma_start(out=cw3[:, q4 * 8:(q4 + 1) * 8, j], in_=dram["dn_conv"][j, q4 * 1024:(q4 + 1) * 1024].rearrange("(blk p) -> p blk", p=128),
                                                          allow_slow_non_contiguous=True), writes=["cw"], dma="cw")
    C.S.barrier()
    return C


MLA_SCALE = 192.0 ** -0.5
NA_SCALE = 128.0 ** -0.5


def phase2(C):
    for b in range(NB):
        _phase2_b(C, b)


def _phase2_b(C, b):
    nc, S, A = C.nc, C.S, C.A
    A.reset()
    F0, M0, VM = C.dram["F0"], C.dram["M0"], C.dram["VM"]
    wqb = A.alloc(4 * 1536, BF16)
    wqb3 = _v3(wqb, 4)
    wqsw = A.alloc(4 * 512, BF16)
    wqsw4 = wqsw.rearrange("p (k h c) -> p k h c", k=4, h=8)
    wkvb = A.alloc(4 * 2048, BF16)
    wkvb3 = _v3(wkvb, 4)
    wkvb4 = wkvb.rearrange("p (k h c) -> p k h c", k=4, h=8)
    cq = A.alloc(4 * T, BF16)
    ckv = A.alloc(4 * T, BF16)
    cqn = A.alloc(4 * T, BF16)
    ckvn = A.alloc(4 * T, BF16)
    sq = [A.alloc(4 * 512, BF16) for _ in range(2)]
    lnv = [A.alloc(512) for _ in range(2)]
    rst = [A.alloc(512) for _ in range(2)]
    qnm = A.alloc(4)
    kvnm = A.alloc(4)
    cos_t = A.alloc(SEQ)
    sin_t = A.alloc(SEQ)
    stg = [A.alloc(T, BF16) for _ in range(3)]
    vst = [A.alloc(1024, BF16) for _ in range(2)]
    t1 = [A.alloc(512) for _ in range(2)]
    t2 = [A.alloc(512) for _ in range(2)]
    wq_d, wkv_d = C.dram["ab_w_qb"], C.dram["ab_w_kvb"]
    S.op("pool", lambda e: e.dma_start(out=wqb3, in_=wq_d.rearrange("(k p) c -> p k c", p=128)), writes=["wqb"], dma="p2w")
    S.op("pool", lambda e: e.dma_start(out=wkvb3, in_=wkv_d.rearrange("(k p) c -> p k c", p=128)), writes=["wkvb"], dma="p2w")
    wq4 = wq_d.rearrange("(k p) (h c) -> p k h c", p=128, h=8)
    for k in range(4):
        for (d0, s0) in ((0, 16), (16, 0), (32, 48), (48, 32)):
            S.op("pool", lambda e, k=k, d0=d0, s0=s0: e.dma_start(out=wqsw4[:, k, :, d0:d0 + 16], in_=wq4[:, k, :, 128 + s0:128 + s0 + 16]),
                 writes=["wqsw"], dma="p2w")
    S.op("sp", lambda e: e.dma_start(out=_v3(cq, 4), in_=F0[b, 2048:2560, :].rearrange("(k p) t -> p k t", p=128)), reads=["F0"], writes=["cq"], dma="p2a")
    S.op("sp", lambda e: e.dma_start(out=_v3(ckv, 4), in_=F0[b, 2560:3072, :].rearrange("(k p) t -> p k t", p=128)), reads=["F0"], writes=["ckv"], dma="p2a")
    S.op("sp", lambda e: e.dma_start(out=qnm, in_=C.dram["ab_q_norm"].rearrange("(k p) -> p k", p=128), allow_slow_non_contiguous=True), writes=["qnm"], dma="p2a")
    S.op("sp", lambda e: e.dma_start(out=kvnm, in_=C.dram["ab_kv_norm"].rearrange("(k p) -> p k", p=128), allow_slow_non_contiguous=True), writes=["kvnm"], dma="p2a")
    S.op("sp", lambda e: e.dma_start(out=cos_t[0:64, :], in_=C.dram["rope_cos"]), writes=["cos"], dma="p2a")
    S.op("sp", lambda e: e.dma_start(out=sin_t[0:64, :], in_=C.dram["rope_sin"]), writes=["sin"], dma="p2a")
    it = 0
    for (src, skey, nrm, nkey, dst, dkey) in ((cq, "cq", qnm, "qnm", cqn, "cqn"), (ckv, "ckv", kvnm, "kvnm", ckvn, "ckvn")):
        s3, d3 = _v3(src, 4), _v3(dst, 4)
        for gi, (t0, tn) in enumerate(TG):
            sl = it % 2
            it += 1
            q3 = _v3(sq[sl], 4)
            S.op("pool", lambda e, q3=q3, s3=s3, t0=t0, tn=tn: e.tensor_tensor(out=q3[:, :, 0:tn], in0=s3[:, :, t0:t0 + tn], in1=s3[:, :, t0:t0 + tn], op=ALU.mult),
                 reads=[skey], writes=["sq%d" % sl])
            bank = 4 + sl
            pa = C.ps[:, bank * 512:bank * 512 + tn]
            for k in range(4):
                S.op("pe", lambda e, pa=pa, q3=q3, k=k, tn=tn: e.matmul(pa, lhsT=C.ones, rhs=q3[:, k, 0:tn], start=(k == 0), stop=(k == 3)),
                     reads=["ones", "sq%d" % sl], writes=["psb%d" % bank])
            lv, rs = lnv[sl], rst[sl]
            S.op("act", lambda e, lv=lv, pa=pa, tn=tn: e.activation(out=lv[:, 0:tn], in_=pa, func=AF.Ln, scale=1.0 / 512, bias=C.eps_t[:, 0:1]),
                 reads=["psb%d" % bank], writes=["lnv%d" % sl])
            S.op("act", lambda e, lv=lv, rs=rs, tn=tn: e.activation(out=rs[:, 0:tn], in_=lv[:, 0:tn], func=AF.Exp, scale=-0.5),
                 reads=["lnv%d" % sl], writes=["rst%d" % sl])
            for k in range(4):
                S.op("dve", lambda e, d3=d3, s3=s3, k=k, t0=t0, tn=tn, nrm=nrm, rs=rs: e.scalar_tensor_tensor(
                    out=d3[:, k, t0:t0 + tn], in0=s3[:, k, t0:t0 + tn], scalar=nrm[:, k:k + 1], in1=rs[:, 0:tn], op0=ALU.mult, op1=ALU.mult),
                    reads=[skey, nkey, "rst%d" % sl], writes=[dkey])
    cqn3, ckvn3 = _v3(cqn, 4), _v3(ckvn, 4)
    sc = [0]

    def small_proj(lhs_fn, rhs3, rkey, wkey, m, gi, t0, tn):
        bank = 4 + (C.pcnt % 4)
        C.pcnt += 1
        pa = C.ps[0:m, bank * 512:bank * 512 + tn]
        for k in range(4):
            S.op("pe", lambda e, pa=pa, k=k: e.matmul(pa, lhsT=lhs_fn(k), rhs=rhs3[:, k, t0:t0 + tn], start=(k == 0), stop=(k == 3)),
                 reads=[wkey, rkey], writes=["psb%d" % bank])
        return pa, "psb%d" % bank

    for h in range(8):
        for (nm, lhs_fn, rhs3, rkey, wkey, row0) in (
                ("qn", lambda k, h=h: wqb3[:, k, h * 192:h * 192 + 128], cqn3, "cqn", "wqb", h * 128),
                ("kn", lambda k, h=h: wkvb3[:, k, h * 256:h * 256 + 128], ckvn3, "ckvn", "wkvb", 1536 + h * 128)):
            sc[0] += 1
            si = sc[0] % 3
            sg, sk = stg[si], "p2st%d" % si
            for gi, (t0, tn) in enumerate(TG):
                pa, pk = small_proj(lhs_fn, rhs3, rkey, wkey, 128, gi, t0, tn)
                if gi % 2 == 0:
                    S.op("act", lambda e, sg=sg, pa=pa, t0=t0, tn=tn: e.activation(out=sg[:, t0:t0 + tn], in_=pa, func=AF.Copy), reads=[pk], writes=[sk])
                else:
                    S.op("dve", lambda e, sg=sg, pa=pa, t0=t0, tn=tn: e.tensor_copy(out=sg[:, t0:t0 + tn], in_=pa), reads=[pk], writes=[sk])
            S.op("sp", lambda e, sg=sg, row0=row0: e.dma_start(out=M0[b, row0:row0 + 128, :], in_=sg), reads=[sk], writes=["M0"], dma=sk)
        sc[0] += 1
        si = sc[0] % 3
        sg, sk = stg[si], "p2st%d" % si
        for gi, (t0, tn) in enumerate(TG):
            pa, pk = small_proj(lambda k, h=h: wqb3[:, k, h * 192 + 128:h * 192 + 192], cqn3, "cqn", "wqb", 64, gi, t0, tn)
            if gi == 4:
                S.op("dve", lambda e, sg=sg, pa=pa, t0=t0, tn=tn: e.tensor_copy(out=sg[0:64, t0:t0 + tn], in_=pa), reads=[pk], writes=[sk])
                continue
            pb, pkb = small_proj(lambda k, h=h: wqsw4[:, k, h, :], cqn3, "cqn", "wqsw", 64, gi, t0, tn)
            a1, a2 = t1[gi % 2], t2[gi % 2]
            S.op("dve", lambda e, a1=a1, pa=pa, t0=t0, tn=tn: e.tensor_tensor(out=a1[0:64, 0:tn], in0=pa, in1=cos_t[0:64, t0:t0 + tn], op=ALU.mult),
                 reads=[pk, "cos"], writes=["t1%d" % (gi % 2)])
            S.op("dve", lambda e, a2=a2, pb=pb, t0=t0, tn=tn: e.tensor_tensor(out=a2[0:64, 0:tn], in0=pb, in1=sin_t[0:64, t0:t0 + tn], op=ALU.mult),
                 reads=[pkb, "sin"], writes=["t2%d" % (gi % 2)])
            S.op("pool", lambda e, a1=a1, a2=a2, sg=sg, t0=t0, tn=tn: e.tensor_tensor(out=sg[0:64, t0:t0 + tn], in0=a1[0:64, 0:tn], in1=a2[0:64, 0:tn], op=ALU.add),
                 reads=["t1%d" % (gi % 2), "t2%d" % (gi % 2)], writes=[sk])
        S.op("sp", lambda e, sg=sg, h=h: e.dma_start(out=M0[b, 1024 + h * 64:1024 + h * 64 + 64, :], in_=sg[0:64, :]), reads=[sk], writes=["M0"], dma=sk)
    for tt in range(NTT):
        vs, vk = vst[tt % 2], "p2vs%d" % (tt % 2)
        for half in range(2):
            bank = 4 + (C.pcnt % 4)
            C.pcnt += 1
            pa = C.ps[:, bank * 512:bank * 512 + 512]
            for k in range(4):
                S.op("pe", lambda e, pa=pa, k=k, tt=tt, half=half: e.matmul(pa.rearrange("p (h c) -> p h c", h=4), lhsT=ckvn3[:, k, tt * 128:(tt + 1) * 128],
                                                                         rhs=wkvb4[:, k, half * 4:half * 4 + 4, 128:256], start=(k == 0), stop=(k == 3)),
                     reads=["wkvb", "ckvn"], writes=["psb%d" % bank])
            if half == 0:
                S.op("act", lambda e, vs=vs, pa=pa: e.activation(out=vs[:, 0:512], in_=pa, func=AF.Copy), reads=["psb%d" % bank], writes=[vk])
            else:
                S.op("dve", lambda e, vs=vs, pa=pa: e.tensor_copy(out=vs[:, 512:1024], in_=pa), reads=["psb%d" % bank], writes=[vk])
        S.op("sp", lambda e, vs=vs, tt=tt: e.dma_start(out=VM[b, tt * 128:(tt + 1) * 128, :], in_=vs), reads=[vk], writes=["VM"], dma=vk)
    S.barrier()


def na_geometry():
    rows = 32
    r = np.arange(rows)
    r0 = np.clip(r - 4, 0, rows - 8)
    col = np.arange(64)
    c0 = np.clip(col - 8, 0, 64 - 16)
    tiles = {}
    per_m = []
    for m in range(16):
        lo = min(r0[2 * m], r0[2 * m + 1])
        hi = max(r0[2 * m], r0[2 * m + 1]) + 7
        lst = []
        for kb in range(lo // 2, hi // 2 + 1):
            memb = tuple(tuple(bool(r0[2 * m + bq] <= 2 * kb + a <= r0[2 * m + bq] + 7) for bq in range(2)) for a in range(2))
            key = (kb - m, memb)
            if key not in tiles:
                tiles[key] = len(tiles)
            lst.append((kb, tiles[key]))
        per_m.append(lst)
    nt = len(tiles)
    mask = np.zeros((nt, 128, 128), np.float32)
    drow = np.zeros((nt, 128, 128), np.int64)
    dcol = np.zeros((nt, 128, 128), np.int64)
    for (delta, memb), ti in tiles.items():
        for a in range(2):
            for bq in range(2):
                kc = np.arange(64)[:, None]
                qc = np.arange(64)[None, :]
                ok = memb[a][bq] & (kc >= c0[qc]) & (kc <= c0[qc] + 15)
                dr = 2 * delta + a - bq + 7
                dc = kc - qc + 15
                blk = (slice(a * 64, a * 64 + 64), slice(bq * 64, bq * 64 + 64))
                mask[ti][blk] = np.where(ok, 0.0, -20000.0)
                drow[ti][blk] = np.clip(np.where(ok, dr, 0), 0, 14)
                dcol[ti][blk] = np.clip(np.where(ok, dc, 0), 0, 30)
    return per_m, nt, mask, drow, dcol


def phase3(C):
    for b in range(NB):
        _phase3_b(C, b)


def _phase3_b(C, b):
    nc, S, A = C.nc, C.S, C.A
    A.reset()
    F0, M0, VM, VA, OG = C.dram["F0"], C.dram["M0"], C.dram["VM"], C.dram["VA"], C.dram["OG0"]
    per_m, nt, _, _, _ = C.na_geo
    krT = A.alloc(T, BF16)
    S.op("sp", lambda e: e.dma_start(out=krT[0:64, :], in_=F0[b, 5120:5184, :]), reads=["F0"], writes=["krT"], dma="p3k")
    nam = A.alloc(nt * 128)
    S.op("sp", lambda e: e.dma_start(out=_v3(nam, nt), in_=C.dram["na_mask"].rearrange("t k q -> k t q")), writes=["nam"], dma="p3k")
    hb = []
    for i in range(2):
        hb.append(dict(k=A.alloc(T, BF16), q=A.alloc(T, BF16), qr=A.alloc(T, BF16), v=A.alloc(NTT * 128, BF16), sz=A.alloc(T, BF16),
                       bias=A.alloc(nt * 128)))
    pT = [A.alloc(512, BF16) for _ in range(4)]
    sbf = [A.alloc(128) for _ in range(3)]
    rinv = [A.alloc(512) for _ in range(2)]
    tmp = [A.alloc(512) for _ in range(2)]
    ogs = [A.alloc(T, BF16) for _ in range(2)]
    cnt = dict(s=0, p=0, g=0, sb=0)

    def attend_block(lhsT_list, rhs_list, tn, reads, o_cols, first, last, vlhs, vkey, scale, bias=None, bkey=None, acc=None):
        bank = cnt["s"] % 4
        cnt["s"] += 1
        pS = C.ps[:, bank * 512:bank * 512 + tn]
        pk = "psb%d" % bank
        n = len(lhsT_list)
        for i in range(n):
            S.op("pe", lambda e, i=i: e.matmul(pS[0:128, :], lhsT=lhsT_list[i], rhs=rhs_list[i], start=(i == 0), stop=(i == n - 1)),
                 reads=reads, writes=[pk])
        slot = cnt["p"] % 4
        cnt["p"] += 1
        p_ = pT[slot][:, 0:tn]
        pkey = "pT%d" % slot
        if bias is None:
            S.op("act", lambda e: e.activation(out=p_, in_=pS, func=AF.Exp, scale=scale), reads=[pk], writes=[pkey])
        else:
            sslot = cnt["sb"] % 3
            cnt["sb"] += 1
            sb_ = sbf[sslot][:, 0:tn]
            S.op("dve", lambda e: e.scalar_tensor_tensor(out=sb_, in0=pS, scalar=scale, in1=bias, op0=ALU.mult, op1=ALU.add),
                 reads=[pk, bkey], writes=["sbf%d" % sslot])
            S.op("act", lambda e: e.activation(out=p_, in_=sb_, func=AF.Exp), reads=["sbf%d" % sslot], writes=[pkey])
        po, ps_, ok_, sk_ = acc

        def part2():
            S.op("pe", lambda e: e.matmul(po[:, o_cols[0]:o_cols[0] + tn], lhsT=vlhs, rhs=p_, start=first, stop=last), reads=[pkey, vkey], writes=[ok_])
            S.op("pe", lambda e: e.matmul(ps_[:, o_cols[0]:o_cols[0] + tn], lhsT=C.ones, rhs=p_, start=first, stop=last), reads=[pkey, "ones"], writes=[sk_])
        pend.append(part2)
        while len(pend) > 2:
            pend.pop(0)()

    pend = []

    def finish_group(acc, tn, sz, szkey, og, ogkey, t0):
        while pend:
            pend.pop(0)()
        po, ps_, ok_, sk_ = acc
        g = cnt["g"] % 2
        cnt["g"] += 1
        S.op("dve", lambda e: e.reciprocal(out=rinv[g][:, 0:tn], in_=ps_[:, 0:tn]), reads=[sk_], writes=["rinv%d" % g])
        S.op("dve", lambda e: e.tensor_tensor(out=tmp[g][:, 0:tn], in0=po[:, 0:tn], in1=rinv[g][:, 0:tn], op=ALU.mult),
             reads=[ok_, "rinv%d" % g], writes=["tmp%d" % g])
        S.op("pool", lambda e: e.tensor_tensor(out=og[:, t0:t0 + tn], in0=tmp[g][:, 0:tn], in1=sz[:, t0:t0 + tn], op=ALU.mult),
             reads=["tmp%d" % g, szkey], writes=[ogkey])

    def acc_banks():
        g = cnt["g"] % 2
        return (C.ps[:, (4 + g) * 512:(5 + g) * 512], C.ps[:, (6 + g) * 512:(7 + g) * 512], "psb%d" % (4 + g), "psb%d" % (6 + g))

    for hh in range(16):
        is_mla = hh >= 8
        h = hh % 8
        B_ = hb[hh % 2]
        hk = "hb%d" % (hh % 2)
        og, ogkey = ogs[hh % 2], "ogs%d" % (hh % 2)
        if is_mla:
            S.op("sp", lambda e, B_=B_, h=h: e.dma_start(out=B_["k"], in_=M0[b, 1536 + h * 128:1536 + (h + 1) * 128, :]), reads=["M0"], writes=[hk + "k"], dma=hk)
            S.op("sp", lambda e, B_=B_, h=h: e.dma_start(out=B_["q"], in_=M0[b, h * 128:(h + 1) * 128, :]), reads=["M0"], writes=[hk + "q"], dma=hk)
            S.op("sp", lambda e, B_=B_, h=h: e.dma_start(out=B_["qr"][0:64, :], in_=M0[b, 1024 + h * 64:1024 + (h + 1) * 64, :]), reads=["M0"], writes=[hk + "qr"], dma=hk)
            S.op("sp", lambda e, B_=B_, h=h: e.dma_start(out=_v3(B_["v"], NTT), in_=VM[b, :, h * 128:(h + 1) * 128].rearrange("(kb p) d -> p kb d", p=128)),
                 reads=["VM"], writes=[hk + "v"], dma=hk)
            S.op("sp", lambda e, B_=B_, h=h: e.dma_start(out=B_["sz"], in_=F0[b, 3072 + 1024 + h * 128:3072 + 1024 + (h + 1) * 128, :]), reads=["F0"], writes=[hk + "sz"], dma=hk)
        else:
            S.op("sp", lambda e, B_=B_, h=h: e.dma_start(out=B_["k"], in_=F0[b, 1024 + h * 128:1024 + (h + 1) * 128, :]), reads=["F0"], writes=[hk + "k"], dma=hk)
            S.op("sp", lambda e, B_=B_, h=h: e.dma_start(out=B_["q"], in_=F0[b, h * 128:(h + 1) * 128, :]), reads=["F0"], writes=[hk + "q"], dma=hk)
            S.op("sp", lambda e, B_=B_, h=h: e.dma_start(out=_v3(B_["v"], NTT), in_=VA[b, :, h * 128:(h + 1) * 128].rearrange("(kb p) d -> p kb d", p=128)),
                 reads=["VA"], writes=[hk + "v"], dma=hk)
            S.op("sp", lambda e, B_=B_, h=h: e.dma_start(out=B_["sz"], in_=F0[b, 3072 + h * 128:3072 + (h + 1) * 128, :]), reads=["F0"], writes=[hk + "sz"], dma=hk)
            S.op("sp", lambda e, B_=B_, h=h: e.dma_start(out=_v3(B_["bias"], nt), in_=C.dram["na_bias"][h].rearrange("t k q -> k t q")), writes=[hk + "b"], dma=hk)
            S.op("pool", lambda e, B_=B_: e.tensor_tensor(out=B_["bias"], in0=B_["bias"], in1=nam, op=ALU.add), reads=[hk + "b", "nam"], writes=[hk + "b"])
        k_, q_, qr_, v3_, sz_ = B_["k"], B_["q"], B_["qr"], _v3(B_["v"], NTT), B_["sz"]
        b3_ = _v3(B_["bias"], nt)
        for gi, (t0, tn) in enumerate(TG):
            acc = acc_banks()
            if is_mla:
                kbs = list(range(18)) if gi < 4 else [16, 17]
                for i, kb in enumerate(kbs):
                    attend_block([k_[:, kb * 128:(kb + 1) * 128], krT[0:64, kb * 128:(kb + 1) * 128]], [q_[:, t0:t0 + tn], qr_[0:64, t0:t0 + tn]], tn,
                                 [hk + "k", hk + "q", hk + "qr", "krT"], (0,), i == 0, i == len(kbs) - 1, v3_[:, kb, :], hk + "v", MLA_SCALE, acc=acc)
            else:
                for i, kb in enumerate((16, 17)):
                    attend_block([k_[:, kb * 128:(kb + 1) * 128]], [q_[:, t0:t0 + tn]], tn, [hk + "k", hk + "q"], (0,), i == 0, (gi == 4 and i == 1),
                                 v3_[:, kb, :], hk + "v", NA_SCALE, acc=acc)
                if gi < 4:
                    for pr in range(4):
                        m = gi * 4 + pr
                        lst = per_m[m]
                        for j, (kb, ti) in enumerate(lst):
                            attend_block([k_[:, kb * 128:(kb + 1) * 128]], [q_[:, m * 128:(m + 1) * 128]], 128, [hk + "k", hk + "q"], (pr * 128,), False,
                                         (pr == 3 and j == len(lst) - 1), v3_[:, kb, :], hk + "v", NA_SCALE, bias=b3_[:, ti, :], bkey=hk + "b", acc=acc)
            finish_group(acc, tn, sz_, hk + "sz", og, ogkey, t0)
        row0 = (1024 if is_mla else 0) + h * 128
        S.op("sp", lambda e, og=og, row0=row0: e.dma_start(out=OG[b, row0:row0 + 128, :], in_=og), reads=[ogkey], writes=["OG0"], dma=ogkey)
    S.barrier()


def out_proj(C, b, l, OGd, nk, w_out, res_tiles, dst_tiles, ntt_list, tag, cbw=512, ntok=T):
    nc, S, A = C.nc, C.S, C.A
    A.reset()
    og = A.alloc(nk * ntok, BF16)
    og3 = _v3(og, nk)
    hk = nk // 2
    S.op("sp", lambda e: e.dma_start(out=og3[:, 0:hk, :], in_=OGd[0:hk * 128, :].rearrange("(k p) t -> p k t", p=128)), reads=["DR_OG"], writes=["og"], dma=tag + "og")
    S.op("sp", lambda e: e.dma_start(out=og3[:, hk:nk, :], in_=OGd[hk * 128:nk * 128, :].rearrange("(k p) t -> p k t", p=128)), reads=["DR_OG"], writes=["og"], dma=tag + "og")
    vs = (b, 2) if len(ntt_list) > 16 else (b,)
    wr = A.alloc(nk * cbw, BF16)
    wx2 = [[A.alloc(nk * cbw, BF16) for _ in vs] for _ in range(2)]
    gt = [A.alloc(cbw) for _ in range(2)]
    xr = [A.alloc(cbw) for _ in range(3)]
    xo = [A.alloc(cbw) for _ in range(3)]
    it = 0
    def prep(cb):
        w3 = _v3(wr, nk)
        S.op("pool", lambda e, cb=cb, w3=w3: e.dma_start(out=w3, in_=w_out[:, cb * cbw:(cb + 1) * cbw].rearrange("(k p) c -> p k c", p=128)), writes=["wr"], dma=tag + "w")
        for vi, v in enumerate(vs):
            S.op("sp", lambda e, vi=vi, v=v, cb=cb: e.dma_start(out=gt[vi], in_=C.dram["mod"][l, v, 4096 + cb * cbw:4096 + (cb + 1) * cbw].unsqueeze(0).to_broadcast([128, cbw])),
                 reads=["mod"], writes=["gt%d" % vi], dma=tag + "gt")
            S.op("pool", lambda e, vi=vi, w3=w3, cb=cb: e.tensor_tensor(out=_v3(wx2[cb % 2][vi], nk), in0=w3, in1=gt[vi].unsqueeze(1).to_broadcast([128, nk, cbw]), op=ALU.mult),
                 reads=["wr", "gt%d" % vi], writes=["wx%d_%d" % (vi, cb % 2)])
    ncb = D // cbw
    prep(0)
    for cb in range(ncb):
        if cb + 1 < ncb:
            prep(cb + 1)
        for tt in ntt_list:
            vi = 0 if tt < 16 else 1
            wv = _v3(wx2[cb % 2][vi], nk)
            bank = 4 + (C.pcnt % 4)
            C.pcnt += 1
            pa = C.ps[:, bank * 512:bank * 512 + cbw]
            for k in range(nk):
                S.op("pe", lambda e, pa=pa, k=k, tt=tt, wv=wv: e.matmul(pa, lhsT=og3[:, k, tt * 128:(tt + 1) * 128], rhs=wv[:, k, :], start=(k == 0), stop=(k == nk - 1)),
                     reads=["og", "wx%d_%d" % (vi, cb % 2)], writes=["psb%d" % bank])
            sl = it % 3
            it += 1
            S.op("sp", lambda e, sl=sl, tt=tt, cb=cb: e.dma_start(out=xr[sl], in_=res_tiles(tt, cb)), reads=["DR_res"], writes=["xr%d" % sl], dma=tag + "xr%d" % sl)
            S.op("dve", lambda e, sl=sl, pa=pa: e.tensor_tensor(out=xo[sl], in0=pa, in1=xr[sl], op=ALU.add), reads=["psb%d" % bank, "xr%d" % sl], writes=["xo%d" % sl])
            S.op("sp", lambda e, sl=sl, tt=tt, cb=cb: e.dma_start(out=dst_tiles(tt, cb), in_=xo[sl]), reads=["xo%d" % sl], writes=["DR_dst"], dma=tag + "xo%d" % sl)
    S.barrier()


def phase4(C):
    for b in range(NB):
        def res(tt, cb, b=b):
            if tt < 16:
                return C.dram["x"][b, tt * 128:(tt + 1) * 128, cb * 512:(cb + 1) * 512]
            return C.dram["ctx"][b, (tt - 16) * 128:(tt - 15) * 128, cb * 512:(cb + 1) * 512]

        def dst(tt, cb, b=b):
            return C.dram["X1"][b, tt * 128:(tt + 1) * 128, cb * 512:(cb + 1) * 512]
        out_proj(C, b, 0, C.dram["OG0"][b], 16, C.dram["ab_w_out"], res, dst, list(range(NTT)), "p4")


HP = 2052 + 260


def phase5(C):
    for b in range(NB):
        _phase5_b(C, b)


def _phase5_b(C, b):
    nc, S, A = C.nc, C.S, C.A
    A.reset()
    w_in = C.dram["dn_w_in"]
    X1, F1, BA = C.dram["X1"], C.dram["F1"], C.dram["BA"]
    mv = load_modvecs(C, 1, b, C.dram["dn_norm"], "p5m%d" % b)
    xmT = A.alloc(16 * T, BF16)
    xm3 = _v3(xmT, 16)
    xkey = "xmT"
    mark = A.off
    tiles = [X1[b, tt * 128:(tt + 1) * 128, :] for tt in range(NTT)]
    build_xmT(C, xmT, xkey, tiles, mv, "p5b")
    S.barrier()
    A.off = mark
    wr = [A.alloc(16 * 512, BF16) for _ in range(2)]
    stg = [A.alloc(T, BF16) for _ in range(3)]
    hp = [A.alloc(HP, BF16) for _ in range(2)]
    dg = [A.alloc(5 * 128, BF16) for _ in range(2)]
    sT = [A.alloc(512, BF16) for _ in range(3)]
    sq = [A.alloc(512, BF16) for _ in range(2)]
    lnv = [A.alloc(512) for _ in range(2)]
    rst = [A.alloc(512) for _ in range(2)]
    bast = A.alloc(NTT * 128)
    cw3 = C.cw.rearrange("p (blk j) -> p blk j", j=5)
    for i in range(2):
        S.op("pool", lambda e, i=i: e.memset(hp[i], 0.0), writes=["hp%d" % i])
    segs = [("q", i * 512, 512) for i in range(4)] + [("k", 2048 + i * 512, 512) for i in range(4)] + [("v", 4096 + i * 512, 512) for i in range(8)] + \
           [("z", 8192 + i * 512, 512) for i in range(8)] + [("ba", 12288, 128)]
    cnt = dict(st=0, hp=0, c=0, n=0, s=0)
    for wi, (nm, c0, ncw) in enumerate(segs):
        slot = wi % 2
        w3 = _v3(wr[slot], 16)[:, :, 0:ncw]
        wkey = "p5w%d" % slot
        S.op("pool", lambda e, w3=w3, c0=c0, ncw=ncw: e.dma_start(out=w3, in_=w_in[:, c0:c0 + ncw].rearrange("(k p) c -> p k c", p=128)),
             writes=[wkey], dma=wkey)
        if nm == "ba":
            b3 = _v3(bast, NTT)
            for tt in range(NTT):
                bank = 4 + (C.pcnt % 4)
                C.pcnt += 1
                pa = C.ps[:, bank * 512:bank * 512 + 128]
                for k in range(16):
                    S.op("pe", lambda e, pa=pa, k=k, tt=tt, w3=w3: e.matmul(pa, lhsT=xm3[:, k, tt * 128:(tt + 1) * 128], rhs=w3[:, k, :], start=(k == 0), stop=(k == 15)),
                         reads=[wkey, xkey + str(tt)], writes=["psb%d" % bank])
                S.op("dve", lambda e, pa=pa, tt=tt: e.tensor_copy(out=b3[:, tt, :], in_=pa), reads=["psb%d" % bank], writes=["bast"])
            S.op("sp", lambda e: e.dma_start(out=BA[b].rearrange("(tt p) c -> p tt c", p=128), in_=b3), reads=["bast"], writes=["BA"], dma="bast")
            continue

        def evac(sb, gi, t0, tn, pa, pk, m, nm=nm, c0=c0):
            if nm == "z":
                if gi == 0:
                    cnt["st"] += 1
                si = cnt["st"] % 3
                sg, sk = stg[si], "p5st%d" % si
                S.op("act", lambda e: e.activation(out=sg[:, t0:t0 + tn], in_=pa, func=AF.Silu), reads=[pk], writes=[sk])
                if gi == 4:
                    r0 = c0 + sb * 128
                    S.op("sp", lambda e: e.dma_start(out=F1[b, r0:r0 + 128, :], in_=sg), reads=[sk], writes=["F1"], dma=sk)
                return
            if gi == 0:
                cnt["hp"] += 1
            hi = cnt["hp"] % 2
            hb_, hk_ = hp[hi], "hp%d" % hi
            off = 2 + t0 if gi < 4 else 2052 + 2
            if gi % 2 == 0:
                S.op("act", lambda e: e.activation(out=hb_[:, off:off + tn], in_=pa, func=AF.Copy), reads=[pk], writes=[hk_])
            else:
                S.op("dve", lambda e: e.tensor_copy(out=hb_[:, off:off + tn], in_=pa), reads=[pk], writes=[hk_])
            if gi < 4:
                return
            blk = (c0 + sb * 128) // 128
            di = cnt["hp"] % 2
            d3 = _v3(dg[di], 5)
            for j in range(5):
                S.op("pool", lambda e, j=j: e.tensor_scalar(out=d3[:, j, :], in0=C.ident, scalar1=cw3[:, blk, j:j + 1], scalar2=None, op0=ALU.mult),
                     reads=["ident", "cw"], writes=["dg%d" % di])
            cnt["st"] += 1
            si = cnt["st"] % 3
            sg, sk = stg[si], "p5st%d" % si
            outs = []
            for g2, (u0, un) in enumerate(TG):
                bank = cnt["c"] % 4
                cnt["c"] += 1
                pc = C.ps[:, bank * 512:bank * 512 + un]
                base = u0 if g2 < 4 else 2052
                for j in range(5):
                    S.op("pe", lambda e, pc=pc, j=j, base=base, un=un: e.matmul(pc, lhsT=d3[:, j, :], rhs=hb_[:, base + j:base + j + un], start=(j == 0), stop=(j == 4)),
                         reads=["dg%d" % di, hk_], writes=["psb%d" % bank])
                if nm == "v":
                    S.op("act", lambda e, pc=pc, u0=u0, un=un: e.activation(out=sg[:, u0:u0 + un], in_=pc, func=AF.Silu), reads=["psb%d" % bank], writes=[sk])
                else:
                    ssl = cnt["s"] % 3
                    cnt["s"] += 1
                    s_ = sT[ssl]
                    S.op("act", lambda e, pc=pc, s_=s_, un=un: e.activation(out=s_[:, 0:un], in_=pc, func=AF.Silu), reads=["psb%d" % bank], writes=["sT%d" % ssl])
                    outs.append((s_, "sT%d" % ssl, u0, un))
                    if len(outs) == 3 or g2 == 4:
                        pend = []
                        for (s2, s2k, v0, vn) in outs:
                            nsl = cnt["n"] % 2
                            cnt["n"] += 1
                            S.op("pool", lambda e, s2=s2, vn=vn, nsl=nsl: e.tensor_tensor(out=sq[nsl][:, 0:vn], in0=s2[:, 0:vn], in1=s2[:, 0:vn], op=ALU.mult),
                                 reads=[s2k], writes=["sq%d" % nsl])
                            bank2 = cnt["c"] % 4
                            cnt["c"] += 1
                            p3 = C.ps[:, bank2 * 512:bank2 * 512 + vn]
                            S.op("pe", lambda e, p3=p3, nsl=nsl, vn=vn: e.matmul(p3, lhsT=C.ones, rhs=sq[nsl][:, 0:vn], start=True, stop=True),
                                 reads=["ones", "sq%d" % nsl], writes=["psb%d" % bank2])
                            S.op("act", lambda e, p3=p3, nsl=nsl, vn=vn: e.activation(out=lnv[nsl][:, 0:vn], in_=p3, func=AF.Ln, bias=C.eps_t[:, 0:1]),
                                 reads=["psb%d" % bank2], writes=["lnv%d" % nsl])
                            pend.append((s2, s2k, v0, vn, nsl))
                            if len(pend) == 2 or (s2 is outs[-1][0]):
                                for (s3_, s3k, w0, wn, ns2) in pend:
                                    bias_ap = C.lnq[:, 0:1] if nm == "q" else C.zero_t[:, 0:1]
                                    S.op("act", lambda e, ns2=ns2, wn=wn, bias_ap=bias_ap: e.activation(out=rst[ns2][:, 0:wn], in_=lnv[ns2][:, 0:wn], func=AF.Exp, scale=-0.5, bias=bias_ap),
                                         reads=["lnv%d" % ns2], writes=["rst%d" % ns2])
                                    S.op("dve", lambda e, s3_=s3_, w0=w0, wn=wn, ns2=ns2: e.tensor_tensor(out=sg[:, w0:w0 + wn], in0=s3_[:, 0:wn], in1=rst[ns2][:, 0:wn], op=ALU.mult),
                                         reads=[s3k, "rst%d" % ns2], writes=[sk])
                                pend = []
                        outs = []
            r0 = c0 + sb * 128
            S.op("sp", lambda e: e.dma_start(out=F1[b, r0:r0 + 128, :], in_=sg), reads=[sk], writes=["F1"], dma=sk)

        proj_fm(C, xm3, xkey, w3, wkey, ncw, evac, "p5")
    S.barrier()


FSEQ = [16, 17] + list(range(16))
BSEQ = [17, 16] + list(range(15, -1, -1))


def dn_level_masks():
    s_ = np.arange(128)[:, None]
    c_ = np.arange(128)[None, :]
    out = np.zeros((128, 7, 4, 2, 128), np.float32)
    for k in range(1, 8):
        h = 1 << (k - 1)
        same = (s_ // (2 * h)) == (c_ // (2 * h))
        ur = same & ((s_ % (2 * h)) < h) & ((c_ % (2 * h)) >= h)
        ll = ur.T
        for j in range(4):
            fwd = j < 2
            out[:, k - 1, j, 0, :] = -(ur if fwd else ll).astype(np.float32)
            out[:, k - 1, j, 1, :] = -(ll if fwd else ur).astype(np.float32)
    id8 = np.zeros((128, 4, 2, 128), np.float32)
    id8[:, :, :, :] = np.eye(128, dtype=np.float32)[:, None, None, :]
    return out.reshape(128, 7, 1024), id8.reshape(128, 1024)


def dn_level_masks2():
    lm, _ = dn_level_masks()
    lm = lm.reshape(128, 7, 4, 2, 128).copy()
    lm -= np.eye(128, dtype=np.float32)[:, None, None, None, :]
    return np.ascontiguousarray(lm[:, :, 0::2, :, :]).reshape(128, 7, 512)


def dn_masks():
    s = np.arange(128)[:, None]
    c = np.arange(128)[None, :]
    incl = np.stack([(s <= c), (s <= c), (s >= c), (s >= c)], 0).astype(np.float32)
    strict = np.stack([(s < c), (s < c), (s > c), (s > c)], 0).astype(np.float32)
    ident4 = np.stack([np.eye(128, dtype=np.float32)] * 4, 0)
    return incl.transpose(1, 0, 2).copy(), strict.transpose(1, 0, 2).copy(), ident4.transpose(1, 0, 2).copy()


def phase6(C):
    for b in getattr(C, "p6_batches", range(NB)):
        _phase6_b(C, b)


def dump(C, name, ap, readkeys):
    if not getattr(C, "debug", False):
        return
    t = C.nc.dram_tensor("dbg_" + name, list(ap.shape), ap.dtype, kind="ExternalOutput").ap()
    if ap.shape[1] * (4 if ap.dtype == F32 else 2) > 2048:
        C.S.op("sp", lambda e: e.dma_start(out=t, in_=ap), reads=readkeys, writes=["DR_dbg"], dma=1)
        return
    if not hasattr(C, "dbg_stage"):
        C.dbg_stage = C.es.enter_context(C.nc.sbuf_tensor("dbgst", [128, 512], F32))
    st = C.dbg_stage[:, 0:ap.shape[1]] if ap.dtype == F32 else C.dbg_stage[:, 0:(ap.shape[1] + 1) // 2].bitcast(BF16)[:, 0:ap.shape[1]]
    C.S.op("dve", lambda e: e.tensor_copy(out=st, in_=ap), reads=readkeys, writes=["dbgst"])
    C.S.op("sp", lambda e: e.dma_start(out=t, in_=st), reads=["dbgst"], writes=["DR_dbg"], dma=1)


def _phase6_b(C, b):
    nc, S, A = C.nc, C.S, C.A
    A.reset()
    F1, BA, OG1 = C.dram["F1"], C.dram["BA"], C.dram["OG1"]
    mincl = A.alloc(512)
    mstr = A.alloc(512)
    ones_f = A.alloc(128)
    onorm = A.alloc(1)
    S.op("sp", lambda e: e.dma_start(out=_v3(mincl, 4), in_=C.dram["dn_mincl"]), writes=["mincl"], dma=1)
    S.op("sp", lambda e: e.dma_start(out=_v3(mstr, 4), in_=C.dram["dn_mstrict"]), writes=["mstr"], dma=1)
    S.op("sp", lambda e: e.dma_start(out=onorm, in_=C.dram["dn_o_norm"].rearrange("(p o) -> p o", o=1), allow_slow_non_contiguous=True), writes=["onorm"], dma=1)
    S.op("pool", lambda e: e.memset(ones_f, 1.0), writes=["ones_f"])
    mincl3, mstr3 = _v3(mincl, 4), _v3(mstr, 4)
    lmask = A.alloc(7 * 512, BF16)
    lm4 = lmask.rearrange("p (k d x) -> p k d x", k=7, d=2)
    S.op("sp", lambda e: e.dma_start(out=_v3(lmask, 7), in_=C.dram["dn_lmask2"]), writes=["lmask"], dma=1)
    beta = A.alloc(NTT * 64)
    gg = A.alloc(NTT * 64)
    mark6 = A.off
    ba = A.alloc(NTT * 128)
    ba3 = _v3(ba, NTT)
    S.op("sp", lambda e: e.dma_start(out=ba3, in_=BA[b].rearrange("(tt p) c -> p tt c", p=128)), reads=["BA"], writes=["ba"], dma=1)
    tA = A.alloc(NTT * 64)
    tB = A.alloc(NTT * 64)
    alog = A.alloc(64)
    dtb = A.alloc(64)
    one_t = A.alloc(1)
    S.op("pool", lambda e: e.memset(one_t, 1.0), writes=["one_t"])
    S.op("sp", lambda e: e.dma_start(out=alog, in_=C.dram["dn_a_log"].rearrange("d h -> (d h)").unsqueeze(0).to_broadcast([128, 64])), writes=["alog"], dma=1)
    S.op("sp", lambda e: e.dma_start(out=dtb, in_=C.dram["dn_dt_bias"].rearrange("d h -> (d h)").unsqueeze(0).to_broadcast([128, 64])), writes=["dtb"], dma=1)
    beta3, gg3, tA3, tB3 = _v3(beta, NTT), _v3(gg, NTT), _v3(tA, NTT), _v3(tB, NTT)
    S.op("act", lambda e: e.activation(out=tA3, in_=ba3[:, :, 0:64], func=AF.Exp, scale=-1.0), reads=["ba"], writes=["tA"])
    S.op("dve", lambda e: e.tensor_scalar(out=tA, in0=tA, scalar1=1.0, scalar2=None, op0=ALU.add), reads=["tA"], writes=["tA"])
    S.op("dve", lambda e: e.reciprocal(out=beta, in_=tA), reads=["tA"], writes=["beta"])
    S.op("dve", lambda e: e.tensor_tensor(out=tB3, in0=ba3[:, :, 64:128], in1=dtb.unsqueeze(1).to_broadcast([128, NTT, 64]), op=ALU.add), reads=["ba", "dtb"], writes=["tB"])
    S.op("dve", lambda e: e.scalar_tensor_tensor(out=tA, in0=tB, scalar=-1.0, in1=tB, op0=ALU.mult, op1=ALU.max), reads=["tB", "beta"], writes=["tA"])
    S.op("act", lambda e: e.activation(out=tA, in_=tA, func=AF.Exp, scale=-1.0), reads=["tA"], writes=["tA"])
    S.op("act", lambda e: e.activation(out=tA, in_=tA, func=AF.Ln, bias=one_t[:, 0:1]), reads=["tA", "one_t"], writes=["tA"])
    S.op("dve", lambda e: e.scalar_tensor_tensor(out=tB, in0=tB, scalar=0.0, in1=tA, op0=ALU.max, op1=ALU.add), reads=["tA", "tB"], writes=["tB"])
    S.op("act", lambda e: e.activation(out=alog, in_=alog, func=AF.Exp), reads=["alog"], writes=["alog"])
    S.op("dve", lambda e: e.scalar_tensor_tensor(out=gg3, in0=tB3, scalar=-1.0, in1=alog.unsqueeze(1).to_broadcast([128, NTT, 64]), op0=ALU.mult, op1=ALU.mult),
         reads=["tB", "alog"], writes=["gg"])
    S.barrier()
    A.off = mark6
    SHARED = {"gg", "beta", "mincl", "mstr", "lmask", "ident", "ones", "onorm", "ones_f", "eps", "zero", "lnq"}
    S0 = S

    class _SlotSched:
        def __init__(self, si):
            self.si = si

        def op(self, eng, fn, reads=(), writes=(), dma=None):
            f = lambda k: k if (k in SHARED or k in Sched.DRAMKEYS or k.startswith("DR_")) else "s%d_%s" % (self.si, k)
            return S0.op(eng, fn, reads=[f(k) for k in reads], writes=[f(k) for k in writes], dma=dma)

    def run_slot(si, head_list):
        S = _SlotSched(si)
        hbufs = [dict(q=A.alloc(T, BF16), k=A.alloc(T, BF16), v=A.alloc(2 * T, BF16))]
        ktok = A.alloc(NTT * 128, BF16)
        vtok = A.alloc(NTT * 256, BF16)
        oacc = A.alloc(2 * SEQ)
        o3 = _v3(oacc, 2)
        szb1 = A.alloc(SEQ, BF16)
        def mk():
            dec_ = A.alloc(512)
            tmp_ = A.alloc(512)
            tmp2_ = A.alloc(512)
            return dict(grep=A.alloc(512), d1=dec_, dec=dec_, gam=A.alloc(512), bm=A.alloc(512), tmp=tmp_, tmp2=tmp2_, t3=tmp_, t4=tmp2_,
                        x12=A.alloc(12), e12=A.alloc(12), negb=A.alloc(4), xn=A.alloc(1024, BF16), rt=A.alloc(1024, BF16), yy=A.alloc(1024, BF16),
                        xb=dec_, intra=A.alloc(512, BF16), gq=A.alloc(512, BF16), kd=A.alloc(512, BF16), vd=A.alloc(512, BF16), vn=A.alloc(512, BF16))
        stp = [mk()]
        S4 = A.alloc(512)
        S4b = A.alloc(512, BF16)
        sqb = [A.alloc(512, BF16)] * 2
        lnv = [A.alloc(512)] * 2
        rst = lnv
        osum = [A.alloc(512) for _ in range(2)]
        ps = C.ps
        pbase = si * 2048
        kG = kS1 = "psb%d" % (pbase // 512)
        kAB = kS2 = "psb%d" % (pbase // 512 + 1)
        kM = kT = kC = kN = "psM%d" % (pbase // 512)
        psG = ps[:, pbase:pbase + 512]
        psG3 = _v3(psG, 4)
        psS1 = psG
        psA = ps[:, pbase + 512:pbase + 768]
        psB = ps[:, pbase + 768:pbase + 1024]
        psS2 = ps[:, pbase + 512:pbase + 1024]
        psM = ps[:, pbase + 1024:pbase + 2048]
        psM3 = _v3(psM, 4)
        psT = ps[:, pbase + 1024:pbase + 1536]
        psTb = psT.bitcast(BF16)
        psC = ps[:, pbase + 1536:pbase + 1540]
        psN = psT

        for g in head_list:
            H = hbufs[0]
            hk = "h6"
            S.op("sp", lambda e, H=H, g=g: e.dma_start(out=H["q"], in_=F1[b, g * 128:(g + 1) * 128, :]), reads=["F1"], writes=[hk + "q"], dma=1)
            S.op("sp", lambda e, H=H, g=g: e.dma_start(out=H["k"], in_=F1[b, 2048 + g * 128:2048 + (g + 1) * 128, :]), reads=["F1"], writes=[hk + "k"], dma=1)
            S.op("sp", lambda e, H=H, g=g: e.dma_start(out=_v3(H["v"], 2), in_=F1[b, 4096 + 2 * g * 128:4096 + (2 * g + 2) * 128, :].rearrange("(v p) t -> p v t", p=128)),
                 reads=["F1"], writes=[hk + "v"], dma=1)
            QT, KT, VT3 = H["q"], H["k"], _v3(H["v"], 2)
            kt3 = _v3(ktok, NTT)
            vt4 = vtok.rearrange("p (t v d) -> p t v d", t=NTT, v=2)
            jobs = [("k", tt, 0) for tt in range(NTT)] + [("v", tt, vh) for tt in range(NTT) for vh in range(2)]
            groups = [jobs[0:8], jobs[8:16], jobs[16:18]] + [jobs[18 + i:18 + i + 8] for i in range(0, 36, 8)]
            for j0, grp in enumerate(groups):
                yield
                j0 = j0 * 8
                for i, (kind, tt, vh) in enumerate(grp):
                    src = KT[:, tt * 128:(tt + 1) * 128] if kind == "k" else VT3[:, vh, tt * 128:(tt + 1) * 128]
                    S.op("pe", lambda e, i=i, src=src: e.transpose(out=psTb[:, i * 128:(i + 1) * 128], in_=src, identity=C.ident),
                         reads=[hk + "k", hk + "v", "ident"], writes=[kT])
                kind0, tt0, vh0 = grp[0]
                n = len(grp)
                if kind0 == "k":
                    dst = ktok[:, tt0 * 128:(tt0 + n) * 128]
                    dk_ = "ktok"
                else:
                    dst = vtok[:, (tt0 * 2 + vh0) * 128:(tt0 * 2 + vh0 + n) * 128]
                    dk_ = "vtok"
                if (j0 // 8) % 2 == 0:
                    S.op("act", lambda e, dst=dst, n=n: e.activation(out=dst, in_=psTb[:, 0:n * 128], func=AF.Copy), reads=[kT], writes=[dk_])
                else:
                    S.op("dve", lambda e, dst=dst, n=n: e.tensor_copy(out=dst, in_=psTb[:, 0:n * 128]), reads=[kT], writes=[dk_])
            S.op("pool", lambda e: e.memset(S4, 0.0), writes=["S4"])
            S.op("pool", lambda e: e.memset(S4b, 0.0), writes=["S4b"])
            S43, S4b3 = _v3(S4, 4), _v3(S4b, 4)
            c0 = 2 * g
            def step(s, part, g=g, H=H, hk=hk, QT=QT, KT=KT, VT3=VT3, kt3=kt3, vt4=vt4, c0=c0, S43=S43, S4b3=S4b3):
                P = stp[0]
                pk = "st0"
                blks = (FSEQ[s], BSEQ[s])
                cols = [(d * 32 + c0) for d in range(2)]
                grep3, d13, dec3, gam3, bm3, tmp3, tmp23 = [_v3(P[n_], 4) for n_ in ("grep", "d1", "dec", "gam", "bm", "tmp", "tmp2")]
                xb3, intra3, gq3, kd3, vd3, vn3, t33, t43 = [_v3(P[n_], 4) for n_ in ("xb", "intra", "gq", "kd", "vd", "vn", "t3", "t4")]
                x12, e12, negb = P["x12"], P["e12"], P["negb"]
                RT = P["rt"].rearrange("p (j o c) -> p j o c", j=4, o=2)
                krt = pk + "rt"
                xn4 = P["xn"].rearrange("p (j o c) -> p j o c", j=4, o=2)
                kxn = pk + "xn"
                if part == "A":
                    yield
                    for d in range(2):
                        gs = gg3[:, blks[d], cols[d]:cols[d] + 2]
                        S.op("pool", lambda e, d=d, gs=gs: e.tensor_copy(out=grep3[:, 2 * d:2 * d + 2, :], in_=gs.unsqueeze(2).to_broadcast([128, 2, 128])),
                             reads=["gg"], writes=[pk + "grep"])
                        S.op("dve", lambda e, d=d: e.tensor_scalar(out=negb[:, 2 * d:2 * d + 2], in0=beta3[:, blks[d], cols[d]:cols[d] + 2], scalar1=-1.0, scalar2=None, op0=ALU.mult),
                             reads=["beta"], writes=[pk + "negb"])
                        S.op("pool", lambda e, d=d: e.tensor_tensor(out=bm3[:, 2 * d:2 * d + 2, :], in0=mstr3[:, 2 * d:2 * d + 2, :],
                                                                    in1=beta3[:, blks[d], cols[d]:cols[d] + 2].unsqueeze(2).to_broadcast([128, 2, 128]), op=ALU.mult),
                             reads=["beta", "mstr"], writes=[pk + "bm"])
                    yield
                    for j in range(4):
                        d = j // 2
                        S.op("pe", lambda e, j=j, d=d: e.matmul(psG3[:, j, :], lhsT=grep3[:, j, :], rhs=mincl3[:, 2 * d, :], start=True, stop=True),
                             reads=[pk + "grep", "mincl"], writes=[kG])
                    yield
                    for d in range(2):
                        S.op("pe", lambda e, d=d: e.matmul(psC[:, 2 * d:2 * d + 2], lhsT=mincl3[:, 2 * d, :], rhs=gg3[:, blks[d], cols[d]:cols[d] + 2], start=True, stop=True),
                             reads=["gg", "mincl"], writes=[kC])
                    yield
                    for d in range(2):
                        kb_ = KT[:, blks[d] * 128:(blks[d] + 1) * 128]
                        qb_ = QT[:, blks[d] * 128:(blks[d] + 1) * 128]
                        S.op("pe", lambda e, d=d, kb_=kb_: e.matmul(psA[:, d * 128:(d + 1) * 128], lhsT=kb_, rhs=kb_, start=True, stop=True), reads=[hk + "k"], writes=[kAB])
                        S.op("pe", lambda e, d=d, kb_=kb_, qb_=qb_: e.matmul(psB[:, d * 128:(d + 1) * 128], lhsT=kb_, rhs=qb_, start=True, stop=True), reads=[hk + "k", hk + "q"], writes=[kAB])
                    yield
                    S.op("dve", lambda e: e.tensor_copy(out=x12[:, 0:4], in_=psC), reads=[kC], writes=[pk + "x12"])
                    yield
                    for d in range(2):
                        last = 127 if d == 0 else 0
                        S.op("dve", lambda e, d=d, last=last: e.tensor_copy(out=x12[:, 8 + 2 * d:10 + 2 * d], in_=psG3[:, 2 * d:2 * d + 2, last]), reads=[kG], writes=[pk + "x12"])
                    yield
                    S.op("dve", lambda e: e.tensor_tensor(out=x12[:, 4:8], in0=x12[:, 8:12], in1=x12[:, 0:4], op=ALU.subtract), reads=[pk + "x12"], writes=[pk + "x12"])
                    yield
                    S.op("act", lambda e: e.activation(out=e12, in_=x12, func=AF.Exp), reads=[pk + "x12"], writes=[pk + "e12"])
                    yield
                    S.op("dve", lambda e: e.tensor_tensor(out=d13, in0=psG3, in1=x12[:, 0:4].unsqueeze(2).to_broadcast([128, 4, 128]), op=ALU.subtract),
                         reads=[kG, pk + "x12"], writes=[pk + "dec"])
                    yield
                    S.op("pool", lambda e: e.tensor_scalar(out=P["d1"], in0=P["d1"], scalar1=0.0, scalar2=-80.0, op0=ALU.min, op1=ALU.max), reads=[pk + "dec"], writes=[pk + "dec"])
                    yield
                    S.op("act", lambda e: e.activation(out=P["dec"], in_=P["d1"], func=AF.Exp), reads=[pk + "dec"], writes=[pk + "dec"])
                    yield
                    S.op("act", lambda e: e.activation(out=P["gam"], in_=psG, func=AF.Exp), reads=[kG], writes=[pk + "gam"])
                    dec4 = P["dec"].rearrange("p (d v c) -> p d v c", d=2, v=2)
                    psA4 = psA.rearrange("p (d c) -> p d c", d=2).unsqueeze(2).to_broadcast([128, 2, 2, 128])
                    psB4 = psB.rearrange("p (d c) -> p d c", d=2).unsqueeze(2).to_broadcast([128, 2, 2, 128])
                    yield
                    S.op("dve", lambda e, psA4=psA4, dec4=dec4: e.tensor_tensor(out=P["tmp"].rearrange("p (d v c) -> p d v c", d=2, v=2), in0=psA4, in1=dec4, op=ALU.mult),
                         reads=[kAB, pk + "dec"], writes=[pk + "tmp"])
                    yield
                    S.op("pool", lambda e: e.tensor_tensor(out=xn4[:, :, 0, :], in0=tmp3, in1=bm3, op=ALU.mult), reads=[pk + "tmp", pk + "bm"], writes=[kxn])
                    S.op("pool", lambda e: e.tensor_tensor(out=xn4[:, :, 0, :], in0=xn4[:, :, 0, :], in1=C.ident.unsqueeze(1).to_broadcast([128, 4, 128]), op=ALU.subtract),
                         reads=[kxn, "ident"], writes=[kxn])
                    yield
                    S.op("dve", lambda e, psB4=psB4, dec4=dec4: e.tensor_tensor(out=P["tmp2"].rearrange("p (d v c) -> p d v c", d=2, v=2), in0=psB4, in1=dec4, op=ALU.mult),
                         reads=[kAB, pk + "dec"], writes=[pk + "tmp2"])
                    yield
                    S.op("pool", lambda e: e.tensor_tensor(out=intra3, in0=tmp23, in1=mincl3, op=ALU.mult), reads=[pk + "tmp2", "mincl"], writes=[pk + "intra"])
                    yield
                    for d in range(2):
                        qb_ = QT[:, blks[d] * 128:(blks[d] + 1) * 128]
                        S.op("pool", lambda e, d=d, qb_=qb_: e.tensor_tensor(out=gq3[:, 2 * d:2 * d + 2, :], in0=qb_.unsqueeze(1).to_broadcast([128, 2, 128]), in1=gam3[:, 2 * d:2 * d + 2, :], op=ALU.mult),
                             reads=[hk + "q", pk + "gam"], writes=[pk + "gq"])
                        S.op("pool", lambda e, d=d: e.tensor_tensor(out=kd3[:, 2 * d:2 * d + 2, :], in0=kt3[:, blks[d], :].unsqueeze(1).to_broadcast([128, 2, 128]),
                                                                    in1=e12[:, 4 + 2 * d:6 + 2 * d].unsqueeze(2).to_broadcast([128, 2, 128]), op=ALU.mult),
                             reads=["ktok", pk + "e12"], writes=[pk + "kd"])
                    xn4 = P["xn"].rearrange("p (j o c) -> p j o c", j=4, o=2)
                    rt4 = P["rt"].rearrange("p (j o c) -> p j o c", j=4, o=2)
                    yy4 = P["yy"].rearrange("p (j o c) -> p j o c", j=4, o=2)
                    psM4 = psM.rearrange("p (j o c) -> p j o c", j=4, o=2)
                    kxn, krt_, kyy = pk + "xn", pk + "rt", pk + "yy"
                    yield
                    pass
                    yield
                    for j in range(4):
                        S.op("pe", lambda e, j=j: e.transpose(out=psTb[:, j * 128:(j + 1) * 128], in_=xn4[:, j, 0, :], identity=C.ident), reads=[kxn, "ident"], writes=[kT])
                    yield
                    S.op("act", lambda e: e.activation(out=xn4[:, :, 1, :], in_=_v3(psTb[:, 0:512], 4), func=AF.Copy), reads=[kT], writes=[kxn])
                    psTl = psG.bitcast(BF16)

                    def lmv(lv):
                        return lm4[:, lv, :, 0:128].unsqueeze(2).to_broadcast([128, 2, 2, 128])
                    h4 = lambda ap: ap.rearrange("p (d v) c -> p d v c", d=2)
                    yield
                    S.op("dve", lambda e: e.tensor_tensor(out=h4(rt4[:, :, 0, :]), in0=h4(xn4[:, :, 0, :]), in1=lmv(0), op=ALU.mult), reads=[kxn, "lmask"], writes=[krt_])
                    for lv in range(1, 7):
                        yield
                        for j in range(4):
                            S.op("pe", lambda e, j=j: e.matmul(psM4[:, j, 0, :], lhsT=xn4[:, j, 1, :], rhs=rt4[:, j, 0, :], start=True, stop=True), reads=[kxn, krt_], writes=[kM])
                        for j in range(4):
                            S.op("pe", lambda e, j=j: e.transpose(out=psTl[:, j * 128:(j + 1) * 128], in_=rt4[:, j, 0, :], identity=C.ident), reads=[krt_, "ident"], writes=[kG])
                        yield
                        S.op("dve", lambda e, lv=lv: e.tensor_tensor(out=h4(yy4[:, :, 0, :]), in0=h4(psM4[:, :, 0, :]), in1=lmv(lv), op=ALU.mult), reads=[kM, "lmask"], writes=[kyy])
                        S.op("act", lambda e: e.activation(out=rt4[:, :, 1, :], in_=_v3(psTl[:, 0:512], 4), func=AF.Copy), reads=[kG], writes=[pk + "tm"])
                        yield
                        for j in range(4):
                            S.op("pe", lambda e, j=j: e.matmul(psM4[:, j, 0, :], lhsT=rt4[:, j, 1, :], rhs=yy4[:, j, 0, :], start=True, stop=True), reads=[kyy, pk + "tm"], writes=[kM])
                        yield
                        S.op("act", lambda e: e.activation(out=rt4[:, :, 0, :], in_=psM4[:, :, 0, :], func=AF.Copy), reads=[kM], writes=[krt_])
                    RT = rt4
                    krt = krt_
                    return
                yield
                for j in range(4):
                    d = j // 2
                    kb_ = KT[:, blks[d] * 128:(blks[d] + 1) * 128]
                    S.op("pe", lambda e, j=j, kb_=kb_: e.matmul(psS1[:, j * 128:(j + 1) * 128], lhsT=kb_, rhs=S4b3[:, j, :], start=True, stop=True), reads=[hk + "k", "S4b"], writes=[kS1])
                yield
                S.op("dve", lambda e: e.tensor_tensor(out=t33, in0=_v3(psS1, 4), in1=e12[:, 0:4].unsqueeze(2).to_broadcast([128, 4, 128]), op=ALU.mult),
                     reads=[kS1, pk + "e12"], writes=[pk + "tmp"])
                yield
                for d in range(2):
                    S.op("pool" if d == 0 else "dve", lambda e, d=d: e.tensor_tensor(out=vd3[:, 2 * d:2 * d + 2, :], in0=t33[:, 2 * d:2 * d + 2, :], in1=vt4[:, blks[d], :, :], op=ALU.subtract),
                         reads=[pk + "tmp", "vtok"], writes=[pk + "vd"])
                yield
                for j in range(4):
                    S.op("pe", lambda e, j=j: e.matmul(psS2[:, j * 128:(j + 1) * 128], lhsT=RT[:, j, 0, :], rhs=vd3[:, j, :], start=True, stop=True), reads=[krt, pk + "vd"], writes=[kS2])
                yield
                S.op("dve", lambda e: e.tensor_tensor(out=vn3, in0=_v3(psS2, 4), in1=negb.unsqueeze(2).to_broadcast([128, 4, 128]), op=ALU.mult),
                     reads=[kS2, pk + "negb"], writes=[pk + "vn"])
                if s >= 2:
                    for j in range(4):
                        S.op("pe", lambda e, j=j: e.matmul(psS1[:, j * 128:(j + 1) * 128], lhsT=S4b3[:, j, :], rhs=gq3[:, j, :], start=True, stop=False), reads=["S4b", pk + "gq"], writes=[kS1])
                        S.op("pe", lambda e, j=j: e.matmul(psS1[:, j * 128:(j + 1) * 128], lhsT=vn3[:, j, :], rhs=intra3[:, j, :], start=False, stop=True), reads=[pk + "vn", pk + "intra"], writes=[kS1])
                    for d in range(2):
                        dstv = o3[:, :, blks[d] * 128:(blks[d] + 1) * 128]
                        srcv = _v3(psS1[:, d * 256:(d + 1) * 256], 2)
                        if s <= 9:
                            S.op("act", lambda e, dstv=dstv, srcv=srcv: e.activation(out=dstv, in_=srcv, func=AF.Copy), reads=[kS1], writes=["oacc"])
                        else:
                            S.op("dve", lambda e, dstv=dstv, srcv=srcv: e.tensor_tensor(out=dstv, in0=srcv, in1=dstv, op=ALU.add), reads=[kS1, "oacc"], writes=["oacc"])
                if s < NTT - 1:
                    for j in range(4):
                        S.op("pe", lambda e, j=j: e.matmul(psS2[:, j * 128:(j + 1) * 128], lhsT=kd3[:, j, :], rhs=vn3[:, j, :], start=True, stop=True), reads=[pk + "kd", pk + "vn"], writes=[kS2])
                    S.op("pool", lambda e: e.tensor_tensor(out=t43, in0=S43, in1=e12[:, 8:12].unsqueeze(2).to_broadcast([128, 4, 128]), op=ALU.mult),
                         reads=["S4", pk + "e12"], writes=[pk + "tmp2"])
                    S.op("dve", lambda e: e.tensor_tensor(out=S4, in0=psS2, in1=P["t4"], op=ALU.add), reads=[kS2, pk + "tmp2"], writes=["S4"])
                    S.op("act", lambda e: e.activation(out=S4b, in_=S4, func=AF.Copy), reads=["S4"], writes=["S4b"])
            nst_ = getattr(C, "p6_nsteps", NTT)
            for s_ in range(nst_):
                yield from step(s_, "A")
                yield from step(s_, "S")
            for vh in range(2):
                S.op("sp", lambda e, vh=vh, g=g: e.dma_start(out=szb1, in_=F1[b, 8192 + (2 * g + vh) * 128:8192 + (2 * g + vh + 1) * 128, 0:SEQ]),
                     reads=["F1"], writes=["szb1"], dma=1)
                for gi in range(4):
                    yield
                    t0 = gi * 512
                    sl = gi % 2
                    a_ = o3[:, vh, t0:t0 + 512]
                    S.op("pool", lambda e, a_=a_, sl=sl: e.tensor_tensor(out=sqb[sl], in0=a_, in1=a_, op=ALU.mult), reads=["oacc"], writes=["sqb6"])
                    S.op("pe", lambda e, sl=sl: e.matmul(psS1, lhsT=C.ones, rhs=sqb[sl], start=True, stop=True), reads=["ones", "sqb6"], writes=[kS1])
                    S.op("act", lambda e, sl=sl: e.activation(out=lnv[sl], in_=psS1, func=AF.Ln, scale=1.0 / 128, bias=C.eps_t[:, 0:1]), reads=[kS1], writes=["lnv6"])
                    S.op("act", lambda e, sl=sl: e.activation(out=rst[sl], in_=lnv[sl], func=AF.Exp, scale=-0.5), reads=["lnv6"], writes=["lnv6"])
                    S.op("dve", lambda e, sl=sl, a_=a_: e.scalar_tensor_tensor(out=osum[sl], in0=a_, scalar=onorm[:, 0:1], in1=rst[sl], op0=ALU.mult, op1=ALU.mult),
                         reads=["oacc", "lnv6", "onorm"], writes=["osum%d" % sl])
                    S.op("pool", lambda e, sl=sl, t0=t0: e.tensor_tensor(out=szb1[:, t0:t0 + 512], in0=osum[sl], in1=szb1[:, t0:t0 + 512], op=ALU.mult),
                         reads=["osum%d" % sl, "szb1"], writes=["szb1"])
                r0 = (2 * g + vh) * 128
                S.op("sp", lambda e, r0=r0: e.dma_start(out=OG1[b, r0:r0 + 128, :], in_=szb1), reads=["szb1"], writes=["OG1"], dma=1)

    heads_all = list(getattr(C, "p6_heads", range(16)))
    gens = [run_slot(0, heads_all[0::2]), run_slot(1, heads_all[1::2])]
    for _ in range(getattr(C, "p6_offset", 0)):
        try:
            next(gens[0])
        except StopIteration:
            gens.pop(0)
            break
    while gens:
        for g_ in list(gens):
            try:
                next(g_)
            except StopIteration:
                gens.remove(g_)
    S.barrier()


def phase7(C):
    nc, S, A = C.nc, C.S, C.A
    for b in range(NB):
        def res(tt, cb, b=b):
            return C.dram["X1"][b, tt * 128:(tt + 1) * 128, cb * 256:(cb + 1) * 256]

        def dst(tt, cb, b=b):
            return C.dram["X2"][b, tt * 128:(tt + 1) * 128, cb * 256:(cb + 1) * 256]
        out_proj(C, b, 1, C.dram["OG1"][b], 32, C.dram["dn_w_out"], res, dst, list(range(16)), "p7", cbw=256, ntok=SEQ)
    A.reset()
    fn = A.alloc(D)
    S.op("sp", lambda e: e.dma_start(out=fn, in_=C.dram["final_norm"].unsqueeze(0).to_broadcast([128, D])), writes=["fn"], dma=1)
    xr = [A.alloc(D) for _ in range(3)]
    xo = [A.alloc(D) for _ in range(3)]
    junk = A.alloc(D, BF16)
    st = [A.alloc(4) for _ in range(3)]
    it = 0
    for b in range(NB):
        for tt in range(16):
            sl = it % 3
            it += 1
            xt, xo_, s4 = xr[sl], xo[sl], st[sl]
            S.op("sp", lambda e, xt=xt, b=b, tt=tt: e.dma_start(out=xt, in_=C.dram["X2"][b, tt * 128:(tt + 1) * 128, :]), reads=["X2"], writes=["fxr%d" % sl], dma=1)
            S.op("act", lambda e, xt=xt, s4=s4: e.activation(out=junk, in_=xt, func=AF.Square, accum_out=s4[:, 0:1]), reads=["fxr%d" % sl], writes=["fjunk", "fst%d" % sl])
            S.op("act", lambda e, s4=s4: e.activation(out=s4[:, 1:2], in_=s4[:, 0:1], func=AF.Sqrt, scale=1.0 / D, bias=C.eps_t[:, 0:1]), reads=["fst%d" % sl], writes=["fst%d" % sl])
            S.op("dve", lambda e, s4=s4: e.reciprocal(out=s4[:, 2:3], in_=s4[:, 1:2]), reads=["fst%d" % sl], writes=["fst%d" % sl])
            S.op("dve", lambda e, xt=xt, xo_=xo_, s4=s4: e.scalar_tensor_tensor(out=xo_, in0=xt, scalar=s4[:, 2:3], in1=fn, op0=ALU.mult, op1=ALU.mult),
                 reads=["fxr%d" % sl, "fst%d" % sl, "fn"], writes=["fxo%d" % sl])
            S.op("sp", lambda e, xo_=xo_, b=b, tt=tt: e.dma_start(out=C.dram["OUT"][b, tt * 128:(tt + 1) * 128, :], in_=xo_), reads=["fxo%d" % sl], writes=["OUT"], dma=1)
    S.barrier()


NCORES = 8
_PHASES = (phase0, phase1, phase2, phase3, phase4, phase5, phase6, phase7)


def _host_inputs(inp):
    cos, sin = rope_tables()
    per_m, nt, mask, drow, dcol = na_geometry()
    rpb = np.asarray(inp["ab_rpb"][0], np.float32)
    nab = np.stack([rpb[h][drow, dcol] for h in range(8)], 0).astype(np.float32)
    mi, ms, id4 = dn_masks()
    lm, id8 = dn_level_masks()
    f = lambda a: np.ascontiguousarray(np.asarray(a, np.float32))
    shared = {
        "w_mod0": f(inp["ab_w_mod"][0]), "w_mod1": f(inp["dn_w_mod"][0]), "b_mod0": f(inp["ab_b_mod"][0]), "b_mod1": f(inp["dn_b_mod"][0]),
        "ab_norm": f(inp["ab_norm"][0]), "ab_w_in": f(inp["ab_w_in"][0]), "ab_w_qb": f(inp["ab_w_qb"][0]), "ab_w_kvb": f(inp["ab_w_kvb"][0]),
        "ab_q_norm": f(inp["ab_q_norm"][0]), "ab_kv_norm": f(inp["ab_kv_norm"][0]), "ab_w_out": f(inp["ab_w_out"][0]),
        "na_mask": mask, "na_bias": nab, "rope_cos": cos, "rope_sin": sin, "ident": np.eye(128, dtype=np.float32).astype(NPBF),
        "dn_norm": f(inp["dn_norm"][0]), "dn_w_in": f(inp["dn_w_in"][0]), "dn_conv": f(inp["dn_conv"][0]), "dn_a_log": f(inp["dn_a_log"][0]),
        "dn_dt_bias": f(inp["dn_dt_bias"][0]), "dn_o_norm": f(inp["dn_o_norm"][0]), "dn_w_out": f(inp["dn_w_out"][0]), "final_norm": f(inp["final_norm"]),
        "dn_mincl": mi, "dn_mstrict": ms, "dn_ident4": id4.astype(NPBF), "dn_lmask2": dn_level_masks2().astype(NPBF),
    }
    maps = []
    for i in range(NCORES):
        m = dict(shared)
        m["x"] = f(inp["x"][NB * i:NB * (i + 1)])
        m["ctx"] = f(inp["ctx"][NB * i:NB * (i + 1)])
        m["cvec"] = np.concatenate([f(inp["c"][NB * i:NB * (i + 1)]), f(inp["c_ctx"])[None]], 0)
        maps.append(m)
    return maps


_INTERNAL = {
    "mod": ([2, 3, 6144], F32), "F0": ([NB, 5184, T], BF16), "VA": ([NB, T, 1024], BF16), "M0": ([NB, 2560, T], BF16), "VM": ([NB, T, 1024], BF16),
    "OG0": ([NB, 2048, T], BF16), "X1": ([NB, T, D], F32), "F1": ([NB, 12288, T], BF16), "BA": ([NB, T, 128], F32), "OG1": ([NB, 4096, SEQ], BF16),
    "X2": ([NB, SEQ, D], F32),
}


def build_program(maps0, phases=_PHASES):
    nc = bass.Bass("TRN2", target_bir_lowering=False)
    with ExitStack() as es:
        dram = {}
        for nm, a in maps0.items():
            dram[nm] = nc.dram_tensor(nm, list(a.shape), BF16 if a.dtype == NPBF else F32, kind="ExternalInput").ap()
        for nm, (shape, dt_) in _INTERNAL.items():
            dram[nm] = nc.dram_tensor(nm, shape, dt_, kind="Internal").ap()
        dram["OUT"] = nc.dram_tensor("OUT", [NB, SEQ, D], F32, kind="ExternalOutput").ap()
        C = make_ctx(nc, es, dram)
        for p in phases:
            p(C)
        C.S.finalize()
    return nc


def kernel(**inputs):
    maps = _host_inputs(inputs)
    nc = build_program(maps[0])
    res = run_bass_kernel_spmd(nc, maps, core_ids=list(range(NCORES)))
    out = np.concatenate([np.asarray(r["OUT"], np.float32) for r in res.results], axis=0)
    return out
```
